# Optimizing a Trainium2 kernel written in Bass

```python
import jax, jax.numpy as jnp
from jax import lax
import numpy as np

D_MODEL = 1024
BATCH = 4
SEQ = 8192
DEPTH = 2

CTX_LEN = 256
GRID_W = 64

FOURIER_WIDTH = D_MODEL // 4
N_FOURIER_HEADS = 4
FOURIER_HEAD_DIM = FOURIER_WIDTH // N_FOURIER_HEADS

V_HEAD_DIM = 64
QK_NOPE_DIM = 64
QK_ROPE_DIM = 32
QK_HEAD_DIM = QK_NOPE_DIM + QK_ROPE_DIM
N_MLA_HEADS = (D_MODEL - FOURIER_WIDTH) // V_HEAD_DIM
MLA_WIDTH = N_MLA_HEADS * V_HEAD_DIM
Q_LORA_RANK = 384
KV_LORA_RANK = 128

MIX_WIDTH = FOURIER_WIDTH + MLA_WIDTH
IN_PROJ_WIDTH = FOURIER_WIDTH + Q_LORA_RANK + KV_LORA_RANK + QK_ROPE_DIM

D_FF = 2816
CONV_WIDTH = 3

ROPE_THETA = 10000.0
NORM_EPS = 1e-6
Q_BLOCK = 128
SOFTMAX_SCALE = QK_HEAD_DIM ** -0.5
N_MOD = 6

kernel_name = "hybrid_fourier_mla_convffn_dit"


def rms_norm(x, g):
    xf = x.astype(jnp.float32)
    y = xf * lax.rsqrt(jnp.mean(xf * xf, axis=-1, keepdims=True) + NORM_EPS)
    return (y * g.astype(jnp.float32)).astype(x.dtype)


def modulate(h, shift, scale):
    return h * (1 + scale) + shift


def axial_rope_tables(rows, dtype):
    row_ids = jnp.broadcast_to(jnp.arange(rows)[:, None], (rows, GRID_W)).reshape(-1).astype(jnp.float32)
    col_ids = jnp.broadcast_to(jnp.arange(GRID_W)[None, :], (rows, GRID_W)).reshape(-1).astype(jnp.float32)
    n_freq = QK_ROPE_DIM // 4
    inv_freq = ROPE_THETA ** (-jnp.arange(n_freq, dtype=jnp.float32) / n_freq)
    ang = jnp.concatenate([row_ids[:, None] * inv_freq, col_ids[:, None] * inv_freq], axis=-1)
    return jnp.cos(ang).astype(dtype), jnp.sin(ang).astype(dtype)


def apply_rope(x, cos, sin):
    x1, x2 = jnp.split(x, 2, axis=-1)
    return jnp.concatenate([x1 * cos - x2 * sin, x2 * cos + x1 * sin], axis=-1)


def fourier_mix(u, w_f, b_f):
    b, n, _ = u.shape
    uh = u.reshape(b, n, N_FOURIER_HEADS, FOURIER_HEAD_DIM).astype(jnp.float32)
    f = jnp.fft.fft2(uh, axes=(1, 3), norm="ortho").real
    f = f.reshape(b, n, FOURIER_WIDTH).astype(u.dtype)
    return f @ w_f + b_f


def mla_q(c_q, g_q, w_q_b, cos, sin):
    b, n, _ = c_q.shape
    q = (rms_norm(c_q, g_q) @ w_q_b).reshape(b, n, N_MLA_HEADS, QK_HEAD_DIM)
    q_nope, q_rope = q[..., :QK_NOPE_DIM], q[..., QK_NOPE_DIM:]
    if cos is not None:
        q_rope = apply_rope(q_rope, cos[:, None, :], sin[:, None, :])
    return q_nope, q_rope


def mla_kv(c_kv, k_rope, g_kv, w_kv_b, cos, sin):
    b, n, _ = c_kv.shape
    kv = (rms_norm(c_kv, g_kv) @ w_kv_b).reshape(b, n, N_MLA_HEADS, QK_NOPE_DIM + V_HEAD_DIM)
    k_nope, v = kv[..., :QK_NOPE_DIM], kv[..., QK_NOPE_DIM:]
    if cos is not None:
        k_rope = apply_rope(k_rope, cos, sin)
    return k_nope, k_rope, v


def mla_attend(q_nope, q_rope, k_nope, k_rope, v):
    s = jnp.einsum('bqhd,bkhd->bhqk', q_nope, k_nope) + jnp.einsum('bqhr,bkr->bhqk', q_rope, k_rope)
    p = jax.nn.softmax(s.astype(jnp.float32) * SOFTMAX_SCALE, axis=-1).astype(v.dtype)
    return jnp.einsum('bhqk,bkhd->bqhd', p, v)


def blocked_attention(q_nope, q_rope, k_nope, k_rope, v):
    b, n, h, _ = q_nope.shape
    nb = n // Q_BLOCK
    qn = q_nope.reshape(b, nb, Q_BLOCK, h, QK_NOPE_DIM).transpose(1, 0, 2, 3, 4)
    qr = q_rope.reshape(b, nb, Q_BLOCK, h, QK_ROPE_DIM).transpose(1, 0, 2, 3, 4)
    out = lax.map(lambda blk: mla_attend(blk[0], blk[1], k_nope, k_rope, v), (qn, qr))
    return out.transpose(1, 0, 2, 3, 4).reshape(b, n, h * V_HEAD_DIM)


def split_in_proj(p):
    f_in = p[..., :FOURIER_WIDTH]
    c_q = p[..., FOURIER_WIDTH:FOURIER_WIDTH + Q_LORA_RANK]
    c_kv = p[..., FOURIER_WIDTH + Q_LORA_RANK:FOURIER_WIDTH + Q_LORA_RANK + KV_LORA_RANK]
    k_rope = p[..., FOURIER_WIDTH + Q_LORA_RANK + KV_LORA_RANK:]
    return f_in, c_q, c_kv, k_rope


def conv_ffn(h, w_up, w_dw, b_dw, w_down):
    u = h @ w_up
    u = lax.conv_general_dilated(u, w_dw[:, None, :], window_strides=(1,),
                                 padding=((CONV_WIDTH // 2, CONV_WIDTH // 2),),
                                 dimension_numbers=('NWC', 'WIO', 'NWC'),
                                 feature_group_count=2 * D_FF) + b_dw
    gate, val = jnp.split(u, 2, axis=-1)
    return (jax.nn.silu(gate) * val) @ w_down


def setup_inputs(seed: int = 0) -> dict:
    key = jax.random.key(seed)
    ks = jax.random.split(key, 24)
    f32 = jnp.float32

    def nrm(k, shape, scale):
        return jax.random.normal(k, shape, f32) * scale

    def gain(k, shape):
        return 1.0 + 0.02 * jax.random.normal(k, shape, f32)

    L, D = DEPTH, D_MODEL
    return {
        "x": jax.random.normal(ks[0], (BATCH, SEQ, D), f32),
        "c": jax.random.normal(ks[1], (BATCH, D), f32),
        "ctx": jax.random.normal(ks[2], (BATCH, CTX_LEN, D), f32),
        "c_ctx": jax.random.normal(ks[3], (D,), f32),
        "w_ada": nrm(ks[4], (L, D, N_MOD * D), 0.5 * D ** -0.5),
        "b_ada": nrm(ks[5], (L, N_MOD * D), 0.02),
        "g_mix": gain(ks[6], (L, D)),
        "w_in": nrm(ks[7], (L, D, IN_PROJ_WIDTH), D ** -0.5),
        "w_fourier": nrm(ks[8], (L, FOURIER_WIDTH, FOURIER_WIDTH), FOURIER_WIDTH ** -0.5),
        "b_fourier": nrm(ks[9], (L, FOURIER_WIDTH), 0.02),
        "g_q_a": gain(ks[10], (L, Q_LORA_RANK)),
        "w_q_b": nrm(ks[11], (L, Q_LORA_RANK, N_MLA_HEADS * QK_HEAD_DIM), Q_LORA_RANK ** -0.5),
        "g_kv_a": gain(ks[12], (L, KV_LORA_RANK)),
        "w_kv_b": nrm(ks[13], (L, KV_LORA_RANK, N_MLA_HEADS * (QK_NOPE_DIM + V_HEAD_DIM)), KV_LORA_RANK ** -0.5),
        "w_out": nrm(ks[14], (L, MIX_WIDTH, D), MIX_WIDTH ** -0.5),
        "g_ffn": gain(ks[15], (L, D)),
        "w_up": nrm(ks[16], (L, D, 2 * D_FF), D ** -0.5),
        "w_dw": nrm(ks[17], (L, CONV_WIDTH, 2 * D_FF), CONV_WIDTH ** -0.5),
        "b_dw": nrm(ks[18], (L, 2 * D_FF), 0.02),
        "w_down": nrm(ks[19], (L, D_FF, D), D_FF ** -0.5),
        "g_final": gain(ks[20], (D,)),
    }


def reference(x, c, ctx, c_ctx, w_ada, b_ada, g_mix, w_in, w_fourier, b_fourier,
              g_q_a, w_q_b, g_kv_a, w_kv_b, w_out, g_ffn, w_up, w_dw, b_dw, w_down, g_final):
    b, n_lat, _ = x.shape
    ROWS = n_lat // GRID_W
    cos, sin = axial_rope_tables(ROWS, x.dtype)
    silu_c = jax.nn.silu(c)
    silu_cc = jax.nn.silu(c_ctx)

    for l in range(DEPTH):
        last = l == DEPTH - 1
        mod_lat = (silu_c @ w_ada[l] + b_ada[l])[:, None, :]
        mod_ctx = silu_cc @ w_ada[l] + b_ada[l]
        sh1, sc1, gt1, sh2, sc2, gt2 = jnp.split(mod_lat, N_MOD, axis=-1)
        csh1, csc1, cgt1, csh2, csc2, cgt2 = jnp.split(mod_ctx, N_MOD, axis=-1)

        h = modulate(rms_norm(x, g_mix[l]), sh1, sc1)
        hc = modulate(rms_norm(ctx, g_mix[l]), csh1, csc1)
        f_in, c_q, c_kv, k_rope_raw = split_in_proj(h @ w_in[l])
        f_in_c, c_q_c, c_kv_c, k_rope_raw_c = split_in_proj(hc @ w_in[l])

        kn_c, kr_c, v_c = mla_kv(c_kv_c, k_rope_raw_c, g_kv_a[l], w_kv_b[l], None, None)
        kn_l, kr_l, v_l = mla_kv(c_kv, k_rope_raw, g_kv_a[l], w_kv_b[l], cos, sin)
        qn_l, qr_l = mla_q(c_q, g_q_a[l], w_q_b[l], cos, sin)

        k_nope_all = jnp.concatenate([kn_l, kn_c], axis=1)
        k_rope_all = jnp.concatenate([kr_l, kr_c], axis=1)
        v_all = jnp.concatenate([v_l, v_c], axis=1)
        att_l = blocked_attention(qn_l, qr_l, k_nope_all, k_rope_all, v_all)
        four_l = fourier_mix(f_in, w_fourier[l], b_fourier[l])
        mix_l = jnp.concatenate([four_l, att_l], axis=-1) @ w_out[l]
        x = x + gt1 * mix_l

        x = x + gt2 * conv_ffn(modulate(rms_norm(x, g_ffn[l]), sh2, sc2),
                               w_up[l], w_dw[l], b_dw[l], w_down[l])

        if not last:
            qn_c, qr_c = mla_q(c_q_c, g_q_a[l], w_q_b[l], None, None)
            att_c = mla_attend(qn_c, qr_c, kn_c, kr_c, v_c).reshape(b, -1, MLA_WIDTH)
            four_c = fourier_mix(f_in_c, w_fourier[l], b_fourier[l])
            mix_c = jnp.concatenate([four_c, att_c], axis=-1) @ w_out[l]
            ctx = ctx + cgt1 * mix_c
            ctx = ctx + cgt2 * conv_ffn(modulate(rms_norm(ctx, g_ffn[l]), csh2, csc2),
                                        w_up[l], w_dw[l], b_dw[l], w_down[l])

    return rms_norm(x, g_final)
```

```python
import numpy as np
import ml_dtypes
from contextlib import ExitStack
import concourse.bass as bass
import concourse.mybir as mybir
from concourse.bass_utils import run_bass_kernel_spmd

F32 = mybir.dt.float32
BF16 = mybir.dt.bfloat16
ACTF = mybir.ActivationFunctionType
ALU = mybir.AluOpType
bf = ml_dtypes.bfloat16

D = 1024
NCTX = 256
DFF = 2816
NCH = 22
EPS = 1e-6
SCALE = 96 ** -0.5


class Buf:
    def __init__(self, t, name, multi=False):
        self.t = t
        self.name = name
        self.multi = multi
        self.excl = False
        self.w = {}
        self.r = {}

    def __getitem__(self, k):
        return self.t[k]


def _merge(d, ev):
    k = id(ev[0])
    if k not in d or d[k][1] < ev[1]:
        d[k] = ev


class Eng:
    def __init__(self, fw, e, name, is_pe=False):
        self.e = e
        self.name = name
        self.is_pe = is_pe
        self.sem = fw.new_sem("s_" + name)
        self.cnt = 0
        self.seen = {}
        self.nwait = 0
        self.nins = 0

    def wait(self, ev):
        sem, val = ev
        if self.seen.get(id(sem), 0) >= val:
            return
        self.e.wait_ge(sem, val)
        self.seen[id(sem)] = val
        self.nwait += 1


class FW:
    def __init__(self, nc, stack, n_dma_sems=14):
        self.nc = nc
        self.stack = stack
        self.pe = Eng(self, nc.tensor, "pe", is_pe=True)
        self.act = Eng(self, nc.scalar, "act")
        self.dve = Eng(self, nc.vector, "dve")
        self.pool = Eng(self, nc.gpsimd, "pool")
        self.sp = Eng(self, nc.sync, "sp")
        self.dq = {}
        for q in (self.sp, self.pool, self.act):
            self.dq[q.name] = dict(sems=[self.new_sem(f"d_{q.name}{i}") for i in range(n_dma_sems)],
                                   vals=[0] * n_dma_sems, idx=0)
        self.uid = 0
        self.stopped = False

    def new_sem(self, name):
        return self.stack.enter_context(self.nc.semaphore(name))

    def sb(self, shape, dt, name, stack=None, multi=False):
        self.uid += 1
        t = (stack or self.stack).enter_context(self.nc.sbuf_tensor(f"{name}_{self.uid}", list(shape), dt))
        return Buf(t, name, multi)

    def ps(self, shape, dt, name, stack=None):
        self.uid += 1
        t = (stack or self.stack).enter_context(self.nc.psum_tensor(f"{name}_{self.uid}", list(shape), dt))
        b = Buf(t, name)
        b.excl = True
        return b

    def dram(self, name, shape, dt, multi=True):
        t = self.nc.dram_tensor(name, list(shape), dt, kind="Internal")
        return Buf(t.ap(), name, multi)

    def _deps(self, eng, reads, writes):
        evs = {}
        for b in reads:
            for ev in b.w.values():
                evs[(id(ev[0]), ev[1])] = ev
            if b.excl:
                for ev in b.r.values():
                    if ev[0] is not eng.sem:
                        evs[(id(ev[0]), ev[1])] = ev
        for b in writes:
            if not b.multi:
                for ev in b.w.values():
                    evs[(id(ev[0]), ev[1])] = ev
            for ev in b.r.values():
                evs[(id(ev[0]), ev[1])] = ev
        for ev in evs.values():
            if eng.is_pe and ev[0] is eng.sem:
                continue
            eng.wait(ev)

    def _record(self, ev, reads, writes):
        for b in reads:
            _merge(b.r, ev)
        for b in writes:
            if b.multi:
                _merge(b.w, ev)
            else:
                b.w = {id(ev[0]): ev}
                b.r = {}

    def op(self, eng, fn, reads=(), writes=()):
        if self.stopped:
            return None
        self._deps(eng, reads, writes)
        ins = fn(eng.e)
        eng.cnt += 1
        eng.nins += 1
        ins.then_inc(eng.sem, 1)
        self._record((eng.sem, eng.cnt), reads, writes)
        return ins

    def dma(self, q, out, in_, reads=(), writes=(), **kw):
        if self.stopped:
            return None
        self._deps(q, reads, writes)
        d = self.dq[q.name]
        i = d["idx"] % len(d["sems"])
        d["idx"] += 1
        if d["vals"][i] > 0:
            q.wait((d["sems"][i], d["vals"][i]))
        ins = q.e.dma_start(out=out, in_=in_, **kw)
        d["vals"][i] += 16
        ins.then_inc(d["sems"][i], 16)
        q.nins += 1
        self._record((d["sems"][i], d["vals"][i]), reads, writes)
        return ins

    def barrier(self):
        if self.stopped:
            return
        engs = (self.pe, self.act, self.dve, self.pool, self.sp)
        evs = [(e.sem, e.cnt) for e in engs if e.cnt > 0]
        for d in self.dq.values():
            for sm, v in zip(d["sems"], d["vals"]):
                if v > 0:
                    evs.append((sm, v))
        for e in engs:
            for ev in evs:
                if ev[0] is e.sem:
                    continue
                e.wait(ev)

    def finish(self, bufs):
        for b in bufs:
            for ev in list(b.w.values()):
                self.sp.wait(ev)


class Ring:
    def __init__(self, bufs):
        self.bufs = bufs
        self.i = 0

    def next(self):
        b = self.bufs[self.i % len(self.bufs)]
        self.i += 1
        return b


class _Stop(Exception):
    pass


def build_program(NLAT=8192, depth=2, dbg=False, stop_at=None):
    NT_L = NLAT // 128
    NT_C = NCTX // 128
    NT = NT_L + NT_C
    NTOK = NLAT + NCTX
    R = NLAT // 64
    assert R <= 128 and R % 16 == 0
    QB = 512 if NLAT % 512 == 0 else 128

    nc = bass.Bass("TRN2", target_bir_lowering=False)

    def din(name, shape, dt=F32):
        return Buf(nc.dram_tensor(name, list(shape), dt, kind="ExternalInput").ap(), name)

    x_in = din("x", [NLAT, D])
    ctx_in = din("ctx", [NCTX, D])
    cvec = din("cvec", [128, 8, 2])
    w_ada = din("w_ada", [depth, D, 6 * D])
    b_ada2 = din("b_ada2", [depth, 2, 6 * D])
    gmix2 = din("gmix2", [depth, 2, D])
    gffn2 = din("gffn2", [depth, 2, D])
    gfin = din("gfin", [128, D])
    w_in = din("w_in", [depth, D, 800])
    w_four = din("w_four", [depth, 256, 256])
    b_four = din("b_four", [depth, 128, 2, 256])
    gq = din("gq", [depth, 128, 3])
    w_qb = din("w_qb", [depth, 384, 1152])
    gkv = din("gkv", [depth, 128, 1])
    w_kvb = din("w_kvb", [depth, 128, 1536])
    w_out = din("w_out", [depth, D, D])
    w_up = din("w_up", [depth, D, 2 * DFF])
    wdw = din("wdw", [depth, 128, 2 * NCH, 3])
    bdw = din("bdw", [depth, 128, 2 * NCH])
    w_down = din("w_down", [depth, DFF, D])
    ident_d = din("ident", [128, 128], BF16)
    sel_d = din("sel", [2, 256])
    c4s4_d = din("c4s4", [128, 2, 2, 256], BF16)
    crs_d = din("crs", [R, 3, R], BF16)
    tw_d = din("tw", [R, 2, 64])
    c64_d = din("c64", [64, 2, 64], BF16)
    c256_d = din("c256", [128, 2, 2, 256], BF16)
    rope_d = din("rope", [128, NT_L, 2, 32])
    out_d = Buf(nc.dram_tensor("out", [NLAT, D], F32, kind="ExternalOutput").ap(), "out", multi=True)
    dbg_d = {}

    with ExitStack() as st:
        fw = FW(nc, st)
        pe, act, dve, pool, sp = fw.pe, fw.act, fw.dve, fw.pool, fw.sp

        XRES = [fw.dram(f"xres{t}", [128, D], F32, multi=False) for t in range(NT)]
        MB = fw.dram("mb", [2, 6, 128, D], F32)
        ABd = fw.dram("abd", [NTOK, 512], BF16)
        Y2d = fw.dram("y2d", [R, 64, 2, 256], BF16)
        FOURd = fw.dram("fourd", [NTOK, 256], BF16)
        QTd = fw.dram("qtd", [12, 96, NTOK], BF16)
        ATTd = fw.dram("attd", [NTOK, 768], BF16)
        XHTl = fw.dram("xhtl", [8, 128, NLAT + 2], BF16)
        XHTc = fw.dram("xhtc", [8, 128, NCTX + 2], BF16)

        ident = fw.sb([128, 128], BF16, "ident")
        sel = fw.sb([2, 256], F32, "sel")
        zero_t = fw.sb([128, 8, 2], BF16, "zero")
        fw.dma(sp, ident[:], ident_d[:], reads=[ident_d], writes=[ident])
        fw.dma(sp, sel[:], sel_d[:], reads=[sel_d], writes=[sel])
        fw.op(dve, lambda e: e.memset(zero_t[:], 0.0), writes=[zero_t])
        cast_rr = [0]

        def cast_eng():
            cast_rr[0] += 1
            return (dve, pool)[cast_rr[0] % 2]

        def rstd_from_ss(stt, n_cols, inv_n, ncol=1):
            fw.op(dve, lambda e: e.tensor_scalar(out=stt[:, 2:2 + ncol], in0=stt[:, 0:ncol], scalar1=inv_n,
                                                 scalar2=EPS, op0=ALU.mult, op1=ALU.add), reads=[stt], writes=[stt])
            fw.op(act, lambda e: e.sqrt(out=stt[:, 2:2 + ncol], in_=stt[:, 2:2 + ncol]), reads=[stt], writes=[stt])
            fw.op(dve, lambda e: e.reciprocal(out=stt[:, 4:4 + ncol], in_=stt[:, 2:2 + ncol]), reads=[stt], writes=[stt])

        def chk(name):
            if stop_at == name:
                fw.stopped = True

        try:
            for l in range(depth):
                last = (l == depth - 1)
                src_tiles = None if l == 0 else XRES

                def src_ap(t):
                    if l == 0:
                        if t < NT_L:
                            return x_in, x_in[t * 128:(t + 1) * 128, :]
                        return ctx_in, ctx_in[(t - NT_L) * 128:(t - NT_L + 1) * 128, :]
                    return XRES[t], XRES[t][:, :]

                fw.barrier()
                with ExitStack() as ph:
                    sil = fw.sb([128, 8, 2], F32, "sil", ph)
                    mrow = fw.sb([2, 6 * D], F32, "mrow", ph)
                    brow = fw.sb([2, 6 * D], F32, "brow", ph)
                    g2 = fw.sb([2, 2, D], F32, "g2", ph)
                    wst = Ring([fw.sb([128, 3072], F32, f"wst{i}", ph) for i in range(3)])
                    bst = Ring([fw.sb([128, D], F32, f"bst{i}", ph) for i in range(2)])
                    psm = [fw.ps([128, 512], F32, f"psm{i}", ph) for i in range(8)]
                    fw.dma(sp, sil[:], cvec[:], reads=[cvec], writes=[sil])
                    fw.dma(sp, brow[:], b_ada2[l], reads=[b_ada2], writes=[brow])
                    fw.dma(sp, g2[:, 0, :], gmix2[l], reads=[gmix2], writes=[g2])
                    fw.dma(sp, g2[:, 1, :], gffn2[l], reads=[gffn2], writes=[g2])
                    fw.op(act, lambda e: e.activation(out=sil[:], in_=sil[:], func=ACTF.Silu), reads=[sil], writes=[sil])
                    for half in range(2):
                        for k in range(8):
                            wt = wst.next()
                            fw.dma(sp, wt[:], w_ada[l, k * 128:(k + 1) * 128, half * 3072:(half + 1) * 3072],
                                   reads=[w_ada], writes=[wt])
                            for j in range(6):
                                fw.op(pe, lambda e: e.matmul(psm[j][0:2, :], lhsT=sil[:, k, :], rhs=wt[:, j * 512:(j + 1) * 512],
                                                             start=(k == 0), stop=(k == 7)), reads=[sil, wt], writes=[psm[j]])
                        for j in range(6):
                            c0 = half * 3072 + j * 512
                            fw.op(dve, lambda e: e.tensor_tensor(out=mrow[:, c0:c0 + 512], in0=psm[j][0:2, :],
                                                                 in1=brow[:, c0:c0 + 512], op=ALU.add),
                                  reads=[psm[j], brow], writes=[mrow])
                    for (c0, gi) in ((1 * D, 0), (4 * D, 1)):
                        fw.op(dve, lambda e: e.scalar_tensor_tensor(out=mrow[:, c0:c0 + D], in0=mrow[:, c0:c0 + D], scalar=1.0,
                                                                    in1=g2[:, gi, :], op0=ALU.add, op1=ALU.mult),
                              reads=[mrow, g2], writes=[mrow])
                    pi = 0
                    for s in range(2):
                        for v in range(6):
                            bt = bst.next()
                            for hh in range(2):
                                p_ = psm[pi % 8]
                                pi += 1
                                fw.op(pe, lambda e: e.matmul(p_[:, :], lhsT=sel[:, s * 128:(s + 1) * 128],
                                                             rhs=mrow[:, v * D + hh * 512: v * D + (hh + 1) * 512],
                                                             start=True, stop=True), reads=[sel, mrow], writes=[p_])
                                fw.op(act if hh else dve, lambda e: e.tensor_copy(out=bt[:, hh * 512:(hh + 1) * 512], in_=p_[:, :])
                                      if not hh else e.copy(out=bt[:, hh * 512:(hh + 1) * 512], in_=p_[:, :]),
                                      reads=[p_], writes=[bt])
                            fw.dma(pool, MB[s, v], bt[:], reads=[bt], writes=[MB])

                fw.barrier()
                chk("M%d" % l)
                with ExitStack() as mx:
                    ckvnT = fw.sb([128, NTOK], BF16, "ckvnT", mx, multi=True)
                    KT = [fw.sb([96, NTOK], BF16, f"KT{i}", mx, multi=True) for i in range(2)]
                    wkvb = fw.sb([128, 1536], BF16, "wkvb", mx)

                    with ExitStack() as ph:
                        winb = fw.sb([128, 8, 800], BF16, "winb", ph, multi=True)
                        wab = fw.sb([128, 8, 512], BF16, "wab", ph, multi=True)
                        wqb = fw.sb([128, 3, 1152], BF16, "wqb", ph, multi=True)
                        gqt = fw.sb([128, 3], F32, "gqt", ph)
                        gkvt = fw.sb([128, 1], F32, "gkvt", ph)
                        GS = [[fw.sb([128, D], F32, f"gs{s}{v}", ph) for v in range(2)] for s in range(2)]
                        psb = [fw.ps([128, 1024], BF16, f"psb{i}", ph) for i in range(3)]
                        psf = [fw.ps([128, 512], F32, f"psf{i}", ph) for i in range(5)]
                        prep = ExitStack()
                        stg = Ring([fw.sb([128, 1536], F32, f"stg{i}", prep) for i in range(2)])
                        wfb = fw.sb([128, 2, 256], BF16, "wfb", prep, multi=True)
                        mab = fw.sb([128, 2, 512], BF16, "mab", prep, multi=True)
                        wfT = fw.sb([128, 2, D], BF16, "wfT", prep, multi=True)
                        c4s4 = fw.sb([128, 2, 2, 256], BF16, "c4s4", prep)

                        fw.dma(sp, c4s4[:], c4s4_d[:], reads=[c4s4_d], writes=[c4s4])
                        fw.dma(sp, gqt[:], gq[l], reads=[gq], writes=[gqt])
                        fw.dma(sp, gkvt[:], gkv[l], reads=[gkv], writes=[gkvt])
                        for s in range(2):
                            fw.dma(sp, GS[s][0][:], MB[s, 1], reads=[MB], writes=[GS[s][0]])
                            fw.dma(sp, GS[s][1][:], MB[s, 0], reads=[MB], writes=[GS[s][1]])
                        for k in range(8):
                            sg_ = stg.next()
                            fw.dma(sp, sg_[:, 0:800], w_in[l, k * 128:(k + 1) * 128, :], reads=[w_in], writes=[sg_])
                            fw.op(cast_eng(), lambda e: e.tensor_copy(out=winb[:, k, :], in_=sg_[:, 0:800]), reads=[sg_], writes=[winb])
                        for kc in range(3):
                            sg_ = stg.next()
                            fw.dma(sp, sg_[:, 0:1152], w_qb[l, kc * 128:(kc + 1) * 128, :], reads=[w_qb], writes=[sg_])
                            fw.op(cast_eng(), lambda e: e.tensor_scalar(out=wqb[:, kc, :], in0=sg_[:, 0:1152], scalar1=gqt[:, kc:kc + 1],
                                                                        scalar2=None, op0=ALU.mult), reads=[sg_, gqt], writes=[wqb])
                        sg_ = stg.next()
                        fw.dma(sp, sg_[:, 0:1536], w_kvb[l], reads=[w_kvb], writes=[sg_])
                        fw.op(cast_eng(), lambda e: e.tensor_scalar(out=wkvb[:], in0=sg_[:, 0:1536], scalar1=gkvt[:, 0:1],
                                                                    scalar2=None, op0=ALU.mult), reads=[sg_, gkvt], writes=[wkvb])
                        for cc in range(2):
                            sg_ = stg.next()
                            fw.dma(sp, sg_[:, 0:256], w_four[l, cc * 128:(cc + 1) * 128, :], reads=[w_four], writes=[sg_])
                            fw.op(cast_eng(), lambda e: e.tensor_copy(out=wfb[:, cc, :], in_=sg_[:, 0:256]), reads=[sg_], writes=[wfb])
                        for cc in range(2):
                            p_ = psf[cc]
                            for X in range(2):
                                for c2 in range(2):
                                    fw.op(pe, lambda e: e.matmul(p_[:, X * 256:(X + 1) * 256],
                                                                 lhsT=c4s4[:, c2, X, cc * 128:(cc + 1) * 128], rhs=wfb[:, c2, :],
                                                                 start=(c2 == 0), stop=(c2 == 1)), reads=[c4s4, wfb], writes=[p_])
                            fw.op(dve, lambda e: e.tensor_copy(out=mab[:, cc, 0:256], in_=p_[:, 0:256]), reads=[p_], writes=[mab])
                            fw.op(dve, lambda e: e.tensor_scalar(out=mab[:, cc, 256:512], in0=p_[:, 256:512], scalar1=-1.0,
                                                                 scalar2=None, op0=ALU.mult), reads=[p_], writes=[mab])
                        for cc in range(2):
                            for k in range(8):
                                fw.op(pe, lambda e: e.transpose(out=psb[cc][:, k * 128:(k + 1) * 128],
                                                                in_=winb[:, k, cc * 128:(cc + 1) * 128], identity=ident[:]),
                                      reads=[winb, ident], writes=[psb[cc]])
                            fw.op(dve, lambda e: e.tensor_copy(out=wfT[:, cc, :], in_=psb[cc][:, :]), reads=[psb[cc]], writes=[wfT])
                        for k in range(8):
                            p_ = psf[k % 5]
                            for cc in range(2):
                                fw.op(pe, lambda e: e.matmul(p_[:, :], lhsT=wfT[:, cc, k * 128:(k + 1) * 128], rhs=mab[:, cc, :],
                                                             start=(cc == 0), stop=(cc == 1)), reads=[wfT, mab], writes=[p_])
                            fw.op(cast_eng() if False else dve, lambda e: e.tensor_copy(out=wab[:, k, :], in_=p_[:, :]),
                                  reads=[p_], writes=[wab])

                        chk('INW%d' % l)
                        fw.barrier()
                        prep.close()
                        xr = Ring([fw.sb([128, D], F32, f"xt{i}", ph) for i in range(2)])
                        junk = fw.sb([128, D], BF16, "junk", ph)
                        xh = Ring([fw.sb([128, D], F32, f"xh{i}", ph) for i in range(1)])
                        xb = Ring([fw.sb([128, D], BF16, f"xb{i}", ph) for i in range(2)])
                        hT = Ring([fw.sb([128, 8, 128], BF16, f"hT{i}", ph) for i in range(2)])
                        stt = Ring([fw.sb([128, 8], F32, f"stt{i}", ph) for i in range(3)])
                        abt = Ring([fw.sb([128, 512], BF16, f"abt{i}", ph) for i in range(2)])
                        cn = Ring([fw.sb([128, 512], BF16, f"cn{i}", ph) for i in range(2)])
                        cn2 = Ring([fw.sb([128, 96], BF16, f"cn2{i}", ph) for i in range(2)])
                        rtmp = Ring([fw.sb([128, 2, 32], F32, f"rtmp{i}", ph) for i in range(2)])
                        cqT = Ring([fw.sb([128, 3, 128], BF16, f"cqT{i}", ph) for i in range(2)])
                        qsb = Ring([fw.sb([128, 12, 96], F32, f"qsb{i}", ph) for i in range(1)])
                        qtmp = Ring([fw.sb([128, 2, 12, 32], F32, f"qtmp{i}", ph) for i in range(1)])
                        qbf = Ring([fw.sb([128, 12, 96], BF16, f"qbf{i}", ph) for i in range(2)])
                        qTt = Ring([fw.sb([96, 12, 128], BF16, f"qTt{i}", ph) for i in range(2)])
                        for b_ in cn2.bufs:
                            fw.op(dve, lambda e: e.memset(b_[:], 0.0), writes=[b_])
                        QTv = QTd.t.rearrange("h d t -> d h t")

                        roper = Ring([fw.sb([128, 2, 32], F32, f"rope{i}", ph) for i in range(3)])

                        def load_x(t):
                            xt = xr.next()
                            sb_, sap = src_ap(t)
                            fw.dma(sp, xt[:], sap, reads=[sb_], writes=[xt])
                            rp = None
                            if t < NT_L:
                                rp = roper.next()
                                fw.dma(sp, rp[:], rope_d[:, t, :, :], reads=[rope_d], writes=[rp])
                            return xt, rp

                        nxt = load_x(0)
                        for t in range(NT):
                            xt, ropet = nxt
                            if t + 1 < NT:
                                nxt = load_x(t + 1)
                            s = 0 if t < NT_L else 1
                            s_ = stt.next()
                            fw.op(dve, lambda e: e.memset(s_[:], 0.0), writes=[s_])
                            fw.op(act, lambda e: e.activation(out=junk[:], in_=xt[:], func=ACTF.Square, accum_out=s_[:, 0:1]),
                                  reads=[xt], writes=[junk, s_])
                            rstd_from_ss(s_, 1, 1.0 / D)
                            xh_ = xh.next()
                            fw.op(dve, lambda e: e.scalar_tensor_tensor(out=xh_[:], in0=xt[:], scalar=s_[:, 4:5], in1=GS[s][0][:],
                                                                        op0=ALU.mult, op1=ALU.mult), reads=[xt, s_, GS[s][0]], writes=[xh_])
                            xb_ = xb.next()
                            fw.op(pool, lambda e: e.tensor_tensor(out=xb_[:], in0=xh_[:], in1=GS[s][1][:], op=ALU.add),
                                  reads=[xh_, GS[s][1]], writes=[xb_])
                            for k in range(8):
                                fw.op(pe, lambda e: e.transpose(out=psb[0][:, k * 128:(k + 1) * 128], in_=xb_[:, k * 128:(k + 1) * 128],
                                                                identity=ident[:]), reads=[xb_, ident], writes=[psb[0]])
                            hT_ = hT.next()
                            fw.op(act, lambda e: e.copy(out=hT_[:].rearrange("p k t -> p (k t)"), in_=psb[0][:, :]),
                                  reads=[psb[0]], writes=[hT_])
                            if t == 0: chk('INTa%d' % l)
                            for k in range(8):
                                fw.op(pe, lambda e: e.matmul(psf[0][:, :], lhsT=hT_[:, k, :], rhs=wab[:, k, :], start=(k == 0), stop=(k == 7)),
                                      reads=[hT_, wab], writes=[psf[0]])
                            for k in range(8):
                                fw.op(pe, lambda e: e.matmul(psf[1][:, :], lhsT=hT_[:, k, :], rhs=winb[:, k, 256:768], start=(k == 0), stop=(k == 7)),
                                      reads=[hT_, winb], writes=[psf[1]])
                            for k in range(8):
                                fw.op(pe, lambda e: e.matmul(psf[2][:, 0:32], lhsT=hT_[:, k, :], rhs=winb[:, k, 768:800], start=(k == 0), stop=(k == 7)),
                                      reads=[hT_, winb], writes=[psf[2]])
                            ab_ = abt.next()
                            fw.op(act, lambda e: e.copy(out=ab_[:], in_=psf[0][:, :]), reads=[psf[0]], writes=[ab_])
                            fw.dma(pool, ABd[t * 128:(t + 1) * 128, :], ab_[:], reads=[ab_], writes=[ABd])
                            if t == 0: chk('INTb%d' % l)
                            s2 = stt.next()
                            fw.op(dve, lambda e: e.memset(s2[:], 0.0), writes=[s2])
                            fw.op(act, lambda e: e.activation(out=junk[:, 0:384], in_=psf[1][:, 0:384], func=ACTF.Square, accum_out=s2[:, 0:1]),
                                  reads=[psf[1]], writes=[junk, s2])
                            fw.op(act, lambda e: e.activation(out=junk[:, 384:512], in_=psf[1][:, 384:512], func=ACTF.Square, accum_out=s2[:, 1:2]),
                                  reads=[psf[1]], writes=[junk, s2])
                            fw.op(dve, lambda e: e.tensor_scalar(out=s2[:, 0:1], in0=s2[:, 0:1], scalar1=128.0 / 384.0, scalar2=None, op0=ALU.mult),
                                  reads=[s2], writes=[s2])
                            rstd_from_ss(s2, 2, 1.0 / 128, ncol=2)
                            cn_ = cn.next()
                            fw.op(dve, lambda e: e.tensor_scalar(out=cn_[:, 0:384], in0=psf[1][:, 0:384], scalar1=s2[:, 4:5], scalar2=None, op0=ALU.mult),
                                  reads=[psf[1], s2], writes=[cn_])
                            fw.op(dve, lambda e: e.tensor_scalar(out=cn_[:, 384:512], in0=psf[1][:, 384:512], scalar1=s2[:, 5:6], scalar2=None, op0=ALU.mult),
                                  reads=[psf[1], s2], writes=[cn_])
                            c2_ = cn2.next()
                            if s == 0:
                                rt = rtmp.next()
                                fw.op(dve, lambda e: e.tensor_tensor(out=rt[:, 0, :], in0=psf[2][:, 0:32], in1=ropet[:, 0, :], op=ALU.mult),
                                      reads=[psf[2], ropet], writes=[rt])
                                fw.op(dve, lambda e: e.tensor_tensor(out=rt[:, 1, :], in0=psf[2][:, 0:32], in1=ropet[:, 1, :], op=ALU.mult),
                                      reads=[psf[2], ropet], writes=[rt])
                                fw.op(pool, lambda e: e.tensor_tensor(out=c2_[:, 64:80], in0=rt[:, 0, 0:16], in1=rt[:, 1, 16:32], op=ALU.subtract),
                                      reads=[rt], writes=[c2_])
                                fw.op(pool, lambda e: e.tensor_tensor(out=c2_[:, 80:96], in0=rt[:, 0, 16:32], in1=rt[:, 1, 0:16], op=ALU.add),
                                      reads=[rt], writes=[c2_])
                            else:
                                fw.op(dve, lambda e: e.tensor_copy(out=c2_[:, 64:96], in_=psf[2][:, 0:32]), reads=[psf[2]], writes=[c2_])
                            if t == 0: chk('INTc%d' % l)
                            for j in range(4):
                                fw.op(pe, lambda e: e.transpose(out=psb[1][:, j * 128:(j + 1) * 128], in_=cn_[:, j * 128:(j + 1) * 128],
                                                                identity=ident[:]), reads=[cn_, ident], writes=[psb[1]])
                            fw.op(pe, lambda e: e.transpose(out=psb[1][0:96, 512:640], in_=c2_[:, 0:96], identity=ident[:]),
                                  reads=[c2_, ident], writes=[psb[1]])
                            if t == 0: chk('INTc2%d' % l)
                            cq_ = cqT.next()
                            fw.op(act, lambda e: e.copy(out=cq_[:].rearrange("p k t -> p (k t)"), in_=psb[1][:, 0:384]), reads=[psb[1]], writes=[cq_])
                            fw.op(dve, lambda e: e.tensor_copy(out=ckvnT[:, t * 128:(t + 1) * 128], in_=psb[1][:, 384:512]),
                                  reads=[psb[1]], writes=[ckvnT])
                            if t == 0: chk('INTc3%d' % l)
                            for i in range(2):
                                fw.op(dve if i else act, (lambda e: e.tensor_copy(out=KT[1][64:96, t * 128:(t + 1) * 128], in_=psb[1][64:96, 512:640]))
                                      if i else (lambda e: e.copy(out=KT[0][64:96, t * 128:(t + 1) * 128], in_=psb[1][64:96, 512:640])),
                                      reads=[psb[1]], writes=[KT[i]])
                            if t == 0: chk('INTd%d' % l)
                            if s == 1 and last:
                                continue
                            for (pi_, c0, c1) in ((2, 0, 512), (3, 512, 1024), (4, 1024, 1152)):
                                for kc in range(3):
                                    fw.op(pe, lambda e: e.matmul(psf[pi_][:, 0:c1 - c0], lhsT=cq_[:, kc, :], rhs=wqb[:, kc, c0:c1],
                                                                 start=(kc == 0), stop=(kc == 2)), reads=[cq_, wqb], writes=[psf[pi_]])
                            q_ = qsb.next()
                            qf = q_[:].rearrange("p h d -> p (h d)")
                            fw.op(act, lambda e: e.copy(out=qf[:, 0:512], in_=psf[2][:, :]), reads=[psf[2]], writes=[q_])
                            fw.op(dve, lambda e: e.tensor_copy(out=qf[:, 512:1024], in_=psf[3][:, :]), reads=[psf[3]], writes=[q_])
                            fw.op(act, lambda e: e.copy(out=qf[:, 1024:1152], in_=psf[4][:, 0:128]), reads=[psf[4]], writes=[q_])
                            qb_ = qbf.next()
                            fw.op(pool, lambda e: e.tensor_copy(out=qb_[:, :, 0:64], in_=q_[:, :, 0:64]), reads=[q_], writes=[qb_])
                            if s == 0:
                                qt_ = qtmp.next()
                                for h in range(12):
                                    eng = (dve, pool)[h % 2]
                                    fw.op(eng, lambda e: e.tensor_tensor(out=qt_[:, 0, h, :], in0=q_[:, h, 64:96], in1=ropet[:, 0, :], op=ALU.mult),
                                          reads=[q_, ropet], writes=[qt_])
                                    fw.op(eng, lambda e: e.tensor_tensor(out=qt_[:, 1, h, :], in0=q_[:, h, 64:96], in1=ropet[:, 1, :], op=ALU.mult),
                                          reads=[q_, ropet], writes=[qt_])
                                fw.op(dve, lambda e: e.tensor_tensor(out=qb_[:, :, 64:80], in0=qt_[:, 0, :, 0:16], in1=qt_[:, 1, :, 16:32], op=ALU.subtract),
                                      reads=[qt_], writes=[qb_])
                                fw.op(pool, lambda e: e.tensor_tensor(out=qb_[:, :, 80:96], in0=qt_[:, 0, :, 16:32], in1=qt_[:, 1, :, 0:16], op=ALU.add),
                                      reads=[qt_], writes=[qb_])
                            else:
                                fw.op(dve, lambda e: e.tensor_copy(out=qb_[:, :, 64:96], in_=q_[:, :, 64:96]), reads=[q_], writes=[qb_])
                            if t == 0: chk('INTe%d' % l)
                            for h in range(12):
                                pb = psb[2] if h < 8 else psb[0]
                                fw.op(pe, lambda e: e.transpose(out=pb[0:96, (h % 8) * 128:(h % 8 + 1) * 128], in_=qb_[:, h, :], identity=ident[:]),
                                      reads=[qb_, ident], writes=[pb])
                            qT_ = qTt.next()
                            fw.op(act, lambda e: e.copy(out=qT_[:, 0:8, :].rearrange("p h t -> p (h t)"), in_=psb[2][0:96, :]), reads=[psb[2]], writes=[qT_])
                            fw.op(dve, lambda e: e.tensor_copy(out=qT_[:, 8:12, :].rearrange("p h t -> p (h t)"), in_=psb[0][0:96, 0:512]),
                                  reads=[psb[0]], writes=[qT_])
                            fw.dma(pool, QTv[:, :, t * 128:(t + 1) * 128], qT_[:], reads=[qT_], writes=[QTd])

                    fw.barrier()
                    chk("IN%d" % l)
                    with ExitStack() as ph:
                        crs = fw.sb([R, 3, R], BF16, "crs", ph)
                        tw = fw.sb([R, 2, 64], F32, "tw", ph)
                        c64 = fw.sb([64, 2, 64], BF16, "c64", ph)
                        bft = fw.sb([128, 2, 256], F32, "bft", ph)
                        fw.dma(sp, crs[:], crs_d[:], reads=[crs_d], writes=[crs])
                        fw.dma(sp, tw[:], tw_d[:], reads=[tw_d], writes=[tw])
                        fw.dma(sp, c64[:], c64_d[:], reads=[c64_d], writes=[c64])
                        fw.dma(sp, bft[:], b_four[l], reads=[b_four], writes=[bft])
                        abg = Ring([fw.sb([R, 16, 512], BF16, f"abg{i}", ph) for i in range(2)])
                        y2g = Ring([fw.sb([R, 16, 2, 256], BF16, f"y2g{i}", ph) for i in range(2)])
                        y2t = Ring([fw.sb([64, 16, 2, 256], BF16, f"y2t{i}", ph) for i in range(2)])
                        fog = Ring([fw.sb([64, 16, 256], BF16, f"fog{i}", ph) for i in range(2)])
                        tt = Ring([fw.sb([R, 256], F32, f"tt{i}", ph) for i in range(4)])
                        psy = Ring([fw.ps([128, 512], F32, f"psy{i}", ph) for i in range(8)])
                        ABv = ABd.t[0:NLAT, :].rearrange("(a b) c -> a b c", b=64)
                        for g in range(4):
                            ab_ = abg.next()
                            fw.dma(sp, ab_[:], ABv[:, g * 16:(g + 1) * 16, :], reads=[ABd], writes=[ab_])
                            y2_ = y2g.next()
                            for pm in range(8):
                                pr = psy.next()
                                pi_ = psy.next()
                                a_ = ab_[:, 2 * pm:2 * pm + 2, 0:256]
                                b_ = ab_[:, 2 * pm:2 * pm + 2, 256:512]
                                prv = pr[0:R, :].rearrange("p (a b) -> p a b", b=256)
                                piv = pi_[0:R, :].rearrange("p (a b) -> p a b", b=256)
                                fw.op(pe, lambda e: e.matmul(prv, lhsT=crs[:, 0, :], rhs=a_, start=True, stop=False), reads=[crs, ab_], writes=[pr])
                                fw.op(pe, lambda e: e.matmul(prv, lhsT=crs[:, 1, :], rhs=b_, start=False, stop=True), reads=[crs, ab_], writes=[pr])
                                fw.op(pe, lambda e: e.matmul(piv, lhsT=crs[:, 0, :], rhs=b_, start=True, stop=False), reads=[crs, ab_], writes=[pi_])
                                fw.op(pe, lambda e: e.matmul(piv, lhsT=crs[:, 2, :], rhs=a_, start=False, stop=True), reads=[crs, ab_], writes=[pi_])
                                for j in range(2):
                                    ml = 2 * pm + j
                                    m2 = g * 16 + ml
                                    t1 = tt.next()
                                    t2 = tt.next()
                                    fw.op(dve, lambda e: e.tensor_scalar(out=t1[:], in0=pi_[0:R, j * 256:(j + 1) * 256], scalar1=tw[:, 1, m2:m2 + 1],
                                                                         scalar2=None, op0=ALU.mult), reads=[pi_, tw], writes=[t1])
                                    fw.op(dve, lambda e: e.scalar_tensor_tensor(out=y2_[:, ml, 0, :], in0=pr[0:R, j * 256:(j + 1) * 256],
                                                                                scalar=tw[:, 0, m2:m2 + 1], in1=t1[:], op0=ALU.mult, op1=ALU.add),
                                          reads=[pr, tw, t1], writes=[y2_])
                                    fw.op(dve, lambda e: e.tensor_scalar(out=t2[:], in0=pr[0:R, j * 256:(j + 1) * 256], scalar1=tw[:, 1, m2:m2 + 1],
                                                                         scalar2=None, op0=ALU.mult), reads=[pr, tw], writes=[t2])
                                    fw.op(dve, lambda e: e.scalar_tensor_tensor(out=y2_[:, ml, 1, :], in0=pi_[0:R, j * 256:(j + 1) * 256],
                                                                                scalar=tw[:, 0, m2:m2 + 1], in1=t2[:], op0=ALU.mult, op1=ALU.subtract),
                                          reads=[pi_, tw, t2], writes=[y2_])
                            fw.dma(pool, Y2d[:, g * 16:(g + 1) * 16, :, :], y2_[:], reads=[y2_], writes=[Y2d])
                        Y2v = Y2d.t.rearrange("a b c d -> b a c d")
                        FOv = FOURd.t[0:NLAT, :].rearrange("(a b) c -> a b c", b=R)
                        for g in range(R // 16):
                            yt_ = y2t.next()
                            fw.dma(sp, yt_[:], Y2v[:, g * 16:(g + 1) * 16, :, :], reads=[Y2d], writes=[yt_])
                            fo_ = fog.next()
                            for p2 in range(8):
                                pz = psy.next()
                                pzv = pz[0:64, :].rearrange("p (a b) -> p a b", b=256)
                                fw.op(pe, lambda e: e.matmul(pzv, lhsT=c64[:, 0, :], rhs=yt_[:, 2 * p2:2 * p2 + 2, 0, :], start=True, stop=False),
                                      reads=[c64, yt_], writes=[pz])
                                fw.op(pe, lambda e: e.matmul(pzv, lhsT=c64[:, 1, :], rhs=yt_[:, 2 * p2:2 * p2 + 2, 1, :], start=False, stop=True),
                                      reads=[c64, yt_], writes=[pz])
                                fw.op(dve, lambda e: e.tensor_tensor(out=fo_[:, 2 * p2:2 * p2 + 2, :], in0=pzv, in1=bft[0:64, :, :], op=ALU.add),
                                      reads=[pz, bft], writes=[fo_])
                            fw.dma(pool, FOv[:, g * 16:(g + 1) * 16, :], fo_[:], reads=[fo_], writes=[FOURd])
                        if not last:
                            c256 = fw.sb([128, 2, 2, 256], BF16, "c256", ph)
                            abc = fw.sb([128, 2, 512], BF16, "abc", ph)
                            foc = fw.sb([128, 2, 256], BF16, "foc", ph)
                            fw.dma(sp, c256[:], c256_d[:], reads=[c256_d], writes=[c256])
                            fw.dma(sp, abc[:], ABd.t[NLAT:NTOK, :].rearrange("(a p) c -> p a c", p=128), reads=[ABd], writes=[abc])
                            for n_ in range(2):
                                pz = psy.next()
                                for mc in range(2):
                                    fw.op(pe, lambda e: e.matmul(pz[:, 0:256], lhsT=c256[:, mc, 0, n_ * 128:(n_ + 1) * 128], rhs=abc[:, mc, 0:256],
                                                                 start=(mc == 0), stop=False), reads=[c256, abc], writes=[pz])
                                for mc in range(2):
                                    fw.op(pe, lambda e: e.matmul(pz[:, 0:256], lhsT=c256[:, mc, 1, n_ * 128:(n_ + 1) * 128], rhs=abc[:, mc, 256:512],
                                                                 start=False, stop=(mc == 1)), reads=[c256, abc], writes=[pz])
                                fw.op(dve, lambda e: e.tensor_tensor(out=foc[:, n_, :], in0=pz[:, 0:256], in1=bft[:, 0, :], op=ALU.add),
                                      reads=[pz, bft], writes=[foc])
                            fw.dma(pool, FOURd.t[NLAT:NTOK, :].rearrange("(a p) c -> p a c", p=128), foc[:], reads=[foc], writes=[FOURd])

                    fw.barrier()
                    chk("F%d" % l)
                    with ExitStack() as ph:
                        Va = [fw.sb([128, NT, 65], BF16, f"Va{i}", ph, multi=True) for i in range(2)]
                        NQ = NTOK if not last else NLAT
                        qts = [fw.sb([96, NQ], BF16, f"qts{i}", ph) for i in range(2)]
                        ptr = Ring([fw.sb([128, 2, QB], BF16, f"pt{i}", ph) for i in range(3)])
                        rcr = Ring([fw.sb([128, 4], F32, f"rc{i}", ph) for i in range(2)])
                        att = Ring([fw.sb([128, 4, 64], BF16, f"att{i}", ph) for i in range(3)])
                        pss = Ring([fw.ps([128, 1024], F32, f"pss{i}", ph) for i in range(2)])
                        pso = Ring([fw.ps([128, 512], F32, f"pso{i}", ph) for i in range(2)])
                        pkv = Ring([fw.ps([128, 512], F32, f"pkv{i}", ph) for i in range(2)])
                        for i in range(2):
                            fw.op(pool, lambda e: e.memset(Va[i][:, :, 64:65], 1.0), writes=[Va[i]])

                        def build_kv(h):
                            b = h % 2
                            c0 = 0
                            while c0 < NTOK:
                                w_ = min(512, NTOK - c0)
                                p_ = pkv.next()
                                fw.op(pe, lambda e: e.matmul(p_[0:64, 0:w_], lhsT=wkvb[:, h * 128:h * 128 + 64], rhs=ckvnT[:, c0:c0 + w_],
                                                             start=True, stop=True), reads=[wkvb, ckvnT], writes=[p_])
                                fw.op(dve, lambda e: e.tensor_copy(out=KT[b][0:64, c0:c0 + w_], in_=p_[0:64, 0:w_]), reads=[p_], writes=[KT[b]])
                                c0 += w_
                            c0 = 0
                            while c0 < NT:
                                n_ = min(8, NT - c0)
                                p_ = pkv.next()
                                for c in range(n_):
                                    fw.op(pe, lambda e: e.matmul(p_[:, c * 64:(c + 1) * 64], lhsT=ckvnT[:, (c0 + c) * 128:(c0 + c + 1) * 128],
                                                                 rhs=wkvb[:, h * 128 + 64:h * 128 + 128], start=True, stop=True),
                                          reads=[wkvb, ckvnT], writes=[p_])
                                fw.op(pool if False else dve, lambda e: e.tensor_copy(out=Va[b][:, c0:c0 + n_, 0:64],
                                                                                     in_=p_[:, 0:n_ * 64].rearrange("p (a b) -> p a b", b=64)),
                                      reads=[p_], writes=[Va[b]])
                                c0 += n_
                            fw.dma(sp, qts[b][:], QTd[h, :, 0:NQ], reads=[QTd], writes=[qts[b]])

                        def attend(h, q0, W, kcs):
                            b = h % 2
                            nq = W // 128
                            pob = pso.next()
                            po = Buf(pob.t[:, 0:260].rearrange("p (a b) -> p a b", b=65), "po")
                            po.excl = True
                            nk = len(kcs)
                            for i0 in range(0, nk, 2):
                                grp = kcs[i0:i0 + 2]
                                p_ = pss.next()
                                for j, kc in enumerate(grp):
                                    fw.op(pe, lambda e: e.matmul(p_[:, j * 512:j * 512 + W], lhsT=KT[b][0:96, kc * 128:(kc + 1) * 128],
                                                                 rhs=qts[b][0:96, q0:q0 + W], start=True, stop=True),
                                          reads=[KT[b], qts[b]], writes=[p_])
                                pt = ptr.next()
                                ng = len(grp)
                                fw.op(act, lambda e: e.activation(out=pt[:, 0:ng, 0:W], in_=p_[:, 0:ng * 512].rearrange("p (a b) -> p a b", b=512)[:, :, 0:W],
                                                                  func=ACTF.Exp, scale=SCALE), reads=[p_], writes=[pt])
                                for j, kc in enumerate(grp):
                                    for qi in range(nq):
                                        fw.op(pe, lambda e: e.matmul(po[:, qi, :], lhsT=pt[:, j, qi * 128:(qi + 1) * 128], rhs=Va[b][:, kc, :],
                                                                     start=(i0 + j == 0 and qi == 0), stop=(i0 + j == nk - 1 and qi == nq - 1)),
                                              reads=[pt, Va[b]], writes=[pob])
                            rc = rcr.next()
                            fw.op(dve, lambda e: e.reciprocal(out=rc[:, 0:nq], in_=po[:, 0:nq, 64]), reads=[pob], writes=[rc])
                            at = att.next()
                            for qi in range(nq):
                                fw.op(dve, lambda e: e.tensor_scalar(out=at[:, qi, :], in0=po[:, qi, 0:64], scalar1=rc[:, qi:qi + 1], scalar2=None,
                                                                     op0=ALU.mult), reads=[pob, rc], writes=[at])
                            fw.dma(pool, ATTd.t[q0:q0 + W, h * 64:(h + 1) * 64].rearrange("(a p) d -> p a d", p=128), at[:, 0:nq, :],
                                   reads=[at], writes=[ATTd])

                        build_kv(0)
                        for h in range(12):
                            if h + 1 < 12:
                                build_kv(h + 1)
                            for q0 in range(0, NLAT, QB):
                                attend(h, q0, QB, list(range(NT)))
                            if not last:
                                attend(h, NLAT, NCTX, list(range(NT_L, NT)))

                fw.barrier()
                chk("ATT%d" % l)
                with ExitStack() as ph:
                    stg = Ring([fw.sb([128, 1024], F32, f"stg{i}", ph) for i in range(2)])
                    woutb = fw.sb([128, 8, D], BF16, "woutb", ph, multi=True)
                    GT = [[fw.sb([128, D], F32, f"gt{s}{v}", ph) for v in range(3)] for s in range(2)]
                    for s in range(2):
                        for v, mv in enumerate((2, 4, 3)):
                            fw.dma(sp, GT[s][v][:], MB[s, mv], reads=[MB], writes=[GT[s][v]])
                    for k in range(8):
                        sg_ = stg.next()
                        fw.dma(sp, sg_[:], w_out[l, k * 128:(k + 1) * 128, :], reads=[w_out], writes=[sg_])
                        fw.op(cast_eng(), lambda e: e.tensor_copy(out=woutb[:, k, :], in_=sg_[:]), reads=[sg_], writes=[woutb])
                    mixr = Ring([fw.sb([128, D], BF16, f"mix{i}", ph) for i in range(3)])
                    mixT = Ring([fw.sb([128, 8, 128], BF16, f"mixT{i}", ph) for i in range(2)])
                    xr = Ring([fw.sb([128, D], F32, f"xo{i}", ph) for i in range(3)])
                    tmpr = Ring([fw.sb([128, D], F32, f"tmp{i}", ph) for i in range(2)])
                    xnr = Ring([fw.sb([128, D], F32, f"xn{i}", ph) for i in range(2)])
                    junk = fw.sb([128, D], BF16, "junk2", ph)
                    stt = Ring([fw.sb([128, 8], F32, f"stt{i}", ph) for i in range(3)])
                    xh = Ring([fw.sb([128, D], F32, f"xh{i}", ph) for i in range(2)])
                    xb = Ring([fw.sb([128, D], BF16, f"xb{i}", ph) for i in range(2)])
                    hT = Ring([fw.sb([128, 8, 128], BF16, f"hT{i}", ph) for i in range(2)])
                    psb = Ring([fw.ps([128, 1024], BF16, f"psb{i}", ph) for i in range(2)])
                    psd = Ring([fw.ps([128, 1024], F32, f"psd{i}", ph) for i in range(2)])
                    XHlv = XHTl.t.rearrange("k p t -> p k t")
                    XHcv = XHTc.t.rearrange("k p t -> p k t")
                    ntile = NT_L if last else NT

                    def load_o(t):
                        m_ = mixr.next()
                        fw.dma(sp, m_[:, 0:256], FOURd[t * 128:(t + 1) * 128, :], reads=[FOURd], writes=[m_])
                        fw.dma(sp, m_[:, 256:1024], ATTd[t * 128:(t + 1) * 128, :], reads=[ATTd], writes=[m_])
                        xt = xr.next()
                        sb_, sap = src_ap(t)
                        fw.dma(sp, xt[:], sap, reads=[sb_], writes=[xt])
                        return m_, xt

                    nxt = load_o(0)
                    for t in range(ntile):
                        m_, xt = nxt
                        if t + 1 < ntile:
                            nxt = load_o(t + 1)
                        s = 0 if t < NT_L else 1
                        pb = psb.next()
                        for k in range(8):
                            fw.op(pe, lambda e: e.transpose(out=pb[:, k * 128:(k + 1) * 128], in_=m_[:, k * 128:(k + 1) * 128], identity=ident[:]),
                                  reads=[m_, ident], writes=[pb])
                        mT = mixT.next()
                        fw.op(act, lambda e: e.copy(out=mT[:].rearrange("p k t -> p (k t)"), in_=pb[:, :]), reads=[pb], writes=[mT])
                        pd = psd.next()
                        for hh in range(2):
                            for k in range(8):
                                fw.op(pe, lambda e: e.matmul(pd[:, hh * 512:(hh + 1) * 512], lhsT=mT[:, k, :], rhs=woutb[:, k, hh * 512:(hh + 1) * 512],
                                                             start=(k == 0), stop=(k == 7)), reads=[mT, woutb], writes=[pd])
                        tm = tmpr.next()
                        fw.op(dve, lambda e: e.tensor_tensor(out=tm[:], in0=pd[:, :], in1=GT[s][0][:], op=ALU.mult), reads=[pd, GT[s][0]], writes=[tm])
                        xn = xnr.next()
                        fw.op(pool, lambda e: e.tensor_tensor(out=xn[:], in0=tm[:], in1=xt[:], op=ALU.add), reads=[tm, xt], writes=[xn])
                        fw.dma(pool, XRES[t][:, :], xn[:], reads=[xn], writes=[XRES[t]])
                        s_ = stt.next()
                        fw.op(dve, lambda e: e.memset(s_[:], 0.0), writes=[s_])
                        fw.op(act, lambda e: e.activation(out=junk[:], in_=xn[:], func=ACTF.Square, accum_out=s_[:, 0:1]), reads=[xn], writes=[junk, s_])
                        rstd_from_ss(s_, 1, 1.0 / D)
                        xh_ = xh.next()
                        fw.op(dve, lambda e: e.scalar_tensor_tensor(out=xh_[:], in0=xn[:], scalar=s_[:, 4:5], in1=GT[s][1][:],
                                                                    op0=ALU.mult, op1=ALU.mult), reads=[xn, s_, GT[s][1]], writes=[xh_])
                        xb_ = xb.next()
                        fw.op(pool, lambda e: e.tensor_tensor(out=xb_[:], in0=xh_[:], in1=GT[s][2][:], op=ALU.add), reads=[xh_, GT[s][2]], writes=[xb_])
                        pb = psb.next()
                        for k in range(8):
                            fw.op(pe, lambda e: e.transpose(out=pb[:, k * 128:(k + 1) * 128], in_=xb_[:, k * 128:(k + 1) * 128], identity=ident[:]),
                                  reads=[xb_, ident], writes=[pb])
                        hT_ = hT.next()
                        fw.op(act, lambda e: e.copy(out=hT_[:].rearrange("p k t -> p (k t)"), in_=pb[:, :]), reads=[pb], writes=[hT_])
                        if s == 0:
                            fw.dma(pool, XHlv[:, :, 1 + t * 128:1 + (t + 1) * 128], hT_[:], reads=[hT_], writes=[XHTl])
                        else:
                            tc_ = t - NT_L
                            fw.dma(pool, XHcv[:, :, 1 + tc_ * 128:1 + (tc_ + 1) * 128], hT_[:], reads=[hT_], writes=[XHTc])

                fw.barrier()
                chk("OUT%d" % l)
                with ExitStack() as ph:
                    wdnb = fw.sb([128, NCH, D], BF16, "wdnb", ph, multi=True)
                    wdwt = fw.sb([128, 2 * NCH, 3], F32, "wdwt", ph)
                    bdwt = fw.sb([128, 2 * NCH], F32, "bdwt", ph)
                    GT2 = [fw.sb([128, D], F32, f"gt2{s}", ph) for s in range(2)]
                    WUPd = fw.dram(f"wupd{l}", [2 * NCH, 128, 8, 128], BF16)
                    fw.dma(sp, wdwt[:], wdw[l], reads=[wdw], writes=[wdwt])
                    fw.dma(sp, bdwt[:], bdw[l], reads=[bdw], writes=[bdwt])
                    for s in range(2):
                        fw.dma(sp, GT2[s][:], MB[s, 5], reads=[MB], writes=[GT2[s]])
                    if last:
                        gfint = fw.sb([128, D], F32, "gfint", ph)
                        fw.dma(sp, gfint[:], gfin[:], reads=[gfin], writes=[gfint])
                    with ExitStack() as prep:
                        stg = Ring([fw.sb([128, 1408], F32, f"stg{i}", prep) for i in range(2)])
                        cst = Ring([fw.sb([128, 1408], BF16, f"cst{i}", prep) for i in range(2)])
                        WUv = WUPd.t.rearrange("c p k n -> p c k n")
                        for k in range(8):
                            for cb in range(4):
                                sg_ = stg.next()
                                fw.dma(sp, sg_[:], w_up[l, k * 128:(k + 1) * 128, cb * 1408:(cb + 1) * 1408], reads=[w_up], writes=[sg_])
                                ct = cst.next()
                                fw.op(cast_eng(), lambda e: e.tensor_copy(out=ct[:], in_=sg_[:]), reads=[sg_], writes=[ct])
                                fw.dma(pool, WUv[:, cb * 11:(cb + 1) * 11, k, :], ct[:].rearrange("p (c n) -> p c n", n=128),
                                       reads=[ct], writes=[WUPd])
                        for i in range(NCH):
                            sg_ = stg.next()
                            fw.dma(sp, sg_[:, 0:D], w_down[l, i * 128:(i + 1) * 128, :], reads=[w_down], writes=[sg_])
                            fw.op(cast_eng(), lambda e: e.tensor_copy(out=wdnb[:, i, :], in_=sg_[:, 0:D]), reads=[sg_], writes=[wdnb])
                    fw.barrier()
                    wur = Ring([fw.sb([128, 8, 128], BF16, f"wur{i}", ph) for i in range(6)])
                    xhb = Ring([fw.sb([128, 8, 514], BF16, f"xhb{i}", ph) for i in range(2)])
                    yT = fw.sb([128, NCH, 512], BF16, "yT", ph)
                    cr_ = Ring([fw.sb([128, 512], F32, f"cv{i}", ph) for i in range(4)])
                    sgr = Ring([fw.sb([128, 512], F32, f"sg{i}", ph) for i in range(2)])
                    xr = Ring([fw.sb([128, D], F32, f"xf{i}", ph) for i in range(2)])
                    tmpr = Ring([fw.sb([128, D], F32, f"tf{i}", ph) for i in range(2)])
                    junk = fw.sb([128, D], BF16, "junk3", ph)
                    stt = Ring([fw.sb([128, 8], F32, f"stt{i}", ph) for i in range(2)])
                    psu = Ring([fw.ps([128, 512], F32, f"psu{i}", ph) for i in range(3)])
                    psh = Ring([fw.ps([128, 512], F32, f"psh{i}", ph) for i in range(2)])
                    psd = Ring([fw.ps([128, 1024], F32, f"psd{i}", ph) for i in range(1)])
                    XHlv = XHTl.t.rearrange("k p t -> p k t")
                    XHcv = XHTc.t.rearrange("k p t -> p k t")
                    blocks = [(0, t0, min(512, NLAT - t0)) for t0 in range(0, NLAT, 512)]
                    if not last:
                        blocks.append((1, 0, NCTX))

                    def load_blk(bi):
                        s, t0, W = blocks[bi]
                        xb_ = xhb.next()
                        v = XHlv if s == 0 else XHcv
                        n_ = NLAT if s == 0 else NCTX
                        lo = 1 if t0 == 0 else 0
                        hi = W + 1 if t0 + W == n_ else W + 2
                        if lo:
                            fw.op(pool, lambda e: e.memset(xb_[:, :, 0:1], 0.0), writes=[xb_])
                        if hi == W + 1:
                            fw.op(pool, lambda e: e.memset(xb_[:, :, W + 1:W + 2], 0.0), writes=[xb_])
                        fw.dma(sp, xb_[:, :, lo:hi], v[:, :, t0 + lo:t0 + hi], reads=[XHTl if s == 0 else XHTc], writes=[xb_])
                        return xb_

                    nxt = load_blk(0)
                    for bi, (s, t0, W) in enumerate(blocks):
                        xb_ = nxt
                        if bi + 1 < len(blocks):
                            nxt = load_blk(bi + 1)
                        for i in range(NCH):
                            cvs = []
                            for hv, ch in ((0, i), (1, NCH + i)):
                                pu = psu.next()
                                ph_ = psh.next()
                                wu = wur.next()
                                fw.dma(sp, wu[:], WUPd[ch], reads=[WUPd], writes=[wu])
                                for k in range(8):
                                    fw.op(pe, lambda e: e.matmul(pu[:, 0:W], lhsT=wu[:, k, :], rhs=xb_[:, k, 1:W + 1],
                                                                 start=(k == 0), stop=(k == 7)), reads=[wu, xb_], writes=[pu])
                                for k in range(8):
                                    fw.op(pe, lambda e: e.matmul(ph_[:, 0:2], lhsT=wu[:, k, :], rhs=xb_[:, k, 0:W + 2:W + 1],
                                                                 start=(k == 0), stop=(k == 7)), reads=[wu, xb_], writes=[ph_])
                                c_ = cr_.next()
                                w0 = wdwt[:, ch, 0:1]
                                w2 = wdwt[:, ch, 2:3]
                                fw.op(act, lambda e: e.activation(out=c_[:, 0:W], in_=pu[:, 0:W], func=ACTF.Identity, scale=wdwt[:, ch, 1:2],
                                                                  bias=bdwt[:, ch:ch + 1]), reads=[pu, wdwt, bdwt], writes=[c_])
                                fw.op(dve, lambda e: e.scalar_tensor_tensor(out=c_[:, 1:W], in0=pu[:, 0:W - 1], scalar=w0, in1=c_[:, 1:W],
                                                                            op0=ALU.mult, op1=ALU.add), reads=[pu, wdwt, c_], writes=[c_])
                                fw.op(dve, lambda e: e.scalar_tensor_tensor(out=c_[:, 0:W - 1], in0=pu[:, 1:W], scalar=w2, in1=c_[:, 0:W - 1],
                                                                            op0=ALU.mult, op1=ALU.add), reads=[pu, wdwt, c_], writes=[c_])
                                fw.op(dve, lambda e: e.scalar_tensor_tensor(out=c_[:, 0:1], in0=ph_[:, 0:1], scalar=w0, in1=c_[:, 0:1],
                                                                            op0=ALU.mult, op1=ALU.add), reads=[ph_, wdwt, c_], writes=[c_])
                                fw.op(dve, lambda e: e.scalar_tensor_tensor(out=c_[:, W - 1:W], in0=ph_[:, 1:2], scalar=w2, in1=c_[:, W - 1:W],
                                                                            op0=ALU.mult, op1=ALU.add), reads=[ph_, wdwt, c_], writes=[c_])
                                cvs.append(c_)
                            sg_ = sgr.next()
                            fw.op(act, lambda e: e.activation(out=sg_[:, 0:W], in_=cvs[0][:, 0:W], func=ACTF.Silu), reads=[cvs[0]], writes=[sg_])
                            fw.op(pool, lambda e: e.tensor_tensor(out=yT[:, i, 0:W], in0=sg_[:, 0:W], in1=cvs[1][:, 0:W], op=ALU.mult),
                                  reads=[sg_, cvs[1]], writes=[yT])
                        for qi in range(W // 128):
                            tg = (t0 // 128 + qi) if s == 0 else NT_L + qi
                            pd = psd.next()
                            for hh in range(2):
                                for i in range(NCH):
                                    fw.op(pe, lambda e: e.matmul(pd[:, hh * 512:(hh + 1) * 512], lhsT=yT[:, i, qi * 128:(qi + 1) * 128],
                                                                 rhs=wdnb[:, i, hh * 512:(hh + 1) * 512], start=(i == 0), stop=(i == NCH - 1)),
                                          reads=[yT, wdnb], writes=[pd])
                            xt = xr.next()
                            fw.dma(sp, xt[:], XRES[tg][:, :], reads=[XRES[tg]], writes=[xt])
                            tm = tmpr.next()
                            fw.op(dve, lambda e: e.tensor_tensor(out=tm[:], in0=pd[:, :], in1=GT2[s][:], op=ALU.mult), reads=[pd, GT2[s]], writes=[tm])
                            fw.op(pool, lambda e: e.tensor_tensor(out=tm[:], in0=tm[:], in1=xt[:], op=ALU.add), reads=[tm, xt], writes=[tm])
                            if not last:
                                fw.dma(pool, XRES[tg][:, :], tm[:], reads=[tm], writes=[XRES[tg]])
                            else:
                                s_ = stt.next()
                                fw.op(dve, lambda e: e.memset(s_[:], 0.0), writes=[s_])
                                fw.op(act, lambda e: e.activation(out=junk[:], in_=tm[:], func=ACTF.Square, accum_out=s_[:, 0:1]),
                                      reads=[tm], writes=[junk, s_])
                                rstd_from_ss(s_, 1, 1.0 / D)
                                fw.op(dve, lambda e: e.scalar_tensor_tensor(out=xt[:], in0=tm[:], scalar=s_[:, 4:5], in1=gfint[:],
                                                                            op0=ALU.mult, op1=ALU.mult), reads=[tm, s_, gfint], writes=[xt])
                                fw.dma(pool, out_d[tg * 128:(tg + 1) * 128, :], xt[:], reads=[xt], writes=[out_d])

        except _Stop:
            pass
        fw.barrier()
        fw.finish([out_d])
        stats = {e.name: (e.nins, e.nwait) for e in (pe, act, dve, pool, sp)}
    return nc, stats


def _consts(NLAT):
    R = NLAT // 64
    NT_L = NLAT // 128
    c = {}
    c["ident"] = np.eye(128, dtype=np.float32).astype(bf)
    sel = np.zeros((2, 256), np.float32)
    sel[0, :128] = 1.0
    sel[1, 128:] = 1.0
    c["sel"] = sel
    i64 = np.arange(64)
    a64 = 2 * np.pi * np.outer(i64, i64) / 64.0
    C64 = np.cos(a64) / 8.0
    S64 = np.sin(a64) / 8.0
    C4 = np.kron(np.eye(4), C64)
    S4 = np.kron(np.eye(4), S64)
    c4s4 = np.stack([C4, S4], 0).reshape(2, 2, 128, 256).transpose(2, 1, 0, 3)
    c["c4s4"] = np.ascontiguousarray(c4s4).astype(bf)
    ir = np.arange(R)
    ar = 2 * np.pi * np.outer(ir, ir) / R
    CR = np.cos(ar) / np.sqrt(R)
    SR = np.sin(ar) / np.sqrt(R)
    c["crs"] = np.ascontiguousarray(np.stack([CR, SR, -SR], 1)).astype(bf)
    at = 2 * np.pi * np.outer(ir, i64) / NLAT
    c["tw"] = np.ascontiguousarray(np.stack([np.cos(at), np.sin(at)], 1)).astype(np.float32)
    c["c64"] = np.ascontiguousarray(np.stack([C64, S64], 1)).astype(bf)
    i256 = np.arange(256)
    a256 = 2 * np.pi * np.outer(i256, i256) / 256.0
    C256 = np.cos(a256) / 16.0
    S256 = np.sin(a256) / 16.0
    c256 = np.stack([C256, S256], 0).reshape(2, 2, 128, 256).transpose(2, 1, 0, 3)
    c["c256"] = np.ascontiguousarray(c256).astype(bf)
    tok = np.arange(NLAT)
    row = (tok // 64).astype(np.float64)
    col = (tok % 64).astype(np.float64)
    inv = 10000.0 ** (-np.arange(8, dtype=np.float64) / 8)
    ang = np.concatenate([row[:, None] * inv, col[:, None] * inv], -1).astype(np.float32)
    cs = np.stack([np.cos(ang), np.sin(ang)], 1)
    cs = np.concatenate([cs, cs], -1)
    c["rope"] = np.ascontiguousarray(cs.reshape(NT_L, 128, 2, 32).transpose(1, 0, 2, 3)).astype(np.float32)
    return c


def make_inmaps(inputs, NLAT, nb):
    f = lambda a: np.ascontiguousarray(np.asarray(a, dtype=np.float32))
    L = inputs["w_ada"].shape[0]
    shared = dict(
        w_ada=f(inputs["w_ada"]),
        b_ada2=f(np.repeat(np.asarray(inputs["b_ada"])[:, None, :], 2, 1)),
        gmix2=f(np.repeat(np.asarray(inputs["g_mix"])[:, None, :], 2, 1)),
        gffn2=f(np.repeat(np.asarray(inputs["g_ffn"])[:, None, :], 2, 1)),
        gfin=f(np.repeat(np.asarray(inputs["g_final"])[None, :], 128, 0)),
        w_in=f(inputs["w_in"]),
        w_four=f(inputs["w_fourier"]),
        b_four=f(np.broadcast_to(np.asarray(inputs["b_fourier"])[:, None, None, :], (L, 128, 2, 256))),
        gq=f(np.asarray(inputs["g_q_a"]).reshape(L, 3, 128).transpose(0, 2, 1)),
        w_qb=f(inputs["w_q_b"]),
        gkv=f(np.asarray(inputs["g_kv_a"]).reshape(L, 128, 1)),
        w_kvb=f(inputs["w_kv_b"]),
        w_out=f(inputs["w_out"]),
        w_up=f(inputs["w_up"]),
        wdw=f(np.asarray(inputs["w_dw"]).reshape(L, 3, 2 * NCH, 128).transpose(0, 3, 2, 1)),
        bdw=f(np.asarray(inputs["b_dw"]).reshape(L, 2 * NCH, 128).transpose(0, 2, 1)),
        w_down=f(inputs["w_down"]),
    )
    shared.update(_consts(NLAT))
    maps = []
    x = np.asarray(inputs["x"])
    c = np.asarray(inputs["c"], dtype=np.float32)
    ctx = np.asarray(inputs["ctx"])
    cc = np.asarray(inputs["c_ctx"], dtype=np.float32)
    for b in range(nb):
        cv = np.stack([c[b].reshape(8, 128).T, cc.reshape(8, 128).T], -1)
        m = dict(shared)
        m["x"] = f(x[b])
        m["ctx"] = f(ctx[b])
        m["cvec"] = f(cv)
        maps.append(m)
    return maps


_CACHE = {}


def kernel(**inputs):
    x = np.asarray(inputs["x"])
    B, NLAT, _ = x.shape
    key = (NLAT,)
    if key not in _CACHE:
        _CACHE[key] = build_program(NLAT)
    nc, _ = _CACHE[key]
    maps = make_inmaps(inputs, NLAT, B)
    res = run_bass_kernel_spmd(nc, maps, core_ids=list(range(B)))
    out = np.stack([np.asarray(res.results[b]["out"], dtype=np.float32) for b in range(B)], 0)
    return out
```

```python
import numpy as np
import ml_dtypes
from contextlib import ExitStack
import concourse.bass as bass
import concourse.mybir as mybir
from concourse.bass_utils import run_bass_kernel_spmd

F32 = mybir.dt.float32
BF16 = mybir.dt.bfloat16
ACTF = mybir.ActivationFunctionType
ALU = mybir.AluOpType
bf = ml_dtypes.bfloat16

D = 1024
NCTX = 256
DFF = 2816
NCH = 22
EPS = 1e-6
SCALE = 96 ** -0.5


class Buf:
    def __init__(self, t, name, multi=False):
        self.t = t
        self.name = name
        self.multi = multi
        self.excl = False
        self.w = {}
        self.r = {}

    def __getitem__(self, k):
        return self.t[k]


def _merge(d, ev):
    k = id(ev[0])
    if k not in d or d[k][1] < ev[1]:
        d[k] = ev


class Eng:
    def __init__(self, fw, e, name, is_pe=False):
        self.e = e
        self.name = name
        self.is_pe = is_pe
        self.sem = fw.new_sem("s_" + name)
        self.cnt = 0
        self.seen = {}
        self.nwait = 0
        self.nins = 0

    def wait(self, ev):
        sem, val = ev
        if self.seen.get(id(sem), 0) >= val:
            return
        self.e.wait_ge(sem, val)
        self.seen[id(sem)] = val
        self.nwait += 1


class FW:
    def __init__(self, nc, stack, n_dma_sems=14):
        self.nc = nc
        self.stack = stack
        self.pe = Eng(self, nc.tensor, "pe", is_pe=True)
        self.act = Eng(self, nc.scalar, "act")
        self.dve = Eng(self, nc.vector, "dve")
        self.pool = Eng(self, nc.gpsimd, "pool")
        self.sp = Eng(self, nc.sync, "sp")
        self.dq = {}
        for q in (self.sp, self.pool, self.act):
            self.dq[q.name] = dict(sems=[self.new_sem(f"d_{q.name}{i}") for i in range(n_dma_sems)],
                                   vals=[0] * n_dma_sems, idx=0)
        self.uid = 0
        self.stopped = False

    def new_sem(self, name):
        return self.stack.enter_context(self.nc.semaphore(name))

    def sb(self, shape, dt, name, stack=None, multi=False):
        self.uid += 1
        t = (stack or self.stack).enter_context(self.nc.sbuf_tensor(f"{name}_{self.uid}", list(shape), dt))
        return Buf(t, name, multi)

    def ps(self, shape, dt, name, stack=None):
        self.uid += 1
        t = (stack or self.stack).enter_context(self.nc.psum_tensor(f"{name}_{self.uid}", list(shape), dt))
        b = Buf(t, name)
        b.excl = True
        return b

    def dram(self, name, shape, dt, multi=True):
        t = self.nc.dram_tensor(name, list(shape), dt, kind="Internal")
        return Buf(t.ap(), name, multi)

    def _deps(self, eng, reads, writes):
        evs = {}
        for b in reads:
            for ev in b.w.values():
                evs[(id(ev[0]), ev[1])] = ev
            if b.excl:
                for ev in b.r.values():
                    if ev[0] is not eng.sem:
                        evs[(id(ev[0]), ev[1])] = ev
        for b in writes:
            if not b.multi:
                for ev in b.w.values():
                    evs[(id(ev[0]), ev[1])] = ev
            for ev in b.r.values():
                evs[(id(ev[0]), ev[1])] = ev
        for ev in evs.values():
            if eng.is_pe and ev[0] is eng.sem:
                continue
            eng.wait(ev)

    def _record(self, ev, reads, writes):
        for b in reads:
            _merge(b.r, ev)
        for b in writes:
            if b.multi:
                _merge(b.w, ev)
            else:
                b.w = {id(ev[0]): ev}
                b.r = {}

    def op(self, eng, fn, reads=(), writes=()):
        if self.stopped:
            return None
        self._deps(eng, reads, writes)
        ins = fn(eng.e)
        eng.cnt += 1
        eng.nins += 1
        ins.then_inc(eng.sem, 1)
        self._record((eng.sem, eng.cnt), reads, writes)
        return ins

    def dma(self, q, out, in_, reads=(), writes=(), **kw):
        if self.stopped:
            return None
        self._deps(q, reads, writes)
        d = self.dq[q.name]
        i = d["idx"] % len(d["sems"])
        d["idx"] += 1
        if d["vals"][i] > 0:
            q.wait((d["sems"][i], d["vals"][i]))
        ins = q.e.dma_start(out=out, in_=in_, **kw)
        d["vals"][i] += 16
        ins.then_inc(d["sems"][i], 16)
        q.nins += 1
        self._record((d["sems"][i], d["vals"][i]), reads, writes)
        return ins

    def barrier(self):
        if self.stopped:
            return
        engs = (self.pe, self.act, self.dve, self.pool, self.sp)
        evs = [(e.sem, e.cnt) for e in engs if e.cnt > 0]
        for d in self.dq.values():
            for sm, v in zip(d["sems"], d["vals"]):
                if v > 0:
                    evs.append((sm, v))
        for e in engs:
            for ev in evs:
                if ev[0] is e.sem:
                    continue
                e.wait(ev)

    def finish(self, bufs):
        for b in bufs:
            for ev in list(b.w.values()):
                self.sp.wait(ev)


class Ring:
    def __init__(self, bufs):
        self.bufs = bufs
        self.i = 0

    def next(self):
        b = self.bufs[self.i % len(self.bufs)]
        self.i += 1
        return b


class _Stop(Exception):
    pass


def build_program(NLAT=8192, depth=2, dbg=False, stop_at=None):
    NT_L = NLAT // 128
    NT_C = NCTX // 128
    NT = NT_L + NT_C
    NTOK = NLAT + NCTX
    R = NLAT // 64
    assert R <= 128 and R % 16 == 0
    QB = 512 if NLAT % 512 == 0 else 128

    nc = bass.Bass("TRN2", target_bir_lowering=False)

    def din(name, shape, dt=F32):
        return Buf(nc.dram_tensor(name, list(shape), dt, kind="ExternalInput").ap(), name)

    x_in = din("x", [NLAT, D])
    ctx_in = din("ctx", [NCTX, D])
    cvec = din("cvec", [128, 8, 2])
    w_ada = din("w_ada", [depth, D, 6 * D])
    b_ada2 = din("b_ada2", [depth, 2, 6 * D])
    gmix2 = din("gmix2", [depth, 2, D])
    gffn2 = din("gffn2", [depth, 2, D])
    gfin = din("gfin", [128, D])
    w_in = din("w_in", [depth, D, 800])
    w_four = din("w_four", [depth, 256, 256])
    b_four = din("b_four", [depth, 128, 2, 256])
    gq = din("gq", [depth, 128, 3])
    w_qb = din("w_qb", [depth, 384, 1152])
    gkv = din("gkv", [depth, 128, 1])
    w_kvb = din("w_kvb", [depth, 128, 1536])
    w_out = din("w_out", [depth, D, D])
    w_up = din("w_up", [depth, D, 2 * DFF])
    wdw = din("wdw", [depth, 128, 2 * NCH, 3])
    bdw = din("bdw", [depth, 128, 2 * NCH])
    w_down = din("w_down", [depth, DFF, D])
    ident_d = din("ident", [128, 128], BF16)
    sel_d = din("sel", [2, 256])
    c4s4_d = din("c4s4", [128, 2, 2, 256], BF16)
    crs_d = din("crs", [R, 3, R], BF16)
    tw_d = din("tw", [R, 2, 64])
    c64_d = din("c64", [64, 2, 64], BF16)
    c256_d = din("c256", [128, 2, 2, 256], BF16)
    rope_d = din("rope", [128, NT_L, 2, 32])
    out_d = Buf(nc.dram_tensor("out", [NLAT, D], F32, kind="ExternalOutput").ap(), "out", multi=True)
    dbg_d = {}

    with ExitStack() as st:
        fw = FW(nc, st)
        pe, act, dve, pool, sp = fw.pe, fw.act, fw.dve, fw.pool, fw.sp

        XRES = [fw.dram(f"xres{t}", [128, D], F32, multi=False) for t in range(NT)]
        MB = fw.dram("mb", [2, 6, 128, D], F32)
        ABd = fw.dram("abd", [NTOK, 512], BF16)
        Y2d = fw.dram("y2d", [R, 64, 2, 256], BF16)
        FOURd = fw.dram("fourd", [NTOK, 256], BF16)
        QTd = fw.dram("qtd", [12, 96, NTOK], BF16)
        ATTd = fw.dram("attd", [NTOK, 768], BF16)
        XHTl = fw.dram("xhtl", [8, 128, NLAT + 2], BF16)
        XHTc = fw.dram("xhtc", [8, 128, NCTX + 2], BF16)

        ident = fw.sb([128, 128], BF16, "ident")
        sel = fw.sb([2, 256], F32, "sel")
        zero_t = fw.sb([128, 8, 2], BF16, "zero")
        fw.dma(sp, ident[:], ident_d[:], reads=[ident_d], writes=[ident])
        fw.dma(sp, sel[:], sel_d[:], reads=[sel_d], writes=[sel])
        fw.op(dve, lambda e: e.memset(zero_t[:], 0.0), writes=[zero_t])
        cast_rr = [0]

        def cast_eng():
            cast_rr[0] += 1
            return (dve, pool)[cast_rr[0] % 2]

        def rstd_from_ss(stt, n_cols, inv_n, ncol=1):
            fw.op(dve, lambda e: e.tensor_scalar(out=stt[:, 2:2 + ncol], in0=stt[:, 0:ncol], scalar1=inv_n,
                                                 scalar2=EPS, op0=ALU.mult, op1=ALU.add), reads=[stt], writes=[stt])
            fw.op(act, lambda e: e.sqrt(out=stt[:, 2:2 + ncol], in_=stt[:, 2:2 + ncol]), reads=[stt], writes=[stt])
            fw.op(dve, lambda e: e.reciprocal(out=stt[:, 4:4 + ncol], in_=stt[:, 2:2 + ncol]), reads=[stt], writes=[stt])

        def chk(name):
            if stop_at == name:
                fw.stopped = True

        try:
            for l in range(depth):
                last = (l == depth - 1)
                src_tiles = None if l == 0 else XRES

                def src_ap(t):
                    if l == 0:
                        if t < NT_L:
                            return x_in, x_in[t * 128:(t + 1) * 128, :]
                        return ctx_in, ctx_in[(t - NT_L) * 128:(t - NT_L + 1) * 128, :]
                    return XRES[t], XRES[t][:, :]

                fw.barrier()
                with ExitStack() as ph:
                    sil = fw.sb([128, 8, 2], F32, "sil", ph)
                    mrow = fw.sb([2, 6 * D], F32, "mrow", ph)
                    brow = fw.sb([2, 6 * D], F32, "brow", ph)
                    g2 = fw.sb([2, 2, D], F32, "g2", ph)
                    wst = Ring([fw.sb([128, 3072], F32, f"wst{i}", ph) for i in range(3)])
                    bst = Ring([fw.sb([128, D], F32, f"bst{i}", ph) for i in range(2)])
                    psm = [fw.ps([128, 512], F32, f"psm{i}", ph) for i in range(8)]
                    fw.dma(sp, sil[:], cvec[:], reads=[cvec], writes=[sil])
                    fw.dma(sp, brow[:], b_ada2[l], reads=[b_ada2], writes=[brow])
                    fw.dma(sp, g2[:, 0, :], gmix2[l], reads=[gmix2], writes=[g2])
                    fw.dma(sp, g2[:, 1, :], gffn2[l], reads=[gffn2], writes=[g2])
                    fw.op(act, lambda e: e.activation(out=sil[:], in_=sil[:], func=ACTF.Silu), reads=[sil], writes=[sil])
                    for half in range(2):
                        for k in range(8):
                            wt = wst.next()
                            fw.dma(sp, wt[:], w_ada[l, k * 128:(k + 1) * 128, half * 3072:(half + 1) * 3072],
                                   reads=[w_ada], writes=[wt])
                            for j in range(6):
                                fw.op(pe, lambda e: e.matmul(psm[j][0:2, :], lhsT=sil[:, k, :], rhs=wt[:, j * 512:(j + 1) * 512],
                                                             start=(k == 0), stop=(k == 7)), reads=[sil, wt], writes=[psm[j]])
                        for j in range(6):
                            c0 = half * 3072 + j * 512
                            fw.op(dve, lambda e: e.tensor_tensor(out=mrow[:, c0:c0 + 512], in0=psm[j][0:2, :],
                                                                 in1=brow[:, c0:c0 + 512], op=ALU.add),
                                  reads=[psm[j], brow], writes=[mrow])
                    for (c0, gi) in ((1 * D, 0), (4 * D, 1)):
                        fw.op(dve, lambda e: e.scalar_tensor_tensor(out=mrow[:, c0:c0 + D], in0=mrow[:, c0:c0 + D], scalar=1.0,
                                                                    in1=g2[:, gi, :], op0=ALU.add, op1=ALU.mult),
                              reads=[mrow, g2], writes=[mrow])
                    pi = 0
                    for s in range(2):
                        for v in range(6):
                            bt = bst.next()
                            for hh in range(2):
                                p_ = psm[pi % 8]
                                pi += 1
                                fw.op(pe, lambda e: e.matmul(p_[:, :], lhsT=sel[:, s * 128:(s + 1) * 128],
                                                             rhs=mrow[:, v * D + hh * 512: v * D + (hh + 1) * 512],
                                                             start=True, stop=True), reads=[sel, mrow], writes=[p_])
                                fw.op(act if hh else dve, lambda e: e.tensor_copy(out=bt[:, hh * 512:(hh + 1) * 512], in_=p_[:, :])
                                      if not hh else e.copy(out=bt[:, hh * 512:(hh + 1) * 512], in_=p_[:, :]),
                                      reads=[p_], writes=[bt])
                            fw.dma(pool, MB[s, v], bt[:], reads=[bt], writes=[MB])

                fw.barrier()
                chk("M%d" % l)
                with ExitStack() as mx:
                    ckvnT = fw.sb([128, NTOK], BF16, "ckvnT", mx, multi=True)
                    KT = [fw.sb([96, NTOK], BF16, f"KT{i}", mx, multi=True) for i in range(2)]
                    wkvb = fw.sb([128, 1536], BF16, "wkvb", mx)

                    with ExitStack() as ph:
                        winb = fw.sb([128, 8, 800], BF16, "winb", ph, multi=True)
                        wab = fw.sb([128, 8, 512], BF16, "wab", ph, multi=True)
                        wqb = fw.sb([128, 3, 1152], BF16, "wqb", ph, multi=True)
                        gqt = fw.sb([128, 3], F32, "gqt", ph)
                        gkvt = fw.sb([128, 1], F32, "gkvt", ph)
                        GS = [[fw.sb([128, D], F32, f"gs{s}{v}", ph) for v in range(2)] for s in range(2)]
                        psb = [fw.ps([128, 1024], BF16, f"psb{i}", ph) for i in range(3)]
                        psf = [fw.ps([128, 512], F32, f"psf{i}", ph) for i in range(5)]
                        prep = ExitStack()
                        stg = Ring([fw.sb([128, 1536], F32, f"stg{i}", prep) for i in range(2)])
                        wfb = fw.sb([128, 2, 256], BF16, "wfb", prep, multi=True)
                        mab = fw.sb([128, 2, 512], BF16, "mab", prep, multi=True)
                        wfT = fw.sb([128, 2, D], BF16, "wfT", prep, multi=True)
                        c4s4 = fw.sb([128, 2, 2, 256], BF16, "c4s4", prep)

                        fw.dma(sp, c4s4[:], c4s4_d[:], reads=[c4s4_d], writes=[c4s4])
                        fw.dma(sp, gqt[:], gq[l], reads=[gq], writes=[gqt])
                        fw.dma(sp, gkvt[:], gkv[l], reads=[gkv], writes=[gkvt])
                        for s in range(2):
                            fw.dma(sp, GS[s][0][:], MB[s, 1], reads=[MB], writes=[GS[s][0]])
                            fw.dma(sp, GS[s][1][:], MB[s, 0], reads=[MB], writes=[GS[s][1]])
                        for k in range(8):
                            sg_ = stg.next()
                            fw.dma(sp, sg_[:, 0:800], w_in[l, k * 128:(k + 1) * 128, :], reads=[w_in], writes=[sg_])
                            fw.op(cast_eng(), lambda e: e.tensor_copy(out=winb[:, k, :], in_=sg_[:, 0:800]), reads=[sg_], writes=[winb])
                        for kc in range(3):
                            sg_ = stg.next()
                            fw.dma(sp, sg_[:, 0:1152], w_qb[l, kc * 128:(kc + 1) * 128, :], reads=[w_qb], writes=[sg_])
                            fw.op(cast_eng(), lambda e: e.tensor_scalar(out=wqb[:, kc, :], in0=sg_[:, 0:1152], scalar1=gqt[:, kc:kc + 1],
                                                                        scalar2=None, op0=ALU.mult), reads=[sg_, gqt], writes=[wqb])
                        sg_ = stg.next()
                        fw.dma(sp, sg_[:, 0:1536], w_kvb[l], reads=[w_kvb], writes=[sg_])
                        fw.op(cast_eng(), lambda e: e.tensor_scalar(out=wkvb[:], in0=sg_[:, 0:1536], scalar1=gkvt[:, 0:1],
                                                                    scalar2=None, op0=ALU.mult), reads=[sg_, gkvt], writes=[wkvb])
                        for cc in range(2):
                            sg_ = stg.next()
                            fw.dma(sp, sg_[:, 0:256], w_four[l, cc * 128:(cc + 1) * 128, :], reads=[w_four], writes=[sg_])
                            fw.op(cast_eng(), lambda e: e.tensor_copy(out=wfb[:, cc, :], in_=sg_[:, 0:256]), reads=[sg_], writes=[wfb])
                        for cc in range(2):
                            p_ = psf[cc]
                            for X in range(2):
                                for c2 in range(2):
                                    fw.op(pe, lambda e: e.matmul(p_[:, X * 256:(X + 1) * 256],
                                                                 lhsT=c4s4[:, c2, X, cc * 128:(cc + 1) * 128], rhs=wfb[:, c2, :],
                                                                 start=(c2 == 0), stop=(c2 == 1)), reads=[c4s4, wfb], writes=[p_])
                            fw.op(dve, lambda e: e.tensor_copy(out=mab[:, cc, 0:256], in_=p_[:, 0:256]), reads=[p_], writes=[mab])
                            fw.op(dve, lambda e: e.tensor_scalar(out=mab[:, cc, 256:512], in0=p_[:, 256:512], scalar1=-1.0,
                                                                 scalar2=None, op0=ALU.mult), reads=[p_], writes=[mab])
                        for cc in range(2):
                            for k in range(8):
                                fw.op(pe, lambda e: e.transpose(out=psb[cc][:, k * 128:(k + 1) * 128],
                                                                in_=winb[:, k, cc * 128:(cc + 1) * 128], identity=ident[:]),
                                      reads=[winb, ident], writes=[psb[cc]])
                            fw.op(dve, lambda e: e.tensor_copy(out=wfT[:, cc, :], in_=psb[cc][:, :]), reads=[psb[cc]], writes=[wfT])
                        for k in range(8):
                            p_ = psf[k % 5]
                            for cc in range(2):
                                fw.op(pe, lambda e: e.matmul(p_[:, :], lhsT=wfT[:, cc, k * 128:(k + 1) * 128], rhs=mab[:, cc, :],
                                                             start=(cc == 0), stop=(cc == 1)), reads=[wfT, mab], writes=[p_])
                            fw.op(cast_eng() if False else dve, lambda e: e.tensor_copy(out=wab[:, k, :], in_=p_[:, :]),
                                  reads=[p_], writes=[wab])

                        chk('INW%d' % l)
                        fw.barrier()
                        prep.close()
                        xr = Ring([fw.sb([128, D], F32, f"xt{i}", ph) for i in range(2)])
                        junk = fw.sb([128, D], BF16, "junk", ph)
                        xh = Ring([fw.sb([128, D], F32, f"xh{i}", ph) for i in range(1)])
                        xb = Ring([fw.sb([128, D], BF16, f"xb{i}", ph) for i in range(2)])
                        hT = Ring([fw.sb([128, 8, 128], BF16, f"hT{i}", ph) for i in range(2)])
                        stt = Ring([fw.sb([128, 8], F32, f"stt{i}", ph) for i in range(3)])
                        abt = Ring([fw.sb([128, 512], BF16, f"abt{i}", ph) for i in range(2)])
                        cn = Ring([fw.sb([128, 512], BF16, f"cn{i}", ph) for i in range(2)])
                        cn2 = Ring([fw.sb([128, 96], BF16, f"cn2{i}", ph) for i in range(2)])
                        rtmp = Ring([fw.sb([128, 2, 32], F32, f"rtmp{i}", ph) for i in range(2)])
                        cqT = Ring([fw.sb([128, 3, 128], BF16, f"cqT{i}", ph) for i in range(2)])
                        qsb = Ring([fw.sb([128, 12, 96], F32, f"qsb{i}", ph) for i in range(1)])
                        qtmp = Ring([fw.sb([128, 2, 12, 32], F32, f"qtmp{i}", ph) for i in range(1)])
                        qbf = Ring([fw.sb([128, 12, 96], BF16, f"qbf{i}", ph) for i in range(2)])
                        qTt = Ring([fw.sb([96, 12, 128], BF16, f"qTt{i}", ph) for i in range(2)])
                        for b_ in cn2.bufs:
                            fw.op(dve, lambda e: e.memset(b_[:], 0.0), writes=[b_])
                        QTv = QTd.t.rearrange("h d t -> d h t")

                        roper = Ring([fw.sb([128, 2, 32], F32, f"rope{i}", ph) for i in range(3)])

                        def load_x(t):
                            xt = xr.next()
                            sb_, sap = src_ap(t)
                            fw.dma(sp, xt[:], sap, reads=[sb_], writes=[xt])
                            rp = None
                            if t < NT_L:
                                rp = roper.next()
                                fw.dma(sp, rp[:], rope_d[:, t, :, :], reads=[rope_d], writes=[rp])
                            return xt, rp

                        nxt = load_x(0)
                        for t in range(NT):
                            xt, ropet = nxt
                            if t + 1 < NT:
                                nxt = load_x(t + 1)
                            s = 0 if t < NT_L else 1
                            s_ = stt.next()
                            fw.op(dve, lambda e: e.memset(s_[:], 0.0), writes=[s_])
                            fw.op(act, lambda e: e.activation(out=junk[:], in_=xt[:], func=ACTF.Square, accum_out=s_[:, 0:1]),
                                  reads=[xt], writes=[junk, s_])
                            rstd_from_ss(s_, 1, 1.0 / D)
                            xh_ = xh.next()
                            fw.op(dve, lambda e: e.scalar_tensor_tensor(out=xh_[:], in0=xt[:], scalar=s_[:, 4:5], in1=GS[s][0][:],
                                                                        op0=ALU.mult, op1=ALU.mult), reads=[xt, s_, GS[s][0]], writes=[xh_])
                            xb_ = xb.next()
                            fw.op(pool, lambda e: e.tensor_tensor(out=xb_[:], in0=xh_[:], in1=GS[s][1][:], op=ALU.add),
                                  reads=[xh_, GS[s][1]], writes=[xb_])
                            for k in range(8):
                                fw.op(pe, lambda e: e.transpose(out=psb[0][:, k * 128:(k + 1) * 128], in_=xb_[:, k * 128:(k + 1) * 128],
                                                                identity=ident[:]), reads=[xb_, ident], writes=[psb[0]])
                            hT_ = hT.next()
                            fw.op(act, lambda e: e.copy(out=hT_[:].rearrange("p k t -> p (k t)"), in_=psb[0][:, :]),
                                  reads=[psb[0]], writes=[hT_])
                            if t == 0: chk('INTa%d' % l)
                            for k in range(8):
                                fw.op(pe, lambda e: e.matmul(psf[0][:, :], lhsT=hT_[:, k, :], rhs=wab[:, k, :], start=(k == 0), stop=(k == 7)),
                                      reads=[hT_, wab], writes=[psf[0]])
                            for k in range(8):
                                fw.op(pe, lambda e: e.matmul(psf[1][:, :], lhsT=hT_[:, k, :], rhs=winb[:, k, 256:768], start=(k == 0), stop=(k == 7)),
                                      reads=[hT_, winb], writes=[psf[1]])
                            for k in range(8):
                                fw.op(pe, lambda e: e.matmul(psf[2][:, 0:32], lhsT=hT_[:, k, :], rhs=winb[:, k, 768:800], start=(k == 0), stop=(k == 7)),
                                      reads=[hT_, winb], writes=[psf[2]])
                            ab_ = abt.next()
                            fw.op(act, lambda e: e.copy(out=ab_[:], in_=psf[0][:, :]), reads=[psf[0]], writes=[ab_])
                            fw.dma(pool, ABd[t * 128:(t + 1) * 128, :], ab_[:], reads=[ab_], writes=[ABd])
                            if t == 0: chk('INTb%d' % l)
                            s2 = stt.next()
                            fw.op(dve, lambda e: e.memset(s2[:], 0.0), writes=[s2])
                            fw.op(act, lambda e: e.activation(out=junk[:, 0:384], in_=psf[1][:, 0:384], func=ACTF.Square, accum_out=s2[:, 0:1]),
                                  reads=[psf[1]], writes=[junk, s2])
                            fw.op(act, lambda e: e.activation(out=junk[:, 384:512], in_=psf[1][:, 384:512], func=ACTF.Square, accum_out=s2[:, 1:2]),
                                  reads=[psf[1]], writes=[junk, s2])
                            fw.op(dve, lambda e: e.tensor_scalar(out=s2[:, 0:1], in0=s2[:, 0:1], scalar1=128.0 / 384.0, scalar2=None, op0=ALU.mult),
                                  reads=[s2], writes=[s2])
                            rstd_from_ss(s2, 2, 1.0 / 128, ncol=2)
                            cn_ = cn.next()
                            fw.op(dve, lambda e: e.tensor_scalar(out=cn_[:, 0:384], in0=psf[1][:, 0:384], scalar1=s2[:, 4:5], scalar2=None, op0=ALU.mult),
                                  reads=[psf[1], s2], writes=[cn_])
                            fw.op(dve, lambda e: e.tensor_scalar(out=cn_[:, 384:512], in0=psf[1][:, 384:512], scalar1=s2[:, 5:6], scalar2=None, op0=ALU.mult),
                                  reads=[psf[1], s2], writes=[cn_])
                            c2_ = cn2.next()
                            if s == 0:
                                rt = rtmp.next()
                                fw.op(dve, lambda e: e.tensor_tensor(out=rt[:, 0, :], in0=psf[2][:, 0:32], in1=ropet[:, 0, :], op=ALU.mult),
                                      reads=[psf[2], ropet], writes=[rt])
                                fw.op(dve, lambda e: e.tensor_tensor(out=rt[:, 1, :], in0=psf[2][:, 0:32], in1=ropet[:, 1, :], op=ALU.mult),
                                      reads=[psf[2], ropet], writes=[rt])
                                fw.op(pool, lambda e: e.tensor_tensor(out=c2_[:, 64:80], in0=rt[:, 0, 0:16], in1=rt[:, 1, 16:32], op=ALU.subtract),
                                      reads=[rt], writes=[c2_])
                                fw.op(pool, lambda e: e.tensor_tensor(out=c2_[:, 80:96], in0=rt[:, 0, 16:32], in1=rt[:, 1, 0:16], op=ALU.add),
                                      reads=[rt], writes=[c2_])
                            else:
                                fw.op(dve, lambda e: e.tensor_copy(out=c2_[:, 64:96], in_=psf[2][:, 0:32]), reads=[psf[2]], writes=[c2_])
                            if t == 0: chk('INTc%d' % l)
                            for j in range(4):
                                fw.op(pe, lambda e: e.transpose(out=psb[1][:, j * 128:(j + 1) * 128], in_=cn_[:, j * 128:(j + 1) * 128],
                                                                identity=ident[:]), reads=[cn_, ident], writes=[psb[1]])
                            fw.op(pe, lambda e: e.transpose(out=psb[1][0:96, 512:640], in_=c2_[:, 0:96], identity=ident[:]),
                                  reads=[c2_, ident], writes=[psb[1]])
                            if t == 0: chk('INTc2%d' % l)
                            cq_ = cqT.next()
                            fw.op(act, lambda e: e.copy(out=cq_[:].rearrange("p k t -> p (k t)"), in_=psb[1][:, 0:384]), reads=[psb[1]], writes=[cq_])
                            fw.op(dve, lambda e: e.tensor_copy(out=ckvnT[:, t * 128:(t + 1) * 128], in_=psb[1][:, 384:512]),
                                  reads=[psb[1]], writes=[ckvnT])
                            if t == 0: chk('INTc3%d' % l)
                            for i in range(2):
                                fw.op(dve if i else act, (lambda e: e.tensor_copy(out=KT[1][64:96, t * 128:(t + 1) * 128], in_=psb[1][64:96, 512:640]))
                                      if i else (lambda e: e.copy(out=KT[0][64:96, t * 128:(t + 1) * 128], in_=psb[1][64:96, 512:640])),
                                      reads=[psb[1]], writes=[KT[i]])
                            if t == 0: chk('INTd%d' % l)
                            if s == 1 and last:
                                continue
                            for (pi_, c0, c1) in ((2, 0, 512), (3, 512, 1024), (4, 1024, 1152)):
                                for kc in range(3):
                                    fw.op(pe, lambda e: e.matmul(psf[pi_][:, 0:c1 - c0], lhsT=cq_[:, kc, :], rhs=wqb[:, kc, c0:c1],
                                                                 start=(kc == 0), stop=(kc == 2)), reads=[cq_, wqb], writes=[psf[pi_]])
                            q_ = qsb.next()
                            qf = q_[:].rearrange("p h d -> p (h d)")
                            fw.op(act, lambda e: e.copy(out=qf[:, 0:512], in_=psf[2][:, :]), reads=[psf[2]], writes=[q_])
                            fw.op(dve, lambda e: e.tensor_copy(out=qf[:, 512:1024], in_=psf[3][:, :]), reads=[psf[3]], writes=[q_])
                            fw.op(act, lambda e: e.copy(out=qf[:, 1024:1152], in_=psf[4][:, 0:128]), reads=[psf[4]], writes=[q_])
                            qb_ = qbf.next()
                            fw.op(pool, lambda e: e.tensor_copy(out=qb_[:, :, 0:64], in_=q_[:, :, 0:64]), reads=[q_], writes=[qb_])
                            if s == 0:
                                qt_ = qtmp.next()
                                for h in range(12):
                                    eng = (dve, pool)[h % 2]
                                    fw.op(eng, lambda e: e.tensor_tensor(out=qt_[:, 0, h, :], in0=q_[:, h, 64:96], in1=ropet[:, 0, :], op=ALU.mult),
                                          reads=[q_, ropet], writes=[qt_])
                                    fw.op(eng, lambda e: e.tensor_tensor(out=qt_[:, 1, h, :], in0=q_[:, h, 64:96], in1=ropet[:, 1, :], op=ALU.mult),
                                          reads=[q_, ropet], writes=[qt_])
                                fw.op(dve, lambda e: e.tensor_tensor(out=qb_[:, :, 64:80], in0=qt_[:, 0, :, 0:16], in1=qt_[:, 1, :, 16:32], op=ALU.subtract),
                                      reads=[qt_], writes=[qb_])
                                fw.op(pool, lambda e: e.tensor_tensor(out=qb_[:, :, 80:96], in0=qt_[:, 0, :, 16:32], in1=qt_[:, 1, :, 0:16], op=ALU.add),
                                      reads=[qt_], writes=[qb_])
                            else:
                                fw.op(dve, lambda e: e.tensor_copy(out=qb_[:, :, 64:96], in_=q_[:, :, 64:96]), reads=[q_], writes=[qb_])
                            if t == 0: chk('INTe%d' % l)
                            for h in range(12):
                                pb = psb[2] if h < 8 else psb[0]
                                fw.op(pe, lambda e: e.transpose(out=pb[0:96, (h % 8) * 128:(h % 8 + 1) * 128], in_=qb_[:, h, :], identity=ident[:]),
                                      reads=[qb_, ident], writes=[pb])
                            qT_ = qTt.next()
                            fw.op(act, lambda e: e.copy(out=qT_[:, 0:8, :].rearrange("p h t -> p (h t)"), in_=psb[2][0:96, :]), reads=[psb[2]], writes=[qT_])
                            fw.op(dve, lambda e: e.tensor_copy(out=qT_[:, 8:12, :].rearrange("p h t -> p (h t)"), in_=psb[0][0:96, 0:512]),
                                  reads=[psb[0]], writes=[qT_])
                            fw.dma(pool, QTv[:, :, t * 128:(t + 1) * 128], qT_[:], reads=[qT_], writes=[QTd])

                    fw.barrier()
                    chk("IN%d" % l)
                    with ExitStack() as ph:
                        crs = fw.sb([R, 3, R], BF16, "crs", ph)
                        tw = fw.sb([R, 2, 64], F32, "tw", ph)
                        c64 = fw.sb([64, 2, 64], BF16, "c64", ph)
                        bft = fw.sb([128, 2, 256], F32, "bft", ph)
                        fw.dma(sp, crs[:], crs_d[:], reads=[crs_d], writes=[crs])
                        fw.dma(sp, tw[:], tw_d[:], reads=[tw_d], writes=[tw])
                        fw.dma(sp, c64[:], c64_d[:], reads=[c64_d], writes=[c64])
                        fw.dma(sp, bft[:], b_four[l], reads=[b_four], writes=[bft])
                        abg = Ring([fw.sb([R, 16, 512], BF16, f"abg{i}", ph) for i in range(2)])
                        y2g = Ring([fw.sb([R, 16, 2, 256], BF16, f"y2g{i}", ph) for i in range(2)])
                        y2t = Ring([fw.sb([64, 16, 2, 256], BF16, f"y2t{i}", ph) for i in range(2)])
                        fog = Ring([fw.sb([64, 16, 256], BF16, f"fog{i}", ph) for i in range(2)])
                        tt = Ring([fw.sb([R, 256], F32, f"tt{i}", ph) for i in range(4)])
                        psy = Ring([fw.ps([128, 512], F32, f"psy{i}", ph) for i in range(8)])
                        ABv = ABd.t[0:NLAT, :].rearrange("(a b) c -> a b c", b=64)
                        for g in range(4):
                            ab_ = abg.next()
                            fw.dma(sp, ab_[:], ABv[:, g * 16:(g + 1) * 16, :], reads=[ABd], writes=[ab_])
                            y2_ = y2g.next()
                            for pm in range(8):
                                pr = psy.next()
                                pi_ = psy.next()
                                a_ = ab_[:, 2 * pm:2 * pm + 2, 0:256]
                                b_ = ab_[:, 2 * pm:2 * pm + 2, 256:512]
                                prv = pr[0:R, :].rearrange("p (a b) -> p a b", b=256)
                                piv = pi_[0:R, :].rearrange("p (a b) -> p a b", b=256)
                                fw.op(pe, lambda e: e.matmul(prv, lhsT=crs[:, 0, :], rhs=a_, start=True, stop=False), reads=[crs, ab_], writes=[pr])
                                fw.op(pe, lambda e: e.matmul(prv, lhsT=crs[:, 1, :], rhs=b_, start=False, stop=True), reads=[crs, ab_], writes=[pr])
                                fw.op(pe, lambda e: e.matmul(piv, lhsT=crs[:, 0, :], rhs=b_, start=True, stop=False), reads=[crs, ab_], writes=[pi_])
                                fw.op(pe, lambda e: e.matmul(piv, lhsT=crs[:, 2, :], rhs=a_, start=False, stop=True), reads=[crs, ab_], writes=[pi_])
                                for j in range(2):
                                    ml = 2 * pm + j
                                    m2 = g * 16 + ml
                                    t1 = tt.next()
                                    t2 = tt.next()
                                    fw.op(dve, lambda e: e.tensor_scalar(out=t1[:], in0=pi_[0:R, j * 256:(j + 1) * 256], scalar1=tw[:, 1, m2:m2 + 1],
                                                                         scalar2=None, op0=ALU.mult), reads=[pi_, tw], writes=[t1])
                                    fw.op(dve, lambda e: e.scalar_tensor_tensor(out=y2_[:, ml, 0, :], in0=pr[0:R, j * 256:(j + 1) * 256],
                                                                                scalar=tw[:, 0, m2:m2 + 1], in1=t1[:], op0=ALU.mult, op1=ALU.add),
                                          reads=[pr, tw, t1], writes=[y2_])
                                    fw.op(dve, lambda e: e.tensor_scalar(out=t2[:], in0=pr[0:R, j * 256:(j + 1) * 256], scalar1=tw[:, 1, m2:m2 + 1],
                                                                         scalar2=None, op0=ALU.mult), reads=[pr, tw], writes=[t2])
                                    fw.op(dve, lambda e: e.scalar_tensor_tensor(out=y2_[:, ml, 1, :], in0=pi_[0:R, j * 256:(j + 1) * 256],
                                                                                scalar=tw[:, 0, m2:m2 + 1], in1=t2[:], op0=ALU.mult, op1=ALU.subtract),
                                          reads=[pi_, tw, t2], writes=[y2_])
                            fw.dma(pool, Y2d[:, g * 16:(g + 1) * 16, :, :], y2_[:], reads=[y2_], writes=[Y2d])
                        Y2v = Y2d.t.rearrange("a b c d -> b a c d")
                        FOv = FOURd.t[0:NLAT, :].rearrange("(a b) c -> a b c", b=R)
                        for g in range(R // 16):
                            yt_ = y2t.next()
                            fw.dma(sp, yt_[:], Y2v[:, g * 16:(g + 1) * 16, :, :], reads=[Y2d], writes=[yt_])
                            fo_ = fog.next()
                            for p2 in range(8):
                                pz = psy.next()
                                pzv = pz[0:64, :].rearrange("p (a b) -> p a b", b=256)
                                fw.op(pe, lambda e: e.matmul(pzv, lhsT=c64[:, 0, :], rhs=yt_[:, 2 * p2:2 * p2 + 2, 0, :], start=True, stop=False),
                                      reads=[c64, yt_], writes=[pz])
                                fw.op(pe, lambda e: e.matmul(pzv, lhsT=c64[:, 1, :], rhs=yt_[:, 2 * p2:2 * p2 + 2, 1, :], start=False, stop=True),
                                      reads=[c64, yt_], writes=[pz])
                                fw.op(dve, lambda e: e.tensor_tensor(out=fo_[:, 2 * p2:2 * p2 + 2, :], in0=pzv, in1=bft[0:64, :, :], op=ALU.add),
                                      reads=[pz, bft], writes=[fo_])
                            fw.dma(pool, FOv[:, g * 16:(g + 1) * 16, :], fo_[:], reads=[fo_], writes=[FOURd])
                        if not last:
                            c256 = fw.sb([128, 2, 2, 256], BF16, "c256", ph)
                            abc = fw.sb([128, 2, 512], BF16, "abc", ph)
                            foc = fw.sb([128, 2, 256], BF16, "foc", ph)
                            fw.dma(sp, c256[:], c256_d[:], reads=[c256_d], writes=[c256])
                            fw.dma(sp, abc[:], ABd.t[NLAT:NTOK, :].rearrange("(a p) c -> p a c", p=128), reads=[ABd], writes=[abc])
                            for n_ in range(2):
                                pz = psy.next()
                                for mc in range(2):
                                    fw.op(pe, lambda e: e.matmul(pz[:, 0:256], lhsT=c256[:, mc, 0, n_ * 128:(n_ + 1) * 128], rhs=abc[:, mc, 0:256],
                                                                 start=(mc == 0), stop=False), reads=[c256, abc], writes=[pz])
                                for mc in range(2):
                                    fw.op(pe, lambda e: e.matmul(pz[:, 0:256], lhsT=c256[:, mc, 1, n_ * 128:(n_ + 1) * 128], rhs=abc[:, mc, 256:512],
                                                                 start=False, stop=(mc == 1)), reads=[c256, abc], writes=[pz])
                                fw.op(dve, lambda e: e.tensor_tensor(out=foc[:, n_, :], in0=pz[:, 0:256], in1=bft[:, 0, :], op=ALU.add),
                                      reads=[pz, bft], writes=[foc])
                            fw.dma(pool, FOURd.t[NLAT:NTOK, :].rearrange("(a p) c -> p a c", p=128), foc[:], reads=[foc], writes=[FOURd])

                    fw.barrier()
                    chk("F%d" % l)
                    with ExitStack() as ph:
                        Va = [fw.sb([128, NT, 65], BF16, f"Va{i}", ph, multi=True) for i in range(2)]
                        NQ = NTOK if not last else NLAT
                        qts = [fw.sb([96, NQ], BF16, f"qts{i}", ph) for i in range(2)]
                        ptr = Ring([fw.sb([128, 2, QB], BF16, f"pt{i}", ph) for i in range(3)])
                        rcr = Ring([fw.sb([128, 4], F32, f"rc{i}", ph) for i in range(2)])
                        att = Ring([fw.sb([128, 4, 64], BF16, f"att{i}", ph) for i in range(3)])
                        pss = Ring([fw.ps([128, 1024], F32, f"pss{i}", ph) for i in range(2)])
                        pso = Ring([fw.ps([128, 512], F32, f"pso{i}", ph) for i in range(2)])
                        pkv = Ring([fw.ps([128, 512], F32, f"pkv{i}", ph) for i in range(2)])
                        for i in range(2):
                            fw.op(pool, lambda e: e.memset(Va[i][:, :, 64:65], 1.0), writes=[Va[i]])

                        def build_kv(h):
                            b = h % 2
                            c0 = 0
                            while c0 < NTOK:
                                w_ = min(512, NTOK - c0)
                                p_ = pkv.next()
                                fw.op(pe, lambda e: e.matmul(p_[0:64, 0:w_], lhsT=wkvb[:, h * 128:h * 128 + 64], rhs=ckvnT[:, c0:c0 + w_],
                                                             start=True, stop=True), reads=[wkvb, ckvnT], writes=[p_])
                                fw.op(dve, lambda e: e.tensor_copy(out=KT[b][0:64, c0:c0 + w_], in_=p_[0:64, 0:w_]), reads=[p_], writes=[KT[b]])
                                c0 += w_
                            c0 = 0
                            while c0 < NT:
                                n_ = min(8, NT - c0)
                                p_ = pkv.next()
                                for c in range(n_):
                                    fw.op(pe, lambda e: e.matmul(p_[:, c * 64:(c + 1) * 64], lhsT=ckvnT[:, (c0 + c) * 128:(c0 + c + 1) * 128],
                                                                 rhs=wkvb[:, h * 128 + 64:h * 128 + 128], start=True, stop=True),
                                          reads=[wkvb, ckvnT], writes=[p_])
                                fw.op(pool if False else dve, lambda e: e.tensor_copy(out=Va[b][:, c0:c0 + n_, 0:64],
                                                                                     in_=p_[:, 0:n_ * 64].rearrange("p (a b) -> p a b", b=64)),
                                      reads=[p_], writes=[Va[b]])
                                c0 += n_
                            fw.dma(sp, qts[b][:], QTd[h, :, 0:NQ], reads=[QTd], writes=[qts[b]])

                        steps = []
                        for h in range(12):
                            blks = [(q0, QB, list(range(NT))) for q0 in range(0, NLAT, QB)]
                            if not last:
                                blks.append((NLAT, NCTX, list(range(NT_L, NT))))
                            for bi, (q0, W, kcs) in enumerate(blks):
                                nk = len(kcs)
                                for i0 in range(0, nk, 2):
                                    steps.append(dict(h=h, q0=q0, W=W, grp=kcs[i0:i0 + 2], i0=i0, nk=nk, first=(i0 == 0),
                                                      last=(i0 + 2 >= nk), newhead=(bi == 0 and i0 == 0)))
                        cur = {}

                        def emit_S(st_):
                            h, q0, W = st_["h"], st_["q0"], st_["W"]
                            b = h % 2
                            p_ = pss.next()
                            for j, kc in enumerate(st_["grp"]):
                                fw.op(pe, lambda e: e.matmul(p_[:, j * 512:j * 512 + W], lhsT=KT[b][0:96, kc * 128:(kc + 1) * 128],
                                                             rhs=qts[b][0:96, q0:q0 + W], start=True, stop=True),
                                      reads=[KT[b], qts[b]], writes=[p_])
                            st_["p"] = p_

                        def emit_exp(st_):
                            W = st_["W"]
                            p_ = st_["p"]
                            pt = ptr.next()
                            ng = len(st_["grp"])
                            fw.op(act, lambda e: e.activation(out=pt[:, 0:ng, 0:W], in_=p_[:, 0:ng * 512].rearrange("p (a b) -> p a b", b=512)[:, :, 0:W],
                                                              func=ACTF.Exp, scale=SCALE), reads=[p_], writes=[pt])
                            st_["pt"] = pt

                        def emit_PV(st_):
                            h, q0, W, i0, nk = st_["h"], st_["q0"], st_["W"], st_["i0"], st_["nk"]
                            b = h % 2
                            nq = W // 128
                            pt = st_["pt"]
                            if st_["newhead"] and h + 1 < 12:
                                build_kv(h + 1)
                            if st_["first"]:
                                pob = pso.next()
                                po = Buf(pob.t[:, 0:260].rearrange("p (a b) -> p a b", b=65), "po")
                                cur["pob"], cur["po"] = pob, po
                            pob, po = cur["pob"], cur["po"]
                            for j, kc in enumerate(st_["grp"]):
                                for qi in range(nq):
                                    fw.op(pe, lambda e: e.matmul(po[:, qi, :], lhsT=pt[:, j, qi * 128:(qi + 1) * 128], rhs=Va[b][:, kc, :],
                                                                 start=(i0 + j == 0 and qi == 0), stop=(i0 + j == nk - 1 and qi == nq - 1)),
                                          reads=[pt, Va[b]], writes=[pob])
                            if st_["last"]:
                                rc = rcr.next()
                                fw.op(dve, lambda e: e.reciprocal(out=rc[:, 0:nq], in_=po[:, 0:nq, 64]), reads=[pob], writes=[rc])
                                at = att.next()
                                for qi in range(nq):
                                    fw.op(dve, lambda e: e.tensor_scalar(out=at[:, qi, :], in0=po[:, qi, 0:64], scalar1=rc[:, qi:qi + 1], scalar2=None,
                                                                         op0=ALU.mult), reads=[pob, rc], writes=[at])
                                fw.dma(pool, ATTd.t[q0:q0 + W, h * 64:(h + 1) * 64].rearrange("(a p) d -> p a d", p=128), at[:, 0:nq, :],
                                       reads=[at], writes=[ATTd])

                        build_kv(0)
                        emit_S(steps[0])
                        for i_, st_ in enumerate(steps):
                            emit_exp(st_)
                            if i_ + 1 < len(steps):
                                emit_S(steps[i_ + 1])
                            emit_PV(st_)

                fw.barrier()
                chk("ATT%d" % l)
                with ExitStack() as ph:
                    stg = Ring([fw.sb([128, 1024], F32, f"stg{i}", ph) for i in range(2)])
                    woutb = fw.sb([128, 8, D], BF16, "woutb", ph, multi=True)
                    GT = [[fw.sb([128, D], F32, f"gt{s}{v}", ph) for v in range(3)] for s in range(2)]
                    for s in range(2):
                        for v, mv in enumerate((2, 4, 3)):
                            fw.dma(sp, GT[s][v][:], MB[s, mv], reads=[MB], writes=[GT[s][v]])
                    for k in range(8):
                        sg_ = stg.next()
                        fw.dma(sp, sg_[:], w_out[l, k * 128:(k + 1) * 128, :], reads=[w_out], writes=[sg_])
                        fw.op(cast_eng(), lambda e: e.tensor_copy(out=woutb[:, k, :], in_=sg_[:]), reads=[sg_], writes=[woutb])
                    mixr = Ring([fw.sb([128, D], BF16, f"mix{i}", ph) for i in range(3)])
                    mixT = Ring([fw.sb([128, 8, 128], BF16, f"mixT{i}", ph) for i in range(2)])
                    xr = Ring([fw.sb([128, D], F32, f"xo{i}", ph) for i in range(3)])
                    tmpr = Ring([fw.sb([128, D], F32, f"tmp{i}", ph) for i in range(2)])
                    xnr = Ring([fw.sb([128, D], F32, f"xn{i}", ph) for i in range(2)])
                    junk = fw.sb([128, D], BF16, "junk2", ph)
                    stt = Ring([fw.sb([128, 8], F32, f"stt{i}", ph) for i in range(3)])
                    xh = Ring([fw.sb([128, D], F32, f"xh{i}", ph) for i in range(2)])
                    xb = Ring([fw.sb([128, D], BF16, f"xb{i}", ph) for i in range(2)])
                    hT = Ring([fw.sb([128, 8, 128], BF16, f"hT{i}", ph) for i in range(2)])
                    psb = Ring([fw.ps([128, 1024], BF16, f"psb{i}", ph) for i in range(2)])
                    psd = Ring([fw.ps([128, 1024], F32, f"psd{i}", ph) for i in range(2)])
                    XHlv = XHTl.t.rearrange("k p t -> p k t")
                    XHcv = XHTc.t.rearrange("k p t -> p k t")
                    ntile = NT_L if last else NT

                    def load_o(t):
                        m_ = mixr.next()
                        fw.dma(sp, m_[:, 0:256], FOURd[t * 128:(t + 1) * 128, :], reads=[FOURd], writes=[m_])
                        fw.dma(sp, m_[:, 256:1024], ATTd[t * 128:(t + 1) * 128, :], reads=[ATTd], writes=[m_])
                        xt = xr.next()
                        sb_, sap = src_ap(t)
                        fw.dma(sp, xt[:], sap, reads=[sb_], writes=[xt])
                        return m_, xt

                    nxt = load_o(0)
                    for t in range(ntile):
                        m_, xt = nxt
                        if t + 1 < ntile:
                            nxt = load_o(t + 1)
                        s = 0 if t < NT_L else 1
                        pb = psb.next()
                        for k in range(8):
                            fw.op(pe, lambda e: e.transpose(out=pb[:, k * 128:(k + 1) * 128], in_=m_[:, k * 128:(k + 1) * 128], identity=ident[:]),
                                  reads=[m_, ident], writes=[pb])
                        mT = mixT.next()
                        fw.op(act, lambda e: e.copy(out=mT[:].rearrange("p k t -> p (k t)"), in_=pb[:, :]), reads=[pb], writes=[mT])
                        pd = psd.next()
                        for hh in range(2):
                            for k in range(8):
                                fw.op(pe, lambda e: e.matmul(pd[:, hh * 512:(hh + 1) * 512], lhsT=mT[:, k, :], rhs=woutb[:, k, hh * 512:(hh + 1) * 512],
                                                             start=(k == 0), stop=(k == 7)), reads=[mT, woutb], writes=[pd])
                        tm = tmpr.next()
                        fw.op(dve, lambda e: e.tensor_tensor(out=tm[:], in0=pd[:, :], in1=GT[s][0][:], op=ALU.mult), reads=[pd, GT[s][0]], writes=[tm])
                        xn = xnr.next()
                        fw.op(pool, lambda e: e.tensor_tensor(out=xn[:], in0=tm[:], in1=xt[:], op=ALU.add), reads=[tm, xt], writes=[xn])
                        fw.dma(pool, XRES[t][:, :], xn[:], reads=[xn], writes=[XRES[t]])
                        s_ = stt.next()
                        fw.op(dve, lambda e: e.memset(s_[:], 0.0), writes=[s_])
                        fw.op(act, lambda e: e.activation(out=junk[:], in_=xn[:], func=ACTF.Square, accum_out=s_[:, 0:1]), reads=[xn], writes=[junk, s_])
                        rstd_from_ss(s_, 1, 1.0 / D)
                        xh_ = xh.next()
                        fw.op(dve, lambda e: e.scalar_tensor_tensor(out=xh_[:], in0=xn[:], scalar=s_[:, 4:5], in1=GT[s][1][:],
                                                                    op0=ALU.mult, op1=ALU.mult), reads=[xn, s_, GT[s][1]], writes=[xh_])
                        xb_ = xb.next()
                        fw.op(pool, lambda e: e.tensor_tensor(out=xb_[:], in0=xh_[:], in1=GT[s][2][:], op=ALU.add), reads=[xh_, GT[s][2]], writes=[xb_])
                        pb = psb.next()
                        for k in range(8):
                            fw.op(pe, lambda e: e.transpose(out=pb[:, k * 128:(k + 1) * 128], in_=xb_[:, k * 128:(k + 1) * 128], identity=ident[:]),
                                  reads=[xb_, ident], writes=[pb])
                        hT_ = hT.next()
                        fw.op(act, lambda e: e.copy(out=hT_[:].rearrange("p k t -> p (k t)"), in_=pb[:, :]), reads=[pb], writes=[hT_])
                        if s == 0:
                            fw.dma(pool, XHlv[:, :, 1 + t * 128:1 + (t + 1) * 128], hT_[:], reads=[hT_], writes=[XHTl])
                        else:
                            tc_ = t - NT_L
                            fw.dma(pool, XHcv[:, :, 1 + tc_ * 128:1 + (tc_ + 1) * 128], hT_[:], reads=[hT_], writes=[XHTc])

                fw.barrier()
                chk("OUT%d" % l)
                with ExitStack() as ph:
                    wdnb = fw.sb([128, NCH, D], BF16, "wdnb", ph, multi=True)
                    wdwt = fw.sb([128, 2 * NCH, 3], F32, "wdwt", ph)
                    bdwt = fw.sb([128, 2 * NCH], F32, "bdwt", ph)
                    GT2 = [fw.sb([128, D], F32, f"gt2{s}", ph) for s in range(2)]
                    WUPd = fw.dram(f"wupd{l}", [2 * NCH, 128, 8, 128], BF16)
                    fw.dma(sp, wdwt[:], wdw[l], reads=[wdw], writes=[wdwt])
                    fw.dma(sp, bdwt[:], bdw[l], reads=[bdw], writes=[bdwt])
                    for s in range(2):
                        fw.dma(sp, GT2[s][:], MB[s, 5], reads=[MB], writes=[GT2[s]])
                    if last:
                        gfint = fw.sb([128, D], F32, "gfint", ph)
                        fw.dma(sp, gfint[:], gfin[:], reads=[gfin], writes=[gfint])
                    with ExitStack() as prep:
                        stg = Ring([fw.sb([128, 1408], F32, f"stg{i}", prep) for i in range(2)])
                        cst = Ring([fw.sb([128, 1408], BF16, f"cst{i}", prep) for i in range(2)])
                        WUv = WUPd.t.rearrange("c p k n -> p c k n")
                        for k in range(8):
                            for cb in range(4):
                                sg_ = stg.next()
                                fw.dma(sp, sg_[:], w_up[l, k * 128:(k + 1) * 128, cb * 1408:(cb + 1) * 1408], reads=[w_up], writes=[sg_])
                                ct = cst.next()
                                fw.op(cast_eng(), lambda e: e.tensor_copy(out=ct[:], in_=sg_[:]), reads=[sg_], writes=[ct])
                                fw.dma(pool, WUv[:, cb * 11:(cb + 1) * 11, k, :], ct[:].rearrange("p (c n) -> p c n", n=128),
                                       reads=[ct], writes=[WUPd])
                        for i in range(NCH):
                            sg_ = stg.next()
                            fw.dma(sp, sg_[:, 0:D], w_down[l, i * 128:(i + 1) * 128, :], reads=[w_down], writes=[sg_])
                            fw.op(cast_eng(), lambda e: e.tensor_copy(out=wdnb[:, i, :], in_=sg_[:, 0:D]), reads=[sg_], writes=[wdnb])
                    fw.barrier()
                    wur = Ring([fw.sb([128, 8, 128], BF16, f"wur{i}", ph) for i in range(6)])
                    xhb = Ring([fw.sb([128, 8, 514], BF16, f"xhb{i}", ph) for i in range(2)])
                    yT = fw.sb([128, NCH, 512], BF16, "yT", ph)
                    cr_ = Ring([fw.sb([128, 512], F32, f"cv{i}", ph) for i in range(4)])
                    sgr = Ring([fw.sb([128, 512], F32, f"sg{i}", ph) for i in range(2)])
                    xr = Ring([fw.sb([128, D], F32, f"xf{i}", ph) for i in range(2)])
                    tmpr = Ring([fw.sb([128, D], F32, f"tf{i}", ph) for i in range(2)])
                    junk = fw.sb([128, D], BF16, "junk3", ph)
                    stt = Ring([fw.sb([128, 8], F32, f"stt{i}", ph) for i in range(2)])
                    psu = Ring([fw.ps([128, 512], F32, f"psu{i}", ph) for i in range(3)])
                    psh = Ring([fw.ps([128, 512], F32, f"psh{i}", ph) for i in range(2)])
                    psd = Ring([fw.ps([128, 1024], F32, f"psd{i}", ph) for i in range(1)])
                    XHlv = XHTl.t.rearrange("k p t -> p k t")
                    XHcv = XHTc.t.rearrange("k p t -> p k t")
                    blocks = [(0, t0, min(512, NLAT - t0)) for t0 in range(0, NLAT, 512)]
                    if not last:
                        blocks.append((1, 0, NCTX))

                    def load_blk(bi):
                        s, t0, W = blocks[bi]
                        xb_ = xhb.next()
                        v = XHlv if s == 0 else XHcv
                        n_ = NLAT if s == 0 else NCTX
                        lo = 1 if t0 == 0 else 0
                        hi = W + 1 if t0 + W == n_ else W + 2
                        if lo:
                            fw.op(pool, lambda e: e.memset(xb_[:, :, 0:1], 0.0), writes=[xb_])
                        if hi == W + 1:
                            fw.op(pool, lambda e: e.memset(xb_[:, :, W + 1:W + 2], 0.0), writes=[xb_])
                        fw.dma(sp, xb_[:, :, lo:hi], v[:, :, t0 + lo:t0 + hi], reads=[XHTl if s == 0 else XHTc], writes=[xb_])
                        return xb_

                    nxt = load_blk(0)
                    for bi, (s, t0, W) in enumerate(blocks):
                        xb_ = nxt
                        if bi + 1 < len(blocks):
                            nxt = load_blk(bi + 1)
                        for i in range(NCH):
                            cvs = []
                            for hv, ch in ((0, i), (1, NCH + i)):
                                pu = psu.next()
                                ph_ = psh.next()
                                wu = wur.next()
                                fw.dma(sp, wu[:], WUPd[ch], reads=[WUPd], writes=[wu])
                                for k in range(8):
                                    fw.op(pe, lambda e: e.matmul(pu[:, 0:W], lhsT=wu[:, k, :], rhs=xb_[:, k, 1:W + 1],
                                                                 start=(k == 0), stop=(k == 7)), reads=[wu, xb_], writes=[pu])
                                for k in range(8):
                                    fw.op(pe, lambda e: e.matmul(ph_[:, 0:2], lhsT=wu[:, k, :], rhs=xb_[:, k, 0:W + 2:W + 1],
                                                                 start=(k == 0), stop=(k == 7)), reads=[wu, xb_], writes=[ph_])
                                c_ = cr_.next()
                                w0 = wdwt[:, ch, 0:1]
                                w2 = wdwt[:, ch, 2:3]
                                fw.op(act, lambda e: e.activation(out=c_[:, 0:W], in_=pu[:, 0:W], func=ACTF.Identity, scale=wdwt[:, ch, 1:2],
                                                                  bias=bdwt[:, ch:ch + 1]), reads=[pu, wdwt, bdwt], writes=[c_])
                                fw.op(dve, lambda e: e.scalar_tensor_tensor(out=c_[:, 1:W], in0=pu[:, 0:W - 1], scalar=w0, in1=c_[:, 1:W],
                                                                            op0=ALU.mult, op1=ALU.add), reads=[pu, wdwt, c_], writes=[c_])
                                fw.op(dve, lambda e: e.scalar_tensor_tensor(out=c_[:, 0:W - 1], in0=pu[:, 1:W], scalar=w2, in1=c_[:, 0:W - 1],
                                                                            op0=ALU.mult, op1=ALU.add), reads=[pu, wdwt, c_], writes=[c_])
                                fw.op(dve, lambda e: e.scalar_tensor_tensor(out=c_[:, 0:1], in0=ph_[:, 0:1], scalar=w0, in1=c_[:, 0:1],
                                                                            op0=ALU.mult, op1=ALU.add), reads=[ph_, wdwt, c_], writes=[c_])
                                fw.op(dve, lambda e: e.scalar_tensor_tensor(out=c_[:, W - 1:W], in0=ph_[:, 1:2], scalar=w2, in1=c_[:, W - 1:W],
                                                                            op0=ALU.mult, op1=ALU.add), reads=[ph_, wdwt, c_], writes=[c_])
                                cvs.append(c_)
                            sg_ = sgr.next()
                            fw.op(act, lambda e: e.activation(out=sg_[:, 0:W], in_=cvs[0][:, 0:W], func=ACTF.Silu), reads=[cvs[0]], writes=[sg_])
                            fw.op(pool, lambda e: e.tensor_tensor(out=yT[:, i, 0:W], in0=sg_[:, 0:W], in1=cvs[1][:, 0:W], op=ALU.mult),
                                  reads=[sg_, cvs[1]], writes=[yT])
                        for qi in range(W // 128):
                            tg = (t0 // 128 + qi) if s == 0 else NT_L + qi
                            pd = psd.next()
                            for hh in range(2):
                                for i in range(NCH):
                                    fw.op(pe, lambda e: e.matmul(pd[:, hh * 512:(hh + 1) * 512], lhsT=yT[:, i, qi * 128:(qi + 1) * 128],
                                                                 rhs=wdnb[:, i, hh * 512:(hh + 1) * 512], start=(i == 0), stop=(i == NCH - 1)),
                                          reads=[yT, wdnb], writes=[pd])
                            xt = xr.next()
                            fw.dma(sp, xt[:], XRES[tg][:, :], reads=[XRES[tg]], writes=[xt])
                            tm = tmpr.next()
                            fw.op(dve, lambda e: e.tensor_tensor(out=tm[:], in0=pd[:, :], in1=GT2[s][:], op=ALU.mult), reads=[pd, GT2[s]], writes=[tm])
                            fw.op(pool, lambda e: e.tensor_tensor(out=tm[:], in0=tm[:], in1=xt[:], op=ALU.add), reads=[tm, xt], writes=[tm])
                            if not last:
                                fw.dma(pool, XRES[tg][:, :], tm[:], reads=[tm], writes=[XRES[tg]])
                            else:
                                s_ = stt.next()
                                fw.op(dve, lambda e: e.memset(s_[:], 0.0), writes=[s_])
                                fw.op(act, lambda e: e.activation(out=junk[:], in_=tm[:], func=ACTF.Square, accum_out=s_[:, 0:1]),
                                      reads=[tm], writes=[junk, s_])
                                rstd_from_ss(s_, 1, 1.0 / D)
                                fw.op(dve, lambda e: e.scalar_tensor_tensor(out=xt[:], in0=tm[:], scalar=s_[:, 4:5], in1=gfint[:],
                                                                            op0=ALU.mult, op1=ALU.mult), reads=[tm, s_, gfint], writes=[xt])
                                fw.dma(pool, out_d[tg * 128:(tg + 1) * 128, :], xt[:], reads=[xt], writes=[out_d])

        except _Stop:
            pass
        fw.barrier()
        fw.finish([out_d])
        stats = {e.name: (e.nins, e.nwait) for e in (pe, act, dve, pool, sp)}
    return nc, stats


def _consts(NLAT):
    R = NLAT // 64
    NT_L = NLAT // 128
    c = {}
    c["ident"] = np.eye(128, dtype=np.float32).astype(bf)
    sel = np.zeros((2, 256), np.float32)
    sel[0, :128] = 1.0
    sel[1, 128:] = 1.0
    c["sel"] = sel
    i64 = np.arange(64)
    a64 = 2 * np.pi * np.outer(i64, i64) / 64.0
    C64 = np.cos(a64) / 8.0
    S64 = np.sin(a64) / 8.0
    C4 = np.kron(np.eye(4), C64)
    S4 = np.kron(np.eye(4), S64)
    c4s4 = np.stack([C4, S4], 0).reshape(2, 2, 128, 256).transpose(2, 1, 0, 3)
    c["c4s4"] = np.ascontiguousarray(c4s4).astype(bf)
    ir = np.arange(R)
    ar = 2 * np.pi * np.outer(ir, ir) / R
    CR = np.cos(ar) / np.sqrt(R)
    SR = np.sin(ar) / np.sqrt(R)
    c["crs"] = np.ascontiguousarray(np.stack([CR, SR, -SR], 1)).astype(bf)
    at = 2 * np.pi * np.outer(ir, i64) / NLAT
    c["tw"] = np.ascontiguousarray(np.stack([np.cos(at), np.sin(at)], 1)).astype(np.float32)
    c["c64"] = np.ascontiguousarray(np.stack([C64, S64], 1)).astype(bf)
    i256 = np.arange(256)
    a256 = 2 * np.pi * np.outer(i256, i256) / 256.0
    C256 = np.cos(a256) / 16.0
    S256 = np.sin(a256) / 16.0
    c256 = np.stack([C256, S256], 0).reshape(2, 2, 128, 256).transpose(2, 1, 0, 3)
    c["c256"] = np.ascontiguousarray(c256).astype(bf)
    tok = np.arange(NLAT)
    row = (tok // 64).astype(np.float64)
    col = (tok % 64).astype(np.float64)
    inv = 10000.0 ** (-np.arange(8, dtype=np.float64) / 8)
    ang = np.concatenate([row[:, None] * inv, col[:, None] * inv], -1).astype(np.float32)
    cs = np.stack([np.cos(ang), np.sin(ang)], 1)
    cs = np.concatenate([cs, cs], -1)
    c["rope"] = np.ascontiguousarray(cs.reshape(NT_L, 128, 2, 32).transpose(1, 0, 2, 3)).astype(np.float32)
    return c


def make_inmaps(inputs, NLAT, nb):
    f = lambda a: np.ascontiguousarray(np.asarray(a, dtype=np.float32))
    L = inputs["w_ada"].shape[0]
    shared = dict(
        w_ada=f(inputs["w_ada"]),
        b_ada2=f(np.repeat(np.asarray(inputs["b_ada"])[:, None, :], 2, 1)),
        gmix2=f(np.repeat(np.asarray(inputs["g_mix"])[:, None, :], 2, 1)),
        gffn2=f(np.repeat(np.asarray(inputs["g_ffn"])[:, None, :], 2, 1)),
        gfin=f(np.repeat(np.asarray(inputs["g_final"])[None, :], 128, 0)),
        w_in=f(inputs["w_in"]),
        w_four=f(inputs["w_fourier"]),
        b_four=f(np.broadcast_to(np.asarray(inputs["b_fourier"])[:, None, None, :], (L, 128, 2, 256))),
        gq=f(np.asarray(inputs["g_q_a"]).reshape(L, 3, 128).transpose(0, 2, 1)),
        w_qb=f(inputs["w_q_b"]),
        gkv=f(np.asarray(inputs["g_kv_a"]).reshape(L, 128, 1)),
        w_kvb=f(inputs["w_kv_b"]),
        w_out=f(inputs["w_out"]),
        w_up=f(inputs["w_up"]),
        wdw=f(np.asarray(inputs["w_dw"]).reshape(L, 3, 2 * NCH, 128).transpose(0, 3, 2, 1)),
        bdw=f(np.asarray(inputs["b_dw"]).reshape(L, 2 * NCH, 128).transpose(0, 2, 1)),
        w_down=f(inputs["w_down"]),
    )
    shared.update(_consts(NLAT))
    maps = []
    x = np.asarray(inputs["x"])
    c = np.asarray(inputs["c"], dtype=np.float32)
    ctx = np.asarray(inputs["ctx"])
    cc = np.asarray(inputs["c_ctx"], dtype=np.float32)
    for b in range(nb):
        cv = np.stack([c[b].reshape(8, 128).T, cc.reshape(8, 128).T], -1)
        m = dict(shared)
        m["x"] = f(x[b])
        m["ctx"] = f(ctx[b])
        m["cvec"] = f(cv)
        maps.append(m)
    return maps


_CACHE = {}


def kernel(**inputs):
    x = np.asarray(inputs["x"])
    B, NLAT, _ = x.shape
    key = (NLAT,)
    if key not in _CACHE:
        _CACHE[key] = build_program(NLAT)
    nc, _ = _CACHE[key]
    maps = make_inmaps(inputs, NLAT, B)
    res = run_bass_kernel_spmd(nc, maps, core_ids=list(range(B)))
    out = np.stack([np.asarray(res.results[b]["out"], dtype=np.float32) for b in range(B)], 0)
    return out
```

```python
import numpy as np
import ml_dtypes
from contextlib import ExitStack
import concourse.bass as bass
import concourse.mybir as mybir
from concourse.bass_utils import run_bass_kernel_spmd

F32 = mybir.dt.float32
BF16 = mybir.dt.bfloat16
ACTF = mybir.ActivationFunctionType
ALU = mybir.AluOpType
bf = ml_dtypes.bfloat16

D = 1024
NCTX = 256
DFF = 2816
NCH = 22
EPS = 1e-6
SCALE = 96 ** -0.5


class Buf:
    def __init__(self, t, name, multi=False):
        self.t = t
        self.name = name
        self.multi = multi
        self.excl = False
        self.w = {}
        self.r = {}

    def __getitem__(self, k):
        return self.t[k]


def _merge(d, ev):
    k = id(ev[0])
    if k not in d or d[k][1] < ev[1]:
        d[k] = ev


class Eng:
    def __init__(self, fw, e, name, is_pe=False):
        self.e = e
        self.name = name
        self.is_pe = is_pe
        self.sem = fw.new_sem("s_" + name)
        self.cnt = 0
        self.seen = {}
        self.nwait = 0
        self.nins = 0

    def wait(self, ev):
        sem, val = ev
        if self.seen.get(id(sem), 0) >= val:
            return
        self.e.wait_ge(sem, val)
        self.seen[id(sem)] = val
        self.nwait += 1


class FW:
    def __init__(self, nc, stack, n_dma_sems=14):
        self.nc = nc
        self.stack = stack
        self.pe = Eng(self, nc.tensor, "pe", is_pe=True)
        self.act = Eng(self, nc.scalar, "act")
        self.dve = Eng(self, nc.vector, "dve")
        self.pool = Eng(self, nc.gpsimd, "pool")
        self.sp = Eng(self, nc.sync, "sp")
        self.dq = {}
        for q in (self.sp, self.pool, self.act):
            self.dq[q.name] = dict(sems=[self.new_sem(f"d_{q.name}{i}") for i in range(n_dma_sems)],
                                   vals=[0] * n_dma_sems, idx=0)
        self.uid = 0
        self.stopped = False
        self.cc_sem = self.new_sem("cc")
        self.cc_cnt = 0

    def new_sem(self, name):
        return self.stack.enter_context(self.nc.semaphore(name))

    def sb(self, shape, dt, name, stack=None, multi=False):
        self.uid += 1
        t = (stack or self.stack).enter_context(self.nc.sbuf_tensor(f"{name}_{self.uid}", list(shape), dt))
        return Buf(t, name, multi)

    def ps(self, shape, dt, name, stack=None):
        self.uid += 1
        t = (stack or self.stack).enter_context(self.nc.psum_tensor(f"{name}_{self.uid}", list(shape), dt))
        b = Buf(t, name)
        b.excl = True
        return b

    def dram(self, name, shape, dt, multi=True):
        t = self.nc.dram_tensor(name, list(shape), dt, kind="Internal")
        return Buf(t.ap(), name, multi)

    def _deps(self, eng, reads, writes):
        evs = {}
        for b in reads:
            for ev in b.w.values():
                evs[(id(ev[0]), ev[1])] = ev
            if b.excl:
                for ev in b.r.values():
                    if ev[0] is not eng.sem:
                        evs[(id(ev[0]), ev[1])] = ev
        for b in writes:
            if not b.multi:
                for ev in b.w.values():
                    evs[(id(ev[0]), ev[1])] = ev
            for ev in b.r.values():
                evs[(id(ev[0]), ev[1])] = ev
        for ev in evs.values():
            if eng.is_pe and ev[0] is eng.sem:
                continue
            eng.wait(ev)

    def _record(self, ev, reads, writes):
        for b in reads:
            _merge(b.r, ev)
        for b in writes:
            if b.multi:
                _merge(b.w, ev)
            else:
                b.w = {id(ev[0]): ev}
                b.r = {}

    def op(self, eng, fn, reads=(), writes=()):
        if self.stopped:
            return None
        self._deps(eng, reads, writes)
        ins = fn(eng.e)
        eng.cnt += 1
        eng.nins += 1
        ins.then_inc(eng.sem, 1)
        self._record((eng.sem, eng.cnt), reads, writes)
        return ins

    def dma(self, q, out, in_, reads=(), writes=(), **kw):
        if self.stopped:
            return None
        self._deps(q, reads, writes)
        d = self.dq[q.name]
        i = d["idx"] % len(d["sems"])
        d["idx"] += 1
        if d["vals"][i] > 0:
            q.wait((d["sems"][i], d["vals"][i]))
        ins = q.e.dma_start(out=out, in_=in_, **kw)
        d["vals"][i] += 16
        ins.then_inc(d["sems"][i], 16)
        q.nins += 1
        self._record((d["sems"][i], d["vals"][i]), reads, writes)
        return ins

    def barrier(self):
        if self.stopped:
            return
        engs = (self.pe, self.act, self.dve, self.pool, self.sp)
        evs = [(e.sem, e.cnt) for e in engs if e.cnt > 0]
        for d in self.dq.values():
            for sm, v in zip(d["sems"], d["vals"]):
                if v > 0:
                    evs.append((sm, v))
        if self.cc_cnt > 0:
            evs.append((self.cc_sem, self.cc_cnt))
        for e in engs:
            for ev in evs:
                if ev[0] is e.sem:
                    continue
                e.wait(ev)

    def finish(self, bufs):
        for b in bufs:
            for ev in list(b.w.values()):
                self.sp.wait(ev)


class Ring:
    def __init__(self, bufs):
        self.bufs = bufs
        self.i = 0

    def next(self):
        b = self.bufs[self.i % len(self.bufs)]
        self.i += 1
        return b


class _Stop(Exception):
    pass


def build_program(NLAT=8192, depth=2, dbg=False, stop_at=None):
    NT_L = NLAT // 128
    NT_C = NCTX // 128
    NT = NT_L + NT_C
    NTOK = NLAT + NCTX
    R = NLAT // 64
    assert R <= 128 and R % 16 == 0
    HALF = NLAT // 2
    NT_O = HALF // 128
    QB = 512 if HALF % 512 == 0 else 128
    QS_CTX = HALF
    QS_EXT = HALF + NCTX
    NQS = HALF + NCTX + 128

    nc = bass.Bass("TRN2", target_bir_lowering=False)

    def din(name, shape, dt=F32):
        return Buf(nc.dram_tensor(name, list(shape), dt, kind="ExternalInput").ap(), name)

    x_in = din("x", [NLAT, D])
    ctx_in = din("ctx", [NCTX, D])
    cvec = din("cvec", [128, 8, 2])
    w_ada = din("w_ada", [depth, D, 6 * D])
    b_ada2 = din("b_ada2", [depth, 2, 6 * D])
    gmix2 = din("gmix2", [depth, 2, D])
    gffn2 = din("gffn2", [depth, 2, D])
    gfin = din("gfin", [128, D])
    w_in = din("w_in", [depth, D, 800])
    w_four = din("w_four", [depth, 256, 256])
    b_four = din("b_four", [depth, 128, 2, 256])
    gq = din("gq", [depth, 128, 3])
    w_qb = din("w_qb", [depth, 384, 1152])
    gkv = din("gkv", [depth, 128, 1])
    w_kvb = din("w_kvb", [depth, 128, 1536])
    w_out = din("w_out", [depth, D, D])
    w_up = din("w_up", [depth, D, 2 * DFF])
    wdw = din("wdw", [depth, 128, 2 * NCH, 3])
    bdw = din("bdw", [depth, 128, 2 * NCH])
    w_down = din("w_down", [depth, DFF, D])
    ident_d = din("ident", [128, 128], BF16)
    sel_d = din("sel", [2, 256])
    c4s4_d = din("c4s4", [128, 2, 2, 256], BF16)
    crs_d = din("crs", [R, 3, R], BF16)
    tw_d = din("tw", [R, 2, 64])
    c64_d = din("c64", [64, 2, 64], BF16)
    c256_d = din("c256", [128, 2, 2, 256], BF16)
    rope_d = din("rope", [128, NT_L + 1, 2, 32])
    hm_d = din("hm", [128, 4])
    out_d = Buf(nc.dram_tensor("out", [HALF, D], F32, kind="ExternalOutput").ap(), "out", multi=True)
    dbg_d = {}

    with ExitStack() as st:
        fw = FW(nc, st)
        pe, act, dve, pool, sp = fw.pe, fw.act, fw.dve, fw.pool, fw.sp

        XOWN = fw.dram("xown", [HALF, D], F32)
        NCK = HALF // 256
        XAG = [fw.dram(f"xag{j}", [512, D], F32) for j in range(NCK)]
        XOTH = fw.dram("xoth", [HALF, D], F32)
        XRES = {t: Buf(XOWN.t[t * 128:(t + 1) * 128, :], f"xres{t}") for t in range(NT_O)}
        for tc_ in range(NT_C):
            XRES[NT_L + tc_] = fw.dram(f"xctx{tc_}", [128, D], F32, multi=False)
        XHTe = fw.dram("xhte", [8, 128, 128], BF16)
        MB = fw.dram("mb", [2, 6, 128, D], F32)
        ABd = fw.dram("abd", [NTOK, 512], BF16)
        Y2d = fw.dram("y2d", [R, 64, 2, 256], BF16)
        FOURd = fw.dram("fourd", [NTOK, 256], BF16)
        QTd = fw.dram("qtd", [12, 96, NQS], BF16)
        ATTd = fw.dram("attd", [NQS, 768], BF16)
        XHTl = fw.dram("xhtl", [8, 128, HALF + 2], BF16)
        XHTc = fw.dram("xhtc", [8, 128, NCTX + 2], BF16)

        ident = fw.sb([128, 128], BF16, "ident")
        sel = fw.sb([2, 256], F32, "sel")
        zero_t = fw.sb([128, 8, 2], BF16, "zero")
        fw.dma(sp, ident[:], ident_d[:], reads=[ident_d], writes=[ident])
        fw.dma(sp, sel[:], sel_d[:], reads=[sel_d], writes=[sel])
        hm = fw.sb([128, 4], F32, "hm")
        fw.dma(sp, hm[:], hm_d[:], reads=[hm_d], writes=[hm])
        TD = []
        for t in range(NT_L):
            TD.append(dict(kind="lat", s=0, pieces=[(0, t * 128, 128)], kvcol=t * 128, q=(t < NT_O), qslot=t * 128, ropei=t, xres=(t if t < NT_O else None)))
        for tc_ in range(NT_C):
            TD.append(dict(kind="ctx", s=1, pieces=[(0, tc_ * 128, 128)], kvcol=NLAT + tc_ * 128, q=True, qslot=QS_CTX + tc_ * 128, ropei=None, xres=NT_L + tc_))
        TD.append(dict(kind="ext", s=0, pieces=[(0, HALF, 64), (64, NLAT - 64, 64)], kvcol=None, q=True, qslot=QS_EXT, ropei=NT_L, xres=None))
        fw.op(dve, lambda e: e.memset(zero_t[:], 0.0), writes=[zero_t])
        cast_rr = [0]

        def cast_eng():
            cast_rr[0] += 1
            return (dve, pool)[cast_rr[0] % 2]

        def rstd_from_ss(stt, n_cols, inv_n, ncol=1):
            fw.op(dve, lambda e: e.tensor_scalar(out=stt[:, 2:2 + ncol], in0=stt[:, 0:ncol], scalar1=inv_n,
                                                 scalar2=EPS, op0=ALU.mult, op1=ALU.add), reads=[stt], writes=[stt])
            fw.op(act, lambda e: e.sqrt(out=stt[:, 2:2 + ncol], in_=stt[:, 2:2 + ncol]), reads=[stt], writes=[stt])
            fw.op(dve, lambda e: e.reciprocal(out=stt[:, 4:4 + ncol], in_=stt[:, 2:2 + ncol]), reads=[stt], writes=[stt])

        def chk(name):
            if stop_at == name:
                fw.stopped = True

        try:
            for l in range(depth):
                last = (l == depth - 1)
                src_tiles = None if l == 0 else XRES

                def load_rows(xt, td):
                    for (p0, r0, n) in td["pieces"]:
                        if td["kind"] == "ctx":
                            if l == 0:
                                sb_, sap = ctx_in, ctx_in[r0:r0 + n, :]
                            else:
                                sb_ = XRES[td["xres"]]
                                sap = sb_[:, :]
                        elif l == 0:
                            sb_, sap = x_in, x_in[r0:r0 + n, :]
                        elif r0 < HALF:
                            sb_ = XRES[r0 // 128]
                            sap = sb_[:, :]
                        else:
                            sb_, sap = XOTH, XOTH[r0 - HALF:r0 - HALF + n, :]
                        fw.dma(sp, xt[p0:p0 + n, :], sap, reads=[sb_], writes=[xt])

                fw.barrier()
                with ExitStack() as ph:
                    sil = fw.sb([128, 8, 2], F32, "sil", ph)
                    mrow = fw.sb([2, 6 * D], F32, "mrow", ph)
                    brow = fw.sb([2, 6 * D], F32, "brow", ph)
                    g2 = fw.sb([2, 2, D], F32, "g2", ph)
                    wst = Ring([fw.sb([128, 3072], F32, f"wst{i}", ph) for i in range(3)])
                    bst = Ring([fw.sb([128, D], F32, f"bst{i}", ph) for i in range(2)])
                    psm = [fw.ps([128, 512], F32, f"psm{i}", ph) for i in range(8)]
                    fw.dma(sp, sil[:], cvec[:], reads=[cvec], writes=[sil])
                    fw.dma(sp, brow[:], b_ada2[l], reads=[b_ada2], writes=[brow])
                    fw.dma(sp, g2[:, 0, :], gmix2[l], reads=[gmix2], writes=[g2])
                    fw.dma(sp, g2[:, 1, :], gffn2[l], reads=[gffn2], writes=[g2])
                    fw.op(act, lambda e: e.activation(out=sil[:], in_=sil[:], func=ACTF.Silu), reads=[sil], writes=[sil])
                    for half in range(2):
                        for k in range(8):
                            wt = wst.next()
                            fw.dma(sp, wt[:], w_ada[l, k * 128:(k + 1) * 128, half * 3072:(half + 1) * 3072],
                                   reads=[w_ada], writes=[wt])
                            for j in range(6):
                                fw.op(pe, lambda e: e.matmul(psm[j][0:2, :], lhsT=sil[:, k, :], rhs=wt[:, j * 512:(j + 1) * 512],
                                                             start=(k == 0), stop=(k == 7)), reads=[sil, wt], writes=[psm[j]])
                        for j in range(6):
                            c0 = half * 3072 + j * 512
                            fw.op(dve, lambda e: e.tensor_tensor(out=mrow[:, c0:c0 + 512], in0=psm[j][0:2, :],
                                                                 in1=brow[:, c0:c0 + 512], op=ALU.add),
                                  reads=[psm[j], brow], writes=[mrow])
                    for (c0, gi) in ((1 * D, 0), (4 * D, 1)):
                        fw.op(dve, lambda e: e.scalar_tensor_tensor(out=mrow[:, c0:c0 + D], in0=mrow[:, c0:c0 + D], scalar=1.0,
                                                                    in1=g2[:, gi, :], op0=ALU.add, op1=ALU.mult),
                              reads=[mrow, g2], writes=[mrow])
                    pi = 0
                    for s in range(2):
                        for v in range(6):
                            bt = bst.next()
                            for hh in range(2):
                                p_ = psm[pi % 8]
                                pi += 1
                                fw.op(pe, lambda e: e.matmul(p_[:, :], lhsT=sel[:, s * 128:(s + 1) * 128],
                                                             rhs=mrow[:, v * D + hh * 512: v * D + (hh + 1) * 512],
                                                             start=True, stop=True), reads=[sel, mrow], writes=[p_])
                                fw.op(act if hh else dve, lambda e: e.tensor_copy(out=bt[:, hh * 512:(hh + 1) * 512], in_=p_[:, :])
                                      if not hh else e.copy(out=bt[:, hh * 512:(hh + 1) * 512], in_=p_[:, :]),
                                      reads=[p_], writes=[bt])
                            fw.dma(pool, MB[s, v], bt[:], reads=[bt], writes=[MB])

                fw.barrier()
                chk("M%d" % l)
                with ExitStack() as mx:
                    ckvnT = fw.sb([128, NTOK], BF16, "ckvnT", mx, multi=True)
                    KT = [fw.sb([96, NTOK], BF16, f"KT{i}", mx, multi=True) for i in range(2)]
                    wkvb = fw.sb([128, 1536], BF16, "wkvb", mx)

                    with ExitStack() as ph:
                        winb = fw.sb([128, 8, 800], BF16, "winb", ph, multi=True)
                        wab = fw.sb([128, 8, 512], BF16, "wab", ph, multi=True)
                        wqb = fw.sb([128, 3, 1152], BF16, "wqb", ph, multi=True)
                        gqt = fw.sb([128, 3], F32, "gqt", ph)
                        gkvt = fw.sb([128, 1], F32, "gkvt", ph)
                        GS = [[fw.sb([128, D], F32, f"gs{s}{v}", ph) for v in range(2)] for s in range(2)]
                        psb = [fw.ps([128, 1024], BF16, f"psb{i}", ph) for i in range(3)]
                        psf = [fw.ps([128, 512], F32, f"psf{i}", ph) for i in range(5)]
                        prep = ExitStack()
                        stg = Ring([fw.sb([128, 1536], F32, f"stg{i}", prep) for i in range(2)])
                        wfb = fw.sb([128, 2, 256], BF16, "wfb", prep, multi=True)
                        mab = fw.sb([128, 2, 512], BF16, "mab", prep, multi=True)
                        wfT = fw.sb([128, 2, D], BF16, "wfT", prep, multi=True)
                        c4s4 = fw.sb([128, 2, 2, 256], BF16, "c4s4", prep)

                        fw.dma(sp, c4s4[:], c4s4_d[:], reads=[c4s4_d], writes=[c4s4])
                        fw.dma(sp, gqt[:], gq[l], reads=[gq], writes=[gqt])
                        fw.dma(sp, gkvt[:], gkv[l], reads=[gkv], writes=[gkvt])
                        for s in range(2):
                            fw.dma(sp, GS[s][0][:], MB[s, 1], reads=[MB], writes=[GS[s][0]])
                            fw.dma(sp, GS[s][1][:], MB[s, 0], reads=[MB], writes=[GS[s][1]])
                        for k in range(8):
                            sg_ = stg.next()
                            fw.dma(sp, sg_[:, 0:800], w_in[l, k * 128:(k + 1) * 128, :], reads=[w_in], writes=[sg_])
                            fw.op(cast_eng(), lambda e: e.tensor_copy(out=winb[:, k, :], in_=sg_[:, 0:800]), reads=[sg_], writes=[winb])
                        for kc in range(3):
                            sg_ = stg.next()
                            fw.dma(sp, sg_[:, 0:1152], w_qb[l, kc * 128:(kc + 1) * 128, :], reads=[w_qb], writes=[sg_])
                            fw.op(cast_eng(), lambda e: e.tensor_scalar(out=wqb[:, kc, :], in0=sg_[:, 0:1152], scalar1=gqt[:, kc:kc + 1],
                                                                        scalar2=None, op0=ALU.mult), reads=[sg_, gqt], writes=[wqb])
                        sg_ = stg.next()
                        fw.dma(sp, sg_[:, 0:1536], w_kvb[l], reads=[w_kvb], writes=[sg_])
                        fw.op(cast_eng(), lambda e: e.tensor_scalar(out=wkvb[:], in0=sg_[:, 0:1536], scalar1=gkvt[:, 0:1],
                                                                    scalar2=None, op0=ALU.mult), reads=[sg_, gkvt], writes=[wkvb])
                        for cc in range(2):
                            sg_ = stg.next()
                            fw.dma(sp, sg_[:, 0:256], w_four[l, cc * 128:(cc + 1) * 128, :], reads=[w_four], writes=[sg_])
                            fw.op(cast_eng(), lambda e: e.tensor_copy(out=wfb[:, cc, :], in_=sg_[:, 0:256]), reads=[sg_], writes=[wfb])
                        for cc in range(2):
                            p_ = psf[cc]
                            for X in range(2):
                                for c2 in range(2):
                                    fw.op(pe, lambda e: e.matmul(p_[:, X * 256:(X + 1) * 256],
                                                                 lhsT=c4s4[:, c2, X, cc * 128:(cc + 1) * 128], rhs=wfb[:, c2, :],
                                                                 start=(c2 == 0), stop=(c2 == 1)), reads=[c4s4, wfb], writes=[p_])
                            fw.op(dve, lambda e: e.tensor_copy(out=mab[:, cc, 0:256], in_=p_[:, 0:256]), reads=[p_], writes=[mab])
                            fw.op(dve, lambda e: e.tensor_scalar(out=mab[:, cc, 256:512], in0=p_[:, 256:512], scalar1=-1.0,
                                                                 scalar2=None, op0=ALU.mult), reads=[p_], writes=[mab])
                        for cc in range(2):
                            for k in range(8):
                                fw.op(pe, lambda e: e.transpose(out=psb[cc][:, k * 128:(k + 1) * 128],
                                                                in_=winb[:, k, cc * 128:(cc + 1) * 128], identity=ident[:]),
                                      reads=[winb, ident], writes=[psb[cc]])
                            fw.op(dve, lambda e: e.tensor_copy(out=wfT[:, cc, :], in_=psb[cc][:, :]), reads=[psb[cc]], writes=[wfT])
                        for k in range(8):
                            p_ = psf[k % 5]
                            for cc in range(2):
                                fw.op(pe, lambda e: e.matmul(p_[:, :], lhsT=wfT[:, cc, k * 128:(k + 1) * 128], rhs=mab[:, cc, :],
                                                             start=(cc == 0), stop=(cc == 1)), reads=[wfT, mab], writes=[p_])
                            fw.op(cast_eng() if False else dve, lambda e: e.tensor_copy(out=wab[:, k, :], in_=p_[:, :]),
                                  reads=[p_], writes=[wab])

                        chk('INW%d' % l)
                        fw.barrier()
                        prep.close()
                        xr = Ring([fw.sb([128, D], F32, f"xt{i}", ph) for i in range(2)])
                        junk = fw.sb([128, D], BF16, "junk", ph)
                        xh = Ring([fw.sb([128, D], F32, f"xh{i}", ph) for i in range(1)])
                        xb = Ring([fw.sb([128, D], BF16, f"xb{i}", ph) for i in range(2)])
                        hT = Ring([fw.sb([128, 8, 128], BF16, f"hT{i}", ph) for i in range(2)])
                        stt = Ring([fw.sb([128, 8], F32, f"stt{i}", ph) for i in range(3)])
                        abt = Ring([fw.sb([128, 512], BF16, f"abt{i}", ph) for i in range(2)])
                        cn = Ring([fw.sb([128, 512], BF16, f"cn{i}", ph) for i in range(2)])
                        cn2 = Ring([fw.sb([128, 96], BF16, f"cn2{i}", ph) for i in range(2)])
                        rtmp = Ring([fw.sb([128, 2, 32], F32, f"rtmp{i}", ph) for i in range(2)])
                        cqT = Ring([fw.sb([128, 3, 128], BF16, f"cqT{i}", ph) for i in range(2)])
                        qsb = Ring([fw.sb([128, 12, 96], F32, f"qsb{i}", ph) for i in range(1)])
                        qtmp = Ring([fw.sb([128, 2, 12, 32], F32, f"qtmp{i}", ph) for i in range(1)])
                        qbf = Ring([fw.sb([128, 12, 96], BF16, f"qbf{i}", ph) for i in range(2)])
                        qTt = Ring([fw.sb([96, 12, 128], BF16, f"qTt{i}", ph) for i in range(2)])
                        for b_ in cn2.bufs:
                            fw.op(dve, lambda e: e.memset(b_[:], 0.0), writes=[b_])
                        QTv = QTd.t.rearrange("h d t -> d h t")

                        roper = Ring([fw.sb([128, 2, 32], F32, f"rope{i}", ph) for i in range(3)])

                        tds = [td for td in TD if not (last and td["kind"] == "ctx" and False)]

                        def load_x(i):
                            td = tds[i]
                            xt = xr.next()
                            load_rows(xt, td)
                            rp = None
                            if td["ropei"] is not None:
                                rp = roper.next()
                                fw.dma(sp, rp[:], rope_d[:, td["ropei"], :, :], reads=[rope_d], writes=[rp])
                            return xt, rp

                        nxt = load_x(0)
                        for ti, td in enumerate(tds):
                            xt, ropet = nxt
                            if ti + 1 < len(tds):
                                nxt = load_x(ti + 1)
                            s = td["s"]
                            t = ti
                            kvc = td["kvcol"]
                            do_kv = kvc is not None
                            do_q = td["q"] and not (td["kind"] == "ctx" and last)
                            s_ = stt.next()
                            fw.op(dve, lambda e: e.memset(s_[:], 0.0), writes=[s_])
                            fw.op(act, lambda e: e.activation(out=junk[:], in_=xt[:], func=ACTF.Square, accum_out=s_[:, 0:1]),
                                  reads=[xt], writes=[junk, s_])
                            rstd_from_ss(s_, 1, 1.0 / D)
                            xh_ = xh.next()
                            fw.op(dve, lambda e: e.scalar_tensor_tensor(out=xh_[:], in0=xt[:], scalar=s_[:, 4:5], in1=GS[s][0][:],
                                                                        op0=ALU.mult, op1=ALU.mult), reads=[xt, s_, GS[s][0]], writes=[xh_])
                            xb_ = xb.next()
                            fw.op(pool, lambda e: e.tensor_tensor(out=xb_[:], in0=xh_[:], in1=GS[s][1][:], op=ALU.add),
                                  reads=[xh_, GS[s][1]], writes=[xb_])
                            for k in range(8):
                                fw.op(pe, lambda e: e.transpose(out=psb[0][:, k * 128:(k + 1) * 128], in_=xb_[:, k * 128:(k + 1) * 128],
                                                                identity=ident[:]), reads=[xb_, ident], writes=[psb[0]])
                            hT_ = hT.next()
                            fw.op(act, lambda e: e.copy(out=hT_[:].rearrange("p k t -> p (k t)"), in_=psb[0][:, :]),
                                  reads=[psb[0]], writes=[hT_])
                            if t == 0: chk('INTa%d' % l)
                            for k in range(8):
                                fw.op(pe, lambda e: e.matmul(psf[0][:, :], lhsT=hT_[:, k, :], rhs=wab[:, k, :], start=(k == 0), stop=(k == 7)),
                                      reads=[hT_, wab], writes=[psf[0]])
                            for k in range(8):
                                fw.op(pe, lambda e: e.matmul(psf[1][:, :], lhsT=hT_[:, k, :], rhs=winb[:, k, 256:768], start=(k == 0), stop=(k == 7)),
                                      reads=[hT_, winb], writes=[psf[1]])
                            for k in range(8):
                                fw.op(pe, lambda e: e.matmul(psf[2][:, 0:32], lhsT=hT_[:, k, :], rhs=winb[:, k, 768:800], start=(k == 0), stop=(k == 7)),
                                      reads=[hT_, winb], writes=[psf[2]])
                            if do_kv:
                                ab_ = abt.next()
                                fw.op(act, lambda e: e.copy(out=ab_[:], in_=psf[0][:, :]), reads=[psf[0]], writes=[ab_])
                                fw.dma(pool, ABd[kvc:kvc + 128, :], ab_[:], reads=[ab_], writes=[ABd])
                            if t == 0: chk('INTb%d' % l)
                            s2 = stt.next()
                            fw.op(dve, lambda e: e.memset(s2[:], 0.0), writes=[s2])
                            fw.op(act, lambda e: e.activation(out=junk[:, 0:384], in_=psf[1][:, 0:384], func=ACTF.Square, accum_out=s2[:, 0:1]),
                                  reads=[psf[1]], writes=[junk, s2])
                            fw.op(act, lambda e: e.activation(out=junk[:, 384:512], in_=psf[1][:, 384:512], func=ACTF.Square, accum_out=s2[:, 1:2]),
                                  reads=[psf[1]], writes=[junk, s2])
                            fw.op(dve, lambda e: e.tensor_scalar(out=s2[:, 0:1], in0=s2[:, 0:1], scalar1=128.0 / 384.0, scalar2=None, op0=ALU.mult),
                                  reads=[s2], writes=[s2])
                            rstd_from_ss(s2, 2, 1.0 / 128, ncol=2)
                            cn_ = cn.next()
                            fw.op(dve, lambda e: e.tensor_scalar(out=cn_[:, 0:384], in0=psf[1][:, 0:384], scalar1=s2[:, 4:5], scalar2=None, op0=ALU.mult),
                                  reads=[psf[1], s2], writes=[cn_])
                            fw.op(dve, lambda e: e.tensor_scalar(out=cn_[:, 384:512], in0=psf[1][:, 384:512], scalar1=s2[:, 5:6], scalar2=None, op0=ALU.mult),
                                  reads=[psf[1], s2], writes=[cn_])
                            c2_ = cn2.next()
                            if s == 0:
                                rt = rtmp.next()
                                fw.op(dve, lambda e: e.tensor_tensor(out=rt[:, 0, :], in0=psf[2][:, 0:32], in1=ropet[:, 0, :], op=ALU.mult),
                                      reads=[psf[2], ropet], writes=[rt])
                                fw.op(dve, lambda e: e.tensor_tensor(out=rt[:, 1, :], in0=psf[2][:, 0:32], in1=ropet[:, 1, :], op=ALU.mult),
                                      reads=[psf[2], ropet], writes=[rt])
                                fw.op(pool, lambda e: e.tensor_tensor(out=c2_[:, 64:80], in0=rt[:, 0, 0:16], in1=rt[:, 1, 16:32], op=ALU.subtract),
                                      reads=[rt], writes=[c2_])
                                fw.op(pool, lambda e: e.tensor_tensor(out=c2_[:, 80:96], in0=rt[:, 0, 16:32], in1=rt[:, 1, 0:16], op=ALU.add),
                                      reads=[rt], writes=[c2_])
                            else:
                                fw.op(dve, lambda e: e.tensor_copy(out=c2_[:, 64:96], in_=psf[2][:, 0:32]), reads=[psf[2]], writes=[c2_])
                            if t == 0: chk('INTc%d' % l)
                            for j in range(4):
                                fw.op(pe, lambda e: e.transpose(out=psb[1][:, j * 128:(j + 1) * 128], in_=cn_[:, j * 128:(j + 1) * 128],
                                                                identity=ident[:]), reads=[cn_, ident], writes=[psb[1]])
                            fw.op(pe, lambda e: e.transpose(out=psb[1][0:96, 512:640], in_=c2_[:, 0:96], identity=ident[:]),
                                  reads=[c2_, ident], writes=[psb[1]])
                            if t == 0: chk('INTc2%d' % l)
                            cq_ = cqT.next()
                            fw.op(act, lambda e: e.copy(out=cq_[:].rearrange("p k t -> p (k t)"), in_=psb[1][:, 0:384]), reads=[psb[1]], writes=[cq_])
                            if do_kv:
                                fw.op(dve, lambda e: e.tensor_copy(out=ckvnT[:, kvc:kvc + 128], in_=psb[1][:, 384:512]),
                                      reads=[psb[1]], writes=[ckvnT])
                                for i in range(2):
                                    fw.op(dve if i else act, (lambda e: e.tensor_copy(out=KT[1][64:96, kvc:kvc + 128], in_=psb[1][64:96, 512:640]))
                                          if i else (lambda e: e.copy(out=KT[0][64:96, kvc:kvc + 128], in_=psb[1][64:96, 512:640])),
                                          reads=[psb[1]], writes=[KT[i]])
                            if t == 0: chk('INTd%d' % l)
                            if not do_q:
                                continue
                            for (pi_, c0, c1) in ((2, 0, 512), (3, 512, 1024), (4, 1024, 1152)):
                                for kc in range(3):
                                    fw.op(pe, lambda e: e.matmul(psf[pi_][:, 0:c1 - c0], lhsT=cq_[:, kc, :], rhs=wqb[:, kc, c0:c1],
                                                                 start=(kc == 0), stop=(kc == 2)), reads=[cq_, wqb], writes=[psf[pi_]])
                            q_ = qsb.next()
                            qf = q_[:].rearrange("p h d -> p (h d)")
                            fw.op(act, lambda e: e.copy(out=qf[:, 0:512], in_=psf[2][:, :]), reads=[psf[2]], writes=[q_])
                            fw.op(dve, lambda e: e.tensor_copy(out=qf[:, 512:1024], in_=psf[3][:, :]), reads=[psf[3]], writes=[q_])
                            fw.op(act, lambda e: e.copy(out=qf[:, 1024:1152], in_=psf[4][:, 0:128]), reads=[psf[4]], writes=[q_])
                            qb_ = qbf.next()
                            fw.op(pool, lambda e: e.tensor_copy(out=qb_[:, :, 0:64], in_=q_[:, :, 0:64]), reads=[q_], writes=[qb_])
                            if s == 0:
                                qt_ = qtmp.next()
                                for h in range(12):
                                    eng = (dve, pool)[h % 2]
                                    fw.op(eng, lambda e: e.tensor_tensor(out=qt_[:, 0, h, :], in0=q_[:, h, 64:96], in1=ropet[:, 0, :], op=ALU.mult),
                                          reads=[q_, ropet], writes=[qt_])
                                    fw.op(eng, lambda e: e.tensor_tensor(out=qt_[:, 1, h, :], in0=q_[:, h, 64:96], in1=ropet[:, 1, :], op=ALU.mult),
                                          reads=[q_, ropet], writes=[qt_])
                                fw.op(dve, lambda e: e.tensor_tensor(out=qb_[:, :, 64:80], in0=qt_[:, 0, :, 0:16], in1=qt_[:, 1, :, 16:32], op=ALU.subtract),
                                      reads=[qt_], writes=[qb_])
                                fw.op(pool, lambda e: e.tensor_tensor(out=qb_[:, :, 80:96], in0=qt_[:, 0, :, 16:32], in1=qt_[:, 1, :, 0:16], op=ALU.add),
                                      reads=[qt_], writes=[qb_])
                            else:
                                fw.op(dve, lambda e: e.tensor_copy(out=qb_[:, :, 64:96], in_=q_[:, :, 64:96]), reads=[q_], writes=[qb_])
                            if t == 0: chk('INTe%d' % l)
                            for h in range(12):
                                pb = psb[2] if h < 8 else psb[0]
                                fw.op(pe, lambda e: e.transpose(out=pb[0:96, (h % 8) * 128:(h % 8 + 1) * 128], in_=qb_[:, h, :], identity=ident[:]),
                                      reads=[qb_, ident], writes=[pb])
                            qT_ = qTt.next()
                            fw.op(act, lambda e: e.copy(out=qT_[:, 0:8, :].rearrange("p h t -> p (h t)"), in_=psb[2][0:96, :]), reads=[psb[2]], writes=[qT_])
                            fw.op(dve, lambda e: e.tensor_copy(out=qT_[:, 8:12, :].rearrange("p h t -> p (h t)"), in_=psb[0][0:96, 0:512]),
                                  reads=[psb[0]], writes=[qT_])
                            fw.dma(pool, QTv[:, :, td["qslot"]:td["qslot"] + 128], qT_[:], reads=[qT_], writes=[QTd])

                    fw.barrier()
                    chk("IN%d" % l)
                    with ExitStack() as ph:
                        crs = fw.sb([R, 3, R], BF16, "crs", ph)
                        tw = fw.sb([R, 2, 64], F32, "tw", ph)
                        c64 = fw.sb([64, 2, 64], BF16, "c64", ph)
                        bft = fw.sb([128, 2, 256], F32, "bft", ph)
                        fw.dma(sp, crs[:], crs_d[:], reads=[crs_d], writes=[crs])
                        fw.dma(sp, tw[:], tw_d[:], reads=[tw_d], writes=[tw])
                        fw.dma(sp, c64[:], c64_d[:], reads=[c64_d], writes=[c64])
                        fw.dma(sp, bft[:], b_four[l], reads=[b_four], writes=[bft])
                        abg = Ring([fw.sb([R, 16, 512], BF16, f"abg{i}", ph) for i in range(2)])
                        y2g = Ring([fw.sb([R, 16, 2, 256], BF16, f"y2g{i}", ph) for i in range(2)])
                        y2t = Ring([fw.sb([64, 16, 2, 256], BF16, f"y2t{i}", ph) for i in range(2)])
                        fog = Ring([fw.sb([64, 16, 256], BF16, f"fog{i}", ph) for i in range(2)])
                        tt = Ring([fw.sb([R, 256], F32, f"tt{i}", ph) for i in range(4)])
                        psy = Ring([fw.ps([128, 512], F32, f"psy{i}", ph) for i in range(8)])
                        ABv = ABd.t[0:NLAT, :].rearrange("(a b) c -> a b c", b=64)
                        for g in range(4):
                            ab_ = abg.next()
                            fw.dma(sp, ab_[:], ABv[:, g * 16:(g + 1) * 16, :], reads=[ABd], writes=[ab_])
                            y2_ = y2g.next()
                            for pm in range(8):
                                pr = psy.next()
                                pi_ = psy.next()
                                a_ = ab_[:, 2 * pm:2 * pm + 2, 0:256]
                                b_ = ab_[:, 2 * pm:2 * pm + 2, 256:512]
                                prv = pr[0:R, :].rearrange("p (a b) -> p a b", b=256)
                                piv = pi_[0:R, :].rearrange("p (a b) -> p a b", b=256)
                                fw.op(pe, lambda e: e.matmul(prv, lhsT=crs[:, 0, :], rhs=a_, start=True, stop=False), reads=[crs, ab_], writes=[pr])
                                fw.op(pe, lambda e: e.matmul(prv, lhsT=crs[:, 1, :], rhs=b_, start=False, stop=True), reads=[crs, ab_], writes=[pr])
                                fw.op(pe, lambda e: e.matmul(piv, lhsT=crs[:, 0, :], rhs=b_, start=True, stop=False), reads=[crs, ab_], writes=[pi_])
                                fw.op(pe, lambda e: e.matmul(piv, lhsT=crs[:, 2, :], rhs=a_, start=False, stop=True), reads=[crs, ab_], writes=[pi_])
                                for j in range(2):
                                    ml = 2 * pm + j
                                    m2 = g * 16 + ml
                                    t1 = tt.next()
                                    t2 = tt.next()
                                    fw.op(dve, lambda e: e.tensor_scalar(out=t1[:], in0=pi_[0:R, j * 256:(j + 1) * 256], scalar1=tw[:, 1, m2:m2 + 1],
                                                                         scalar2=None, op0=ALU.mult), reads=[pi_, tw], writes=[t1])
                                    fw.op(dve, lambda e: e.scalar_tensor_tensor(out=y2_[:, ml, 0, :], in0=pr[0:R, j * 256:(j + 1) * 256],
                                                                                scalar=tw[:, 0, m2:m2 + 1], in1=t1[:], op0=ALU.mult, op1=ALU.add),
                                          reads=[pr, tw, t1], writes=[y2_])
                                    fw.op(dve, lambda e: e.tensor_scalar(out=t2[:], in0=pr[0:R, j * 256:(j + 1) * 256], scalar1=tw[:, 1, m2:m2 + 1],
                                                                         scalar2=None, op0=ALU.mult), reads=[pr, tw], writes=[t2])
                                    fw.op(dve, lambda e: e.scalar_tensor_tensor(out=y2_[:, ml, 1, :], in0=pi_[0:R, j * 256:(j + 1) * 256],
                                                                                scalar=tw[:, 0, m2:m2 + 1], in1=t2[:], op0=ALU.mult, op1=ALU.subtract),
                                          reads=[pi_, tw, t2], writes=[y2_])
                            fw.dma(pool, Y2d[:, g * 16:(g + 1) * 16, :, :], y2_[:], reads=[y2_], writes=[Y2d])
                        Y2v = Y2d.t.rearrange("a b c d -> b a c d")
                        FOv = FOURd.t[0:NLAT, :].rearrange("(a b) c -> a b c", b=R)
                        for g in range(R // 16):
                            yt_ = y2t.next()
                            fw.dma(sp, yt_[:], Y2v[:, g * 16:(g + 1) * 16, :, :], reads=[Y2d], writes=[yt_])
                            fo_ = fog.next()
                            for p2 in range(8):
                                pz = psy.next()
                                pzv = pz[0:64, :].rearrange("p (a b) -> p a b", b=256)
                                fw.op(pe, lambda e: e.matmul(pzv, lhsT=c64[:, 0, :], rhs=yt_[:, 2 * p2:2 * p2 + 2, 0, :], start=True, stop=False),
                                      reads=[c64, yt_], writes=[pz])
                                fw.op(pe, lambda e: e.matmul(pzv, lhsT=c64[:, 1, :], rhs=yt_[:, 2 * p2:2 * p2 + 2, 1, :], start=False, stop=True),
                                      reads=[c64, yt_], writes=[pz])
                                fw.op(dve, lambda e: e.tensor_tensor(out=fo_[:, 2 * p2:2 * p2 + 2, :], in0=pzv, in1=bft[0:64, :, :], op=ALU.add),
                                      reads=[pz, bft], writes=[fo_])
                            fw.dma(pool, FOv[:, g * 16:(g + 1) * 16, :], fo_[:], reads=[fo_], writes=[FOURd])
                        if not last:
                            c256 = fw.sb([128, 2, 2, 256], BF16, "c256", ph)
                            abc = fw.sb([128, 2, 512], BF16, "abc", ph)
                            foc = fw.sb([128, 2, 256], BF16, "foc", ph)
                            fw.dma(sp, c256[:], c256_d[:], reads=[c256_d], writes=[c256])
                            fw.dma(sp, abc[:], ABd.t[NLAT:NTOK, :].rearrange("(a p) c -> p a c", p=128), reads=[ABd], writes=[abc])
                            for n_ in range(2):
                                pz = psy.next()
                                for mc in range(2):
                                    fw.op(pe, lambda e: e.matmul(pz[:, 0:256], lhsT=c256[:, mc, 0, n_ * 128:(n_ + 1) * 128], rhs=abc[:, mc, 0:256],
                                                                 start=(mc == 0), stop=False), reads=[c256, abc], writes=[pz])
                                for mc in range(2):
                                    fw.op(pe, lambda e: e.matmul(pz[:, 0:256], lhsT=c256[:, mc, 1, n_ * 128:(n_ + 1) * 128], rhs=abc[:, mc, 256:512],
                                                                 start=False, stop=(mc == 1)), reads=[c256, abc], writes=[pz])
                                fw.op(dve, lambda e: e.tensor_tensor(out=foc[:, n_, :], in0=pz[:, 0:256], in1=bft[:, 0, :], op=ALU.add),
                                      reads=[pz, bft], writes=[foc])
                            fw.dma(pool, FOURd.t[NLAT:NTOK, :].rearrange("(a p) c -> p a c", p=128), foc[:], reads=[foc], writes=[FOURd])

                    fw.barrier()
                    chk("F%d" % l)
                    with ExitStack() as ph:
                        Va = [fw.sb([128, NT, 65], BF16, f"Va{i}", ph, multi=True) for i in range(2)]
                        NQ = NQS
                        qts = [fw.sb([96, NQ], BF16, f"qts{i}", ph) for i in range(2)]
                        ptr = Ring([fw.sb([128, 2, QB], BF16, f"pt{i}", ph) for i in range(3)])
                        rcr = Ring([fw.sb([128, 4], F32, f"rc{i}", ph) for i in range(2)])
                        att = Ring([fw.sb([128, 4, 64], BF16, f"att{i}", ph) for i in range(3)])
                        pss = Ring([fw.ps([128, 1024], F32, f"pss{i}", ph) for i in range(2)])
                        pso = Ring([fw.ps([128, 512], F32, f"pso{i}", ph) for i in range(2)])
                        pkv = Ring([fw.ps([128, 512], F32, f"pkv{i}", ph) for i in range(2)])
                        for i in range(2):
                            fw.op(pool, lambda e: e.memset(Va[i][:, :, 64:65], 1.0), writes=[Va[i]])

                        def build_kv(h):
                            b = h % 2
                            c0 = 0
                            while c0 < NTOK:
                                w_ = min(512, NTOK - c0)
                                p_ = pkv.next()
                                fw.op(pe, lambda e: e.matmul(p_[0:64, 0:w_], lhsT=wkvb[:, h * 128:h * 128 + 64], rhs=ckvnT[:, c0:c0 + w_],
                                                             start=True, stop=True), reads=[wkvb, ckvnT], writes=[p_])
                                fw.op(dve, lambda e: e.tensor_copy(out=KT[b][0:64, c0:c0 + w_], in_=p_[0:64, 0:w_]), reads=[p_], writes=[KT[b]])
                                c0 += w_
                            c0 = 0
                            while c0 < NT:
                                n_ = min(8, NT - c0)
                                p_ = pkv.next()
                                for c in range(n_):
                                    fw.op(pe, lambda e: e.matmul(p_[:, c * 64:(c + 1) * 64], lhsT=ckvnT[:, (c0 + c) * 128:(c0 + c + 1) * 128],
                                                                 rhs=wkvb[:, h * 128 + 64:h * 128 + 128], start=True, stop=True),
                                          reads=[wkvb, ckvnT], writes=[p_])
                                fw.op(pool if False else dve, lambda e: e.tensor_copy(out=Va[b][:, c0:c0 + n_, 0:64],
                                                                                     in_=p_[:, 0:n_ * 64].rearrange("p (a b) -> p a b", b=64)),
                                      reads=[p_], writes=[Va[b]])
                                c0 += n_
                            fw.dma(sp, qts[b][:], QTd[h, :, 0:NQ], reads=[QTd], writes=[qts[b]])

                        steps = []
                        for h in range(12):
                            blks = [(q0, QB, list(range(NT))) for q0 in range(0, HALF, QB)]
                            blks.append((QS_EXT, 128, list(range(NT))))
                            if not last:
                                blks.append((QS_CTX, NCTX, list(range(NT_L, NT))))
                            for bi, (q0, W, kcs) in enumerate(blks):
                                nk = len(kcs)
                                for i0 in range(0, nk, 2):
                                    steps.append(dict(h=h, q0=q0, W=W, grp=kcs[i0:i0 + 2], i0=i0, nk=nk, first=(i0 == 0),
                                                      last=(i0 + 2 >= nk), newhead=(bi == 0 and i0 == 0)))
                        cur = {}

                        def emit_S(st_):
                            h, q0, W = st_["h"], st_["q0"], st_["W"]
                            b = h % 2
                            p_ = pss.next()
                            for j, kc in enumerate(st_["grp"]):
                                fw.op(pe, lambda e: e.matmul(p_[:, j * 512:j * 512 + W], lhsT=KT[b][0:96, kc * 128:(kc + 1) * 128],
                                                             rhs=qts[b][0:96, q0:q0 + W], start=True, stop=True),
                                      reads=[KT[b], qts[b]], writes=[p_])
                            st_["p"] = p_

                        def emit_exp(st_):
                            W = st_["W"]
                            p_ = st_["p"]
                            pt = ptr.next()
                            ng = len(st_["grp"])
                            fw.op(act, lambda e: e.activation(out=pt[:, 0:ng, 0:W], in_=p_[:, 0:ng * 512].rearrange("p (a b) -> p a b", b=512)[:, :, 0:W],
                                                              func=ACTF.Exp, scale=SCALE), reads=[p_], writes=[pt])
                            st_["pt"] = pt

                        def emit_PV(st_):
                            h, q0, W, i0, nk = st_["h"], st_["q0"], st_["W"], st_["i0"], st_["nk"]
                            b = h % 2
                            nq = W // 128
                            pt = st_["pt"]
                            if st_["newhead"] and h + 1 < 12:
                                build_kv(h + 1)
                            if st_["first"]:
                                pob = pso.next()
                                po = Buf(pob.t[:, 0:260].rearrange("p (a b) -> p a b", b=65), "po")
                                cur["pob"], cur["po"] = pob, po
                            pob, po = cur["pob"], cur["po"]
                            for j, kc in enumerate(st_["grp"]):
                                for qi in range(nq):
                                    fw.op(pe, lambda e: e.matmul(po[:, qi, :], lhsT=pt[:, j, qi * 128:(qi + 1) * 128], rhs=Va[b][:, kc, :],
                                                                 start=(i0 + j == 0 and qi == 0), stop=(i0 + j == nk - 1 and qi == nq - 1)),
                                          reads=[pt, Va[b]], writes=[pob])
                            if st_["last"]:
                                rc = rcr.next()
                                fw.op(dve, lambda e: e.reciprocal(out=rc[:, 0:nq], in_=po[:, 0:nq, 64]), reads=[pob], writes=[rc])
                                at = att.next()
                                for qi in range(nq):
                                    fw.op(dve, lambda e: e.tensor_scalar(out=at[:, qi, :], in0=po[:, qi, 0:64], scalar1=rc[:, qi:qi + 1], scalar2=None,
                                                                         op0=ALU.mult), reads=[pob, rc], writes=[at])
                                fw.dma(pool, ATTd.t[q0:q0 + W, h * 64:(h + 1) * 64].rearrange("(a p) d -> p a d", p=128), at[:, 0:nq, :],
                                       reads=[at], writes=[ATTd])

                        build_kv(0)
                        emit_S(steps[0])
                        for i_, st_ in enumerate(steps):
                            emit_exp(st_)
                            if i_ + 1 < len(steps):
                                emit_S(steps[i_ + 1])
                            emit_PV(st_)

                fw.barrier()
                chk("ATT%d" % l)
                with ExitStack() as ph:
                    stg = Ring([fw.sb([128, 1024], F32, f"stg{i}", ph) for i in range(2)])
                    woutb = fw.sb([128, 8, D], BF16, "woutb", ph, multi=True)
                    GT = [[fw.sb([128, D], F32, f"gt{s}{v}", ph) for v in range(3)] for s in range(2)]
                    for s in range(2):
                        for v, mv in enumerate((2, 4, 3)):
                            fw.dma(sp, GT[s][v][:], MB[s, mv], reads=[MB], writes=[GT[s][v]])
                    for k in range(8):
                        sg_ = stg.next()
                        fw.dma(sp, sg_[:], w_out[l, k * 128:(k + 1) * 128, :], reads=[w_out], writes=[sg_])
                        fw.op(cast_eng(), lambda e: e.tensor_copy(out=woutb[:, k, :], in_=sg_[:]), reads=[sg_], writes=[woutb])
                    mixr = Ring([fw.sb([128, D], BF16, f"mix{i}", ph) for i in range(3)])
                    mixT = Ring([fw.sb([128, 8, 128], BF16, f"mixT{i}", ph) for i in range(2)])
                    xr = Ring([fw.sb([128, D], F32, f"xo{i}", ph) for i in range(3)])
                    tmpr = Ring([fw.sb([128, D], F32, f"tmp{i}", ph) for i in range(2)])
                    xnr = Ring([fw.sb([128, D], F32, f"xn{i}", ph) for i in range(2)])
                    junk = fw.sb([128, D], BF16, "junk2", ph)
                    stt = Ring([fw.sb([128, 8], F32, f"stt{i}", ph) for i in range(3)])
                    xh = Ring([fw.sb([128, D], F32, f"xh{i}", ph) for i in range(2)])
                    xb = Ring([fw.sb([128, D], BF16, f"xb{i}", ph) for i in range(2)])
                    hT = Ring([fw.sb([128, 8, 128], BF16, f"hT{i}", ph) for i in range(2)])
                    psb = Ring([fw.ps([128, 1024], BF16, f"psb{i}", ph) for i in range(2)])
                    psd = Ring([fw.ps([128, 1024], F32, f"psd{i}", ph) for i in range(2)])
                    XHlv = XHTl.t.rearrange("k p t -> p k t")
                    XHcv = XHTc.t.rearrange("k p t -> p k t")
                    otds = [td for td in TD if td["q"] and not (last and td["kind"] == "ctx")]
                    ntile = len(otds)
                    XHev = XHTe.t.rearrange("k p t -> p k t")

                    def load_o(i):
                        td = otds[i]
                        m_ = mixr.next()
                        for (p0, r0, n) in td["pieces"]:
                            fr = r0 if td["kind"] != "ctx" else NLAT + r0
                            fw.dma(sp, m_[p0:p0 + n, 0:256], FOURd[fr:fr + n, :], reads=[FOURd], writes=[m_])
                        qs = td["qslot"]
                        fw.dma(sp, m_[:, 256:1024], ATTd[qs:qs + 128, :], reads=[ATTd], writes=[m_])
                        xt = xr.next()
                        load_rows(xt, td)
                        return m_, xt

                    nxt = load_o(0)
                    for ti, td in enumerate(otds):
                        m_, xt = nxt
                        if ti + 1 < ntile:
                            nxt = load_o(ti + 1)
                        s = td["s"]
                        t = ti
                        pb = psb.next()
                        for k in range(8):
                            fw.op(pe, lambda e: e.transpose(out=pb[:, k * 128:(k + 1) * 128], in_=m_[:, k * 128:(k + 1) * 128], identity=ident[:]),
                                  reads=[m_, ident], writes=[pb])
                        mT = mixT.next()
                        fw.op(act, lambda e: e.copy(out=mT[:].rearrange("p k t -> p (k t)"), in_=pb[:, :]), reads=[pb], writes=[mT])
                        pd = psd.next()
                        for hh in range(2):
                            for k in range(8):
                                fw.op(pe, lambda e: e.matmul(pd[:, hh * 512:(hh + 1) * 512], lhsT=mT[:, k, :], rhs=woutb[:, k, hh * 512:(hh + 1) * 512],
                                                             start=(k == 0), stop=(k == 7)), reads=[mT, woutb], writes=[pd])
                        tm = tmpr.next()
                        fw.op(dve, lambda e: e.tensor_tensor(out=tm[:], in0=pd[:, :], in1=GT[s][0][:], op=ALU.mult), reads=[pd, GT[s][0]], writes=[tm])
                        xn = xnr.next()
                        fw.op(pool, lambda e: e.tensor_tensor(out=xn[:], in0=tm[:], in1=xt[:], op=ALU.add), reads=[tm, xt], writes=[xn])
                        if td["xres"] is not None:
                            xr_ = XRES[td["xres"]]
                            fw.dma(pool, xr_[:, :], xn[:], reads=[xn], writes=[xr_])
                        s_ = stt.next()
                        fw.op(dve, lambda e: e.memset(s_[:], 0.0), writes=[s_])
                        fw.op(act, lambda e: e.activation(out=junk[:], in_=xn[:], func=ACTF.Square, accum_out=s_[:, 0:1]), reads=[xn], writes=[junk, s_])
                        rstd_from_ss(s_, 1, 1.0 / D)
                        xh_ = xh.next()
                        fw.op(dve, lambda e: e.scalar_tensor_tensor(out=xh_[:], in0=xn[:], scalar=s_[:, 4:5], in1=GT[s][1][:],
                                                                    op0=ALU.mult, op1=ALU.mult), reads=[xn, s_, GT[s][1]], writes=[xh_])
                        xb_ = xb.next()
                        fw.op(pool, lambda e: e.tensor_tensor(out=xb_[:], in0=xh_[:], in1=GT[s][2][:], op=ALU.add), reads=[xh_, GT[s][2]], writes=[xb_])
                        pb = psb.next()
                        for k in range(8):
                            fw.op(pe, lambda e: e.transpose(out=pb[:, k * 128:(k + 1) * 128], in_=xb_[:, k * 128:(k + 1) * 128], identity=ident[:]),
                                  reads=[xb_, ident], writes=[pb])
                        hT_ = hT.next()
                        fw.op(act, lambda e: e.copy(out=hT_[:].rearrange("p k t -> p (k t)"), in_=pb[:, :]), reads=[pb], writes=[hT_])
                        if td["kind"] == "lat":
                            fw.dma(pool, XHlv[:, :, 1 + t * 128:1 + (t + 1) * 128], hT_[:], reads=[hT_], writes=[XHTl])
                        elif td["kind"] == "ext":
                            fw.dma(pool, XHev[:, :, :], hT_[:], reads=[hT_], writes=[XHTe])
                        else:
                            tc_ = td["xres"] - NT_L
                            fw.dma(pool, XHcv[:, :, 1 + tc_ * 128:1 + (tc_ + 1) * 128], hT_[:], reads=[hT_], writes=[XHTc])

                fw.barrier()
                chk("OUT%d" % l)
                with ExitStack() as ph:
                    wdnb = fw.sb([128, NCH, D], BF16, "wdnb", ph, multi=True)
                    wdwt = fw.sb([128, 2 * NCH, 3], F32, "wdwt", ph)
                    bdwt = fw.sb([128, 2 * NCH], F32, "bdwt", ph)
                    GT2 = [fw.sb([128, D], F32, f"gt2{s}", ph) for s in range(2)]
                    WUPd = fw.dram(f"wupd{l}", [2 * NCH, 128, 8, 128], BF16)
                    fw.dma(sp, wdwt[:], wdw[l], reads=[wdw], writes=[wdwt])
                    fw.dma(sp, bdwt[:], bdw[l], reads=[bdw], writes=[bdwt])
                    for s in range(2):
                        fw.dma(sp, GT2[s][:], MB[s, 5], reads=[MB], writes=[GT2[s]])
                    if last:
                        gfint = fw.sb([128, D], F32, "gfint", ph)
                        fw.dma(sp, gfint[:], gfin[:], reads=[gfin], writes=[gfint])
                    with ExitStack() as prep:
                        stg = Ring([fw.sb([128, 1408], F32, f"stg{i}", prep) for i in range(2)])
                        cst = Ring([fw.sb([128, 1408], BF16, f"cst{i}", prep) for i in range(2)])
                        WUv = WUPd.t.rearrange("c p k n -> p c k n")
                        for k in range(8):
                            for cb in range(4):
                                sg_ = stg.next()
                                fw.dma(sp, sg_[:], w_up[l, k * 128:(k + 1) * 128, cb * 1408:(cb + 1) * 1408], reads=[w_up], writes=[sg_])
                                ct = cst.next()
                                fw.op(cast_eng(), lambda e: e.tensor_copy(out=ct[:], in_=sg_[:]), reads=[sg_], writes=[ct])
                                fw.dma(pool, WUv[:, cb * 11:(cb + 1) * 11, k, :], ct[:].rearrange("p (c n) -> p c n", n=128),
                                       reads=[ct], writes=[WUPd])
                        for i in range(NCH):
                            sg_ = stg.next()
                            fw.dma(sp, sg_[:, 0:D], w_down[l, i * 128:(i + 1) * 128, :], reads=[w_down], writes=[sg_])
                            fw.op(cast_eng(), lambda e: e.tensor_copy(out=wdnb[:, i, :], in_=sg_[:, 0:D]), reads=[sg_], writes=[wdnb])
                    fw.barrier()
                    wur = Ring([fw.sb([128, 8, 128], BF16, f"wur{i}", ph) for i in range(6)])
                    xhb = Ring([fw.sb([128, 8, 514], BF16, f"xhb{i}", ph) for i in range(2)])
                    yT = fw.sb([128, NCH, 512], BF16, "yT", ph)
                    cr_ = Ring([fw.sb([128, 512], F32, f"cv{i}", ph) for i in range(4)])
                    sgr = Ring([fw.sb([128, 512], F32, f"sg{i}", ph) for i in range(2)])
                    xr = Ring([fw.sb([128, D], F32, f"xf{i}", ph) for i in range(2)])
                    tmpr = Ring([fw.sb([128, D], F32, f"tf{i}", ph) for i in range(2)])
                    junk = fw.sb([128, D], BF16, "junk3", ph)
                    stt = Ring([fw.sb([128, 8], F32, f"stt{i}", ph) for i in range(2)])
                    psu = Ring([fw.ps([128, 512], F32, f"psu{i}", ph) for i in range(3)])
                    psh = Ring([fw.ps([128, 512], F32, f"psh{i}", ph) for i in range(2)])
                    psd = Ring([fw.ps([128, 1024], F32, f"psd{i}", ph) for i in range(1)])
                    XHlv = XHTl.t.rearrange("k p t -> p k t")
                    XHcv = XHTc.t.rearrange("k p t -> p k t")
                    blocks = [(0, t0, min(512, HALF - t0)) for t0 in range(0, HALF, 512)]
                    if not last:
                        blocks.append((1, 0, NCTX))
                    xe = fw.sb([128, 8, 128], BF16, "xe", ph)
                    fw.dma(sp, xe[:], XHTe.t.rearrange("k p t -> p k t"), reads=[XHTe], writes=[xe])

                    def load_blk(bi):
                        s, t0, W = blocks[bi]
                        xb_ = xhb.next()
                        v = XHlv if s == 0 else XHcv
                        n_ = HALF if s == 0 else NCTX
                        lo = 1 if t0 == 0 else 0
                        hi = W + 1 if t0 + W == n_ else W + 2
                        if lo:
                            if s == 0:
                                fw.op(dve, lambda e: e.tensor_scalar(out=xb_[:, :, 0:1], in0=xe[:, :, 127:128], scalar1=hm[:, 0:1], scalar2=None,
                                                                     op0=ALU.mult), reads=[xe, hm], writes=[xb_])
                            else:
                                fw.op(pool, lambda e: e.memset(xb_[:, :, 0:1], 0.0), writes=[xb_])
                        if hi == W + 1:
                            if s == 0:
                                fw.op(dve, lambda e: e.tensor_scalar(out=xb_[:, :, W + 1:W + 2], in0=xe[:, :, 0:1], scalar1=hm[:, 1:2], scalar2=None,
                                                                     op0=ALU.mult), reads=[xe, hm], writes=[xb_])
                            else:
                                fw.op(pool, lambda e: e.memset(xb_[:, :, W + 1:W + 2], 0.0), writes=[xb_])
                        fw.dma(sp, xb_[:, :, lo:hi], v[:, :, t0 + lo:t0 + hi], reads=[XHTl if s == 0 else XHTc], writes=[xb_])
                        return xb_

                    nxt = load_blk(0)
                    for bi, (s, t0, W) in enumerate(blocks):
                        xb_ = nxt
                        if bi + 1 < len(blocks):
                            nxt = load_blk(bi + 1)
                        for i in range(NCH):
                            cvs = []
                            for hv, ch in ((0, i), (1, NCH + i)):
                                pu = psu.next()
                                ph_ = psh.next()
                                wu = wur.next()
                                fw.dma(sp, wu[:], WUPd[ch], reads=[WUPd], writes=[wu])
                                for k in range(8):
                                    fw.op(pe, lambda e: e.matmul(pu[:, 0:W], lhsT=wu[:, k, :], rhs=xb_[:, k, 1:W + 1],
                                                                 start=(k == 0), stop=(k == 7)), reads=[wu, xb_], writes=[pu])
                                for k in range(8):
                                    fw.op(pe, lambda e: e.matmul(ph_[:, 0:2], lhsT=wu[:, k, :], rhs=xb_[:, k, 0:W + 2:W + 1],
                                                                 start=(k == 0), stop=(k == 7)), reads=[wu, xb_], writes=[ph_])
                                c_ = cr_.next()
                                w0 = wdwt[:, ch, 0:1]
                                w2 = wdwt[:, ch, 2:3]
                                fw.op(act, lambda e: e.activation(out=c_[:, 0:W], in_=pu[:, 0:W], func=ACTF.Identity, scale=wdwt[:, ch, 1:2],
                                                                  bias=bdwt[:, ch:ch + 1]), reads=[pu, wdwt, bdwt], writes=[c_])
                                fw.op(dve, lambda e: e.scalar_tensor_tensor(out=c_[:, 1:W], in0=pu[:, 0:W - 1], scalar=w0, in1=c_[:, 1:W],
                                                                            op0=ALU.mult, op1=ALU.add), reads=[pu, wdwt, c_], writes=[c_])
                                fw.op(dve, lambda e: e.scalar_tensor_tensor(out=c_[:, 0:W - 1], in0=pu[:, 1:W], scalar=w2, in1=c_[:, 0:W - 1],
                                                                            op0=ALU.mult, op1=ALU.add), reads=[pu, wdwt, c_], writes=[c_])
                                fw.op(dve, lambda e: e.scalar_tensor_tensor(out=c_[:, 0:1], in0=ph_[:, 0:1], scalar=w0, in1=c_[:, 0:1],
                                                                            op0=ALU.mult, op1=ALU.add), reads=[ph_, wdwt, c_], writes=[c_])
                                fw.op(dve, lambda e: e.scalar_tensor_tensor(out=c_[:, W - 1:W], in0=ph_[:, 1:2], scalar=w2, in1=c_[:, W - 1:W],
                                                                            op0=ALU.mult, op1=ALU.add), reads=[ph_, wdwt, c_], writes=[c_])
                                cvs.append(c_)
                            sg_ = sgr.next()
                            fw.op(act, lambda e: e.activation(out=sg_[:, 0:W], in_=cvs[0][:, 0:W], func=ACTF.Silu), reads=[cvs[0]], writes=[sg_])
                            fw.op(pool, lambda e: e.tensor_tensor(out=yT[:, i, 0:W], in0=sg_[:, 0:W], in1=cvs[1][:, 0:W], op=ALU.mult),
                                  reads=[sg_, cvs[1]], writes=[yT])
                        for qi in range(W // 128):
                            tg = (t0 // 128 + qi) if s == 0 else NT_L + qi
                            pd = psd.next()
                            for hh in range(2):
                                for i in range(NCH):
                                    fw.op(pe, lambda e: e.matmul(pd[:, hh * 512:(hh + 1) * 512], lhsT=yT[:, i, qi * 128:(qi + 1) * 128],
                                                                 rhs=wdnb[:, i, hh * 512:(hh + 1) * 512], start=(i == 0), stop=(i == NCH - 1)),
                                          reads=[yT, wdnb], writes=[pd])
                            xt = xr.next()
                            fw.dma(sp, xt[:], XRES[tg][:, :], reads=[XRES[tg]], writes=[xt])
                            tm = tmpr.next()
                            fw.op(dve, lambda e: e.tensor_tensor(out=tm[:], in0=pd[:, :], in1=GT2[s][:], op=ALU.mult), reads=[pd, GT2[s]], writes=[tm])
                            fw.op(pool, lambda e: e.tensor_tensor(out=tm[:], in0=tm[:], in1=xt[:], op=ALU.add), reads=[tm, xt], writes=[tm])
                            if not last:
                                fw.dma(pool, XRES[tg][:, :], tm[:], reads=[tm], writes=[XRES[tg]])
                            else:
                                s_ = stt.next()
                                fw.op(dve, lambda e: e.memset(s_[:], 0.0), writes=[s_])
                                fw.op(act, lambda e: e.activation(out=junk[:], in_=tm[:], func=ACTF.Square, accum_out=s_[:, 0:1]),
                                      reads=[tm], writes=[junk, s_])
                                rstd_from_ss(s_, 1, 1.0 / D)
                                fw.op(dve, lambda e: e.scalar_tensor_tensor(out=xt[:], in0=tm[:], scalar=s_[:, 4:5], in1=gfint[:],
                                                                            op0=ALU.mult, op1=ALU.mult), reads=[tm, s_, gfint], writes=[xt])
                                fw.dma(pool, out_d[tg * 128:(tg + 1) * 128, :], xt[:], reads=[xt], writes=[out_d])
                    if not last and not fw.stopped:
                        for j in range(NCK):
                            own_bufs = [XRES[2 * j], XRES[2 * j + 1]]
                            fw._deps(pool, own_bufs, [XAG[j]])
                            ins_ = nc.gpsimd.collective_compute("AllGather", ALU.bypass, replica_groups=[[0, 1], [2, 3], [4, 5], [6, 7]],
                                                                ins=[XOWN.t[j * 256:(j + 1) * 256, :]], outs=[XAG[j].t])
                            fw.cc_cnt += 1
                            ins_.then_inc(fw.cc_sem)
                            fw._record((fw.cc_sem, fw.cc_cnt), own_bufs, [XAG[j]])
                        for j in range(NT_O):
                            a_ = xr.next()
                            b_ = tmpr.next()
                            g_ = XAG[j // 2]
                            o_ = (j % 2) * 128
                            fw.dma(sp, a_[:], g_[o_:o_ + 128, :], reads=[g_], writes=[a_])
                            fw.dma(sp, b_[:], g_[256 + o_:256 + o_ + 128, :], reads=[g_], writes=[b_])
                            fw.op(dve, lambda e: e.tensor_scalar(out=a_[:], in0=a_[:], scalar1=hm[:, 2:3], scalar2=None, op0=ALU.mult),
                                  reads=[a_, hm], writes=[a_])
                            fw.op(dve, lambda e: e.scalar_tensor_tensor(out=b_[:], in0=b_[:], scalar=hm[:, 3:4], in1=a_[:], op0=ALU.mult, op1=ALU.add),
                                  reads=[b_, hm, a_], writes=[b_])
                            fw.dma(pool, XOTH[j * 128:(j + 1) * 128, :], b_[:], reads=[b_], writes=[XOTH])

        except _Stop:
            pass
        fw.barrier()
        fw.finish([out_d])
        stats = {e.name: (e.nins, e.nwait) for e in (pe, act, dve, pool, sp)}
    return nc, stats


def _consts(NLAT, hf):
    R = NLAT // 64
    NT_L = NLAT // 128
    HALF = NLAT // 2
    c = {}
    c["ident"] = np.eye(128, dtype=np.float32).astype(bf)
    sel = np.zeros((2, 256), np.float32)
    sel[0, :128] = 1.0
    sel[1, 128:] = 1.0
    c["sel"] = sel
    i64 = np.arange(64)
    a64 = 2 * np.pi * np.outer(i64, i64) / 64.0
    C64 = np.cos(a64) / 8.0
    S64 = np.sin(a64) / 8.0
    C4 = np.kron(np.eye(4), C64)
    S4 = np.kron(np.eye(4), S64)
    c4s4 = np.stack([C4, S4], 0).reshape(2, 2, 128, 256).transpose(2, 1, 0, 3)
    c["c4s4"] = np.ascontiguousarray(c4s4).astype(bf)
    ir = np.arange(R)
    ar = 2 * np.pi * np.outer(ir, ir) / R
    CR = np.cos(ar) / np.sqrt(R)
    SR = np.sin(ar) / np.sqrt(R)
    C64p, S64p = C64, S64
    if hf:
        sg = (-1.0) ** ir
        CR = CR * sg[None, :]
        SR = SR * sg[None, :]
        perm = (i64 + 32) % 64
        C64p, S64p = C64[:, perm], S64[:, perm]
    c["crs"] = np.ascontiguousarray(np.stack([CR, SR, -SR], 1)).astype(bf)
    at = 2 * np.pi * np.outer(ir, i64) / NLAT
    c["tw"] = np.ascontiguousarray(np.stack([np.cos(at), np.sin(at)], 1)).astype(np.float32)
    c["c64"] = np.ascontiguousarray(np.stack([C64p, S64p], 1)).astype(bf)
    i256 = np.arange(256)
    a256 = 2 * np.pi * np.outer(i256, i256) / 256.0
    C256 = np.cos(a256) / 16.0
    S256 = np.sin(a256) / 16.0
    c256 = np.stack([C256, S256], 0).reshape(2, 2, 128, 256).transpose(2, 1, 0, 3)
    c["c256"] = np.ascontiguousarray(c256).astype(bf)
    loc = np.arange(NLAT)
    ext = np.concatenate([np.arange(HALF, HALF + 64), np.arange(NLAT - 64, NLAT)])
    loc = np.concatenate([loc, ext])
    tok = (loc + hf * HALF) % NLAT
    row = (tok // 64).astype(np.float64)
    col = (tok % 64).astype(np.float64)
    inv = 10000.0 ** (-np.arange(8, dtype=np.float64) / 8)
    ang = np.concatenate([row[:, None] * inv, col[:, None] * inv], -1).astype(np.float32)
    cs = np.stack([np.cos(ang), np.sin(ang)], 1)
    cs = np.concatenate([cs, cs], -1)
    c["rope"] = np.ascontiguousarray(cs.reshape(NT_L + 1, 128, 2, 32).transpose(1, 0, 2, 3)).astype(np.float32)
    hm = np.zeros((128, 4), np.float32)
    hm[:, 0] = float(hf)
    hm[:, 1] = float(1 - hf)
    hm[:, 2] = float(hf)
    hm[:, 3] = float(1 - hf)
    c["hm"] = hm
    return c


def make_inmaps(inputs, NLAT, nb):
    f = lambda a: np.ascontiguousarray(np.asarray(a, dtype=np.float32))
    L = inputs["w_ada"].shape[0]
    HALF = NLAT // 2
    shared = dict(
        w_ada=f(inputs["w_ada"]),
        b_ada2=f(np.repeat(np.asarray(inputs["b_ada"])[:, None, :], 2, 1)),
        gmix2=f(np.repeat(np.asarray(inputs["g_mix"])[:, None, :], 2, 1)),
        gffn2=f(np.repeat(np.asarray(inputs["g_ffn"])[:, None, :], 2, 1)),
        gfin=f(np.repeat(np.asarray(inputs["g_final"])[None, :], 128, 0)),
        w_in=f(inputs["w_in"]),
        w_four=f(inputs["w_fourier"]),
        b_four=f(np.broadcast_to(np.asarray(inputs["b_fourier"])[:, None, None, :], (L, 128, 2, 256))),
        gq=f(np.asarray(inputs["g_q_a"]).reshape(L, 3, 128).transpose(0, 2, 1)),
        w_qb=f(inputs["w_q_b"]),
        gkv=f(np.asarray(inputs["g_kv_a"]).reshape(L, 128, 1)),
        w_kvb=f(inputs["w_kv_b"]),
        w_out=f(inputs["w_out"]),
        w_up=f(inputs["w_up"]),
        wdw=f(np.asarray(inputs["w_dw"]).reshape(L, 3, 2 * NCH, 128).transpose(0, 3, 2, 1)),
        bdw=f(np.asarray(inputs["b_dw"]).reshape(L, 2 * NCH, 128).transpose(0, 2, 1)),
        w_down=f(inputs["w_down"]),
    )
    cons = [_consts(NLAT, 0), _consts(NLAT, 1)]
    maps = []
    x = np.asarray(inputs["x"])
    c = np.asarray(inputs["c"], dtype=np.float32)
    ctx = np.asarray(inputs["ctx"])
    cc = np.asarray(inputs["c_ctx"], dtype=np.float32)
    for b in range(nb):
        cv = np.stack([c[b].reshape(8, 128).T, cc.reshape(8, 128).T], -1)
        for hf in range(2):
            m = dict(shared)
            m.update(cons[hf])
            m["x"] = f(np.roll(x[b], -hf * HALF, axis=0))
            m["ctx"] = f(ctx[b])
            m["cvec"] = f(cv)
            maps.append(m)
    return maps


_CACHE = {}


def kernel(**inputs):
    x = np.asarray(inputs["x"])
    B, NLAT, _ = x.shape
    HALF = NLAT // 2
    key = (NLAT,)
    if key not in _CACHE:
        _CACHE[key] = build_program(NLAT)
    nc, _ = _CACHE[key]
    maps = make_inmaps(inputs, NLAT, B)
    assert len(maps) == 8
    res = run_bass_kernel_spmd(nc, maps, core_ids=list(range(8)))
    out = np.empty((B, NLAT, D), np.float32)
    for b in range(B):
        for hf in range(2):
            out[b, hf * HALF:(hf + 1) * HALF] = np.asarray(res.results[2 * b + hf]["out"], dtype=np.float32)
    return out
```

```python
import numpy as np
import ml_dtypes
from contextlib import ExitStack
import concourse.bass as bass
import concourse.mybir as mybir
from concourse.bass_utils import run_bass_kernel_spmd

F32 = mybir.dt.float32
BF16 = mybir.dt.bfloat16
ACTF = mybir.ActivationFunctionType
ALU = mybir.AluOpType
bf = ml_dtypes.bfloat16

D = 1024
NCTX = 256
DFF = 2816
NCH = 22
EPS = 1e-6
SCALE = 96 ** -0.5


class Buf:
    def __init__(self, t, name, multi=False):
        self.t = t
        self.name = name
        self.multi = multi
        self.excl = False
        self.w = {}
        self.r = {}

    def __getitem__(self, k):
        return self.t[k]


def _merge(d, ev):
    k = id(ev[0])
    if k not in d or d[k][1] < ev[1]:
        d[k] = ev


class Eng:
    def __init__(self, fw, e, name, is_pe=False):
        self.e = e
        self.name = name
        self.is_pe = is_pe
        self.sem = fw.new_sem("s_" + name)
        self.cnt = 0
        self.seen = {}
        self.nwait = 0
        self.nins = 0

    def wait(self, ev):
        sem, val = ev
        if self.seen.get(id(sem), 0) >= val:
            return
        self.e.wait_ge(sem, val)
        self.seen[id(sem)] = val
        self.nwait += 1


class FW:
    def __init__(self, nc, stack, n_dma_sems=14):
        self.nc = nc
        self.stack = stack
        self.pe = Eng(self, nc.tensor, "pe", is_pe=True)
        self.act = Eng(self, nc.scalar, "act")
        self.dve = Eng(self, nc.vector, "dve")
        self.pool = Eng(self, nc.gpsimd, "pool")
        self.sp = Eng(self, nc.sync, "sp")
        self.dq = {}
        for q in (self.sp, self.pool, self.act):
            self.dq[q.name] = dict(sems=[self.new_sem(f"d_{q.name}{i}") for i in range(n_dma_sems)],
                                   vals=[0] * n_dma_sems, idx=0)
        self.uid = 0
        self.stopped = False
        self.cc_sem = self.new_sem("cc")
        self.cc_cnt = 0

    def new_sem(self, name):
        return self.stack.enter_context(self.nc.semaphore(name))

    def sb(self, shape, dt, name, stack=None, multi=False):
        self.uid += 1
        t = (stack or self.stack).enter_context(self.nc.sbuf_tensor(f"{name}_{self.uid}", list(shape), dt))
        return Buf(t, name, multi)

    def ps(self, shape, dt, name, stack=None):
        self.uid += 1
        t = (stack or self.stack).enter_context(self.nc.psum_tensor(f"{name}_{self.uid}", list(shape), dt))
        b = Buf(t, name)
        b.excl = True
        return b

    def dram(self, name, shape, dt, multi=True):
        t = self.nc.dram_tensor(name, list(shape), dt, kind="Internal")
        return Buf(t.ap(), name, multi)

    def _deps(self, eng, reads, writes):
        evs = {}
        for b in reads:
            for ev in b.w.values():
                evs[(id(ev[0]), ev[1])] = ev
            if b.excl:
                for ev in b.r.values():
                    if ev[0] is not eng.sem:
                        evs[(id(ev[0]), ev[1])] = ev
        for b in writes:
            if not b.multi:
                for ev in b.w.values():
                    evs[(id(ev[0]), ev[1])] = ev
            for ev in b.r.values():
                evs[(id(ev[0]), ev[1])] = ev
        for ev in evs.values():
            if eng.is_pe and ev[0] is eng.sem:
                continue
            eng.wait(ev)

    def _record(self, ev, reads, writes):
        for b in reads:
            _merge(b.r, ev)
        for b in writes:
            if b.multi:
                _merge(b.w, ev)
            else:
                b.w = {id(ev[0]): ev}
                b.r = {}

    def op(self, eng, fn, reads=(), writes=()):
        if self.stopped:
            return None
        self._deps(eng, reads, writes)
        ins = fn(eng.e)
        eng.cnt += 1
        eng.nins += 1
        ins.then_inc(eng.sem, 1)
        self._record((eng.sem, eng.cnt), reads, writes)
        return ins

    def dma(self, q, out, in_, reads=(), writes=(), **kw):
        if self.stopped:
            return None
        self._deps(q, reads, writes)
        d = self.dq[q.name]
        i = d["idx"] % len(d["sems"])
        d["idx"] += 1
        if d["vals"][i] > 0:
            q.wait((d["sems"][i], d["vals"][i]))
        ins = q.e.dma_start(out=out, in_=in_, **kw)
        d["vals"][i] += 16
        ins.then_inc(d["sems"][i], 16)
        q.nins += 1
        self._record((d["sems"][i], d["vals"][i]), reads, writes)
        return ins

    def barrier(self):
        if self.stopped:
            return
        engs = (self.pe, self.act, self.dve, self.pool, self.sp)
        evs = [(e.sem, e.cnt) for e in engs if e.cnt > 0]
        for d in self.dq.values():
            for sm, v in zip(d["sems"], d["vals"]):
                if v > 0:
                    evs.append((sm, v))
        if self.cc_cnt > 0:
            evs.append((self.cc_sem, self.cc_cnt))
        for e in engs:
            for ev in evs:
                if ev[0] is e.sem:
                    continue
                e.wait(ev)

    def finish(self, bufs):
        for b in bufs:
            for ev in list(b.w.values()):
                self.sp.wait(ev)


class Ring:
    def __init__(self, bufs):
        self.bufs = bufs
        self.i = 0

    def next(self):
        b = self.bufs[self.i % len(self.bufs)]
        self.i += 1
        return b


class _Stop(Exception):
    pass


def build_program(NLAT=8192, depth=2, dbg=False, stop_at=None):
    NT_L = NLAT // 128
    NT_C = NCTX // 128
    NT = NT_L + NT_C
    NTOK = NLAT + NCTX
    R = NLAT // 64
    assert R <= 128 and R % 16 == 0
    HALF = NLAT // 2
    NT_O = HALF // 128
    QB = 512 if HALF % 512 == 0 else 128
    QS_CTX = HALF
    QS_EXT = HALF + NCTX
    NQS = HALF + NCTX + 128

    nc = bass.Bass("TRN2", target_bir_lowering=False)

    def din(name, shape, dt=F32):
        return Buf(nc.dram_tensor(name, list(shape), dt, kind="ExternalInput").ap(), name)

    x_in = din("x", [NLAT, D])
    ctx_in = din("ctx", [NCTX, D])
    cvec = din("cvec", [128, 8, 2])
    w_ada = din("w_ada", [depth, D, 6 * D])
    b_ada2 = din("b_ada2", [depth, 2, 6 * D])
    gmix2 = din("gmix2", [depth, 2, D])
    gffn2 = din("gffn2", [depth, 2, D])
    gfin = din("gfin", [128, D])
    w_in = din("w_in", [depth, D, 800])
    w_four = din("w_four", [depth, 256, 256])
    b_four = din("b_four", [depth, 128, 2, 256])
    gq = din("gq", [depth, 128, 3])
    w_qb = din("w_qb", [depth, 384, 1152])
    gkv = din("gkv", [depth, 128, 1])
    w_kvb = din("w_kvb", [depth, 128, 1536])
    w_out = din("w_out", [depth, D, D])
    w_up = din("w_up", [depth, D, 2 * DFF])
    wdw = din("wdw", [depth, 128, 2 * NCH, 3])
    bdw = din("bdw", [depth, 128, 2 * NCH])
    w_down = din("w_down", [depth, DFF, D])
    ident_d = din("ident", [128, 128], BF16)
    sel_d = din("sel", [2, 256])
    c4s4_d = din("c4s4", [128, 2, 2, 256], BF16)
    crs_d = din("crs", [R, 3, R], BF16)
    tw_d = din("tw", [R, 2, 64])
    c64_d = din("c64", [64, 2, 64], BF16)
    c256_d = din("c256", [128, 2, 2, 256], BF16)
    rope_d = din("rope", [128, NT_L + 1, 2, 32])
    hm_d = din("hm", [128, 4])
    out_d = Buf(nc.dram_tensor("out", [HALF, D], F32, kind="ExternalOutput").ap(), "out", multi=True)
    dbg_d = {}

    with ExitStack() as st:
        fw = FW(nc, st)
        pe, act, dve, pool, sp = fw.pe, fw.act, fw.dve, fw.pool, fw.sp

        XOWN = fw.dram("xown", [HALF, D], F32)
        NCK = HALF // 256
        XAG = [fw.dram(f"xag{j}", [512, D], F32) for j in range(NCK)]
        XOTH = fw.dram("xoth", [HALF, D], F32)
        XRES = {t: Buf(XOWN.t[t * 128:(t + 1) * 128, :], f"xres{t}") for t in range(NT_O)}
        for tc_ in range(NT_C):
            XRES[NT_L + tc_] = fw.dram(f"xctx{tc_}", [128, D], F32, multi=False)
        XHTe = fw.dram("xhte", [8, 128, 128], BF16)
        MB = fw.dram("mb", [2, 6, 128, D], F32)
        ABd = fw.dram("abd", [NTOK, 512], BF16)
        Y2d = fw.dram("y2d", [R, 64, 2, 256], BF16)
        FOURd = fw.dram("fourd", [NTOK, 256], BF16)
        QTd = fw.dram("qtd", [12, 96, NQS], BF16)
        ATTd = fw.dram("attd", [NQS, 768], BF16)
        XHTl = fw.dram("xhtl", [8, 128, HALF + 2], BF16)
        XHTc = fw.dram("xhtc", [8, 128, NCTX + 2], BF16)

        ident = fw.sb([128, 128], BF16, "ident")
        sel = fw.sb([2, 256], F32, "sel")
        zero_t = fw.sb([128, 8, 2], BF16, "zero")
        fw.dma(sp, ident[:], ident_d[:], reads=[ident_d], writes=[ident])
        fw.dma(sp, sel[:], sel_d[:], reads=[sel_d], writes=[sel])
        hm = fw.sb([128, 4], F32, "hm")
        fw.dma(sp, hm[:], hm_d[:], reads=[hm_d], writes=[hm])
        TD = []
        for t in range(NT_L):
            TD.append(dict(kind="lat", s=0, pieces=[(0, t * 128, 128)], kvcol=t * 128, q=(t < NT_O), qslot=t * 128, ropei=t, xres=(t if t < NT_O else None)))
        for tc_ in range(NT_C):
            TD.append(dict(kind="ctx", s=1, pieces=[(0, tc_ * 128, 128)], kvcol=NLAT + tc_ * 128, q=True, qslot=QS_CTX + tc_ * 128, ropei=None, xres=NT_L + tc_))
        TD.append(dict(kind="ext", s=0, pieces=[(0, HALF, 64), (64, NLAT - 64, 64)], kvcol=None, q=True, qslot=QS_EXT, ropei=NT_L, xres=None))
        fw.op(dve, lambda e: e.memset(zero_t[:], 0.0), writes=[zero_t])
        cast_rr = [0]

        def cast_eng():
            cast_rr[0] += 1
            return (dve, pool)[cast_rr[0] % 2]

        def rstd_from_ss(stt, n_cols, inv_n, ncol=1):
            fw.op(dve, lambda e: e.tensor_scalar(out=stt[:, 2:2 + ncol], in0=stt[:, 0:ncol], scalar1=inv_n,
                                                 scalar2=EPS, op0=ALU.mult, op1=ALU.add), reads=[stt], writes=[stt])
            fw.op(act, lambda e: e.sqrt(out=stt[:, 2:2 + ncol], in_=stt[:, 2:2 + ncol]), reads=[stt], writes=[stt])
            fw.op(dve, lambda e: e.reciprocal(out=stt[:, 4:4 + ncol], in_=stt[:, 2:2 + ncol]), reads=[stt], writes=[stt])

        def chk(name):
            if stop_at == name:
                fw.stopped = True

        try:
            for l in range(depth):
                last = (l == depth - 1)
                src_tiles = None if l == 0 else XRES

                def load_rows(xt, td):
                    for (p0, r0, n) in td["pieces"]:
                        if td["kind"] == "ctx":
                            if l == 0:
                                sb_, sap = ctx_in, ctx_in[r0:r0 + n, :]
                            else:
                                sb_ = XRES[td["xres"]]
                                sap = sb_[:, :]
                        elif l == 0:
                            sb_, sap = x_in, x_in[r0:r0 + n, :]
                        elif r0 < HALF:
                            sb_ = XRES[r0 // 128]
                            sap = sb_[:, :]
                        else:
                            sb_, sap = XOTH, XOTH[r0 - HALF:r0 - HALF + n, :]
                        fw.dma(sp, xt[p0:p0 + n, :], sap, reads=[sb_], writes=[xt])

                fw.barrier()
                with ExitStack() as ph:
                    sil = fw.sb([128, 8, 2], F32, "sil", ph)
                    mrow = fw.sb([2, 6 * D], F32, "mrow", ph)
                    brow = fw.sb([2, 6 * D], F32, "brow", ph)
                    g2 = fw.sb([2, 2, D], F32, "g2", ph)
                    wst = Ring([fw.sb([128, 3072], F32, f"wst{i}", ph) for i in range(3)])
                    bst = Ring([fw.sb([128, D], F32, f"bst{i}", ph) for i in range(2)])
                    psm = [fw.ps([128, 512], F32, f"psm{i}", ph) for i in range(8)]
                    fw.dma(sp, sil[:], cvec[:], reads=[cvec], writes=[sil])
                    fw.dma(sp, brow[:], b_ada2[l], reads=[b_ada2], writes=[brow])
                    fw.dma(sp, g2[:, 0, :], gmix2[l], reads=[gmix2], writes=[g2])
                    fw.dma(sp, g2[:, 1, :], gffn2[l], reads=[gffn2], writes=[g2])
                    fw.op(act, lambda e: e.activation(out=sil[:], in_=sil[:], func=ACTF.Silu), reads=[sil], writes=[sil])
                    for half in range(2):
                        for k in range(8):
                            wt = wst.next()
                            fw.dma(sp, wt[:], w_ada[l, k * 128:(k + 1) * 128, half * 3072:(half + 1) * 3072],
                                   reads=[w_ada], writes=[wt])
                            for j in range(6):
                                fw.op(pe, lambda e: e.matmul(psm[j][0:2, :], lhsT=sil[:, k, :], rhs=wt[:, j * 512:(j + 1) * 512],
                                                             start=(k == 0), stop=(k == 7)), reads=[sil, wt], writes=[psm[j]])
                        for j in range(6):
                            c0 = half * 3072 + j * 512
                            fw.op(dve, lambda e: e.tensor_tensor(out=mrow[:, c0:c0 + 512], in0=psm[j][0:2, :],
                                                                 in1=brow[:, c0:c0 + 512], op=ALU.add),
                                  reads=[psm[j], brow], writes=[mrow])
                    for (c0, gi) in ((1 * D, 0), (4 * D, 1)):
                        fw.op(dve, lambda e: e.scalar_tensor_tensor(out=mrow[:, c0:c0 + D], in0=mrow[:, c0:c0 + D], scalar=1.0,
                                                                    in1=g2[:, gi, :], op0=ALU.add, op1=ALU.mult),
                              reads=[mrow, g2], writes=[mrow])
                    pi = 0
                    for s in range(2):
                        for v in range(6):
                            bt = bst.next()
                            for hh in range(2):
                                p_ = psm[pi % 8]
                                pi += 1
                                fw.op(pe, lambda e: e.matmul(p_[:, :], lhsT=sel[:, s * 128:(s + 1) * 128],
                                                             rhs=mrow[:, v * D + hh * 512: v * D + (hh + 1) * 512],
                                                             start=True, stop=True), reads=[sel, mrow], writes=[p_])
                                fw.op(act if hh else dve, lambda e: e.tensor_copy(out=bt[:, hh * 512:(hh + 1) * 512], in_=p_[:, :])
                                      if not hh else e.copy(out=bt[:, hh * 512:(hh + 1) * 512], in_=p_[:, :]),
                                      reads=[p_], writes=[bt])
                            fw.dma(pool, MB[s, v], bt[:], reads=[bt], writes=[MB])

                fw.barrier()
                chk("M%d" % l)
                with ExitStack() as mx:
                    WUPd = fw.dram(f"wupd{l}", [2 * NCH, 128, 8, 128], BF16)
                    WDNd = fw.dram(f"wdnd{l}", [NCH, 128, D], BF16)
                    WOUTd = fw.dram(f"woutd{l}", [8, 128, D], BF16)
                    ckvnT = fw.sb([128, NTOK], BF16, "ckvnT", mx, multi=True)
                    KT = [fw.sb([96, NTOK], BF16, f"KT{i}", mx, multi=True) for i in range(2)]
                    wkvb = fw.sb([128, 1536], BF16, "wkvb", mx)

                    with ExitStack() as ph:
                        winb = fw.sb([128, 8, 800], BF16, "winb", ph, multi=True)
                        wab = fw.sb([128, 8, 512], BF16, "wab", ph, multi=True)
                        wqb = fw.sb([128, 3, 1152], BF16, "wqb", ph, multi=True)
                        gqt = fw.sb([128, 3], F32, "gqt", ph)
                        gkvt = fw.sb([128, 1], F32, "gkvt", ph)
                        GS = [[fw.sb([128, D], F32, f"gs{s}{v}", ph) for v in range(2)] for s in range(2)]
                        psb = [fw.ps([128, 1024], BF16, f"psb{i}", ph) for i in range(2)]
                        psf = [fw.ps([128, 512], F32, f"psf{i}", ph) for i in range(6)]
                        prep = ExitStack()
                        stg = Ring([fw.sb([128, 1536], F32, f"stg{i}", prep) for i in range(2)])
                        wfb = fw.sb([128, 2, 256], BF16, "wfb", prep, multi=True)
                        mab = fw.sb([128, 2, 512], BF16, "mab", prep, multi=True)
                        wfT = fw.sb([128, 2, D], BF16, "wfT", prep, multi=True)
                        c4s4 = fw.sb([128, 2, 2, 256], BF16, "c4s4", prep)

                        fw.dma(sp, c4s4[:], c4s4_d[:], reads=[c4s4_d], writes=[c4s4])
                        fw.dma(sp, gqt[:], gq[l], reads=[gq], writes=[gqt])
                        fw.dma(sp, gkvt[:], gkv[l], reads=[gkv], writes=[gkvt])
                        for s in range(2):
                            fw.dma(sp, GS[s][0][:], MB[s, 1], reads=[MB], writes=[GS[s][0]])
                            fw.dma(sp, GS[s][1][:], MB[s, 0], reads=[MB], writes=[GS[s][1]])
                        for k in range(8):
                            sg_ = stg.next()
                            fw.dma(sp, sg_[:, 0:800], w_in[l, k * 128:(k + 1) * 128, :], reads=[w_in], writes=[sg_])
                            fw.op(cast_eng(), lambda e: e.tensor_copy(out=winb[:, k, :], in_=sg_[:, 0:800]), reads=[sg_], writes=[winb])
                        for kc in range(3):
                            sg_ = stg.next()
                            fw.dma(sp, sg_[:, 0:1152], w_qb[l, kc * 128:(kc + 1) * 128, :], reads=[w_qb], writes=[sg_])
                            fw.op(cast_eng(), lambda e: e.tensor_scalar(out=wqb[:, kc, :], in0=sg_[:, 0:1152], scalar1=gqt[:, kc:kc + 1],
                                                                        scalar2=None, op0=ALU.mult), reads=[sg_, gqt], writes=[wqb])
                        sg_ = stg.next()
                        fw.dma(sp, sg_[:, 0:1536], w_kvb[l], reads=[w_kvb], writes=[sg_])
                        fw.op(cast_eng(), lambda e: e.tensor_scalar(out=wkvb[:], in0=sg_[:, 0:1536], scalar1=gkvt[:, 0:1],
                                                                    scalar2=None, op0=ALU.mult), reads=[sg_, gkvt], writes=[wkvb])
                        for cc in range(2):
                            sg_ = stg.next()
                            fw.dma(sp, sg_[:, 0:256], w_four[l, cc * 128:(cc + 1) * 128, :], reads=[w_four], writes=[sg_])
                            fw.op(cast_eng(), lambda e: e.tensor_copy(out=wfb[:, cc, :], in_=sg_[:, 0:256]), reads=[sg_], writes=[wfb])
                        for cc in range(2):
                            p_ = psf[cc]
                            for X in range(2):
                                for c2 in range(2):
                                    fw.op(pe, lambda e: e.matmul(p_[:, X * 256:(X + 1) * 256],
                                                                 lhsT=c4s4[:, c2, X, cc * 128:(cc + 1) * 128], rhs=wfb[:, c2, :],
                                                                 start=(c2 == 0), stop=(c2 == 1)), reads=[c4s4, wfb], writes=[p_])
                            fw.op(dve, lambda e: e.tensor_copy(out=mab[:, cc, 0:256], in_=p_[:, 0:256]), reads=[p_], writes=[mab])
                            fw.op(dve, lambda e: e.tensor_scalar(out=mab[:, cc, 256:512], in0=p_[:, 256:512], scalar1=-1.0,
                                                                 scalar2=None, op0=ALU.mult), reads=[p_], writes=[mab])
                        for cc in range(2):
                            for k in range(8):
                                fw.op(pe, lambda e: e.transpose(out=psb[cc][:, k * 128:(k + 1) * 128],
                                                                in_=winb[:, k, cc * 128:(cc + 1) * 128], identity=ident[:]),
                                      reads=[winb, ident], writes=[psb[cc]])
                            fw.op(dve, lambda e: e.tensor_copy(out=wfT[:, cc, :], in_=psb[cc][:, :]), reads=[psb[cc]], writes=[wfT])
                        for k in range(8):
                            p_ = psf[k % 5]
                            for cc in range(2):
                                fw.op(pe, lambda e: e.matmul(p_[:, :], lhsT=wfT[:, cc, k * 128:(k + 1) * 128], rhs=mab[:, cc, :],
                                                             start=(cc == 0), stop=(cc == 1)), reads=[wfT, mab], writes=[p_])
                            fw.op(cast_eng() if False else dve, lambda e: e.tensor_copy(out=wab[:, k, :], in_=p_[:, :]),
                                  reads=[p_], writes=[wab])

                        chk('INW%d' % l)
                        fw.barrier()
                        prep.close()
                        G_IN = 3
                        QTv = QTd.t.rearrange("h d t -> d h t")
                        CX = []
                        for g in range(G_IN):
                            c_ = dict(
                                xt=fw.sb([128, D], F32, f"xt{g}", ph), ckv32=fw.sb([128, 544], F32, f"ckv32{g}", ph),
                                xb=fw.sb([128, D], BF16, f"xb{g}", ph), hT=fw.sb([128, 8, 128], BF16, f"hT{g}", ph),
                                s1=fw.sb([128, 8], F32, f"s1{g}", ph), s2=fw.sb([128, 8], F32, f"s2{g}", ph),
                                ab=fw.sb([128, 512], BF16, f"ab{g}", ph), cn=fw.sb([128, 512], BF16, f"cn{g}", ph),
                                cn2=fw.sb([128, 96], BF16, f"cn2{g}", ph), rt=fw.sb([128, 2, 32], F32, f"rt{g}", ph),
                                cq=fw.sb([128, 3, 128], BF16, f"cq{g}", ph), q=fw.sb([128, 12, 96], F32, f"q{g}", ph),
                                qt=fw.sb([128, 2, 12, 32], F32, f"qt{g}", ph), qb=fw.sb([128, 12, 96], BF16, f"qb{g}", ph),
                                qT=fw.sb([96, 12, 128], BF16, f"qT{g}", ph), rp=fw.sb([128, 2, 32], F32, f"rp{g}", ph),
                                junk=fw.sb([128, D], BF16, f"junk{g}", ph),
                            )
                            CX.append(c_)
                            fw.op(dve, lambda e: e.memset(c_["cn2"][:], 0.0), writes=[c_["cn2"]])
                        pbr = Ring([psb[0], psb[1]])
                        pfr = Ring([[psf[0], psf[1], psf[2]], [psf[3], psf[4], psf[5]]])

                        def tile_gen(ti, td, cx, g):
                            s = td["s"]
                            kvc = td["kvcol"]
                            do_kv = kvc is not None
                            do_q = td["q"] and not (td["kind"] == "ctx" and last)
                            xt, xb_, hT_, s_, s2, ab_, cn_, c2_, rt = (cx[k] for k in ("xt", "xb", "hT", "s1", "s2", "ab", "cn", "cn2", "rt"))
                            cq_, q_, qt_, qb_, qT_, ropet, c32 = (cx[k] for k in ("cq", "q", "qt", "qb", "qT", "rp", "ckv32"))
                            xh_ = Buf(q_.t[:].rearrange("p h d -> p (h d)")[:, 0:D], "xh_alias")
                            junk = cx["junk"]
                            load_rows(xt, td)
                            if td["ropei"] is not None:
                                fw.dma(sp, ropet[:], rope_d[:, td["ropei"], :, :], reads=[rope_d], writes=[ropet])
                            yield
                            fw.op(dve, lambda e: e.memset(s_[:], 0.0), writes=[s_])
                            fw.op(act, lambda e: e.activation(out=junk[:], in_=xt[:], func=ACTF.Square, accum_out=s_[:, 0:1]),
                                  reads=[xt], writes=[junk, s_])
                            yield
                            fw.op(dve, lambda e: e.tensor_scalar(out=s_[:, 2:3], in0=s_[:, 0:1], scalar1=1.0 / D, scalar2=EPS, op0=ALU.mult, op1=ALU.add),
                                  reads=[s_], writes=[s_])
                            yield
                            fw.op(act, lambda e: e.sqrt(out=s_[:, 2:3], in_=s_[:, 2:3]), reads=[s_], writes=[s_])
                            yield
                            fw.op(dve, lambda e: e.reciprocal(out=s_[:, 4:5], in_=s_[:, 2:3]), reads=[s_], writes=[s_])
                            yield
                            fw.op(dve, lambda e: e.scalar_tensor_tensor(out=xh_[:, :], in0=xt[:], scalar=s_[:, 4:5], in1=GS[s][0][:],
                                                                        op0=ALU.mult, op1=ALU.mult), reads=[xt, s_, GS[s][0]], writes=[q_])
                            yield
                            fw.op(dve, lambda e: e.tensor_tensor(out=xb_[:], in0=xh_[:, :], in1=GS[s][1][:], op=ALU.add),
                                  reads=[q_, GS[s][1]], writes=[xb_])
                            yield
                            PB = pbr.next()
                            for k in range(8):
                                fw.op(pe, lambda e: e.transpose(out=PB[:, k * 128:(k + 1) * 128], in_=xb_[:, k * 128:(k + 1) * 128],
                                                                identity=ident[:]), reads=[xb_, ident], writes=[PB])
                            fw.op(act, lambda e: e.copy(out=hT_[:].rearrange("p k t -> p (k t)"), in_=PB[:, :]), reads=[PB], writes=[hT_])
                            yield
                            F0, F1, F2 = pfr.next()
                            if do_kv:
                                for k in range(8):
                                    fw.op(pe, lambda e: e.matmul(F0[:, 0:512], lhsT=hT_[:, k, :], rhs=wab[:, k, :], start=(k == 0), stop=(k == 7)),
                                          reads=[hT_, wab], writes=[F0])
                            for k in range(8):
                                fw.op(pe, lambda e: e.matmul(F1[:, 0:512], lhsT=hT_[:, k, :], rhs=winb[:, k, 256:768], start=(k == 0), stop=(k == 7)),
                                      reads=[hT_, winb], writes=[F1])
                            for k in range(8):
                                fw.op(pe, lambda e: e.matmul(F2[:, 0:32], lhsT=hT_[:, k, :], rhs=winb[:, k, 768:800], start=(k == 0), stop=(k == 7)),
                                      reads=[hT_, winb], writes=[F2])
                            if do_kv:
                                fw.op(act, lambda e: e.copy(out=ab_[:], in_=F0[:, 0:512]), reads=[F0], writes=[ab_])
                            fw.op(dve, lambda e: e.tensor_copy(out=c32[:, 0:512], in_=F1[:, 0:512]), reads=[F1], writes=[c32])
                            fw.op(dve, lambda e: e.tensor_copy(out=c32[:, 512:544], in_=F2[:, 0:32]), reads=[F2], writes=[c32])
                            yield
                            if do_kv:
                                fw.dma(sp, ABd[kvc:kvc + 128, :], ab_[:], reads=[ab_], writes=[ABd])
                            fw.op(dve, lambda e: e.memset(s2[:], 0.0), writes=[s2])
                            yield
                            fw.op(act, lambda e: e.activation(out=junk[:, 0:384], in_=c32[:, 0:384], func=ACTF.Square, accum_out=s2[:, 0:1]),
                                  reads=[c32], writes=[junk, s2])
                            fw.op(act, lambda e: e.activation(out=junk[:, 384:512], in_=c32[:, 384:512], func=ACTF.Square, accum_out=s2[:, 1:2]),
                                  reads=[c32], writes=[junk, s2])
                            yield
                            fw.op(dve, lambda e: e.tensor_scalar(out=s2[:, 0:1], in0=s2[:, 0:1], scalar1=128.0 / 384.0, scalar2=None, op0=ALU.mult),
                                  reads=[s2], writes=[s2])
                            yield
                            fw.op(dve, lambda e: e.tensor_scalar(out=s2[:, 2:4], in0=s2[:, 0:2], scalar1=1.0 / 128, scalar2=EPS, op0=ALU.mult, op1=ALU.add),
                                  reads=[s2], writes=[s2])
                            yield
                            fw.op(act, lambda e: e.sqrt(out=s2[:, 2:4], in_=s2[:, 2:4]), reads=[s2], writes=[s2])
                            yield
                            fw.op(dve, lambda e: e.reciprocal(out=s2[:, 4:6], in_=s2[:, 2:4]), reads=[s2], writes=[s2])
                            yield
                            fw.op(dve, lambda e: e.tensor_scalar(out=cn_[:, 0:384], in0=c32[:, 0:384], scalar1=s2[:, 4:5], scalar2=None, op0=ALU.mult),
                                  reads=[c32, s2], writes=[cn_])
                            fw.op(pool, lambda e: e.tensor_scalar(out=cn_[:, 384:512], in0=c32[:, 384:512], scalar1=s2[:, 5:6], scalar2=None, op0=ALU.mult),
                                  reads=[c32, s2], writes=[cn_])
                            if do_kv:
                                if s == 0:
                                    fw.op(dve, lambda e: e.tensor_tensor(out=rt[:, 0, :], in0=c32[:, 512:544], in1=ropet[:, 0, :], op=ALU.mult),
                                          reads=[c32, ropet], writes=[rt])
                                    fw.op(pool, lambda e: e.tensor_tensor(out=rt[:, 1, :], in0=c32[:, 512:544], in1=ropet[:, 1, :], op=ALU.mult),
                                          reads=[c32, ropet], writes=[rt])
                                    yield
                                    fw.op(pool, lambda e: e.tensor_tensor(out=c2_[:, 64:80], in0=rt[:, 0, 0:16], in1=rt[:, 1, 16:32], op=ALU.subtract),
                                          reads=[rt], writes=[c2_])
                                    fw.op(pool, lambda e: e.tensor_tensor(out=c2_[:, 80:96], in0=rt[:, 0, 16:32], in1=rt[:, 1, 0:16], op=ALU.add),
                                          reads=[rt], writes=[c2_])
                                else:
                                    fw.op(dve, lambda e: e.tensor_copy(out=c2_[:, 64:96], in_=c32[:, 512:544]), reads=[c32], writes=[c2_])
                            yield
                            PB = pbr.next()
                            for j in range(4 if do_kv else 3):
                                fw.op(pe, lambda e: e.transpose(out=PB[:, j * 128:(j + 1) * 128], in_=cn_[:, j * 128:(j + 1) * 128],
                                                                identity=ident[:]), reads=[cn_, ident], writes=[PB])
                            if do_kv:
                                fw.op(pe, lambda e: e.transpose(out=PB[0:96, 512:640], in_=c2_[:, 0:96], identity=ident[:]),
                                      reads=[c2_, ident], writes=[PB])
                            fw.op(act, lambda e: e.copy(out=cq_[:].rearrange("p k t -> p (k t)"), in_=PB[:, 0:384]), reads=[PB], writes=[cq_])
                            if do_kv:
                                fw.op(dve, lambda e: e.tensor_copy(out=ckvnT[:, kvc:kvc + 128], in_=PB[:, 384:512]), reads=[PB], writes=[ckvnT])
                                fw.op(act, lambda e: e.copy(out=KT[0][64:96, kvc:kvc + 128], in_=PB[64:96, 512:640]), reads=[PB], writes=[KT[0]])
                                fw.op(dve, lambda e: e.tensor_copy(out=KT[1][64:96, kvc:kvc + 128], in_=PB[64:96, 512:640]), reads=[PB], writes=[KT[1]])
                            yield
                            if not do_q:
                                return
                            F0, F1, F2 = pfr.next()
                            for (pf_, c0, c1) in ((F0, 0, 512), (F1, 512, 1024), (F2, 1024, 1152)):
                                for kc in range(3):
                                    fw.op(pe, lambda e: e.matmul(pf_[:, 0:c1 - c0], lhsT=cq_[:, kc, :], rhs=wqb[:, kc, c0:c1],
                                                                 start=(kc == 0), stop=(kc == 2)), reads=[cq_, wqb], writes=[pf_])
                            qf = q_[:].rearrange("p h d -> p (h d)")
                            fw.op(act, lambda e: e.copy(out=qf[:, 0:512], in_=F0[:, 0:512]), reads=[F0], writes=[q_])
                            fw.op(dve, lambda e: e.tensor_copy(out=qf[:, 512:1024], in_=F1[:, 0:512]), reads=[F1], writes=[q_])
                            fw.op(act, lambda e: e.copy(out=qf[:, 1024:1152], in_=F2[:, 0:128]), reads=[F2], writes=[q_])
                            yield
                            fw.op(pool, lambda e: e.tensor_copy(out=qb_[:, :, 0:64], in_=q_[:, :, 0:64]), reads=[q_], writes=[qb_])
                            if s == 0:
                                def bc(ap_):
                                    return bass.AP(ap_.tensor, ap_.offset, [list(ap_.ap[0]), [0, 12], list(ap_.ap[-1])])
                                fw.op(dve, lambda e: e.tensor_tensor(out=qt_[:, 0, :, :], in0=q_[:, :, 64:96], in1=bc(ropet[:, 0, :]), op=ALU.mult),
                                      reads=[q_, ropet], writes=[qt_])
                                fw.op(pool, lambda e: e.tensor_tensor(out=qt_[:, 1, :, :], in0=q_[:, :, 64:96], in1=bc(ropet[:, 1, :]), op=ALU.mult),
                                      reads=[q_, ropet], writes=[qt_])
                                yield
                                fw.op(dve, lambda e: e.tensor_tensor(out=qb_[:, :, 64:80], in0=qt_[:, 0, :, 0:16], in1=qt_[:, 1, :, 16:32], op=ALU.subtract),
                                      reads=[qt_], writes=[qb_])
                                fw.op(pool, lambda e: e.tensor_tensor(out=qb_[:, :, 80:96], in0=qt_[:, 0, :, 16:32], in1=qt_[:, 1, :, 0:16], op=ALU.add),
                                      reads=[qt_], writes=[qb_])
                            else:
                                fw.op(dve, lambda e: e.tensor_copy(out=qb_[:, :, 64:96], in_=q_[:, :, 64:96]), reads=[q_], writes=[qb_])
                            yield
                            for rnd in range(2):
                                PB = pbr.next()
                                for h6 in range(6):
                                    h = rnd * 6 + h6
                                    fw.op(pe, lambda e: e.transpose(out=PB[0:96, h6 * 128:(h6 + 1) * 128], in_=qb_[:, h, :], identity=ident[:]),
                                          reads=[qb_, ident], writes=[PB])
                                fw.op(act if rnd else dve, (lambda e: e.copy(out=qT_[:, 6:12, :].rearrange("p h t -> p (h t)"), in_=PB[0:96, 0:768])) if rnd else
                                      (lambda e: e.tensor_copy(out=qT_[:, 0:6, :].rearrange("p h t -> p (h t)"), in_=PB[0:96, 0:768])),
                                      reads=[PB], writes=[qT_])
                                yield
                            fw.dma(sp, QTv[:, :, td["qslot"]:td["qslot"] + 128], qT_[:], reads=[qT_], writes=[QTd])

                        tds = list(TD)
                        active = []
                        nxt_i = [0]

                        free_cx = list(range(G_IN))

                        def refill():
                            while free_cx and nxt_i[0] < len(tds):
                                i_ = nxt_i[0]
                                nxt_i[0] += 1
                                g_ = free_cx.pop(0)
                                active.append((tile_gen(i_, tds[i_], CX[g_], g_), g_))

                        refill()
                        while active:
                            for item in list(active):
                                try:
                                    next(item[0])
                                except StopIteration:
                                    active.remove(item)
                                    free_cx.append(item[1])
                                    refill()

                    fw.barrier()
                    chk("IN%d" % l)
                    with ExitStack() as ph:
                        crs = fw.sb([R, 3, R], BF16, "crs", ph)
                        tw = fw.sb([R, 2, 64], F32, "tw", ph)
                        c64 = fw.sb([64, 2, 64], BF16, "c64", ph)
                        bft = fw.sb([128, 2, 256], F32, "bft", ph)
                        fw.dma(sp, crs[:], crs_d[:], reads=[crs_d], writes=[crs])
                        fw.dma(sp, tw[:], tw_d[:], reads=[tw_d], writes=[tw])
                        fw.dma(sp, c64[:], c64_d[:], reads=[c64_d], writes=[c64])
                        fw.dma(sp, bft[:], b_four[l], reads=[b_four], writes=[bft])
                        abg = Ring([fw.sb([R, 16, 512], BF16, f"abg{i}", ph) for i in range(2)])
                        y2g = Ring([fw.sb([R, 16, 2, 256], BF16, f"y2g{i}", ph) for i in range(2)])
                        y2t = Ring([fw.sb([64, 16, 2, 256], BF16, f"y2t{i}", ph) for i in range(2)])
                        fog = Ring([fw.sb([64, 16, 256], BF16, f"fog{i}", ph) for i in range(2)])
                        tt = Ring([fw.sb([R, 256], F32, f"tt{i}", ph) for i in range(4)])
                        psy = Ring([fw.ps([128, 512], F32, f"psy{i}", ph) for i in range(8)])
                        ABv = ABd.t[0:NLAT, :].rearrange("(a b) c -> a b c", b=64)
                        for g in range(4):
                            ab_ = abg.next()
                            fw.dma(sp, ab_[:], ABv[:, g * 16:(g + 1) * 16, :], reads=[ABd], writes=[ab_])
                            y2_ = y2g.next()
                            for pm in range(8):
                                pr = psy.next()
                                pi_ = psy.next()
                                a_ = ab_[:, 2 * pm:2 * pm + 2, 0:256]
                                b_ = ab_[:, 2 * pm:2 * pm + 2, 256:512]
                                prv = pr[0:R, :].rearrange("p (a b) -> p a b", b=256)
                                piv = pi_[0:R, :].rearrange("p (a b) -> p a b", b=256)
                                fw.op(pe, lambda e: e.matmul(prv, lhsT=crs[:, 0, :], rhs=a_, start=True, stop=False), reads=[crs, ab_], writes=[pr])
                                fw.op(pe, lambda e: e.matmul(prv, lhsT=crs[:, 1, :], rhs=b_, start=False, stop=True), reads=[crs, ab_], writes=[pr])
                                fw.op(pe, lambda e: e.matmul(piv, lhsT=crs[:, 0, :], rhs=b_, start=True, stop=False), reads=[crs, ab_], writes=[pi_])
                                fw.op(pe, lambda e: e.matmul(piv, lhsT=crs[:, 2, :], rhs=a_, start=False, stop=True), reads=[crs, ab_], writes=[pi_])
                                for j in range(2):
                                    ml = 2 * pm + j
                                    m2 = g * 16 + ml
                                    t1 = tt.next()
                                    t2 = tt.next()
                                    fw.op(dve, lambda e: e.tensor_scalar(out=t1[:], in0=pi_[0:R, j * 256:(j + 1) * 256], scalar1=tw[:, 1, m2:m2 + 1],
                                                                         scalar2=None, op0=ALU.mult), reads=[pi_, tw], writes=[t1])
                                    fw.op(dve, lambda e: e.scalar_tensor_tensor(out=y2_[:, ml, 0, :], in0=pr[0:R, j * 256:(j + 1) * 256],
                                                                                scalar=tw[:, 0, m2:m2 + 1], in1=t1[:], op0=ALU.mult, op1=ALU.add),
                                          reads=[pr, tw, t1], writes=[y2_])
                                    fw.op(dve, lambda e: e.tensor_scalar(out=t2[:], in0=pr[0:R, j * 256:(j + 1) * 256], scalar1=tw[:, 1, m2:m2 + 1],
                                                                         scalar2=None, op0=ALU.mult), reads=[pr, tw], writes=[t2])
                                    fw.op(dve, lambda e: e.scalar_tensor_tensor(out=y2_[:, ml, 1, :], in0=pi_[0:R, j * 256:(j + 1) * 256],
                                                                                scalar=tw[:, 0, m2:m2 + 1], in1=t2[:], op0=ALU.mult, op1=ALU.subtract),
                                          reads=[pi_, tw, t2], writes=[y2_])
                            fw.dma(pool, Y2d[:, g * 16:(g + 1) * 16, :, :], y2_[:], reads=[y2_], writes=[Y2d])
                        Y2v = Y2d.t.rearrange("a b c d -> b a c d")
                        FOv = FOURd.t[0:NLAT, :].rearrange("(a b) c -> a b c", b=R)
                        for g in range(R // 16):
                            yt_ = y2t.next()
                            fw.dma(sp, yt_[:], Y2v[:, g * 16:(g + 1) * 16, :, :], reads=[Y2d], writes=[yt_])
                            fo_ = fog.next()
                            for p2 in range(8):
                                pz = psy.next()
                                pzv = pz[0:64, :].rearrange("p (a b) -> p a b", b=256)
                                fw.op(pe, lambda e: e.matmul(pzv, lhsT=c64[:, 0, :], rhs=yt_[:, 2 * p2:2 * p2 + 2, 0, :], start=True, stop=False),
                                      reads=[c64, yt_], writes=[pz])
                                fw.op(pe, lambda e: e.matmul(pzv, lhsT=c64[:, 1, :], rhs=yt_[:, 2 * p2:2 * p2 + 2, 1, :], start=False, stop=True),
                                      reads=[c64, yt_], writes=[pz])
                                fw.op(dve, lambda e: e.tensor_tensor(out=fo_[:, 2 * p2:2 * p2 + 2, :], in0=pzv, in1=bft[0:64, :, :], op=ALU.add),
                                      reads=[pz, bft], writes=[fo_])
                            fw.dma(pool, FOv[:, g * 16:(g + 1) * 16, :], fo_[:], reads=[fo_], writes=[FOURd])
                        if not last:
                            c256 = fw.sb([128, 2, 2, 256], BF16, "c256", ph)
                            abc = fw.sb([128, 2, 512], BF16, "abc", ph)
                            foc = fw.sb([128, 2, 256], BF16, "foc", ph)
                            fw.dma(sp, c256[:], c256_d[:], reads=[c256_d], writes=[c256])
                            fw.dma(sp, abc[:], ABd.t[NLAT:NTOK, :].rearrange("(a p) c -> p a c", p=128), reads=[ABd], writes=[abc])
                            for n_ in range(2):
                                pz = psy.next()
                                for mc in range(2):
                                    fw.op(pe, lambda e: e.matmul(pz[:, 0:256], lhsT=c256[:, mc, 0, n_ * 128:(n_ + 1) * 128], rhs=abc[:, mc, 0:256],
                                                                 start=(mc == 0), stop=False), reads=[c256, abc], writes=[pz])
                                for mc in range(2):
                                    fw.op(pe, lambda e: e.matmul(pz[:, 0:256], lhsT=c256[:, mc, 1, n_ * 128:(n_ + 1) * 128], rhs=abc[:, mc, 256:512],
                                                                 start=False, stop=(mc == 1)), reads=[c256, abc], writes=[pz])
                                fw.op(dve, lambda e: e.tensor_tensor(out=foc[:, n_, :], in0=pz[:, 0:256], in1=bft[:, 0, :], op=ALU.add),
                                      reads=[pz, bft], writes=[foc])
                            fw.dma(pool, FOURd.t[NLAT:NTOK, :].rearrange("(a p) c -> p a c", p=128), foc[:], reads=[foc], writes=[FOURd])

                    fw.barrier()
                    chk("F%d" % l)
                    with ExitStack() as ph:
                        Va = [fw.sb([128, NT, 65], BF16, f"Va{i}", ph, multi=True) for i in range(2)]
                        NQ = NQS
                        qts = [fw.sb([96, NQ], BF16, f"qts{i}", ph) for i in range(2)]
                        ptr = Ring([fw.sb([128, 2, QB], BF16, f"pt{i}", ph) for i in range(3)])
                        rcr = Ring([fw.sb([128, 4], F32, f"rc{i}", ph) for i in range(2)])
                        att = Ring([fw.sb([128, 4, 64], BF16, f"att{i}", ph) for i in range(3)])
                        pss = Ring([fw.ps([128, 1024], F32, f"pss{i}", ph) for i in range(2)])
                        pso = Ring([fw.ps([128, 512], F32, f"pso{i}", ph) for i in range(2)])
                        pkv = Ring([fw.ps([128, 512], F32, f"pkv{i}", ph) for i in range(2)])
                        for i in range(2):
                            fw.op(pool, lambda e: e.memset(Va[i][:, :, 64:65], 1.0), writes=[Va[i]])
                        pstg = Ring([fw.sb([128, 1408], F32, f"pstg{i}", ph) for i in range(2)])
                        pcst = Ring([fw.sb([128, 1408], BF16, f"pcst{i}", ph) for i in range(2)])
                        WUv = WUPd.t.rearrange("c p k n -> p c k n")
                        for k in range(8):
                            for cb in range(4):
                                sg_ = pstg.next()
                                fw.dma(pool, sg_[:], w_up[l, k * 128:(k + 1) * 128, cb * 1408:(cb + 1) * 1408], reads=[w_up], writes=[sg_])
                                ct = pcst.next()
                                fw.op(pool, lambda e: e.tensor_copy(out=ct[:], in_=sg_[:]), reads=[sg_], writes=[ct])
                                fw.dma(pool, WUv[:, cb * 11:(cb + 1) * 11, k, :], ct[:].rearrange("p (c n) -> p c n", n=128),
                                       reads=[ct], writes=[WUPd])
                        for (src_w, dst_w, n_) in ((w_down, WDNd, NCH), (w_out, WOUTd, 8)):
                            for i in range(n_):
                                sg_ = pstg.next()
                                fw.dma(pool, sg_[:, 0:D], src_w[l, i * 128:(i + 1) * 128, :], reads=[src_w], writes=[sg_])
                                ct = pcst.next()
                                fw.op(pool, lambda e: e.tensor_copy(out=ct[:, 0:D], in_=sg_[:, 0:D]), reads=[sg_], writes=[ct])
                                fw.dma(pool, dst_w[i], ct[:, 0:D], reads=[ct], writes=[dst_w])

                        def build_kv(h):
                            b = h % 2
                            c0 = 0
                            while c0 < NTOK:
                                w_ = min(512, NTOK - c0)
                                p_ = pkv.next()
                                fw.op(pe, lambda e: e.matmul(p_[0:64, 0:w_], lhsT=wkvb[:, h * 128:h * 128 + 64], rhs=ckvnT[:, c0:c0 + w_],
                                                             start=True, stop=True), reads=[wkvb, ckvnT], writes=[p_])
                                fw.op(dve, lambda e: e.tensor_copy(out=KT[b][0:64, c0:c0 + w_], in_=p_[0:64, 0:w_]), reads=[p_], writes=[KT[b]])
                                c0 += w_
                            c0 = 0
                            while c0 < NT:
                                n_ = min(8, NT - c0)
                                p_ = pkv.next()
                                for c in range(n_):
                                    fw.op(pe, lambda e: e.matmul(p_[:, c * 64:(c + 1) * 64], lhsT=ckvnT[:, (c0 + c) * 128:(c0 + c + 1) * 128],
                                                                 rhs=wkvb[:, h * 128 + 64:h * 128 + 128], start=True, stop=True),
                                          reads=[wkvb, ckvnT], writes=[p_])
                                fw.op(pool if False else dve, lambda e: e.tensor_copy(out=Va[b][:, c0:c0 + n_, 0:64],
                                                                                     in_=p_[:, 0:n_ * 64].rearrange("p (a b) -> p a b", b=64)),
                                      reads=[p_], writes=[Va[b]])
                                c0 += n_
                            fw.dma(sp, qts[b][:], QTd[h, :, 0:NQ], reads=[QTd], writes=[qts[b]])

                        steps = []
                        for h in range(12):
                            blks = [(q0, QB, list(range(NT))) for q0 in range(0, HALF, QB)]
                            blks.append((QS_EXT, 128, list(range(NT))))
                            if not last:
                                blks.append((QS_CTX, NCTX, list(range(NT_L, NT))))
                            for bi, (q0, W, kcs) in enumerate(blks):
                                nk = len(kcs)
                                for i0 in range(0, nk, 2):
                                    steps.append(dict(h=h, q0=q0, W=W, grp=kcs[i0:i0 + 2], i0=i0, nk=nk, first=(i0 == 0),
                                                      last=(i0 + 2 >= nk), newhead=(bi == 0 and i0 == 0)))
                        cur = {}

                        def emit_S(st_):
                            h, q0, W = st_["h"], st_["q0"], st_["W"]
                            b = h % 2
                            p_ = pss.next()
                            for j, kc in enumerate(st_["grp"]):
                                fw.op(pe, lambda e: e.matmul(p_[:, j * 512:j * 512 + W], lhsT=KT[b][0:96, kc * 128:(kc + 1) * 128],
                                                             rhs=qts[b][0:96, q0:q0 + W], start=True, stop=True),
                                      reads=[KT[b], qts[b]], writes=[p_])
                            st_["p"] = p_

                        def emit_exp(st_):
                            W = st_["W"]
                            p_ = st_["p"]
                            pt = ptr.next()
                            ng = len(st_["grp"])
                            fw.op(act, lambda e: e.activation(out=pt[:, 0:ng, 0:W], in_=p_[:, 0:ng * 512].rearrange("p (a b) -> p a b", b=512)[:, :, 0:W],
                                                              func=ACTF.Exp, scale=SCALE), reads=[p_], writes=[pt])
                            st_["pt"] = pt

                        def emit_PV(st_):
                            h, q0, W, i0, nk = st_["h"], st_["q0"], st_["W"], st_["i0"], st_["nk"]
                            b = h % 2
                            nq = W // 128
                            pt = st_["pt"]
                            if st_["newhead"] and h + 1 < 12:
                                build_kv(h + 1)
                            if st_["first"]:
                                pob = pso.next()
                                po = Buf(pob.t[:, 0:260].rearrange("p (a b) -> p a b", b=65), "po")
                                cur["pob"], cur["po"] = pob, po
                            pob, po = cur["pob"], cur["po"]
                            for j, kc in enumerate(st_["grp"]):
                                for qi in range(nq):
                                    fw.op(pe, lambda e: e.matmul(po[:, qi, :], lhsT=pt[:, j, qi * 128:(qi + 1) * 128], rhs=Va[b][:, kc, :],
                                                                 start=(i0 + j == 0 and qi == 0), stop=(i0 + j == nk - 1 and qi == nq - 1)),
                                          reads=[pt, Va[b]], writes=[pob])
                            if st_["last"]:
                                rc = rcr.next()
                                fw.op(dve, lambda e: e.reciprocal(out=rc[:, 0:nq], in_=po[:, 0:nq, 64]), reads=[pob], writes=[rc])
                                at = att.next()
                                for qi in range(nq):
                                    fw.op(dve, lambda e: e.tensor_scalar(out=at[:, qi, :], in0=po[:, qi, 0:64], scalar1=rc[:, qi:qi + 1], scalar2=None,
                                                                         op0=ALU.mult), reads=[pob, rc], writes=[at])
                                fw.dma(sp, ATTd.t[q0:q0 + W, h * 64:(h + 1) * 64].rearrange("(a p) d -> p a d", p=128), at[:, 0:nq, :],
                                       reads=[at], writes=[ATTd])

                        build_kv(0)
                        emit_S(steps[0])
                        for i_, st_ in enumerate(steps):
                            emit_exp(st_)
                            if i_ + 1 < len(steps):
                                emit_S(steps[i_ + 1])
                            emit_PV(st_)

                fw.barrier()
                chk("ATT%d" % l)
                with ExitStack() as ph:
                    stg = Ring([fw.sb([128, 1024], F32, f"stg{i}", ph) for i in range(2)])
                    woutb = fw.sb([128, 8, D], BF16, "woutb", ph, multi=True)
                    GT = [[fw.sb([128, D], F32, f"gt{s}{v}", ph) for v in range(3)] for s in range(2)]
                    for s in range(2):
                        for v, mv in enumerate((2, 4, 3)):
                            fw.dma(sp, GT[s][v][:], MB[s, mv], reads=[MB], writes=[GT[s][v]])
                    fw.dma(sp, woutb[:], WOUTd.t.rearrange("k p n -> p k n"), reads=[WOUTd], writes=[woutb])
                    G_O = 2
                    XHlv = XHTl.t.rearrange("k p t -> p k t")
                    XHcv = XHTc.t.rearrange("k p t -> p k t")
                    XHev = XHTe.t.rearrange("k p t -> p k t")
                    otds = [td for td in TD if td["q"] and not (last and td["kind"] == "ctx")]
                    OCX = []
                    for g in range(G_O):
                        OCX.append(dict(
                            m=fw.sb([128, D], BF16, f"mix{g}", ph), mT=fw.sb([128, 8, 128], BF16, f"mixT{g}", ph),
                            xt=fw.sb([128, D], F32, f"xo{g}", ph), tm=fw.sb([128, D], F32, f"tmp{g}", ph),
                            xn=fw.sb([128, D], F32, f"xn{g}", ph), junk=fw.sb([128, D], BF16, f"junk2{g}", ph),
                            s=fw.sb([128, 8], F32, f"stt{g}", ph), xh=fw.sb([128, D], F32, f"xh{g}", ph),
                            xb=fw.sb([128, D], BF16, f"xb{g}", ph), hT=fw.sb([128, 8, 128], BF16, f"hT{g}", ph),
                            pb=fw.ps([128, 1024], BF16, f"psb{g}", ph), pd=fw.ps([128, 1024], F32, f"psd{g}", ph),
                        ))

                    def out_gen(ti, td, cx):
                        s = td["s"]
                        t = ti
                        m_, mT, xt, tm, xn, junk, s_, xh_, xb_, hT_, pb, pd = (cx[k] for k in ("m", "mT", "xt", "tm", "xn", "junk", "s", "xh", "xb", "hT", "pb", "pd"))
                        for (p0, r0, n) in td["pieces"]:
                            fr = r0 if td["kind"] != "ctx" else NLAT + r0
                            fw.dma(sp, m_[p0:p0 + n, 0:256], FOURd[fr:fr + n, :], reads=[FOURd], writes=[m_])
                        qs = td["qslot"]
                        fw.dma(sp, m_[:, 256:1024], ATTd[qs:qs + 128, :], reads=[ATTd], writes=[m_])
                        load_rows(xt, td)
                        yield
                        for k in range(8):
                            fw.op(pe, lambda e: e.transpose(out=pb[:, k * 128:(k + 1) * 128], in_=m_[:, k * 128:(k + 1) * 128], identity=ident[:]),
                                  reads=[m_, ident], writes=[pb])
                        yield
                        fw.op(act, lambda e: e.copy(out=mT[:].rearrange("p k t -> p (k t)"), in_=pb[:, :]), reads=[pb], writes=[mT])
                        yield
                        for hh in range(2):
                            for k in range(8):
                                fw.op(pe, lambda e: e.matmul(pd[:, hh * 512:(hh + 1) * 512], lhsT=mT[:, k, :], rhs=woutb[:, k, hh * 512:(hh + 1) * 512],
                                                             start=(k == 0), stop=(k == 7)), reads=[mT, woutb], writes=[pd])
                        yield
                        fw.op(dve, lambda e: e.tensor_tensor(out=tm[:], in0=pd[:, :], in1=GT[s][0][:], op=ALU.mult), reads=[pd, GT[s][0]], writes=[tm])
                        yield
                        fw.op(pool, lambda e: e.tensor_tensor(out=xn[:], in0=tm[:], in1=xt[:], op=ALU.add), reads=[tm, xt], writes=[xn])
                        fw.op(dve, lambda e: e.memset(s_[:], 0.0), writes=[s_])
                        yield
                        if td["xres"] is not None:
                            xr_ = XRES[td["xres"]]
                            fw.dma(sp, xr_[:, :], xn[:], reads=[xn], writes=[xr_])
                        fw.op(act, lambda e: e.activation(out=junk[:], in_=xn[:], func=ACTF.Square, accum_out=s_[:, 0:1]), reads=[xn], writes=[junk, s_])
                        yield
                        fw.op(dve, lambda e: e.tensor_scalar(out=s_[:, 2:3], in0=s_[:, 0:1], scalar1=1.0 / D, scalar2=EPS, op0=ALU.mult, op1=ALU.add),
                              reads=[s_], writes=[s_])
                        yield
                        fw.op(act, lambda e: e.sqrt(out=s_[:, 2:3], in_=s_[:, 2:3]), reads=[s_], writes=[s_])
                        yield
                        fw.op(dve, lambda e: e.reciprocal(out=s_[:, 4:5], in_=s_[:, 2:3]), reads=[s_], writes=[s_])
                        yield
                        fw.op(dve, lambda e: e.scalar_tensor_tensor(out=xh_[:], in0=xn[:], scalar=s_[:, 4:5], in1=GT[s][1][:],
                                                                    op0=ALU.mult, op1=ALU.mult), reads=[xn, s_, GT[s][1]], writes=[xh_])
                        yield
                        fw.op(pool, lambda e: e.tensor_tensor(out=xb_[:], in0=xh_[:], in1=GT[s][2][:], op=ALU.add), reads=[xh_, GT[s][2]], writes=[xb_])
                        yield
                        for k in range(8):
                            fw.op(pe, lambda e: e.transpose(out=pb[:, k * 128:(k + 1) * 128], in_=xb_[:, k * 128:(k + 1) * 128], identity=ident[:]),
                                  reads=[xb_, ident], writes=[pb])
                        yield
                        fw.op(act, lambda e: e.copy(out=hT_[:].rearrange("p k t -> p (k t)"), in_=pb[:, :]), reads=[pb], writes=[hT_])
                        yield
                        if td["kind"] == "lat":
                            fw.dma(sp, XHlv[:, :, 1 + t * 128:1 + (t + 1) * 128], hT_[:], reads=[hT_], writes=[XHTl])
                        elif td["kind"] == "ext":
                            fw.dma(sp, XHev[:, :, :], hT_[:], reads=[hT_], writes=[XHTe])
                        else:
                            tc_ = td["xres"] - NT_L
                            fw.dma(sp, XHcv[:, :, 1 + tc_ * 128:1 + (tc_ + 1) * 128], hT_[:], reads=[hT_], writes=[XHTc])

                    active = []
                    nxt_i = [0]

                    free_ocx = list(range(G_O))

                    def refill_o():
                        while free_ocx and nxt_i[0] < len(otds):
                            i_ = nxt_i[0]
                            nxt_i[0] += 1
                            g_ = free_ocx.pop(0)
                            active.append((out_gen(i_, otds[i_], OCX[g_]), g_))

                    refill_o()
                    while active:
                        for item in list(active):
                            try:
                                next(item[0])
                            except StopIteration:
                                active.remove(item)
                                free_ocx.append(item[1])
                                refill_o()

                fw.barrier()
                chk("OUT%d" % l)
                with ExitStack() as ph:
                    wdnb = fw.sb([128, NCH, D], BF16, "wdnb", ph, multi=True)
                    wdwt = fw.sb([128, 2 * NCH, 3], F32, "wdwt", ph)
                    bdwt = fw.sb([128, 2 * NCH], F32, "bdwt", ph)
                    GT2 = [fw.sb([128, D], F32, f"gt2{s}", ph) for s in range(2)]
                    fw.dma(sp, wdwt[:], wdw[l], reads=[wdw], writes=[wdwt])
                    fw.dma(sp, bdwt[:], bdw[l], reads=[bdw], writes=[bdwt])
                    for s in range(2):
                        fw.dma(sp, GT2[s][:], MB[s, 5], reads=[MB], writes=[GT2[s]])
                    if last:
                        gfint = fw.sb([128, D], F32, "gfint", ph)
                        fw.dma(sp, gfint[:], gfin[:], reads=[gfin], writes=[gfint])
                    fw.dma(sp, wdnb[:], WDNd.t.rearrange("c p n -> p c n"), reads=[WDNd], writes=[wdnb])
                    fw.barrier()
                    wur = Ring([fw.sb([128, 8, 128], BF16, f"wur{i}", ph) for i in range(6)])
                    xhb = Ring([fw.sb([128, 8, 514], BF16, f"xhb{i}", ph) for i in range(2)])
                    yT = fw.sb([128, NCH, 512], BF16, "yT", ph)
                    cr_ = Ring([fw.sb([128, 512], F32, f"cv{i}", ph) for i in range(4)])
                    sgr = Ring([fw.sb([128, 512], F32, f"sg{i}", ph) for i in range(2)])
                    xr = Ring([fw.sb([128, D], F32, f"xf{i}", ph) for i in range(2)])
                    tmpr = Ring([fw.sb([128, D], F32, f"tf{i}", ph) for i in range(2)])
                    junk = fw.sb([128, D], BF16, "junk3", ph)
                    stt = Ring([fw.sb([128, 8], F32, f"stt{i}", ph) for i in range(2)])
                    psu = Ring([fw.ps([128, 512], F32, f"psu{i}", ph) for i in range(3)])
                    psh = Ring([fw.ps([128, 512], F32, f"psh{i}", ph) for i in range(2)])
                    psd = Ring([fw.ps([128, 1024], F32, f"psd{i}", ph) for i in range(1)])
                    XHlv = XHTl.t.rearrange("k p t -> p k t")
                    XHcv = XHTc.t.rearrange("k p t -> p k t")
                    blocks = [(0, t0, min(512, HALF - t0)) for t0 in range(0, HALF, 512)]
                    if not last:
                        blocks.append((1, 0, NCTX))
                    xe = fw.sb([128, 8, 128], BF16, "xe", ph)
                    fw.dma(sp, xe[:], XHTe.t.rearrange("k p t -> p k t"), reads=[XHTe], writes=[xe])

                    def load_blk(bi):
                        s, t0, W = blocks[bi]
                        xb_ = xhb.next()
                        v = XHlv if s == 0 else XHcv
                        n_ = HALF if s == 0 else NCTX
                        lo = 1 if t0 == 0 else 0
                        hi = W + 1 if t0 + W == n_ else W + 2
                        if lo:
                            if s == 0:
                                fw.op(dve, lambda e: e.tensor_scalar(out=xb_[:, :, 0:1], in0=xe[:, :, 127:128], scalar1=hm[:, 0:1], scalar2=None,
                                                                     op0=ALU.mult), reads=[xe, hm], writes=[xb_])
                            else:
                                fw.op(pool, lambda e: e.memset(xb_[:, :, 0:1], 0.0), writes=[xb_])
                        if hi == W + 1:
                            if s == 0:
                                fw.op(dve, lambda e: e.tensor_scalar(out=xb_[:, :, W + 1:W + 2], in0=xe[:, :, 0:1], scalar1=hm[:, 1:2], scalar2=None,
                                                                     op0=ALU.mult), reads=[xe, hm], writes=[xb_])
                            else:
                                fw.op(pool, lambda e: e.memset(xb_[:, :, W + 1:W + 2], 0.0), writes=[xb_])
                        fw.dma(sp, xb_[:, :, lo:hi], v[:, :, t0 + lo:t0 + hi], reads=[XHTl if s == 0 else XHTc], writes=[xb_])
                        return xb_

                    nxt = load_blk(0)
                    for bi, (s, t0, W) in enumerate(blocks):
                        xb_ = nxt
                        if bi + 1 < len(blocks):
                            nxt = load_blk(bi + 1)
                        for i in range(NCH):
                            cvs = []
                            for hv, ch in ((0, i), (1, NCH + i)):
                                pu = psu.next()
                                ph_ = psh.next()
                                wu = wur.next()
                                fw.dma(sp, wu[:], WUPd[ch], reads=[WUPd], writes=[wu])
                                for k in range(8):
                                    fw.op(pe, lambda e: e.matmul(pu[:, 0:W], lhsT=wu[:, k, :], rhs=xb_[:, k, 1:W + 1],
                                                                 start=(k == 0), stop=(k == 7)), reads=[wu, xb_], writes=[pu])
                                for k in range(8):
                                    fw.op(pe, lambda e: e.matmul(ph_[:, 0:2], lhsT=wu[:, k, :], rhs=xb_[:, k, 0:W + 2:W + 1],
                                                                 start=(k == 0), stop=(k == 7)), reads=[wu, xb_], writes=[ph_])
                                c_ = cr_.next()
                                w0 = wdwt[:, ch, 0:1]
                                w2 = wdwt[:, ch, 2:3]
                                fw.op(act, lambda e: e.activation(out=c_[:, 0:W], in_=pu[:, 0:W], func=ACTF.Identity, scale=wdwt[:, ch, 1:2],
                                                                  bias=bdwt[:, ch:ch + 1]), reads=[pu, wdwt, bdwt], writes=[c_])
                                fw.op(dve, lambda e: e.scalar_tensor_tensor(out=c_[:, 1:W], in0=pu[:, 0:W - 1], scalar=w0, in1=c_[:, 1:W],
                                                                            op0=ALU.mult, op1=ALU.add), reads=[pu, wdwt, c_], writes=[c_])
                                fw.op(dve, lambda e: e.scalar_tensor_tensor(out=c_[:, 0:W - 1], in0=pu[:, 1:W], scalar=w2, in1=c_[:, 0:W - 1],
                                                                            op0=ALU.mult, op1=ALU.add), reads=[pu, wdwt, c_], writes=[c_])
                                fw.op(dve, lambda e: e.scalar_tensor_tensor(out=c_[:, 0:1], in0=ph_[:, 0:1], scalar=w0, in1=c_[:, 0:1],
                                                                            op0=ALU.mult, op1=ALU.add), reads=[ph_, wdwt, c_], writes=[c_])
                                fw.op(dve, lambda e: e.scalar_tensor_tensor(out=c_[:, W - 1:W], in0=ph_[:, 1:2], scalar=w2, in1=c_[:, W - 1:W],
                                                                            op0=ALU.mult, op1=ALU.add), reads=[ph_, wdwt, c_], writes=[c_])
                                cvs.append(c_)
                            sg_ = sgr.next()
                            fw.op(act, lambda e: e.activation(out=sg_[:, 0:W], in_=cvs[0][:, 0:W], func=ACTF.Silu), reads=[cvs[0]], writes=[sg_])
                            fw.op(pool, lambda e: e.tensor_tensor(out=yT[:, i, 0:W], in0=sg_[:, 0:W], in1=cvs[1][:, 0:W], op=ALU.mult),
                                  reads=[sg_, cvs[1]], writes=[yT])
                        for qi in range(W // 128):
                            tg = (t0 // 128 + qi) if s == 0 else NT_L + qi
                            pd = psd.next()
                            for hh in range(2):
                                for i in range(NCH):
                                    fw.op(pe, lambda e: e.matmul(pd[:, hh * 512:(hh + 1) * 512], lhsT=yT[:, i, qi * 128:(qi + 1) * 128],
                                                                 rhs=wdnb[:, i, hh * 512:(hh + 1) * 512], start=(i == 0), stop=(i == NCH - 1)),
                                          reads=[yT, wdnb], writes=[pd])
                            xt = xr.next()
                            fw.dma(sp, xt[:], XRES[tg][:, :], reads=[XRES[tg]], writes=[xt])
                            tm = tmpr.next()
                            fw.op(dve, lambda e: e.tensor_tensor(out=tm[:], in0=pd[:, :], in1=GT2[s][:], op=ALU.mult), reads=[pd, GT2[s]], writes=[tm])
                            fw.op(pool, lambda e: e.tensor_tensor(out=tm[:], in0=tm[:], in1=xt[:], op=ALU.add), reads=[tm, xt], writes=[tm])
                            if not last:
                                fw.dma(pool, XRES[tg][:, :], tm[:], reads=[tm], writes=[XRES[tg]])
                            else:
                                s_ = stt.next()
                                fw.op(dve, lambda e: e.memset(s_[:], 0.0), writes=[s_])
                                fw.op(act, lambda e: e.activation(out=junk[:], in_=tm[:], func=ACTF.Square, accum_out=s_[:, 0:1]),
                                      reads=[tm], writes=[junk, s_])
                                rstd_from_ss(s_, 1, 1.0 / D)
                                fw.op(dve, lambda e: e.scalar_tensor_tensor(out=xt[:], in0=tm[:], scalar=s_[:, 4:5], in1=gfint[:],
                                                                            op0=ALU.mult, op1=ALU.mult), reads=[tm, s_, gfint], writes=[xt])
                                fw.dma(pool, out_d[tg * 128:(tg + 1) * 128, :], xt[:], reads=[xt], writes=[out_d])
                    if not last and not fw.stopped:
                        for j in range(NCK):
                            own_bufs = [XRES[2 * j], XRES[2 * j + 1]]
                            fw._deps(pool, own_bufs, [XAG[j]])
                            ins_ = nc.gpsimd.collective_compute("AllGather", ALU.bypass, replica_groups=[[0, 1], [2, 3], [4, 5], [6, 7]],
                                                                ins=[XOWN.t[j * 256:(j + 1) * 256, :]], outs=[XAG[j].t])
                            fw.cc_cnt += 1
                            ins_.then_inc(fw.cc_sem)
                            fw._record((fw.cc_sem, fw.cc_cnt), own_bufs, [XAG[j]])
                        for j in range(NT_O):
                            a_ = xr.next()
                            b_ = tmpr.next()
                            g_ = XAG[j // 2]
                            o_ = (j % 2) * 128
                            fw.dma(sp, a_[:], g_[o_:o_ + 128, :], reads=[g_], writes=[a_])
                            fw.dma(sp, b_[:], g_[256 + o_:256 + o_ + 128, :], reads=[g_], writes=[b_])
                            fw.op(dve, lambda e: e.tensor_scalar(out=a_[:], in0=a_[:], scalar1=hm[:, 2:3], scalar2=None, op0=ALU.mult),
                                  reads=[a_, hm], writes=[a_])
                            fw.op(dve, lambda e: e.scalar_tensor_tensor(out=b_[:], in0=b_[:], scalar=hm[:, 3:4], in1=a_[:], op0=ALU.mult, op1=ALU.add),
                                  reads=[b_, hm, a_], writes=[b_])
                            fw.dma(pool, XOTH[j * 128:(j + 1) * 128, :], b_[:], reads=[b_], writes=[XOTH])

        except _Stop:
            pass
        fw.barrier()
        fw.finish([out_d])
        stats = {e.name: (e.nins, e.nwait) for e in (pe, act, dve, pool, sp)}
    return nc, stats


def _consts(NLAT, hf):
    R = NLAT // 64
    NT_L = NLAT // 128
    HALF = NLAT // 2
    c = {}
    c["ident"] = np.eye(128, dtype=np.float32).astype(bf)
    sel = np.zeros((2, 256), np.float32)
    sel[0, :128] = 1.0
    sel[1, 128:] = 1.0
    c["sel"] = sel
    i64 = np.arange(64)
    a64 = 2 * np.pi * np.outer(i64, i64) / 64.0
    C64 = np.cos(a64) / 8.0
    S64 = np.sin(a64) / 8.0
    C4 = np.kron(np.eye(4), C64)
    S4 = np.kron(np.eye(4), S64)
    c4s4 = np.stack([C4, S4], 0).reshape(2, 2, 128, 256).transpose(2, 1, 0, 3)
    c["c4s4"] = np.ascontiguousarray(c4s4).astype(bf)
    ir = np.arange(R)
    ar = 2 * np.pi * np.outer(ir, ir) / R
    CR = np.cos(ar) / np.sqrt(R)
    SR = np.sin(ar) / np.sqrt(R)
    C64p, S64p = C64, S64
    if hf:
        sg = (-1.0) ** ir
        CR = CR * sg[None, :]
        SR = SR * sg[None, :]
        perm = (i64 + 32) % 64
        C64p, S64p = C64[:, perm], S64[:, perm]
    c["crs"] = np.ascontiguousarray(np.stack([CR, SR, -SR], 1)).astype(bf)
    at = 2 * np.pi * np.outer(ir, i64) / NLAT
    c["tw"] = np.ascontiguousarray(np.stack([np.cos(at), np.sin(at)], 1)).astype(np.float32)
    c["c64"] = np.ascontiguousarray(np.stack([C64p, S64p], 1)).astype(bf)
    i256 = np.arange(256)
    a256 = 2 * np.pi * np.outer(i256, i256) / 256.0
    C256 = np.cos(a256) / 16.0
    S256 = np.sin(a256) / 16.0
    c256 = np.stack([C256, S256], 0).reshape(2, 2, 128, 256).transpose(2, 1, 0, 3)
    c["c256"] = np.ascontiguousarray(c256).astype(bf)
    loc = np.arange(NLAT)
    ext = np.concatenate([np.arange(HALF, HALF + 64), np.arange(NLAT - 64, NLAT)])
    loc = np.concatenate([loc, ext])
    tok = (loc + hf * HALF) % NLAT
    row = (tok // 64).astype(np.float64)
    col = (tok % 64).astype(np.float64)
    inv = 10000.0 ** (-np.arange(8, dtype=np.float64) / 8)
    ang = np.concatenate([row[:, None] * inv, col[:, None] * inv], -1).astype(np.float32)
    cs = np.stack([np.cos(ang), np.sin(ang)], 1)
    cs = np.concatenate([cs, cs], -1)
    c["rope"] = np.ascontiguousarray(cs.reshape(NT_L + 1, 128, 2, 32).transpose(1, 0, 2, 3)).astype(np.float32)
    hm = np.zeros((128, 4), np.float32)
    hm[:, 0] = float(hf)
    hm[:, 1] = float(1 - hf)
    hm[:, 2] = float(hf)
    hm[:, 3] = float(1 - hf)
    c["hm"] = hm
    return c


def make_inmaps(inputs, NLAT, nb):
    f = lambda a: np.ascontiguousarray(np.asarray(a, dtype=np.float32))
    L = inputs["w_ada"].shape[0]
    HALF = NLAT // 2
    shared = dict(
        w_ada=f(inputs["w_ada"]),
        b_ada2=f(np.repeat(np.asarray(inputs["b_ada"])[:, None, :], 2, 1)),
        gmix2=f(np.repeat(np.asarray(inputs["g_mix"])[:, None, :], 2, 1)),
        gffn2=f(np.repeat(np.asarray(inputs["g_ffn"])[:, None, :], 2, 1)),
        gfin=f(np.repeat(np.asarray(inputs["g_final"])[None, :], 128, 0)),
        w_in=f(inputs["w_in"]),
        w_four=f(inputs["w_fourier"]),
        b_four=f(np.broadcast_to(np.asarray(inputs["b_fourier"])[:, None, None, :], (L, 128, 2, 256))),
        gq=f(np.asarray(inputs["g_q_a"]).reshape(L, 3, 128).transpose(0, 2, 1)),
        w_qb=f(inputs["w_q_b"]),
        gkv=f(np.asarray(inputs["g_kv_a"]).reshape(L, 128, 1)),
        w_kvb=f(inputs["w_kv_b"]),
        w_out=f(inputs["w_out"]),
        w_up=f(inputs["w_up"]),
        wdw=f(np.asarray(inputs["w_dw"]).reshape(L, 3, 2 * NCH, 128).transpose(0, 3, 2, 1)),
        bdw=f(np.asarray(inputs["b_dw"]).reshape(L, 2 * NCH, 128).transpose(0, 2, 1)),
        w_down=f(inputs["w_down"]),
    )
    cons = [_consts(NLAT, 0), _consts(NLAT, 1)]
    maps = []
    x = np.asarray(inputs["x"])
    c = np.asarray(inputs["c"], dtype=np.float32)
    ctx = np.asarray(inputs["ctx"])
    cc = np.asarray(inputs["c_ctx"], dtype=np.float32)
    for b in range(nb):
        cv = np.stack([c[b].reshape(8, 128).T, cc.reshape(8, 128).T], -1)
        for hf in range(2):
            m = dict(shared)
            m.update(cons[hf])
            m["x"] = f(np.roll(x[b], -hf * HALF, axis=0))
            m["ctx"] = f(ctx[b])
            m["cvec"] = f(cv)
            maps.append(m)
    return maps


_CACHE = {}


def kernel(**inputs):
    x = np.asarray(inputs["x"])
    B, NLAT, _ = x.shape
    HALF = NLAT // 2
    key = (NLAT,)
    if key not in _CACHE:
        _CACHE[key] = build_program(NLAT)
    nc, _ = _CACHE[key]
    maps = make_inmaps(inputs, NLAT, B)
    assert len(maps) == 8
    res = run_bass_kernel_spmd(nc, maps, core_ids=list(range(8)))
    out = np.empty((B, NLAT, D), np.float32)
    for b in range(B):
        for hf in range(2):
            out[b, hf * HALF:(hf + 1) * HALF] = np.asarray(res.results[2 * b + hf]["out"], dtype=np.float32)
    return out
```

```python
import numpy as np
import ml_dtypes
from contextlib import ExitStack
import concourse.bass as bass
import concourse.mybir as mybir
from concourse.bass_utils import run_bass_kernel_spmd

F32 = mybir.dt.float32
BF16 = mybir.dt.bfloat16
ACTF = mybir.ActivationFunctionType
ALU = mybir.AluOpType
bf = ml_dtypes.bfloat16

D = 1024
NCTX = 256
DFF = 2816
NCH = 22
EPS = 1e-6
SCALE = 96 ** -0.5


class Buf:
    def __init__(self, t, name, multi=False):
        self.t = t
        self.name = name
        self.multi = multi
        self.excl = False
        self.w = {}
        self.r = {}

    def __getitem__(self, k):
        return self.t[k]


def _merge(d, ev):
    k = id(ev[0])
    if k not in d or d[k][1] < ev[1]:
        d[k] = ev


class Eng:
    def __init__(self, fw, e, name, is_pe=False):
        self.e = e
        self.name = name
        self.is_pe = is_pe
        self.sem = fw.new_sem("s_" + name)
        self.cnt = 0
        self.seen = {}
        self.nwait = 0
        self.nins = 0

    def wait(self, ev):
        sem, val = ev
        if self.seen.get(id(sem), 0) >= val:
            return
        self.e.wait_ge(sem, val)
        self.seen[id(sem)] = val
        self.nwait += 1


class FW:
    def __init__(self, nc, stack, n_dma_sems=14):
        self.nc = nc
        self.stack = stack
        self.pe = Eng(self, nc.tensor, "pe", is_pe=True)
        self.act = Eng(self, nc.scalar, "act")
        self.dve = Eng(self, nc.vector, "dve")
        self.pool = Eng(self, nc.gpsimd, "pool")
        self.sp = Eng(self, nc.sync, "sp")
        self.dq = {}
        for q in (self.sp, self.pool, self.act):
            self.dq[q.name] = dict(sems=[self.new_sem(f"d_{q.name}{i}") for i in range(n_dma_sems)],
                                   vals=[0] * n_dma_sems, idx=0)
        self.uid = 0
        self.stopped = False
        self.cc_sem = self.new_sem("cc")
        self.cc_cnt = 0

    def new_sem(self, name):
        return self.stack.enter_context(self.nc.semaphore(name))

    def sb(self, shape, dt, name, stack=None, multi=False):
        self.uid += 1
        t = (stack or self.stack).enter_context(self.nc.sbuf_tensor(f"{name}_{self.uid}", list(shape), dt))
        return Buf(t, name, multi)

    def ps(self, shape, dt, name, stack=None):
        self.uid += 1
        t = (stack or self.stack).enter_context(self.nc.psum_tensor(f"{name}_{self.uid}", list(shape), dt))
        b = Buf(t, name)
        b.excl = True
        return b

    def dram(self, name, shape, dt, multi=True):
        t = self.nc.dram_tensor(name, list(shape), dt, kind="Internal")
        return Buf(t.ap(), name, multi)

    def _deps(self, eng, reads, writes):
        evs = {}
        for b in reads:
            for ev in b.w.values():
                evs[(id(ev[0]), ev[1])] = ev
            if b.excl:
                for ev in b.r.values():
                    if ev[0] is not eng.sem:
                        evs[(id(ev[0]), ev[1])] = ev
        for b in writes:
            if not b.multi:
                for ev in b.w.values():
                    evs[(id(ev[0]), ev[1])] = ev
            for ev in b.r.values():
                evs[(id(ev[0]), ev[1])] = ev
        for ev in evs.values():
            if eng.is_pe and ev[0] is eng.sem:
                continue
            eng.wait(ev)

    def _record(self, ev, reads, writes):
        for b in reads:
            _merge(b.r, ev)
        for b in writes:
            if b.multi:
                _merge(b.w, ev)
            else:
                b.w = {id(ev[0]): ev}
                b.r = {}

    def op(self, eng, fn, reads=(), writes=()):
        if self.stopped:
            return None
        self._deps(eng, reads, writes)
        ins = fn(eng.e)
        eng.cnt += 1
        eng.nins += 1
        ins.then_inc(eng.sem, 1)
        self._record((eng.sem, eng.cnt), reads, writes)
        return ins

    def dma(self, q, out, in_, reads=(), writes=(), **kw):
        if self.stopped:
            return None
        self._deps(q, reads, writes)
        d = self.dq[q.name]
        i = d["idx"] % len(d["sems"])
        d["idx"] += 1
        if d["vals"][i] > 0:
            q.wait((d["sems"][i], d["vals"][i]))
        ins = q.e.dma_start(out=out, in_=in_, **kw)
        d["vals"][i] += 16
        ins.then_inc(d["sems"][i], 16)
        q.nins += 1
        self._record((d["sems"][i], d["vals"][i]), reads, writes)
        return ins

    def barrier(self):
        if self.stopped:
            return
        engs = (self.pe, self.act, self.dve, self.pool, self.sp)
        evs = [(e.sem, e.cnt) for e in engs if e.cnt > 0]
        for d in self.dq.values():
            for sm, v in zip(d["sems"], d["vals"]):
                if v > 0:
                    evs.append((sm, v))
        if self.cc_cnt > 0:
            evs.append((self.cc_sem, self.cc_cnt))
        for e in engs:
            for ev in evs:
                if ev[0] is e.sem:
                    continue
                e.wait(ev)

    def finish(self, bufs):
        for b in bufs:
            for ev in list(b.w.values()):
                self.sp.wait(ev)


class Ring:
    def __init__(self, bufs):
        self.bufs = bufs
        self.i = 0

    def next(self):
        b = self.bufs[self.i % len(self.bufs)]
        self.i += 1
        return b


class _Stop(Exception):
    pass


def build_program(NLAT=8192, depth=2, dbg=False, stop_at=None):
    NT_L = NLAT // 128
    NT_C = NCTX // 128
    NT = NT_L + NT_C
    NTOK = NLAT + NCTX
    R = NLAT // 64
    assert R <= 128 and R % 16 == 0
    HALF = NLAT // 2
    NT_O = HALF // 128
    QB = 512 if HALF % 512 == 0 else 128
    QS_CTX = HALF
    QS_EXT = HALF + NCTX
    NQS = HALF + NCTX + 128

    nc = bass.Bass("TRN2", target_bir_lowering=False)

    def din(name, shape, dt=F32):
        return Buf(nc.dram_tensor(name, list(shape), dt, kind="ExternalInput").ap(), name)

    x_in = din("x", [NLAT, D])
    ctx_in = din("ctx", [NCTX, D])
    cvec = din("cvec", [128, 8, 2])
    w_ada = din("w_ada", [depth, D, 6 * D])
    b_ada2 = din("b_ada2", [depth, 2, 6 * D])
    gmix2 = din("gmix2", [depth, 2, D])
    gffn2 = din("gffn2", [depth, 2, D])
    gfin = din("gfin", [128, D])
    w_in = din("w_in", [depth, D, 800])
    w_four = din("w_four", [depth, 256, 256])
    b_four = din("b_four", [depth, 128, 2, 256])
    gq = din("gq", [depth, 128, 3])
    w_qb = din("w_qb", [depth, 384, 1152])
    gkv = din("gkv", [depth, 128, 1])
    w_kvb = din("w_kvb", [depth, 128, 1536])
    w_out = din("w_out", [depth, D, D])
    w_up = din("w_up", [depth, D, 2 * DFF])
    wdw = din("wdw", [depth, 128, 2 * NCH, 3])
    bdw = din("bdw", [depth, 128, 2 * NCH])
    w_down = din("w_down", [depth, DFF, D])
    ident_d = din("ident", [128, 128], BF16)
    sel_d = din("sel", [2, 256])
    c4s4_d = din("c4s4", [128, 2, 2, 256], BF16)
    crs_d = din("crs", [R, 3, R], BF16)
    tw_d = din("tw", [R, 2, 64])
    c64_d = din("c64", [64, 2, 64], BF16)
    c256_d = din("c256", [128, 2, 2, 256], BF16)
    rope_d = din("rope", [128, NT_L + 1, 2, 32])
    hm_d = din("hm", [128, 4])
    out_d = Buf(nc.dram_tensor("out", [HALF, D], F32, kind="ExternalOutput").ap(), "out", multi=True)
    dbg_d = {}

    with ExitStack() as st:
        fw = FW(nc, st)
        pe, act, dve, pool, sp = fw.pe, fw.act, fw.dve, fw.pool, fw.sp

        XOWN = fw.dram("xown", [HALF, D], F32)
        NCK = HALF // 256
        XAG = [fw.dram(f"xag{j}", [512, D], F32) for j in range(NCK)]
        XOTH = fw.dram("xoth", [HALF, D], F32)
        XRES = {t: Buf(XOWN.t[t * 128:(t + 1) * 128, :], f"xres{t}") for t in range(NT_O)}
        for tc_ in range(NT_C):
            XRES[NT_L + tc_] = fw.dram(f"xctx{tc_}", [128, D], F32, multi=False)
        XHTe = fw.dram("xhte", [8, 128, 128], BF16)
        MB = fw.dram("mb", [2, 6, 128, D], F32)
        ABd = fw.dram("abd", [NTOK, 512], BF16)
        Y2d = fw.dram("y2d", [R, 64, 2, 256], BF16)
        FOURd = fw.dram("fourd", [NTOK, 256], BF16)
        QTd = fw.dram("qtd", [12, 96, NQS], BF16)
        ATTd = fw.dram("attd", [NQS, 768], BF16)
        XHTl = fw.dram("xhtl", [8, 128, HALF + 2], BF16)
        XHTc = fw.dram("xhtc", [8, 128, NCTX + 2], BF16)

        ident = fw.sb([128, 128], BF16, "ident")
        sel = fw.sb([2, 256], F32, "sel")
        zero_t = fw.sb([128, 8, 2], BF16, "zero")
        fw.dma(sp, ident[:], ident_d[:], reads=[ident_d], writes=[ident])
        fw.dma(sp, sel[:], sel_d[:], reads=[sel_d], writes=[sel])
        hm = fw.sb([128, 4], F32, "hm")
        fw.dma(sp, hm[:], hm_d[:], reads=[hm_d], writes=[hm])
        TD = []
        for t in range(NT_L):
            TD.append(dict(kind="lat", s=0, pieces=[(0, t * 128, 128)], kvcol=t * 128, q=(t < NT_O), qslot=t * 128, ropei=t, xres=(t if t < NT_O else None)))
        for tc_ in range(NT_C):
            TD.append(dict(kind="ctx", s=1, pieces=[(0, tc_ * 128, 128)], kvcol=NLAT + tc_ * 128, q=True, qslot=QS_CTX + tc_ * 128, ropei=None, xres=NT_L + tc_))
        TD.append(dict(kind="ext", s=0, pieces=[(0, HALF, 64), (64, NLAT - 64, 64)], kvcol=None, q=True, qslot=QS_EXT, ropei=NT_L, xres=None))
        fw.op(dve, lambda e: e.memset(zero_t[:], 0.0), writes=[zero_t])
        cast_rr = [0]

        def cast_eng():
            cast_rr[0] += 1
            return (dve, pool)[cast_rr[0] % 2]

        def rstd_from_ss(stt, n_cols, inv_n, ncol=1):
            fw.op(dve, lambda e: e.tensor_scalar(out=stt[:, 2:2 + ncol], in0=stt[:, 0:ncol], scalar1=inv_n,
                                                 scalar2=EPS, op0=ALU.mult, op1=ALU.add), reads=[stt], writes=[stt])
            fw.op(act, lambda e: e.sqrt(out=stt[:, 2:2 + ncol], in_=stt[:, 2:2 + ncol]), reads=[stt], writes=[stt])
            fw.op(dve, lambda e: e.reciprocal(out=stt[:, 4:4 + ncol], in_=stt[:, 2:2 + ncol]), reads=[stt], writes=[stt])

        def chk(name):
            if stop_at == name:
                fw.stopped = True

        try:
            for l in range(depth):
                last = (l == depth - 1)
                src_tiles = None if l == 0 else XRES

                def load_rows(xt, td):
                    for (p0, r0, n) in td["pieces"]:
                        if td["kind"] == "ctx":
                            if l == 0:
                                sb_, sap = ctx_in, ctx_in[r0:r0 + n, :]
                            else:
                                sb_ = XRES[td["xres"]]
                                sap = sb_[:, :]
                        elif l == 0:
                            sb_, sap = x_in, x_in[r0:r0 + n, :]
                        elif r0 < HALF:
                            sb_ = XRES[r0 // 128]
                            sap = sb_[:, :]
                        else:
                            sb_, sap = XOTH, XOTH[r0 - HALF:r0 - HALF + n, :]
                        fw.dma(sp, xt[p0:p0 + n, :], sap, reads=[sb_], writes=[xt])

                fw.barrier()
                with ExitStack() as ph:
                    sil = fw.sb([128, 8, 2], F32, "sil", ph)
                    mrow = fw.sb([2, 6 * D], F32, "mrow", ph)
                    brow = fw.sb([2, 6 * D], F32, "brow", ph)
                    g2 = fw.sb([2, 2, D], F32, "g2", ph)
                    wst = Ring([fw.sb([128, 3072], F32, f"wst{i}", ph) for i in range(3)])
                    bst = Ring([fw.sb([128, D], F32, f"bst{i}", ph) for i in range(2)])
                    psm = [fw.ps([128, 512], F32, f"psm{i}", ph) for i in range(8)]
                    fw.dma(sp, sil[:], cvec[:], reads=[cvec], writes=[sil])
                    fw.dma(sp, brow[:], b_ada2[l], reads=[b_ada2], writes=[brow])
                    fw.dma(sp, g2[:, 0, :], gmix2[l], reads=[gmix2], writes=[g2])
                    fw.dma(sp, g2[:, 1, :], gffn2[l], reads=[gffn2], writes=[g2])
                    fw.op(act, lambda e: e.activation(out=sil[:], in_=sil[:], func=ACTF.Silu), reads=[sil], writes=[sil])
                    for half in range(2):
                        for k in range(8):
                            wt = wst.next()
                            fw.dma(sp, wt[:], w_ada[l, k * 128:(k + 1) * 128, half * 3072:(half + 1) * 3072],
                                   reads=[w_ada], writes=[wt])
                            for j in range(6):
                                fw.op(pe, lambda e: e.matmul(psm[j][0:2, :], lhsT=sil[:, k, :], rhs=wt[:, j * 512:(j + 1) * 512],
                                                             start=(k == 0), stop=(k == 7)), reads=[sil, wt], writes=[psm[j]])
                        for j in range(6):
                            c0 = half * 3072 + j * 512
                            fw.op(dve, lambda e: e.tensor_tensor(out=mrow[:, c0:c0 + 512], in0=psm[j][0:2, :],
                                                                 in1=brow[:, c0:c0 + 512], op=ALU.add),
                                  reads=[psm[j], brow], writes=[mrow])
                    for (c0, gi) in ((1 * D, 0), (4 * D, 1)):
                        fw.op(dve, lambda e: e.scalar_tensor_tensor(out=mrow[:, c0:c0 + D], in0=mrow[:, c0:c0 + D], scalar=1.0,
                                                                    in1=g2[:, gi, :], op0=ALU.add, op1=ALU.mult),
                              reads=[mrow, g2], writes=[mrow])
                    pi = 0
                    for s in range(2):
                        for v in range(6):
                            bt = bst.next()
                            for hh in range(2):
                                p_ = psm[pi % 8]
                                pi += 1
                                fw.op(pe, lambda e: e.matmul(p_[:, :], lhsT=sel[:, s * 128:(s + 1) * 128],
                                                             rhs=mrow[:, v * D + hh * 512: v * D + (hh + 1) * 512],
                                                             start=True, stop=True), reads=[sel, mrow], writes=[p_])
                                fw.op(act if hh else dve, lambda e: e.tensor_copy(out=bt[:, hh * 512:(hh + 1) * 512], in_=p_[:, :])
                                      if not hh else e.copy(out=bt[:, hh * 512:(hh + 1) * 512], in_=p_[:, :]),
                                      reads=[p_], writes=[bt])
                            fw.dma(pool, MB[s, v], bt[:], reads=[bt], writes=[MB])

                fw.barrier()
                chk("M%d" % l)
                with ExitStack() as mx:
                    WUPd = fw.dram(f"wupd{l}", [2 * NCH, 128, 8, 128], BF16)
                    WDNd = fw.dram(f"wdnd{l}", [NCH, 128, D], BF16)
                    WOUTd = fw.dram(f"woutd{l}", [8, 128, D], BF16)
                    ckvnT = fw.sb([128, NTOK], BF16, "ckvnT", mx, multi=True)
                    KT = [fw.sb([96, NTOK], BF16, f"KT{i}", mx, multi=True) for i in range(2)]
                    wkvb = fw.sb([128, 1536], BF16, "wkvb", mx)

                    with ExitStack() as ph:
                        winb = fw.sb([128, 8, 800], BF16, "winb", ph, multi=True)
                        wab = fw.sb([128, 8, 512], BF16, "wab", ph, multi=True)
                        wqb = fw.sb([128, 3, 1152], BF16, "wqb", ph, multi=True)
                        gqt = fw.sb([128, 3], F32, "gqt", ph)
                        gkvt = fw.sb([128, 1], F32, "gkvt", ph)
                        GS = [[fw.sb([128, D], F32, f"gs{s}{v}", ph) for v in range(2)] for s in range(2)]
                        psb = [fw.ps([128, 1024], BF16, f"psb{i}", ph) for i in range(2)]
                        psf = [fw.ps([128, 512], F32, f"psf{i}", ph) for i in range(6)]
                        prep = ExitStack()
                        stg = Ring([fw.sb([128, 1536], F32, f"stg{i}", prep) for i in range(2)])
                        wfb = fw.sb([128, 2, 256], BF16, "wfb", prep, multi=True)
                        mab = fw.sb([128, 2, 512], BF16, "mab", prep, multi=True)
                        wfT = fw.sb([128, 2, D], BF16, "wfT", prep, multi=True)
                        c4s4 = fw.sb([128, 2, 2, 256], BF16, "c4s4", prep)

                        fw.dma(sp, c4s4[:], c4s4_d[:], reads=[c4s4_d], writes=[c4s4])
                        fw.dma(sp, gqt[:], gq[l], reads=[gq], writes=[gqt])
                        fw.dma(sp, gkvt[:], gkv[l], reads=[gkv], writes=[gkvt])
                        for s in range(2):
                            fw.dma(sp, GS[s][0][:], MB[s, 1], reads=[MB], writes=[GS[s][0]])
                            fw.dma(sp, GS[s][1][:], MB[s, 0], reads=[MB], writes=[GS[s][1]])
                        for k in range(8):
                            sg_ = stg.next()
                            fw.dma(sp, sg_[:, 0:800], w_in[l, k * 128:(k + 1) * 128, :], reads=[w_in], writes=[sg_])
                            fw.op(cast_eng(), lambda e: e.tensor_copy(out=winb[:, k, :], in_=sg_[:, 0:800]), reads=[sg_], writes=[winb])
                        for kc in range(3):
                            sg_ = stg.next()
                            fw.dma(sp, sg_[:, 0:1152], w_qb[l, kc * 128:(kc + 1) * 128, :], reads=[w_qb], writes=[sg_])
                            fw.op(cast_eng(), lambda e: e.tensor_scalar(out=wqb[:, kc, :], in0=sg_[:, 0:1152], scalar1=gqt[:, kc:kc + 1],
                                                                        scalar2=None, op0=ALU.mult), reads=[sg_, gqt], writes=[wqb])
                        sg_ = stg.next()
                        fw.dma(sp, sg_[:, 0:1536], w_kvb[l], reads=[w_kvb], writes=[sg_])
                        fw.op(cast_eng(), lambda e: e.tensor_scalar(out=wkvb[:], in0=sg_[:, 0:1536], scalar1=gkvt[:, 0:1],
                                                                    scalar2=None, op0=ALU.mult), reads=[sg_, gkvt], writes=[wkvb])
                        for cc in range(2):
                            sg_ = stg.next()
                            fw.dma(sp, sg_[:, 0:256], w_four[l, cc * 128:(cc + 1) * 128, :], reads=[w_four], writes=[sg_])
                            fw.op(cast_eng(), lambda e: e.tensor_copy(out=wfb[:, cc, :], in_=sg_[:, 0:256]), reads=[sg_], writes=[wfb])
                        for cc in range(2):
                            p_ = psf[cc]
                            for X in range(2):
                                for c2 in range(2):
                                    fw.op(pe, lambda e: e.matmul(p_[:, X * 256:(X + 1) * 256],
                                                                 lhsT=c4s4[:, c2, X, cc * 128:(cc + 1) * 128], rhs=wfb[:, c2, :],
                                                                 start=(c2 == 0), stop=(c2 == 1)), reads=[c4s4, wfb], writes=[p_])
                            fw.op(dve, lambda e: e.tensor_copy(out=mab[:, cc, 0:256], in_=p_[:, 0:256]), reads=[p_], writes=[mab])
                            fw.op(dve, lambda e: e.tensor_scalar(out=mab[:, cc, 256:512], in0=p_[:, 256:512], scalar1=-1.0,
                                                                 scalar2=None, op0=ALU.mult), reads=[p_], writes=[mab])
                        for cc in range(2):
                            for k in range(8):
                                fw.op(pe, lambda e: e.transpose(out=psb[cc][:, k * 128:(k + 1) * 128],
                                                                in_=winb[:, k, cc * 128:(cc + 1) * 128], identity=ident[:]),
                                      reads=[winb, ident], writes=[psb[cc]])
                            fw.op(dve, lambda e: e.tensor_copy(out=wfT[:, cc, :], in_=psb[cc][:, :]), reads=[psb[cc]], writes=[wfT])
                        for k in range(8):
                            p_ = psf[k % 5]
                            for cc in range(2):
                                fw.op(pe, lambda e: e.matmul(p_[:, :], lhsT=wfT[:, cc, k * 128:(k + 1) * 128], rhs=mab[:, cc, :],
                                                             start=(cc == 0), stop=(cc == 1)), reads=[wfT, mab], writes=[p_])
                            fw.op(cast_eng() if False else dve, lambda e: e.tensor_copy(out=wab[:, k, :], in_=p_[:, :]),
                                  reads=[p_], writes=[wab])

                        chk('INW%d' % l)
                        fw.barrier()
                        prep.close()
                        G_IN = 3
                        QTv = QTd.t.rearrange("h d t -> d h t")
                        CX = []
                        for g in range(G_IN):
                            c_ = dict(
                                xt=fw.sb([128, D], F32, f"xt{g}", ph), ckv32=fw.sb([128, 544], F32, f"ckv32{g}", ph),
                                xb=fw.sb([128, D], BF16, f"xb{g}", ph), hT=fw.sb([128, 8, 128], BF16, f"hT{g}", ph),
                                s1=fw.sb([128, 8], F32, f"s1{g}", ph), s2=fw.sb([128, 8], F32, f"s2{g}", ph),
                                ab=fw.sb([128, 512], BF16, f"ab{g}", ph), cn=fw.sb([128, 512], BF16, f"cn{g}", ph),
                                cn2=fw.sb([128, 96], BF16, f"cn2{g}", ph), rt=fw.sb([128, 2, 32], F32, f"rt{g}", ph),
                                cq=fw.sb([128, 3, 128], BF16, f"cq{g}", ph), q=fw.sb([128, 12, 96], F32, f"q{g}", ph),
                                qt=fw.sb([128, 2, 12, 32], F32, f"qt{g}", ph), qb=fw.sb([128, 12, 96], BF16, f"qb{g}", ph),
                                qT=fw.sb([96, 12, 128], BF16, f"qT{g}", ph), rp=fw.sb([128, 2, 32], F32, f"rp{g}", ph),
                                junk=fw.sb([128, D], BF16, f"junk{g}", ph),
                            )
                            CX.append(c_)
                            fw.op(dve, lambda e: e.memset(c_["cn2"][:], 0.0), writes=[c_["cn2"]])
                        pbr = Ring([psb[0], psb[1]])
                        pfr = Ring([[psf[0], psf[1], psf[2]], [psf[3], psf[4], psf[5]]])

                        def tile_gen(ti, td, cx, g):
                            s = td["s"]
                            kvc = td["kvcol"]
                            do_kv = kvc is not None
                            do_q = td["q"] and not (td["kind"] == "ctx" and last)
                            xt, xb_, hT_, s_, s2, ab_, cn_, c2_, rt = (cx[k] for k in ("xt", "xb", "hT", "s1", "s2", "ab", "cn", "cn2", "rt"))
                            cq_, q_, qt_, qb_, qT_, ropet, c32 = (cx[k] for k in ("cq", "q", "qt", "qb", "qT", "rp", "ckv32"))
                            xh_ = Buf(q_.t[:].rearrange("p h d -> p (h d)")[:, 0:D], "xh_alias")
                            junk = cx["junk"]
                            load_rows(xt, td)
                            if td["ropei"] is not None:
                                fw.dma(sp, ropet[:], rope_d[:, td["ropei"], :, :], reads=[rope_d], writes=[ropet])
                            yield
                            fw.op(dve, lambda e: e.memset(s_[:], 0.0), writes=[s_])
                            fw.op(act, lambda e: e.activation(out=junk[:], in_=xt[:], func=ACTF.Square, accum_out=s_[:, 0:1]),
                                  reads=[xt], writes=[junk, s_])
                            yield
                            fw.op(dve, lambda e: e.tensor_scalar(out=s_[:, 2:3], in0=s_[:, 0:1], scalar1=1.0 / D, scalar2=EPS, op0=ALU.mult, op1=ALU.add),
                                  reads=[s_], writes=[s_])
                            yield
                            fw.op(act, lambda e: e.sqrt(out=s_[:, 2:3], in_=s_[:, 2:3]), reads=[s_], writes=[s_])
                            yield
                            fw.op(dve, lambda e: e.reciprocal(out=s_[:, 4:5], in_=s_[:, 2:3]), reads=[s_], writes=[s_])
                            yield
                            fw.op(dve, lambda e: e.scalar_tensor_tensor(out=xh_[:, :], in0=xt[:], scalar=s_[:, 4:5], in1=GS[s][0][:],
                                                                        op0=ALU.mult, op1=ALU.mult), reads=[xt, s_, GS[s][0]], writes=[q_])
                            yield
                            fw.op(dve, lambda e: e.tensor_tensor(out=xb_[:], in0=xh_[:, :], in1=GS[s][1][:], op=ALU.add),
                                  reads=[q_, GS[s][1]], writes=[xb_])
                            yield
                            PB = pbr.next()
                            for k in range(8):
                                fw.op(pe, lambda e: e.transpose(out=PB[:, k * 128:(k + 1) * 128], in_=xb_[:, k * 128:(k + 1) * 128],
                                                                identity=ident[:]), reads=[xb_, ident], writes=[PB])
                            fw.op(act, lambda e: e.copy(out=hT_[:].rearrange("p k t -> p (k t)"), in_=PB[:, :]), reads=[PB], writes=[hT_])
                            yield
                            F0, F1, F2 = pfr.next()
                            if do_kv:
                                for k in range(8):
                                    fw.op(pe, lambda e: e.matmul(F0[:, 0:512], lhsT=hT_[:, k, :], rhs=wab[:, k, :], start=(k == 0), stop=(k == 7)),
                                          reads=[hT_, wab], writes=[F0])
                            for k in range(8):
                                fw.op(pe, lambda e: e.matmul(F1[:, 0:512], lhsT=hT_[:, k, :], rhs=winb[:, k, 256:768], start=(k == 0), stop=(k == 7)),
                                      reads=[hT_, winb], writes=[F1])
                            for k in range(8):
                                fw.op(pe, lambda e: e.matmul(F2[:, 0:32], lhsT=hT_[:, k, :], rhs=winb[:, k, 768:800], start=(k == 0), stop=(k == 7)),
                                      reads=[hT_, winb], writes=[F2])
                            if do_kv:
                                fw.op(act, lambda e: e.copy(out=ab_[:], in_=F0[:, 0:512]), reads=[F0], writes=[ab_])
                            fw.op(dve, lambda e: e.tensor_copy(out=c32[:, 0:512], in_=F1[:, 0:512]), reads=[F1], writes=[c32])
                            fw.op(dve, lambda e: e.tensor_copy(out=c32[:, 512:544], in_=F2[:, 0:32]), reads=[F2], writes=[c32])
                            yield
                            if do_kv:
                                fw.dma(sp, ABd[kvc:kvc + 128, :], ab_[:], reads=[ab_], writes=[ABd])
                            fw.op(dve, lambda e: e.memset(s2[:], 0.0), writes=[s2])
                            yield
                            fw.op(act, lambda e: e.activation(out=junk[:, 0:384], in_=c32[:, 0:384], func=ACTF.Square, accum_out=s2[:, 0:1]),
                                  reads=[c32], writes=[junk, s2])
                            fw.op(act, lambda e: e.activation(out=junk[:, 384:512], in_=c32[:, 384:512], func=ACTF.Square, accum_out=s2[:, 1:2]),
                                  reads=[c32], writes=[junk, s2])
                            yield
                            fw.op(dve, lambda e: e.tensor_scalar(out=s2[:, 0:1], in0=s2[:, 0:1], scalar1=128.0 / 384.0, scalar2=None, op0=ALU.mult),
                                  reads=[s2], writes=[s2])
                            yield
                            fw.op(dve, lambda e: e.tensor_scalar(out=s2[:, 2:4], in0=s2[:, 0:2], scalar1=1.0 / 128, scalar2=EPS, op0=ALU.mult, op1=ALU.add),
                                  reads=[s2], writes=[s2])
                            yield
                            fw.op(act, lambda e: e.sqrt(out=s2[:, 2:4], in_=s2[:, 2:4]), reads=[s2], writes=[s2])
                            yield
                            fw.op(dve, lambda e: e.reciprocal(out=s2[:, 4:6], in_=s2[:, 2:4]), reads=[s2], writes=[s2])
                            yield
                            fw.op(dve, lambda e: e.tensor_scalar(out=cn_[:, 0:384], in0=c32[:, 0:384], scalar1=s2[:, 4:5], scalar2=None, op0=ALU.mult),
                                  reads=[c32, s2], writes=[cn_])
                            fw.op(pool, lambda e: e.tensor_scalar(out=cn_[:, 384:512], in0=c32[:, 384:512], scalar1=s2[:, 5:6], scalar2=None, op0=ALU.mult),
                                  reads=[c32, s2], writes=[cn_])
                            if do_kv:
                                if s == 0:
                                    fw.op(dve, lambda e: e.tensor_tensor(out=rt[:, 0, :], in0=c32[:, 512:544], in1=ropet[:, 0, :], op=ALU.mult),
                                          reads=[c32, ropet], writes=[rt])
                                    fw.op(pool, lambda e: e.tensor_tensor(out=rt[:, 1, :], in0=c32[:, 512:544], in1=ropet[:, 1, :], op=ALU.mult),
                                          reads=[c32, ropet], writes=[rt])
                                    yield
                                    fw.op(pool, lambda e: e.tensor_tensor(out=c2_[:, 64:80], in0=rt[:, 0, 0:16], in1=rt[:, 1, 16:32], op=ALU.subtract),
                                          reads=[rt], writes=[c2_])
                                    fw.op(pool, lambda e: e.tensor_tensor(out=c2_[:, 80:96], in0=rt[:, 0, 16:32], in1=rt[:, 1, 0:16], op=ALU.add),
                                          reads=[rt], writes=[c2_])
                                else:
                                    fw.op(dve, lambda e: e.tensor_copy(out=c2_[:, 64:96], in_=c32[:, 512:544]), reads=[c32], writes=[c2_])
                            yield
                            PB = pbr.next()
                            for j in range(4 if do_kv else 3):
                                fw.op(pe, lambda e: e.transpose(out=PB[:, j * 128:(j + 1) * 128], in_=cn_[:, j * 128:(j + 1) * 128],
                                                                identity=ident[:]), reads=[cn_, ident], writes=[PB])
                            if do_kv:
                                fw.op(pe, lambda e: e.transpose(out=PB[0:96, 512:640], in_=c2_[:, 0:96], identity=ident[:]),
                                      reads=[c2_, ident], writes=[PB])
                            fw.op(act, lambda e: e.copy(out=cq_[:].rearrange("p k t -> p (k t)"), in_=PB[:, 0:384]), reads=[PB], writes=[cq_])
                            if do_kv:
                                fw.op(dve, lambda e: e.tensor_copy(out=ckvnT[:, kvc:kvc + 128], in_=PB[:, 384:512]), reads=[PB], writes=[ckvnT])
                                fw.op(act, lambda e: e.copy(out=KT[0][64:96, kvc:kvc + 128], in_=PB[64:96, 512:640]), reads=[PB], writes=[KT[0]])
                                fw.op(dve, lambda e: e.tensor_copy(out=KT[1][64:96, kvc:kvc + 128], in_=PB[64:96, 512:640]), reads=[PB], writes=[KT[1]])
                            yield
                            if not do_q:
                                return
                            F0, F1, F2 = pfr.next()
                            for (pf_, c0, c1) in ((F0, 0, 512), (F1, 512, 1024), (F2, 1024, 1152)):
                                for kc in range(3):
                                    fw.op(pe, lambda e: e.matmul(pf_[:, 0:c1 - c0], lhsT=cq_[:, kc, :], rhs=wqb[:, kc, c0:c1],
                                                                 start=(kc == 0), stop=(kc == 2)), reads=[cq_, wqb], writes=[pf_])
                            qf = q_[:].rearrange("p h d -> p (h d)")
                            fw.op(act, lambda e: e.copy(out=qf[:, 0:512], in_=F0[:, 0:512]), reads=[F0], writes=[q_])
                            fw.op(dve, lambda e: e.tensor_copy(out=qf[:, 512:1024], in_=F1[:, 0:512]), reads=[F1], writes=[q_])
                            fw.op(act, lambda e: e.copy(out=qf[:, 1024:1152], in_=F2[:, 0:128]), reads=[F2], writes=[q_])
                            yield
                            fw.op(pool, lambda e: e.tensor_copy(out=qb_[:, :, 0:64], in_=q_[:, :, 0:64]), reads=[q_], writes=[qb_])
                            if s == 0:
                                def bc(ap_):
                                    return bass.AP(ap_.tensor, ap_.offset, [list(ap_.ap[0]), [0, 12], list(ap_.ap[-1])])
                                fw.op(dve, lambda e: e.tensor_tensor(out=qt_[:, 0, :, :], in0=q_[:, :, 64:96], in1=bc(ropet[:, 0, :]), op=ALU.mult),
                                      reads=[q_, ropet], writes=[qt_])
                                fw.op(pool, lambda e: e.tensor_tensor(out=qt_[:, 1, :, :], in0=q_[:, :, 64:96], in1=bc(ropet[:, 1, :]), op=ALU.mult),
                                      reads=[q_, ropet], writes=[qt_])
                                yield
                                fw.op(dve, lambda e: e.tensor_tensor(out=qb_[:, :, 64:80], in0=qt_[:, 0, :, 0:16], in1=qt_[:, 1, :, 16:32], op=ALU.subtract),
                                      reads=[qt_], writes=[qb_])
                                fw.op(pool, lambda e: e.tensor_tensor(out=qb_[:, :, 80:96], in0=qt_[:, 0, :, 16:32], in1=qt_[:, 1, :, 0:16], op=ALU.add),
                                      reads=[qt_], writes=[qb_])
                            else:
                                fw.op(dve, lambda e: e.tensor_copy(out=qb_[:, :, 64:96], in_=q_[:, :, 64:96]), reads=[q_], writes=[qb_])
                            yield
                            for rnd in range(2):
                                PB = pbr.next()
                                for h6 in range(6):
                                    h = rnd * 6 + h6
                                    fw.op(pe, lambda e: e.transpose(out=PB[0:96, h6 * 128:(h6 + 1) * 128], in_=qb_[:, h, :], identity=ident[:]),
                                          reads=[qb_, ident], writes=[PB])
                                fw.op(act if rnd else dve, (lambda e: e.copy(out=qT_[:, 6:12, :].rearrange("p h t -> p (h t)"), in_=PB[0:96, 0:768])) if rnd else
                                      (lambda e: e.tensor_copy(out=qT_[:, 0:6, :].rearrange("p h t -> p (h t)"), in_=PB[0:96, 0:768])),
                                      reads=[PB], writes=[qT_])
                                yield
                            fw.dma(sp, QTv[:, :, td["qslot"]:td["qslot"] + 128], qT_[:], reads=[qT_], writes=[QTd])

                        tds = list(TD)
                        active = []
                        nxt_i = [0]

                        free_cx = list(range(G_IN))

                        def refill():
                            while free_cx and nxt_i[0] < len(tds):
                                i_ = nxt_i[0]
                                nxt_i[0] += 1
                                g_ = free_cx.pop(0)
                                active.append((tile_gen(i_, tds[i_], CX[g_], g_), g_))

                        refill()
                        while active:
                            for item in list(active):
                                try:
                                    next(item[0])
                                except StopIteration:
                                    active.remove(item)
                                    free_cx.append(item[1])
                                    refill()

                    fw.barrier()
                    chk("IN%d" % l)
                    with ExitStack() as ph:
                        crs = fw.sb([R, 3, R], BF16, "crs", ph)
                        tw = fw.sb([R, 2, 64], F32, "tw", ph)
                        c64 = fw.sb([64, 2, 64], BF16, "c64", ph)
                        bft = fw.sb([128, 2, 256], F32, "bft", ph)
                        fw.dma(sp, crs[:], crs_d[:], reads=[crs_d], writes=[crs])
                        fw.dma(sp, tw[:], tw_d[:], reads=[tw_d], writes=[tw])
                        fw.dma(sp, c64[:], c64_d[:], reads=[c64_d], writes=[c64])
                        fw.dma(sp, bft[:], b_four[l], reads=[b_four], writes=[bft])
                        abg = Ring([fw.sb([R, 16, 512], BF16, f"abg{i}", ph) for i in range(2)])
                        y2g = Ring([fw.sb([R, 16, 2, 256], BF16, f"y2g{i}", ph) for i in range(2)])
                        y2t = Ring([fw.sb([64, 16, 2, 256], BF16, f"y2t{i}", ph) for i in range(2)])
                        fog = Ring([fw.sb([64, 16, 256], BF16, f"fog{i}", ph) for i in range(2)])
                        tt = Ring([fw.sb([R, 256], F32, f"tt{i}", ph) for i in range(4)])
                        psy = Ring([fw.ps([128, 512], F32, f"psy{i}", ph) for i in range(8)])
                        ABv = ABd.t[0:NLAT, :].rearrange("(a b) c -> a b c", b=64)
                        for g in range(4):
                            ab_ = abg.next()
                            fw.dma(sp, ab_[:], ABv[:, g * 16:(g + 1) * 16, :], reads=[ABd], writes=[ab_])
                            y2_ = y2g.next()
                            for pm in range(8):
                                pr = psy.next()
                                pi_ = psy.next()
                                a_ = ab_[:, 2 * pm:2 * pm + 2, 0:256]
                                b_ = ab_[:, 2 * pm:2 * pm + 2, 256:512]
                                prv = pr[0:R, :].rearrange("p (a b) -> p a b", b=256)
                                piv = pi_[0:R, :].rearrange("p (a b) -> p a b", b=256)
                                fw.op(pe, lambda e: e.matmul(prv, lhsT=crs[:, 0, :], rhs=a_, start=True, stop=False), reads=[crs, ab_], writes=[pr])
                                fw.op(pe, lambda e: e.matmul(prv, lhsT=crs[:, 1, :], rhs=b_, start=False, stop=True), reads=[crs, ab_], writes=[pr])
                                fw.op(pe, lambda e: e.matmul(piv, lhsT=crs[:, 0, :], rhs=b_, start=True, stop=False), reads=[crs, ab_], writes=[pi_])
                                fw.op(pe, lambda e: e.matmul(piv, lhsT=crs[:, 2, :], rhs=a_, start=False, stop=True), reads=[crs, ab_], writes=[pi_])
                                for j in range(2):
                                    ml = 2 * pm + j
                                    m2 = g * 16 + ml
                                    t1 = tt.next()
                                    t2 = tt.next()
                                    fw.op(dve, lambda e: e.tensor_scalar(out=t1[:], in0=pi_[0:R, j * 256:(j + 1) * 256], scalar1=tw[:, 1, m2:m2 + 1],
                                                                         scalar2=None, op0=ALU.mult), reads=[pi_, tw], writes=[t1])
                                    fw.op(dve, lambda e: e.scalar_tensor_tensor(out=y2_[:, ml, 0, :], in0=pr[0:R, j * 256:(j + 1) * 256],
                                                                                scalar=tw[:, 0, m2:m2 + 1], in1=t1[:], op0=ALU.mult, op1=ALU.add),
                                          reads=[pr, tw, t1], writes=[y2_])
                                    fw.op(dve, lambda e: e.tensor_scalar(out=t2[:], in0=pr[0:R, j * 256:(j + 1) * 256], scalar1=tw[:, 1, m2:m2 + 1],
                                                                         scalar2=None, op0=ALU.mult), reads=[pr, tw], writes=[t2])
                                    fw.op(dve, lambda e: e.scalar_tensor_tensor(out=y2_[:, ml, 1, :], in0=pi_[0:R, j * 256:(j + 1) * 256],
                                                                                scalar=tw[:, 0, m2:m2 + 1], in1=t2[:], op0=ALU.mult, op1=ALU.subtract),
                                          reads=[pi_, tw, t2], writes=[y2_])
                            fw.dma(pool, Y2d[:, g * 16:(g + 1) * 16, :, :], y2_[:], reads=[y2_], writes=[Y2d])
                        Y2v = Y2d.t.rearrange("a b c d -> b a c d")
                        FOv = FOURd.t[0:NLAT, :].rearrange("(a b) c -> a b c", b=R)
                        for g in range(R // 16):
                            yt_ = y2t.next()
                            fw.dma(sp, yt_[:], Y2v[:, g * 16:(g + 1) * 16, :, :], reads=[Y2d], writes=[yt_])
                            fo_ = fog.next()
                            for p2 in range(8):
                                pz = psy.next()
                                pzv = pz[0:64, :].rearrange("p (a b) -> p a b", b=256)
                                fw.op(pe, lambda e: e.matmul(pzv, lhsT=c64[:, 0, :], rhs=yt_[:, 2 * p2:2 * p2 + 2, 0, :], start=True, stop=False),
                                      reads=[c64, yt_], writes=[pz])
                                fw.op(pe, lambda e: e.matmul(pzv, lhsT=c64[:, 1, :], rhs=yt_[:, 2 * p2:2 * p2 + 2, 1, :], start=False, stop=True),
                                      reads=[c64, yt_], writes=[pz])
                                fw.op(dve, lambda e: e.tensor_tensor(out=fo_[:, 2 * p2:2 * p2 + 2, :], in0=pzv, in1=bft[0:64, :, :], op=ALU.add),
                                      reads=[pz, bft], writes=[fo_])
                            fw.dma(pool, FOv[:, g * 16:(g + 1) * 16, :], fo_[:], reads=[fo_], writes=[FOURd])
                        if not last:
                            c256 = fw.sb([128, 2, 2, 256], BF16, "c256", ph)
                            abc = fw.sb([128, 2, 512], BF16, "abc", ph)
                            foc = fw.sb([128, 2, 256], BF16, "foc", ph)
                            fw.dma(sp, c256[:], c256_d[:], reads=[c256_d], writes=[c256])
                            fw.dma(sp, abc[:], ABd.t[NLAT:NTOK, :].rearrange("(a p) c -> p a c", p=128), reads=[ABd], writes=[abc])
                            for n_ in range(2):
                                pz = psy.next()
                                for mc in range(2):
                                    fw.op(pe, lambda e: e.matmul(pz[:, 0:256], lhsT=c256[:, mc, 0, n_ * 128:(n_ + 1) * 128], rhs=abc[:, mc, 0:256],
                                                                 start=(mc == 0), stop=False), reads=[c256, abc], writes=[pz])
                                for mc in range(2):
                                    fw.op(pe, lambda e: e.matmul(pz[:, 0:256], lhsT=c256[:, mc, 1, n_ * 128:(n_ + 1) * 128], rhs=abc[:, mc, 256:512],
                                                                 start=False, stop=(mc == 1)), reads=[c256, abc], writes=[pz])
                                fw.op(dve, lambda e: e.tensor_tensor(out=foc[:, n_, :], in0=pz[:, 0:256], in1=bft[:, 0, :], op=ALU.add),
                                      reads=[pz, bft], writes=[foc])
                            fw.dma(pool, FOURd.t[NLAT:NTOK, :].rearrange("(a p) c -> p a c", p=128), foc[:], reads=[foc], writes=[FOURd])

                    fw.barrier()
                    chk("F%d" % l)
                    with ExitStack() as ph:
                        Va = [fw.sb([128, NT, 65], BF16, f"Va{i}", ph, multi=True) for i in range(2)]
                        NQ = NQS
                        qts = [fw.sb([96, NQ], BF16, f"qts{i}", ph) for i in range(2)]
                        ptr = Ring([fw.sb([128, 2, QB], BF16, f"pt{i}", ph) for i in range(3)])
                        rcr = Ring([fw.sb([128, 4], F32, f"rc{i}", ph) for i in range(2)])
                        att = Ring([fw.sb([128, 4, 64], BF16, f"att{i}", ph) for i in range(3)])
                        pss = Ring([fw.ps([128, 1024], F32, f"pss{i}", ph) for i in range(2)])
                        pso = Ring([fw.ps([128, 512], F32, f"pso{i}", ph) for i in range(2)])
                        pkv = Ring([fw.ps([128, 512], F32, f"pkv{i}", ph) for i in range(2)])
                        for i in range(2):
                            fw.op(pool, lambda e: e.memset(Va[i][:, :, 64:65], 1.0), writes=[Va[i]])
                        pstg = Ring([fw.sb([128, 1408], F32, f"pstg{i}", ph) for i in range(2)])
                        pcst = Ring([fw.sb([128, 1408], BF16, f"pcst{i}", ph) for i in range(2)])
                        WUv = WUPd.t.rearrange("c p k n -> p c k n")
                        for k in range(8):
                            for cb in range(4):
                                sg_ = pstg.next()
                                fw.dma(pool, sg_[:], w_up[l, k * 128:(k + 1) * 128, cb * 1408:(cb + 1) * 1408], reads=[w_up], writes=[sg_])
                                ct = pcst.next()
                                fw.op(pool, lambda e: e.tensor_copy(out=ct[:], in_=sg_[:]), reads=[sg_], writes=[ct])
                                fw.dma(pool, WUv[:, cb * 11:(cb + 1) * 11, k, :], ct[:].rearrange("p (c n) -> p c n", n=128),
                                       reads=[ct], writes=[WUPd])
                        for (src_w, dst_w, n_) in ((w_down, WDNd, NCH), (w_out, WOUTd, 8)):
                            for i in range(n_):
                                sg_ = pstg.next()
                                fw.dma(pool, sg_[:, 0:D], src_w[l, i * 128:(i + 1) * 128, :], reads=[src_w], writes=[sg_])
                                ct = pcst.next()
                                fw.op(pool, lambda e: e.tensor_copy(out=ct[:, 0:D], in_=sg_[:, 0:D]), reads=[sg_], writes=[ct])
                                fw.dma(pool, dst_w[i], ct[:, 0:D], reads=[ct], writes=[dst_w])

                        def build_kv(h):
                            for _ in build_kv_gen(h):
                                pass

                        def build_kv_gen(h):
                            b = h % 2
                            fw.dma(sp, qts[b][:], QTd[h, :, 0:NQ], reads=[QTd], writes=[qts[b]])
                            yield
                            c0 = 0
                            while c0 < NTOK:
                                w_ = min(512, NTOK - c0)
                                p_ = pkv.next()
                                fw.op(pe, lambda e: e.matmul(p_[0:64, 0:w_], lhsT=wkvb[:, h * 128:h * 128 + 64], rhs=ckvnT[:, c0:c0 + w_],
                                                             start=True, stop=True), reads=[wkvb, ckvnT], writes=[p_])
                                fw.op(dve, lambda e: e.tensor_copy(out=KT[b][0:64, c0:c0 + w_], in_=p_[0:64, 0:w_]), reads=[p_], writes=[KT[b]])
                                c0 += w_
                                yield
                            c0 = 0
                            while c0 < NT:
                                n_ = min(8, NT - c0)
                                p_ = pkv.next()
                                for c in range(n_):
                                    fw.op(pe, lambda e: e.matmul(p_[:, c * 64:(c + 1) * 64], lhsT=ckvnT[:, (c0 + c) * 128:(c0 + c + 1) * 128],
                                                                 rhs=wkvb[:, h * 128 + 64:h * 128 + 128], start=True, stop=True),
                                          reads=[wkvb, ckvnT], writes=[p_])
                                fw.op(pool if False else dve, lambda e: e.tensor_copy(out=Va[b][:, c0:c0 + n_, 0:64],
                                                                                     in_=p_[:, 0:n_ * 64].rearrange("p (a b) -> p a b", b=64)),
                                      reads=[p_], writes=[Va[b]])
                                c0 += n_
                                yield

                        steps = []
                        for h in range(12):
                            blks = [(q0, QB, list(range(NT))) for q0 in range(0, HALF, QB)]
                            blks.append((QS_EXT, 128, list(range(NT))))
                            if not last:
                                blks.append((QS_CTX, NCTX, list(range(NT_L, NT))))
                            for bi, (q0, W, kcs) in enumerate(blks):
                                nk = len(kcs)
                                for i0 in range(0, nk, 2):
                                    steps.append(dict(h=h, q0=q0, W=W, grp=kcs[i0:i0 + 2], i0=i0, nk=nk, first=(i0 == 0),
                                                      last=(i0 + 2 >= nk), newhead=(bi == 0 and i0 == 0)))
                        cur = {}

                        kvg = {}

                        def emit_S(st_):
                            h, q0, W = st_["h"], st_["q0"], st_["W"]
                            b = h % 2
                            if st_["newhead"] and h in kvg:
                                for _ in kvg.pop(h):
                                    pass
                            p_ = pss.next()
                            for j, kc in enumerate(st_["grp"]):
                                fw.op(pe, lambda e: e.matmul(p_[:, j * 512:j * 512 + W], lhsT=KT[b][0:96, kc * 128:(kc + 1) * 128],
                                                             rhs=qts[b][0:96, q0:q0 + W], start=True, stop=True),
                                      reads=[KT[b], qts[b]], writes=[p_])
                            st_["p"] = p_

                        def emit_exp(st_):
                            W = st_["W"]
                            p_ = st_["p"]
                            pt = ptr.next()
                            ng = len(st_["grp"])
                            fw.op(act, lambda e: e.activation(out=pt[:, 0:ng, 0:W], in_=p_[:, 0:ng * 512].rearrange("p (a b) -> p a b", b=512)[:, :, 0:W],
                                                              func=ACTF.Exp, scale=SCALE), reads=[p_], writes=[pt])
                            st_["pt"] = pt

                        def emit_PV(st_):
                            h, q0, W, i0, nk = st_["h"], st_["q0"], st_["W"], st_["i0"], st_["nk"]
                            b = h % 2
                            nq = W // 128
                            pt = st_["pt"]
                            if st_["newhead"] and h + 1 < 12:
                                kvg[h + 1] = build_kv_gen(h + 1)
                            if (h + 1) in kvg:
                                if next(kvg[h + 1], "done") == "done":
                                    kvg.pop(h + 1)
                            if st_["first"]:
                                pob = pso.next()
                                po = Buf(pob.t[:, 0:260].rearrange("p (a b) -> p a b", b=65), "po")
                                cur["pob"], cur["po"] = pob, po
                            pob, po = cur["pob"], cur["po"]
                            for j, kc in enumerate(st_["grp"]):
                                for qi in range(nq):
                                    fw.op(pe, lambda e: e.matmul(po[:, qi, :], lhsT=pt[:, j, qi * 128:(qi + 1) * 128], rhs=Va[b][:, kc, :],
                                                                 start=(i0 + j == 0 and qi == 0), stop=(i0 + j == nk - 1 and qi == nq - 1)),
                                          reads=[pt, Va[b]], writes=[pob])
                            if st_["last"]:
                                rc = rcr.next()
                                fw.op(dve, lambda e: e.reciprocal(out=rc[:, 0:nq], in_=po[:, 0:nq, 64]), reads=[pob], writes=[rc])
                                at = att.next()
                                for qi in range(nq):
                                    fw.op(dve, lambda e: e.tensor_scalar(out=at[:, qi, :], in0=po[:, qi, 0:64], scalar1=rc[:, qi:qi + 1], scalar2=None,
                                                                         op0=ALU.mult), reads=[pob, rc], writes=[at])
                                fw.dma(sp, ATTd.t[q0:q0 + W, h * 64:(h + 1) * 64].rearrange("(a p) d -> p a d", p=128), at[:, 0:nq, :],
                                       reads=[at], writes=[ATTd])

                        build_kv(0)
                        emit_S(steps[0])
                        for i_, st_ in enumerate(steps):
                            emit_exp(st_)
                            if i_ + 1 < len(steps):
                                emit_S(steps[i_ + 1])
                            emit_PV(st_)

                fw.barrier()
                chk("ATT%d" % l)
                with ExitStack() as ph:
                    stg = Ring([fw.sb([128, 1024], F32, f"stg{i}", ph) for i in range(2)])
                    woutb = fw.sb([128, 8, D], BF16, "woutb", ph, multi=True)
                    GT = [[fw.sb([128, D], F32, f"gt{s}{v}", ph) for v in range(3)] for s in range(2)]
                    for s in range(2):
                        for v, mv in enumerate((2, 4, 3)):
                            fw.dma(sp, GT[s][v][:], MB[s, mv], reads=[MB], writes=[GT[s][v]])
                    fw.dma(sp, woutb[:], WOUTd.t.rearrange("k p n -> p k n"), reads=[WOUTd], writes=[woutb])
                    G_O = 2
                    XHlv = XHTl.t.rearrange("k p t -> p k t")
                    XHcv = XHTc.t.rearrange("k p t -> p k t")
                    XHev = XHTe.t.rearrange("k p t -> p k t")
                    otds = [td for td in TD if td["q"] and not (last and td["kind"] == "ctx")]
                    OCX = []
                    for g in range(G_O):
                        OCX.append(dict(
                            m=fw.sb([128, D], BF16, f"mix{g}", ph), mT=fw.sb([128, 8, 128], BF16, f"mixT{g}", ph),
                            xt=fw.sb([128, D], F32, f"xo{g}", ph), tm=fw.sb([128, D], F32, f"tmp{g}", ph),
                            xn=fw.sb([128, D], F32, f"xn{g}", ph), junk=fw.sb([128, D], BF16, f"junk2{g}", ph),
                            s=fw.sb([128, 8], F32, f"stt{g}", ph), xh=fw.sb([128, D], F32, f"xh{g}", ph),
                            xb=fw.sb([128, D], BF16, f"xb{g}", ph), hT=fw.sb([128, 8, 128], BF16, f"hT{g}", ph),
                            pb=fw.ps([128, 1024], BF16, f"psb{g}", ph), pd=fw.ps([128, 1024], F32, f"psd{g}", ph),
                        ))

                    def out_gen(ti, td, cx):
                        s = td["s"]
                        t = ti
                        m_, mT, xt, tm, xn, junk, s_, xh_, xb_, hT_, pb, pd = (cx[k] for k in ("m", "mT", "xt", "tm", "xn", "junk", "s", "xh", "xb", "hT", "pb", "pd"))
                        for (p0, r0, n) in td["pieces"]:
                            fr = r0 if td["kind"] != "ctx" else NLAT + r0
                            fw.dma(sp, m_[p0:p0 + n, 0:256], FOURd[fr:fr + n, :], reads=[FOURd], writes=[m_])
                        qs = td["qslot"]
                        fw.dma(sp, m_[:, 256:1024], ATTd[qs:qs + 128, :], reads=[ATTd], writes=[m_])
                        load_rows(xt, td)
                        yield
                        for k in range(8):
                            fw.op(pe, lambda e: e.transpose(out=pb[:, k * 128:(k + 1) * 128], in_=m_[:, k * 128:(k + 1) * 128], identity=ident[:]),
                                  reads=[m_, ident], writes=[pb])
                        yield
                        fw.op(act, lambda e: e.copy(out=mT[:].rearrange("p k t -> p (k t)"), in_=pb[:, :]), reads=[pb], writes=[mT])
                        yield
                        for hh in range(2):
                            for k in range(8):
                                fw.op(pe, lambda e: e.matmul(pd[:, hh * 512:(hh + 1) * 512], lhsT=mT[:, k, :], rhs=woutb[:, k, hh * 512:(hh + 1) * 512],
                                                             start=(k == 0), stop=(k == 7)), reads=[mT, woutb], writes=[pd])
                        yield
                        fw.op(dve, lambda e: e.tensor_tensor(out=tm[:], in0=pd[:, :], in1=GT[s][0][:], op=ALU.mult), reads=[pd, GT[s][0]], writes=[tm])
                        yield
                        fw.op(pool, lambda e: e.tensor_tensor(out=xn[:], in0=tm[:], in1=xt[:], op=ALU.add), reads=[tm, xt], writes=[xn])
                        fw.op(dve, lambda e: e.memset(s_[:], 0.0), writes=[s_])
                        yield
                        if td["xres"] is not None:
                            xr_ = XRES[td["xres"]]
                            fw.dma(sp, xr_[:, :], xn[:], reads=[xn], writes=[xr_])
                        fw.op(act, lambda e: e.activation(out=junk[:], in_=xn[:], func=ACTF.Square, accum_out=s_[:, 0:1]), reads=[xn], writes=[junk, s_])
                        yield
                        fw.op(dve, lambda e: e.tensor_scalar(out=s_[:, 2:3], in0=s_[:, 0:1], scalar1=1.0 / D, scalar2=EPS, op0=ALU.mult, op1=ALU.add),
                              reads=[s_], writes=[s_])
                        yield
                        fw.op(act, lambda e: e.sqrt(out=s_[:, 2:3], in_=s_[:, 2:3]), reads=[s_], writes=[s_])
                        yield
                        fw.op(dve, lambda e: e.reciprocal(out=s_[:, 4:5], in_=s_[:, 2:3]), reads=[s_], writes=[s_])
                        yield
                        fw.op(dve, lambda e: e.scalar_tensor_tensor(out=xh_[:], in0=xn[:], scalar=s_[:, 4:5], in1=GT[s][1][:],
                                                                    op0=ALU.mult, op1=ALU.mult), reads=[xn, s_, GT[s][1]], writes=[xh_])
                        yield
                        fw.op(pool, lambda e: e.tensor_tensor(out=xb_[:], in0=xh_[:], in1=GT[s][2][:], op=ALU.add), reads=[xh_, GT[s][2]], writes=[xb_])
                        yield
                        for k in range(8):
                            fw.op(pe, lambda e: e.transpose(out=pb[:, k * 128:(k + 1) * 128], in_=xb_[:, k * 128:(k + 1) * 128], identity=ident[:]),
                                  reads=[xb_, ident], writes=[pb])
                        yield
                        fw.op(act, lambda e: e.copy(out=hT_[:].rearrange("p k t -> p (k t)"), in_=pb[:, :]), reads=[pb], writes=[hT_])
                        yield
                        if td["kind"] == "lat":
                            fw.dma(sp, XHlv[:, :, 1 + t * 128:1 + (t + 1) * 128], hT_[:], reads=[hT_], writes=[XHTl])
                        elif td["kind"] == "ext":
                            fw.dma(sp, XHev[:, :, :], hT_[:], reads=[hT_], writes=[XHTe])
                        else:
                            tc_ = td["xres"] - NT_L
                            fw.dma(sp, XHcv[:, :, 1 + tc_ * 128:1 + (tc_ + 1) * 128], hT_[:], reads=[hT_], writes=[XHTc])

                    active = []
                    nxt_i = [0]

                    free_ocx = list(range(G_O))

                    def refill_o():
                        while free_ocx and nxt_i[0] < len(otds):
                            i_ = nxt_i[0]
                            nxt_i[0] += 1
                            g_ = free_ocx.pop(0)
                            active.append((out_gen(i_, otds[i_], OCX[g_]), g_))

                    refill_o()
                    while active:
                        for item in list(active):
                            try:
                                next(item[0])
                            except StopIteration:
                                active.remove(item)
                                free_ocx.append(item[1])
                                refill_o()

                fw.barrier()
                chk("OUT%d" % l)
                with ExitStack() as ph:
                    wdnb = fw.sb([128, NCH, D], BF16, "wdnb", ph, multi=True)
                    wdwt = fw.sb([128, 2 * NCH, 3], F32, "wdwt", ph)
                    bdwt = fw.sb([128, 2 * NCH], F32, "bdwt", ph)
                    GT2 = [fw.sb([128, D], F32, f"gt2{s}", ph) for s in range(2)]
                    fw.dma(sp, wdwt[:], wdw[l], reads=[wdw], writes=[wdwt])
                    fw.dma(sp, bdwt[:], bdw[l], reads=[bdw], writes=[bdwt])
                    for s in range(2):
                        fw.dma(sp, GT2[s][:], MB[s, 5], reads=[MB], writes=[GT2[s]])
                    if last:
                        gfint = fw.sb([128, D], F32, "gfint", ph)
                        fw.dma(sp, gfint[:], gfin[:], reads=[gfin], writes=[gfint])
                    fw.dma(sp, wdnb[:], WDNd.t.rearrange("c p n -> p c n"), reads=[WDNd], writes=[wdnb])
                    fw.barrier()
                    wur = Ring([fw.sb([128, 8, 128], BF16, f"wur{i}", ph) for i in range(6)])
                    xhb = Ring([fw.sb([128, 8, 514], BF16, f"xhb{i}", ph) for i in range(2)])
                    yT = fw.sb([128, NCH, 512], BF16, "yT", ph)
                    cr_ = Ring([fw.sb([128, 512], F32, f"cv{i}", ph) for i in range(4)])
                    sgr = Ring([fw.sb([128, 512], F32, f"sg{i}", ph) for i in range(2)])
                    xr = Ring([fw.sb([128, D], F32, f"xf{i}", ph) for i in range(2)])
                    tmpr = Ring([fw.sb([128, D], F32, f"tf{i}", ph) for i in range(2)])
                    junk = fw.sb([128, D], BF16, "junk3", ph)
                    stt = Ring([fw.sb([128, 8], F32, f"stt{i}", ph) for i in range(2)])
                    psu = Ring([fw.ps([128, 512], F32, f"psu{i}", ph) for i in range(3)])
                    psh = Ring([fw.ps([128, 512], F32, f"psh{i}", ph) for i in range(2)])
                    psd = Ring([fw.ps([128, 1024], F32, f"psd{i}", ph) for i in range(1)])
                    XHlv = XHTl.t.rearrange("k p t -> p k t")
                    XHcv = XHTc.t.rearrange("k p t -> p k t")
                    blocks = [(0, t0, min(512, HALF - t0)) for t0 in range(0, HALF, 512)]
                    if not last:
                        blocks.append((1, 0, NCTX))
                    xe = fw.sb([128, 8, 128], BF16, "xe", ph)
                    fw.dma(sp, xe[:], XHTe.t.rearrange("k p t -> p k t"), reads=[XHTe], writes=[xe])

                    def load_blk(bi):
                        s, t0, W = blocks[bi]
                        xb_ = xhb.next()
                        v = XHlv if s == 0 else XHcv
                        n_ = HALF if s == 0 else NCTX
                        lo = 1 if t0 == 0 else 0
                        hi = W + 1 if t0 + W == n_ else W + 2
                        if lo:
                            if s == 0:
                                fw.op(dve, lambda e: e.tensor_scalar(out=xb_[:, :, 0:1], in0=xe[:, :, 127:128], scalar1=hm[:, 0:1], scalar2=None,
                                                                     op0=ALU.mult), reads=[xe, hm], writes=[xb_])
                            else:
                                fw.op(pool, lambda e: e.memset(xb_[:, :, 0:1], 0.0), writes=[xb_])
                        if hi == W + 1:
                            if s == 0:
                                fw.op(dve, lambda e: e.tensor_scalar(out=xb_[:, :, W + 1:W + 2], in0=xe[:, :, 0:1], scalar1=hm[:, 1:2], scalar2=None,
                                                                     op0=ALU.mult), reads=[xe, hm], writes=[xb_])
                            else:
                                fw.op(pool, lambda e: e.memset(xb_[:, :, W + 1:W + 2], 0.0), writes=[xb_])
                        fw.dma(sp, xb_[:, :, lo:hi], v[:, :, t0 + lo:t0 + hi], reads=[XHTl if s == 0 else XHTc], writes=[xb_])
                        return xb_

                    nxt = load_blk(0)
                    for bi, (s, t0, W) in enumerate(blocks):
                        xb_ = nxt
                        if bi + 1 < len(blocks):
                            nxt = load_blk(bi + 1)
                        for i in range(NCH):
                            cvs = []
                            for hv, ch in ((0, i), (1, NCH + i)):
                                pu = psu.next()
                                ph_ = psh.next()
                                wu = wur.next()
                                fw.dma(sp, wu[:], WUPd[ch], reads=[WUPd], writes=[wu])
                                for k in range(8):
                                    fw.op(pe, lambda e: e.matmul(pu[:, 0:W], lhsT=wu[:, k, :], rhs=xb_[:, k, 1:W + 1],
                                                                 start=(k == 0), stop=(k == 7)), reads=[wu, xb_], writes=[pu])
                                for k in range(8):
                                    fw.op(pe, lambda e: e.matmul(ph_[:, 0:2], lhsT=wu[:, k, :], rhs=xb_[:, k, 0:W + 2:W + 1],
                                                                 start=(k == 0), stop=(k == 7)), reads=[wu, xb_], writes=[ph_])
                                c_ = cr_.next()
                                w0 = wdwt[:, ch, 0:1]
                                w2 = wdwt[:, ch, 2:3]
                                fw.op(act, lambda e: e.activation(out=c_[:, 0:W], in_=pu[:, 0:W], func=ACTF.Identity, scale=wdwt[:, ch, 1:2],
                                                                  bias=bdwt[:, ch:ch + 1]), reads=[pu, wdwt, bdwt], writes=[c_])
                                fw.op(dve, lambda e: e.scalar_tensor_tensor(out=c_[:, 1:W], in0=pu[:, 0:W - 1], scalar=w0, in1=c_[:, 1:W],
                                                                            op0=ALU.mult, op1=ALU.add), reads=[pu, wdwt, c_], writes=[c_])
                                fw.op(dve, lambda e: e.scalar_tensor_tensor(out=c_[:, 0:W - 1], in0=pu[:, 1:W], scalar=w2, in1=c_[:, 0:W - 1],
                                                                            op0=ALU.mult, op1=ALU.add), reads=[pu, wdwt, c_], writes=[c_])
                                fw.op(dve, lambda e: e.scalar_tensor_tensor(out=c_[:, 0:1], in0=ph_[:, 0:1], scalar=w0, in1=c_[:, 0:1],
                                                                            op0=ALU.mult, op1=ALU.add), reads=[ph_, wdwt, c_], writes=[c_])
                                fw.op(dve, lambda e: e.scalar_tensor_tensor(out=c_[:, W - 1:W], in0=ph_[:, 1:2], scalar=w2, in1=c_[:, W - 1:W],
                                                                            op0=ALU.mult, op1=ALU.add), reads=[ph_, wdwt, c_], writes=[c_])
                                cvs.append(c_)
                            sg_ = sgr.next()
                            fw.op(act, lambda e: e.activation(out=sg_[:, 0:W], in_=cvs[0][:, 0:W], func=ACTF.Silu), reads=[cvs[0]], writes=[sg_])
                            fw.op(pool, lambda e: e.tensor_tensor(out=yT[:, i, 0:W], in0=sg_[:, 0:W], in1=cvs[1][:, 0:W], op=ALU.mult),
                                  reads=[sg_, cvs[1]], writes=[yT])
                        for qi in range(W // 128):
                            tg = (t0 // 128 + qi) if s == 0 else NT_L + qi
                            pd = psd.next()
                            for hh in range(2):
                                for i in range(NCH):
                                    fw.op(pe, lambda e: e.matmul(pd[:, hh * 512:(hh + 1) * 512], lhsT=yT[:, i, qi * 128:(qi + 1) * 128],
                                                                 rhs=wdnb[:, i, hh * 512:(hh + 1) * 512], start=(i == 0), stop=(i == NCH - 1)),
                                          reads=[yT, wdnb], writes=[pd])
                            xt = xr.next()
                            fw.dma(sp, xt[:], XRES[tg][:, :], reads=[XRES[tg]], writes=[xt])
                            tm = tmpr.next()
                            fw.op(dve, lambda e: e.tensor_tensor(out=tm[:], in0=pd[:, :], in1=GT2[s][:], op=ALU.mult), reads=[pd, GT2[s]], writes=[tm])
                            fw.op(pool, lambda e: e.tensor_tensor(out=tm[:], in0=tm[:], in1=xt[:], op=ALU.add), reads=[tm, xt], writes=[tm])
                            if not last:
                                fw.dma(pool, XRES[tg][:, :], tm[:], reads=[tm], writes=[XRES[tg]])
                            else:
                                s_ = stt.next()
                                fw.op(dve, lambda e: e.memset(s_[:], 0.0), writes=[s_])
                                fw.op(act, lambda e: e.activation(out=junk[:], in_=tm[:], func=ACTF.Square, accum_out=s_[:, 0:1]),
                                      reads=[tm], writes=[junk, s_])
                                rstd_from_ss(s_, 1, 1.0 / D)
                                fw.op(dve, lambda e: e.scalar_tensor_tensor(out=xt[:], in0=tm[:], scalar=s_[:, 4:5], in1=gfint[:],
                                                                            op0=ALU.mult, op1=ALU.mult), reads=[tm, s_, gfint], writes=[xt])
                                fw.dma(pool, out_d[tg * 128:(tg + 1) * 128, :], xt[:], reads=[xt], writes=[out_d])
                    if not last and not fw.stopped:
                        for j in range(NCK):
                            own_bufs = [XRES[2 * j], XRES[2 * j + 1]]
                            fw._deps(pool, own_bufs, [XAG[j]])
                            ins_ = nc.gpsimd.collective_compute("AllGather", ALU.bypass, replica_groups=[[0, 1], [2, 3], [4, 5], [6, 7]],
                                                                ins=[XOWN.t[j * 256:(j + 1) * 256, :]], outs=[XAG[j].t])
                            fw.cc_cnt += 1
                            ins_.then_inc(fw.cc_sem)
                            fw._record((fw.cc_sem, fw.cc_cnt), own_bufs, [XAG[j]])
                        for j in range(NT_O):
                            a_ = xr.next()
                            b_ = tmpr.next()
                            g_ = XAG[j // 2]
                            o_ = (j % 2) * 128
                            fw.dma(sp, a_[:], g_[o_:o_ + 128, :], reads=[g_], writes=[a_])
                            fw.dma(sp, b_[:], g_[256 + o_:256 + o_ + 128, :], reads=[g_], writes=[b_])
                            fw.op(dve, lambda e: e.tensor_scalar(out=a_[:], in0=a_[:], scalar1=hm[:, 2:3], scalar2=None, op0=ALU.mult),
                                  reads=[a_, hm], writes=[a_])
                            fw.op(dve, lambda e: e.scalar_tensor_tensor(out=b_[:], in0=b_[:], scalar=hm[:, 3:4], in1=a_[:], op0=ALU.mult, op1=ALU.add),
                                  reads=[b_, hm, a_], writes=[b_])
                            fw.dma(pool, XOTH[j * 128:(j + 1) * 128, :], b_[:], reads=[b_], writes=[XOTH])

        except _Stop:
            pass
        fw.barrier()
        fw.finish([out_d])
        stats = {e.name: (e.nins, e.nwait) for e in (pe, act, dve, pool, sp)}
    return nc, stats


def _consts(NLAT, hf):
    R = NLAT // 64
    NT_L = NLAT // 128
    HALF = NLAT // 2
    c = {}
    c["ident"] = np.eye(128, dtype=np.float32).astype(bf)
    sel = np.zeros((2, 256), np.float32)
    sel[0, :128] = 1.0
    sel[1, 128:] = 1.0
    c["sel"] = sel
    i64 = np.arange(64)
    a64 = 2 * np.pi * np.outer(i64, i64) / 64.0
    C64 = np.cos(a64) / 8.0
    S64 = np.sin(a64) / 8.0
    C4 = np.kron(np.eye(4), C64)
    S4 = np.kron(np.eye(4), S64)
    c4s4 = np.stack([C4, S4], 0).reshape(2, 2, 128, 256).transpose(2, 1, 0, 3)
    c["c4s4"] = np.ascontiguousarray(c4s4).astype(bf)
    ir = np.arange(R)
    ar = 2 * np.pi * np.outer(ir, ir) / R
    CR = np.cos(ar) / np.sqrt(R)
    SR = np.sin(ar) / np.sqrt(R)
    C64p, S64p = C64, S64
    if hf:
        sg = (-1.0) ** ir
        CR = CR * sg[None, :]
        SR = SR * sg[None, :]
        perm = (i64 + 32) % 64
        C64p, S64p = C64[:, perm], S64[:, perm]
    c["crs"] = np.ascontiguousarray(np.stack([CR, SR, -SR], 1)).astype(bf)
    at = 2 * np.pi * np.outer(ir, i64) / NLAT
    c["tw"] = np.ascontiguousarray(np.stack([np.cos(at), np.sin(at)], 1)).astype(np.float32)
    c["c64"] = np.ascontiguousarray(np.stack([C64p, S64p], 1)).astype(bf)
    i256 = np.arange(256)
    a256 = 2 * np.pi * np.outer(i256, i256) / 256.0
    C256 = np.cos(a256) / 16.0
    S256 = np.sin(a256) / 16.0
    c256 = np.stack([C256, S256], 0).reshape(2, 2, 128, 256).transpose(2, 1, 0, 3)
    c["c256"] = np.ascontiguousarray(c256).astype(bf)
    loc = np.arange(NLAT)
    ext = np.concatenate([np.arange(HALF, HALF + 64), np.arange(NLAT - 64, NLAT)])
    loc = np.concatenate([loc, ext])
    tok = (loc + hf * HALF) % NLAT
    row = (tok // 64).astype(np.float64)
    col = (tok % 64).astype(np.float64)
    inv = 10000.0 ** (-np.arange(8, dtype=np.float64) / 8)
    ang = np.concatenate([row[:, None] * inv, col[:, None] * inv], -1).astype(np.float32)
    cs = np.stack([np.cos(ang), np.sin(ang)], 1)
    cs = np.concatenate([cs, cs], -1)
    c["rope"] = np.ascontiguousarray(cs.reshape(NT_L + 1, 128, 2, 32).transpose(1, 0, 2, 3)).astype(np.float32)
    hm = np.zeros((128, 4), np.float32)
    hm[:, 0] = float(hf)
    hm[:, 1] = float(1 - hf)
    hm[:, 2] = float(hf)
    hm[:, 3] = float(1 - hf)
    c["hm"] = hm
    return c


def make_inmaps(inputs, NLAT, nb):
    f = lambda a: np.ascontiguousarray(np.asarray(a, dtype=np.float32))
    L = inputs["w_ada"].shape[0]
    HALF = NLAT // 2
    shared = dict(
        w_ada=f(inputs["w_ada"]),
        b_ada2=f(np.repeat(np.asarray(inputs["b_ada"])[:, None, :], 2, 1)),
        gmix2=f(np.repeat(np.asarray(inputs["g_mix"])[:, None, :], 2, 1)),
        gffn2=f(np.repeat(np.asarray(inputs["g_ffn"])[:, None, :], 2, 1)),
        gfin=f(np.repeat(np.asarray(inputs["g_final"])[None, :], 128, 0)),
        w_in=f(inputs["w_in"]),
        w_four=f(inputs["w_fourier"]),
        b_four=f(np.broadcast_to(np.asarray(inputs["b_fourier"])[:, None, None, :], (L, 128, 2, 256))),
        gq=f(np.asarray(inputs["g_q_a"]).reshape(L, 3, 128).transpose(0, 2, 1)),
        w_qb=f(inputs["w_q_b"]),
        gkv=f(np.asarray(inputs["g_kv_a"]).reshape(L, 128, 1)),
        w_kvb=f(inputs["w_kv_b"]),
        w_out=f(inputs["w_out"]),
        w_up=f(inputs["w_up"]),
        wdw=f(np.asarray(inputs["w_dw"]).reshape(L, 3, 2 * NCH, 128).transpose(0, 3, 2, 1)),
        bdw=f(np.asarray(inputs["b_dw"]).reshape(L, 2 * NCH, 128).transpose(0, 2, 1)),
        w_down=f(inputs["w_down"]),
    )
    cons = [_consts(NLAT, 0), _consts(NLAT, 1)]
    maps = []
    x = np.asarray(inputs["x"])
    c = np.asarray(inputs["c"], dtype=np.float32)
    ctx = np.asarray(inputs["ctx"])
    cc = np.asarray(inputs["c_ctx"], dtype=np.float32)
    for b in range(nb):
        cv = np.stack([c[b].reshape(8, 128).T, cc.reshape(8, 128).T], -1)
        for hf in range(2):
            m = dict(shared)
            m.update(cons[hf])
            m["x"] = f(np.roll(x[b], -hf * HALF, axis=0))
            m["ctx"] = f(ctx[b])
            m["cvec"] = f(cv)
            maps.append(m)
    return maps


_CACHE = {}


def kernel(**inputs):
    x = np.asarray(inputs["x"])
    B, NLAT, _ = x.shape
    HALF = NLAT // 2
    key = (NLAT,)
    if key not in _CACHE:
        _CACHE[key] = build_program(NLAT)
    nc, _ = _CACHE[key]
    maps = make_inmaps(inputs, NLAT, B)
    assert len(maps) == 8
    res = run_bass_kernel_spmd(nc, maps, core_ids=list(range(8)))
    out = np.empty((B, NLAT, D), np.float32)
    for b in range(B):
        for hf in range(2):
            out[b, hf * HALF:(hf + 1) * HALF] = np.asarray(res.results[2 * b + hf]["out"], dtype=np.float32)
    return out
```

```python
import numpy as np
import ml_dtypes
from contextlib import ExitStack
import concourse.bass as bass
import concourse.mybir as mybir
from concourse.bass_utils import run_bass_kernel_spmd

F32 = mybir.dt.float32
BF16 = mybir.dt.bfloat16
ACTF = mybir.ActivationFunctionType
ALU = mybir.AluOpType
bf = ml_dtypes.bfloat16

D = 1024
NCTX = 256
DFF = 2816
NCH = 22
EPS = 1e-6
SCALE = 96 ** -0.5


class Buf:
    def __init__(self, t, name, multi=False):
        self.t = t
        self.name = name
        self.multi = multi
        self.excl = False
        self.w = {}
        self.r = {}

    def __getitem__(self, k):
        return self.t[k]


def _merge(d, ev):
    k = id(ev[0])
    if k not in d or d[k][1] < ev[1]:
        d[k] = ev


class Eng:
    def __init__(self, fw, e, name, is_pe=False):
        self.e = e
        self.name = name
        self.is_pe = is_pe
        self.sem = fw.new_sem("s_" + name)
        self.cnt = 0
        self.seen = {}
        self.nwait = 0
        self.nins = 0

    def wait(self, ev):
        sem, val = ev
        if self.seen.get(id(sem), 0) >= val:
            return
        self.e.wait_ge(sem, val)
        self.seen[id(sem)] = val
        self.nwait += 1


class FW:
    def __init__(self, nc, stack, n_dma_sems=14):
        self.nc = nc
        self.stack = stack
        self.pe = Eng(self, nc.tensor, "pe", is_pe=True)
        self.act = Eng(self, nc.scalar, "act")
        self.dve = Eng(self, nc.vector, "dve")
        self.pool = Eng(self, nc.gpsimd, "pool")
        self.sp = Eng(self, nc.sync, "sp")
        self.dq = {}
        for q in (self.sp, self.pool, self.act):
            self.dq[q.name] = dict(sems=[self.new_sem(f"d_{q.name}{i}") for i in range(n_dma_sems)],
                                   vals=[0] * n_dma_sems, idx=0)
        self.uid = 0
        self.stopped = False
        self.cc_sem = self.new_sem("cc")
        self.cc_cnt = 0

    def new_sem(self, name):
        return self.stack.enter_context(self.nc.semaphore(name))

    def sb(self, shape, dt, name, stack=None, multi=False):
        self.uid += 1
        t = (stack or self.stack).enter_context(self.nc.sbuf_tensor(f"{name}_{self.uid}", list(shape), dt))
        return Buf(t, name, multi)

    def ps(self, shape, dt, name, stack=None):
        self.uid += 1
        t = (stack or self.stack).enter_context(self.nc.psum_tensor(f"{name}_{self.uid}", list(shape), dt))
        b = Buf(t, name)
        b.excl = True
        return b

    def dram(self, name, shape, dt, multi=True):
        t = self.nc.dram_tensor(name, list(shape), dt, kind="Internal")
        return Buf(t.ap(), name, multi)

    def _deps(self, eng, reads, writes):
        evs = {}
        for b in reads:
            for ev in b.w.values():
                evs[(id(ev[0]), ev[1])] = ev
            if b.excl:
                for ev in b.r.values():
                    if ev[0] is not eng.sem:
                        evs[(id(ev[0]), ev[1])] = ev
        for b in writes:
            if not b.multi:
                for ev in b.w.values():
                    evs[(id(ev[0]), ev[1])] = ev
            for ev in b.r.values():
                evs[(id(ev[0]), ev[1])] = ev
        for ev in evs.values():
            if eng.is_pe and ev[0] is eng.sem:
                continue
            eng.wait(ev)

    def _record(self, ev, reads, writes):
        for b in reads:
            _merge(b.r, ev)
        for b in writes:
            if b.multi:
                _merge(b.w, ev)
            else:
                b.w = {id(ev[0]): ev}
                b.r = {}

    def op(self, eng, fn, reads=(), writes=()):
        if self.stopped:
            return None
        self._deps(eng, reads, writes)
        ins = fn(eng.e)
        eng.cnt += 1
        eng.nins += 1
        ins.then_inc(eng.sem, 1)
        self._record((eng.sem, eng.cnt), reads, writes)
        return ins

    def dma(self, q, out, in_, reads=(), writes=(), **kw):
        if self.stopped:
            return None
        self._deps(q, reads, writes)
        d = self.dq[q.name]
        i = d["idx"] % len(d["sems"])
        d["idx"] += 1
        if d["vals"][i] > 0:
            q.wait((d["sems"][i], d["vals"][i]))
        ins = q.e.dma_start(out=out, in_=in_, **kw)
        d["vals"][i] += 16
        ins.then_inc(d["sems"][i], 16)
        q.nins += 1
        self._record((d["sems"][i], d["vals"][i]), reads, writes)
        return ins

    def barrier(self):
        if self.stopped:
            return
        engs = (self.pe, self.act, self.dve, self.pool, self.sp)
        evs = [(e.sem, e.cnt) for e in engs if e.cnt > 0]
        for d in self.dq.values():
            for sm, v in zip(d["sems"], d["vals"]):
                if v > 0:
                    evs.append((sm, v))
        if self.cc_cnt > 0:
            evs.append((self.cc_sem, self.cc_cnt))
        for e in engs:
            for ev in evs:
                if ev[0] is e.sem:
                    continue
                e.wait(ev)

    def finish(self, bufs):
        for b in bufs:
            for ev in list(b.w.values()):
                self.sp.wait(ev)


class Ring:
    def __init__(self, bufs):
        self.bufs = bufs
        self.i = 0

    def next(self):
        b = self.bufs[self.i % len(self.bufs)]
        self.i += 1
        return b


class _Stop(Exception):
    pass


def build_program(NLAT=8192, depth=2, dbg=False, stop_at=None):
    NT_L = NLAT // 128
    NT_C = NCTX // 128
    NT = NT_L + NT_C
    NTOK = NLAT + NCTX
    R = NLAT // 64
    assert R <= 128 and R % 16 == 0
    HALF = NLAT // 2
    NT_O = HALF // 128
    QB = 512 if HALF % 512 == 0 else 128
    QS_CTX = HALF
    QS_EXT = HALF + NCTX
    NQS = HALF + NCTX + 128

    nc = bass.Bass("TRN2", target_bir_lowering=False)

    def din(name, shape, dt=F32):
        return Buf(nc.dram_tensor(name, list(shape), dt, kind="ExternalInput").ap(), name)

    x_in = din("x", [NLAT, D])
    ctx_in = din("ctx", [NCTX, D])
    cvec = din("cvec", [128, 8, 2])
    w_ada = din("w_ada", [depth, D, 6 * D])
    b_ada2 = din("b_ada2", [depth, 2, 6 * D])
    gmix2 = din("gmix2", [depth, 2, D])
    gffn2 = din("gffn2", [depth, 2, D])
    gfin = din("gfin", [128, D])
    w_in = din("w_in", [depth, D, 800])
    w_four = din("w_four", [depth, 256, 256])
    b_four = din("b_four", [depth, 128, 2, 256])
    gq = din("gq", [depth, 128, 3])
    w_qb = din("w_qb", [depth, 384, 1152])
    gkv = din("gkv", [depth, 128, 1])
    w_kvb = din("w_kvb", [depth, 128, 1536])
    w_out = din("w_out", [depth, D, D])
    w_up = din("w_up", [depth, D, 2 * DFF])
    wdw = din("wdw", [depth, 128, 2 * NCH, 3])
    bdw = din("bdw", [depth, 128, 2 * NCH])
    w_down = din("w_down", [depth, DFF, D])
    ident_d = din("ident", [128, 128], BF16)
    sel_d = din("sel", [2, 256])
    c4s4_d = din("c4s4", [128, 2, 2, 256], BF16)
    crs_d = din("crs", [R, 3, R], BF16)
    tw_d = din("tw", [R, 2, 64])
    c64_d = din("c64", [64, 2, 64], BF16)
    c256_d = din("c256", [128, 2, 2, 256], BF16)
    rope_d = din("rope", [128, NT_L + 1, 2, 32])
    hm_d = din("hm", [128, 4])
    out_d = Buf(nc.dram_tensor("out", [HALF, D], F32, kind="ExternalOutput").ap(), "out", multi=True)
    dbg_d = {}

    with ExitStack() as st:
        fw = FW(nc, st)
        pe, act, dve, pool, sp = fw.pe, fw.act, fw.dve, fw.pool, fw.sp

        XOWN = fw.dram("xown", [HALF, D], F32)
        NCK = HALF // 256
        XAG = [fw.dram(f"xag{j}", [512, D], F32) for j in range(NCK)]
        XOTH = fw.dram("xoth", [HALF, D], F32)
        XRES = {t: Buf(XOWN.t[t * 128:(t + 1) * 128, :], f"xres{t}") for t in range(NT_O)}
        for tc_ in range(NT_C):
            XRES[NT_L + tc_] = fw.dram(f"xctx{tc_}", [128, D], F32, multi=False)
        XHTe = fw.dram("xhte", [8, 128, 128], BF16)
        MB = fw.dram("mb", [2, 6, 128, D], F32)
        ABd = fw.dram("abd", [NTOK, 512], BF16)
        Y2d = fw.dram("y2d", [R, 64, 2, 256], BF16)
        FOURd = fw.dram("fourd", [NTOK, 256], BF16)
        QTd = fw.dram("qtd", [12, 96, NQS], BF16)
        ATTd = fw.dram("attd", [NQS, 768], BF16)
        XHTl = fw.dram("xhtl", [8, 128, HALF + 2], BF16)
        XHTc = fw.dram("xhtc", [8, 128, NCTX + 2], BF16)

        ident = fw.sb([128, 128], BF16, "ident")
        sel = fw.sb([2, 256], F32, "sel")
        zero_t = fw.sb([128, 8, 2], BF16, "zero")
        fw.dma(sp, ident[:], ident_d[:], reads=[ident_d], writes=[ident])
        fw.dma(sp, sel[:], sel_d[:], reads=[sel_d], writes=[sel])
        hm = fw.sb([128, 4], F32, "hm")
        fw.dma(sp, hm[:], hm_d[:], reads=[hm_d], writes=[hm])
        TD = []
        for t in range(NT_L):
            TD.append(dict(kind="lat", s=0, pieces=[(0, t * 128, 128)], kvcol=t * 128, q=(t < NT_O), qslot=t * 128, ropei=t, xres=(t if t < NT_O else None)))
        for tc_ in range(NT_C):
            TD.append(dict(kind="ctx", s=1, pieces=[(0, tc_ * 128, 128)], kvcol=NLAT + tc_ * 128, q=True, qslot=QS_CTX + tc_ * 128, ropei=None, xres=NT_L + tc_))
        TD.append(dict(kind="ext", s=0, pieces=[(0, HALF, 64), (64, NLAT - 64, 64)], kvcol=None, q=True, qslot=QS_EXT, ropei=NT_L, xres=None))
        fw.op(dve, lambda e: e.memset(zero_t[:], 0.0), writes=[zero_t])
        cast_rr = [0]

        def cast_eng():
            cast_rr[0] += 1
            return (dve, pool)[cast_rr[0] % 2]

        def rstd_from_ss(stt, n_cols, inv_n, ncol=1):
            fw.op(dve, lambda e: e.tensor_scalar(out=stt[:, 2:2 + ncol], in0=stt[:, 0:ncol], scalar1=inv_n,
                                                 scalar2=EPS, op0=ALU.mult, op1=ALU.add), reads=[stt], writes=[stt])
            fw.op(act, lambda e: e.sqrt(out=stt[:, 2:2 + ncol], in_=stt[:, 2:2 + ncol]), reads=[stt], writes=[stt])
            fw.op(dve, lambda e: e.reciprocal(out=stt[:, 4:4 + ncol], in_=stt[:, 2:2 + ncol]), reads=[stt], writes=[stt])

        def chk(name):
            if stop_at == name:
                fw.stopped = True

        try:
            for l in range(depth):
                last = (l == depth - 1)
                src_tiles = None if l == 0 else XRES

                def load_rows(xt, td):
                    for (p0, r0, n) in td["pieces"]:
                        if td["kind"] == "ctx":
                            if l == 0:
                                sb_, sap = ctx_in, ctx_in[r0:r0 + n, :]
                            else:
                                sb_ = XRES[td["xres"]]
                                sap = sb_[:, :]
                        elif l == 0:
                            sb_, sap = x_in, x_in[r0:r0 + n, :]
                        elif r0 < HALF:
                            sb_ = XRES[r0 // 128]
                            sap = sb_[:, :]
                        else:
                            sb_, sap = XOTH, XOTH[r0 - HALF:r0 - HALF + n, :]
                        fw.dma(sp, xt[p0:p0 + n, :], sap, reads=[sb_], writes=[xt])

                fw.barrier()
                with ExitStack() as ph:
                    sil = fw.sb([128, 8, 2], F32, "sil", ph)
                    mrow = fw.sb([2, 6 * D], F32, "mrow", ph)
                    brow = fw.sb([2, 6 * D], F32, "brow", ph)
                    g2 = fw.sb([2, 2, D], F32, "g2", ph)
                    wst = Ring([fw.sb([128, 3072], F32, f"wst{i}", ph) for i in range(3)])
                    bst = Ring([fw.sb([128, D], F32, f"bst{i}", ph) for i in range(2)])
                    psm = [fw.ps([128, 512], F32, f"psm{i}", ph) for i in range(8)]
                    fw.dma(sp, sil[:], cvec[:], reads=[cvec], writes=[sil])
                    fw.dma(sp, brow[:], b_ada2[l], reads=[b_ada2], writes=[brow])
                    fw.dma(sp, g2[:, 0, :], gmix2[l], reads=[gmix2], writes=[g2])
                    fw.dma(sp, g2[:, 1, :], gffn2[l], reads=[gffn2], writes=[g2])
                    fw.op(act, lambda e: e.activation(out=sil[:], in_=sil[:], func=ACTF.Silu), reads=[sil], writes=[sil])
                    for half in range(2):
                        for k in range(8):
                            wt = wst.next()
                            fw.dma(sp, wt[:], w_ada[l, k * 128:(k + 1) * 128, half * 3072:(half + 1) * 3072],
                                   reads=[w_ada], writes=[wt])
                            for j in range(6):
                                fw.op(pe, lambda e: e.matmul(psm[j][0:2, :], lhsT=sil[:, k, :], rhs=wt[:, j * 512:(j + 1) * 512],
                                                             start=(k == 0), stop=(k == 7)), reads=[sil, wt], writes=[psm[j]])
                        for j in range(6):
                            c0 = half * 3072 + j * 512
                            fw.op(dve, lambda e: e.tensor_tensor(out=mrow[:, c0:c0 + 512], in0=psm[j][0:2, :],
                                                                 in1=brow[:, c0:c0 + 512], op=ALU.add),
                                  reads=[psm[j], brow], writes=[mrow])
                    for (c0, gi) in ((1 * D, 0), (4 * D, 1)):
                        fw.op(dve, lambda e: e.scalar_tensor_tensor(out=mrow[:, c0:c0 + D], in0=mrow[:, c0:c0 + D], scalar=1.0,
                                                                    in1=g2[:, gi, :], op0=ALU.add, op1=ALU.mult),
                              reads=[mrow, g2], writes=[mrow])
                    pi = 0
                    for s in range(2):
                        for v in range(6):
                            bt = bst.next()
                            for hh in range(2):
                                p_ = psm[pi % 8]
                                pi += 1
                                fw.op(pe, lambda e: e.matmul(p_[:, :], lhsT=sel[:, s * 128:(s + 1) * 128],
                                                             rhs=mrow[:, v * D + hh * 512: v * D + (hh + 1) * 512],
                                                             start=True, stop=True), reads=[sel, mrow], writes=[p_])
                                fw.op(act if hh else dve, lambda e: e.tensor_copy(out=bt[:, hh * 512:(hh + 1) * 512], in_=p_[:, :])
                                      if not hh else e.copy(out=bt[:, hh * 512:(hh + 1) * 512], in_=p_[:, :]),
                                      reads=[p_], writes=[bt])
                            fw.dma(pool, MB[s, v], bt[:], reads=[bt], writes=[MB])

                fw.barrier()
                chk("M%d" % l)
                with ExitStack() as mx:
                    WUPd = fw.dram(f"wupd{l}", [2 * NCH, 128, 8, 128], BF16)
                    WDNd = fw.dram(f"wdnd{l}", [NCH, 128, D], BF16)
                    WOUTd = fw.dram(f"woutd{l}", [8, 128, D], BF16)
                    ckvnT = fw.sb([128, NTOK], BF16, "ckvnT", mx, multi=True)
                    KT = [fw.sb([96, NTOK], BF16, f"KT{i}", mx, multi=True) for i in range(2)]
                    wkvb = fw.sb([128, 1536], BF16, "wkvb", mx)

                    with ExitStack() as ph:
                        winb = fw.sb([128, 8, 800], BF16, "winb", ph, multi=True)
                        wab = fw.sb([128, 8, 512], BF16, "wab", ph, multi=True)
                        wqb = fw.sb([128, 3, 1152], BF16, "wqb", ph, multi=True)
                        gqt = fw.sb([128, 3], F32, "gqt", ph)
                        gkvt = fw.sb([128, 1], F32, "gkvt", ph)
                        GS = [[fw.sb([128, D], F32, f"gs{s}{v}", ph) for v in range(2)] for s in range(2)]
                        psb = [fw.ps([128, 1024], BF16, f"psb{i}", ph) for i in range(2)]
                        psf = [fw.ps([128, 512], F32, f"psf{i}", ph) for i in range(6)]
                        prep = ExitStack()
                        stg = Ring([fw.sb([128, 1536], F32, f"stg{i}", prep) for i in range(2)])
                        wfb = fw.sb([128, 2, 256], BF16, "wfb", prep, multi=True)
                        mab = fw.sb([128, 2, 512], BF16, "mab", prep, multi=True)
                        wfT = fw.sb([128, 2, D], BF16, "wfT", prep, multi=True)
                        c4s4 = fw.sb([128, 2, 2, 256], BF16, "c4s4", prep)

                        fw.dma(sp, c4s4[:], c4s4_d[:], reads=[c4s4_d], writes=[c4s4])
                        fw.dma(sp, gqt[:], gq[l], reads=[gq], writes=[gqt])
                        fw.dma(sp, gkvt[:], gkv[l], reads=[gkv], writes=[gkvt])
                        for s in range(2):
                            fw.dma(sp, GS[s][0][:], MB[s, 1], reads=[MB], writes=[GS[s][0]])
                            fw.dma(sp, GS[s][1][:], MB[s, 0], reads=[MB], writes=[GS[s][1]])
                        for k in range(8):
                            sg_ = stg.next()
                            fw.dma(sp, sg_[:, 0:800], w_in[l, k * 128:(k + 1) * 128, :], reads=[w_in], writes=[sg_])
                            fw.op(cast_eng(), lambda e: e.tensor_copy(out=winb[:, k, :], in_=sg_[:, 0:800]), reads=[sg_], writes=[winb])
                        for kc in range(3):
                            sg_ = stg.next()
                            fw.dma(sp, sg_[:, 0:1152], w_qb[l, kc * 128:(kc + 1) * 128, :], reads=[w_qb], writes=[sg_])
                            fw.op(cast_eng(), lambda e: e.tensor_scalar(out=wqb[:, kc, :], in0=sg_[:, 0:1152], scalar1=gqt[:, kc:kc + 1],
                                                                        scalar2=None, op0=ALU.mult), reads=[sg_, gqt], writes=[wqb])
                        sg_ = stg.next()
                        fw.dma(sp, sg_[:, 0:1536], w_kvb[l], reads=[w_kvb], writes=[sg_])
                        fw.op(cast_eng(), lambda e: e.tensor_scalar(out=wkvb[:], in0=sg_[:, 0:1536], scalar1=gkvt[:, 0:1],
                                                                    scalar2=None, op0=ALU.mult), reads=[sg_, gkvt], writes=[wkvb])
                        for cc in range(2):
                            sg_ = stg.next()
                            fw.dma(sp, sg_[:, 0:256], w_four[l, cc * 128:(cc + 1) * 128, :], reads=[w_four], writes=[sg_])
                            fw.op(cast_eng(), lambda e: e.tensor_copy(out=wfb[:, cc, :], in_=sg_[:, 0:256]), reads=[sg_], writes=[wfb])
                        for cc in range(2):
                            p_ = psf[cc]
                            for X in range(2):
                                for c2 in range(2):
                                    fw.op(pe, lambda e: e.matmul(p_[:, X * 256:(X + 1) * 256],
                                                                 lhsT=c4s4[:, c2, X, cc * 128:(cc + 1) * 128], rhs=wfb[:, c2, :],
                                                                 start=(c2 == 0), stop=(c2 == 1)), reads=[c4s4, wfb], writes=[p_])
                            fw.op(dve, lambda e: e.tensor_copy(out=mab[:, cc, 0:256], in_=p_[:, 0:256]), reads=[p_], writes=[mab])
                            fw.op(dve, lambda e: e.tensor_scalar(out=mab[:, cc, 256:512], in0=p_[:, 256:512], scalar1=-1.0,
                                                                 scalar2=None, op0=ALU.mult), reads=[p_], writes=[mab])
                        for cc in range(2):
                            for k in range(8):
                                fw.op(pe, lambda e: e.transpose(out=psb[cc][:, k * 128:(k + 1) * 128],
                                                                in_=winb[:, k, cc * 128:(cc + 1) * 128], identity=ident[:]),
                                      reads=[winb, ident], writes=[psb[cc]])
                            fw.op(dve, lambda e: e.tensor_copy(out=wfT[:, cc, :], in_=psb[cc][:, :]), reads=[psb[cc]], writes=[wfT])
                        for k in range(8):
                            p_ = psf[k % 5]
                            for cc in range(2):
                                fw.op(pe, lambda e: e.matmul(p_[:, :], lhsT=wfT[:, cc, k * 128:(k + 1) * 128], rhs=mab[:, cc, :],
                                                             start=(cc == 0), stop=(cc == 1)), reads=[wfT, mab], writes=[p_])
                            fw.op(cast_eng() if False else dve, lambda e: e.tensor_copy(out=wab[:, k, :], in_=p_[:, :]),
                                  reads=[p_], writes=[wab])

                        chk('INW%d' % l)
                        fw.barrier()
                        prep.close()
                        G_IN = 3
                        QTv = QTd.t.rearrange("h d t -> d h t")
                        CX = []
                        for g in range(G_IN):
                            c_ = dict(
                                xt=fw.sb([128, D], F32, f"xt{g}", ph), ckv32=fw.sb([128, 544], F32, f"ckv32{g}", ph),
                                xb=fw.sb([128, D], BF16, f"xb{g}", ph), hT=fw.sb([128, 8, 128], BF16, f"hT{g}", ph),
                                s1=fw.sb([128, 8], F32, f"s1{g}", ph), s2=fw.sb([128, 8], F32, f"s2{g}", ph),
                                ab=fw.sb([128, 512], BF16, f"ab{g}", ph), cn=fw.sb([128, 512], BF16, f"cn{g}", ph),
                                cn2=fw.sb([128, 96], BF16, f"cn2{g}", ph), rt=fw.sb([128, 2, 32], F32, f"rt{g}", ph),
                                cq=fw.sb([128, 3, 128], BF16, f"cq{g}", ph), q=fw.sb([128, 12, 96], F32, f"q{g}", ph),
                                qt=fw.sb([128, 2, 12, 32], F32, f"qt{g}", ph), qb=fw.sb([128, 12, 96], BF16, f"qb{g}", ph),
                                qT=fw.sb([96, 12, 128], BF16, f"qT{g}", ph), rp=fw.sb([128, 2, 32], F32, f"rp{g}", ph),
                                junk=fw.sb([128, D], BF16, f"junk{g}", ph),
                            )
                            CX.append(c_)
                            fw.op(dve, lambda e: e.memset(c_["cn2"][:], 0.0), writes=[c_["cn2"]])
                        pbr = Ring([psb[0], psb[1]])
                        pfr = Ring([[psf[0], psf[1], psf[2]], [psf[3], psf[4], psf[5]]])

                        def tile_gen(ti, td, cx, g):
                            s = td["s"]
                            kvc = td["kvcol"]
                            do_kv = kvc is not None
                            do_q = td["q"] and not (td["kind"] == "ctx" and last)
                            xt, xb_, hT_, s_, s2, ab_, cn_, c2_, rt = (cx[k] for k in ("xt", "xb", "hT", "s1", "s2", "ab", "cn", "cn2", "rt"))
                            cq_, q_, qt_, qb_, qT_, ropet, c32 = (cx[k] for k in ("cq", "q", "qt", "qb", "qT", "rp", "ckv32"))
                            xh_ = Buf(q_.t[:].rearrange("p h d -> p (h d)")[:, 0:D], "xh_alias")
                            junk = cx["junk"]
                            load_rows(xt, td)
                            if td["ropei"] is not None:
                                fw.dma(sp, ropet[:], rope_d[:, td["ropei"], :, :], reads=[rope_d], writes=[ropet])
                            yield
                            fw.op(dve, lambda e: e.memset(s_[:], 0.0), writes=[s_])
                            fw.op(act, lambda e: e.activation(out=junk[:], in_=xt[:], func=ACTF.Square, accum_out=s_[:, 0:1]),
                                  reads=[xt], writes=[junk, s_])
                            yield
                            fw.op(dve, lambda e: e.tensor_scalar(out=s_[:, 2:3], in0=s_[:, 0:1], scalar1=1.0 / D, scalar2=EPS, op0=ALU.mult, op1=ALU.add),
                                  reads=[s_], writes=[s_])
                            yield
                            fw.op(act, lambda e: e.sqrt(out=s_[:, 2:3], in_=s_[:, 2:3]), reads=[s_], writes=[s_])
                            yield
                            fw.op(dve, lambda e: e.reciprocal(out=s_[:, 4:5], in_=s_[:, 2:3]), reads=[s_], writes=[s_])
                            yield
                            fw.op(dve, lambda e: e.scalar_tensor_tensor(out=xh_[:, :], in0=xt[:], scalar=s_[:, 4:5], in1=GS[s][0][:],
                                                                        op0=ALU.mult, op1=ALU.mult), reads=[xt, s_, GS[s][0]], writes=[q_])
                            yield
                            fw.op(dve, lambda e: e.tensor_tensor(out=xb_[:], in0=xh_[:, :], in1=GS[s][1][:], op=ALU.add),
                                  reads=[q_, GS[s][1]], writes=[xb_])
                            yield
                            PB = pbr.next()
                            for k in range(8):
                                fw.op(pe, lambda e: e.transpose(out=PB[:, k * 128:(k + 1) * 128], in_=xb_[:, k * 128:(k + 1) * 128],
                                                                identity=ident[:]), reads=[xb_, ident], writes=[PB])
                            fw.op(act, lambda e: e.copy(out=hT_[:].rearrange("p k t -> p (k t)"), in_=PB[:, :]), reads=[PB], writes=[hT_])
                            yield
                            F0, F1, F2 = pfr.next()
                            if do_kv:
                                for k in range(8):
                                    fw.op(pe, lambda e: e.matmul(F0[:, 0:512], lhsT=hT_[:, k, :], rhs=wab[:, k, :], start=(k == 0), stop=(k == 7)),
                                          reads=[hT_, wab], writes=[F0])
                            for k in range(8):
                                fw.op(pe, lambda e: e.matmul(F1[:, 0:512], lhsT=hT_[:, k, :], rhs=winb[:, k, 256:768], start=(k == 0), stop=(k == 7)),
                                      reads=[hT_, winb], writes=[F1])
                            for k in range(8):
                                fw.op(pe, lambda e: e.matmul(F2[:, 0:32], lhsT=hT_[:, k, :], rhs=winb[:, k, 768:800], start=(k == 0), stop=(k == 7)),
                                      reads=[hT_, winb], writes=[F2])
                            if do_kv:
                                fw.op(act, lambda e: e.copy(out=ab_[:], in_=F0[:, 0:512]), reads=[F0], writes=[ab_])
                            fw.op(dve, lambda e: e.tensor_copy(out=c32[:, 0:512], in_=F1[:, 0:512]), reads=[F1], writes=[c32])
                            fw.op(dve, lambda e: e.tensor_copy(out=c32[:, 512:544], in_=F2[:, 0:32]), reads=[F2], writes=[c32])
                            yield
                            if do_kv:
                                fw.dma(sp, ABd[kvc:kvc + 128, :], ab_[:], reads=[ab_], writes=[ABd])
                            fw.op(dve, lambda e: e.memset(s2[:], 0.0), writes=[s2])
                            yield
                            fw.op(act, lambda e: e.activation(out=junk[:, 0:384], in_=c32[:, 0:384], func=ACTF.Square, accum_out=s2[:, 0:1]),
                                  reads=[c32], writes=[junk, s2])
                            fw.op(act, lambda e: e.activation(out=junk[:, 384:512], in_=c32[:, 384:512], func=ACTF.Square, accum_out=s2[:, 1:2]),
                                  reads=[c32], writes=[junk, s2])
                            yield
                            fw.op(dve, lambda e: e.tensor_scalar(out=s2[:, 0:1], in0=s2[:, 0:1], scalar1=128.0 / 384.0, scalar2=None, op0=ALU.mult),
                                  reads=[s2], writes=[s2])
                            yield
                            fw.op(dve, lambda e: e.tensor_scalar(out=s2[:, 2:4], in0=s2[:, 0:2], scalar1=1.0 / 128, scalar2=EPS, op0=ALU.mult, op1=ALU.add),
                                  reads=[s2], writes=[s2])
                            yield
                            fw.op(act, lambda e: e.sqrt(out=s2[:, 2:4], in_=s2[:, 2:4]), reads=[s2], writes=[s2])
                            yield
                            fw.op(dve, lambda e: e.reciprocal(out=s2[:, 4:6], in_=s2[:, 2:4]), reads=[s2], writes=[s2])
                            yield
                            fw.op(dve, lambda e: e.tensor_scalar(out=cn_[:, 0:384], in0=c32[:, 0:384], scalar1=s2[:, 4:5], scalar2=None, op0=ALU.mult),
                                  reads=[c32, s2], writes=[cn_])
                            fw.op(pool, lambda e: e.tensor_scalar(out=cn_[:, 384:512], in0=c32[:, 384:512], scalar1=s2[:, 5:6], scalar2=None, op0=ALU.mult),
                                  reads=[c32, s2], writes=[cn_])
                            if do_kv:
                                if s == 0:
                                    fw.op(dve, lambda e: e.tensor_tensor(out=rt[:, 0, :], in0=c32[:, 512:544], in1=ropet[:, 0, :], op=ALU.mult),
                                          reads=[c32, ropet], writes=[rt])
                                    fw.op(pool, lambda e: e.tensor_tensor(out=rt[:, 1, :], in0=c32[:, 512:544], in1=ropet[:, 1, :], op=ALU.mult),
                                          reads=[c32, ropet], writes=[rt])
                                    yield
                                    fw.op(pool, lambda e: e.tensor_tensor(out=c2_[:, 64:80], in0=rt[:, 0, 0:16], in1=rt[:, 1, 16:32], op=ALU.subtract),
                                          reads=[rt], writes=[c2_])
                                    fw.op(pool, lambda e: e.tensor_tensor(out=c2_[:, 80:96], in0=rt[:, 0, 16:32], in1=rt[:, 1, 0:16], op=ALU.add),
                                          reads=[rt], writes=[c2_])
                                else:
                                    fw.op(dve, lambda e: e.tensor_copy(out=c2_[:, 64:96], in_=c32[:, 512:544]), reads=[c32], writes=[c2_])
                            yield
                            PB = pbr.next()
                            for j in range(4 if do_kv else 3):
                                fw.op(pe, lambda e: e.transpose(out=PB[:, j * 128:(j + 1) * 128], in_=cn_[:, j * 128:(j + 1) * 128],
                                                                identity=ident[:]), reads=[cn_, ident], writes=[PB])
                            if do_kv:
                                fw.op(pe, lambda e: e.transpose(out=PB[0:96, 512:640], in_=c2_[:, 0:96], identity=ident[:]),
                                      reads=[c2_, ident], writes=[PB])
                            fw.op(act, lambda e: e.copy(out=cq_[:].rearrange("p k t -> p (k t)"), in_=PB[:, 0:384]), reads=[PB], writes=[cq_])
                            if do_kv:
                                fw.op(dve, lambda e: e.tensor_copy(out=ckvnT[:, kvc:kvc + 128], in_=PB[:, 384:512]), reads=[PB], writes=[ckvnT])
                                fw.op(act, lambda e: e.copy(out=KT[0][64:96, kvc:kvc + 128], in_=PB[64:96, 512:640]), reads=[PB], writes=[KT[0]])
                                fw.op(dve, lambda e: e.tensor_copy(out=KT[1][64:96, kvc:kvc + 128], in_=PB[64:96, 512:640]), reads=[PB], writes=[KT[1]])
                            yield
                            if not do_q:
                                return
                            F0, F1, F2 = pfr.next()
                            for (pf_, c0, c1) in ((F0, 0, 512), (F1, 512, 1024), (F2, 1024, 1152)):
                                for kc in range(3):
                                    fw.op(pe, lambda e: e.matmul(pf_[:, 0:c1 - c0], lhsT=cq_[:, kc, :], rhs=wqb[:, kc, c0:c1],
                                                                 start=(kc == 0), stop=(kc == 2)), reads=[cq_, wqb], writes=[pf_])
                            qf = q_[:].rearrange("p h d -> p (h d)")
                            fw.op(act, lambda e: e.copy(out=qf[:, 0:512], in_=F0[:, 0:512]), reads=[F0], writes=[q_])
                            fw.op(dve, lambda e: e.tensor_copy(out=qf[:, 512:1024], in_=F1[:, 0:512]), reads=[F1], writes=[q_])
                            fw.op(act, lambda e: e.copy(out=qf[:, 1024:1152], in_=F2[:, 0:128]), reads=[F2], writes=[q_])
                            yield
                            fw.op(pool, lambda e: e.tensor_copy(out=qb_[:, :, 0:64], in_=q_[:, :, 0:64]), reads=[q_], writes=[qb_])
                            if s == 0:
                                def bc(ap_):
                                    return bass.AP(ap_.tensor, ap_.offset, [list(ap_.ap[0]), [0, 12], list(ap_.ap[-1])])
                                fw.op(dve, lambda e: e.tensor_tensor(out=qt_[:, 0, :, :], in0=q_[:, :, 64:96], in1=bc(ropet[:, 0, :]), op=ALU.mult),
                                      reads=[q_, ropet], writes=[qt_])
                                fw.op(pool, lambda e: e.tensor_tensor(out=qt_[:, 1, :, :], in0=q_[:, :, 64:96], in1=bc(ropet[:, 1, :]), op=ALU.mult),
                                      reads=[q_, ropet], writes=[qt_])
                                yield
                                fw.op(dve, lambda e: e.tensor_tensor(out=qb_[:, :, 64:80], in0=qt_[:, 0, :, 0:16], in1=qt_[:, 1, :, 16:32], op=ALU.subtract),
                                      reads=[qt_], writes=[qb_])
                                fw.op(pool, lambda e: e.tensor_tensor(out=qb_[:, :, 80:96], in0=qt_[:, 0, :, 16:32], in1=qt_[:, 1, :, 0:16], op=ALU.add),
                                      reads=[qt_], writes=[qb_])
                            else:
                                fw.op(dve, lambda e: e.tensor_copy(out=qb_[:, :, 64:96], in_=q_[:, :, 64:96]), reads=[q_], writes=[qb_])
                            yield
                            for rnd in range(2):
                                PB = pbr.next()
                                for h6 in range(6):
                                    h = rnd * 6 + h6
                                    fw.op(pe, lambda e: e.transpose(out=PB[0:96, h6 * 128:(h6 + 1) * 128], in_=qb_[:, h, :], identity=ident[:]),
                                          reads=[qb_, ident], writes=[PB])
                                fw.op(act if rnd else dve, (lambda e: e.copy(out=qT_[:, 6:12, :].rearrange("p h t -> p (h t)"), in_=PB[0:96, 0:768])) if rnd else
                                      (lambda e: e.tensor_copy(out=qT_[:, 0:6, :].rearrange("p h t -> p (h t)"), in_=PB[0:96, 0:768])),
                                      reads=[PB], writes=[qT_])
                                yield
                            fw.dma(sp, QTv[:, :, td["qslot"]:td["qslot"] + 128], qT_[:], reads=[qT_], writes=[QTd])

                        tds = list(TD)
                        active = []
                        nxt_i = [0]

                        free_cx = list(range(G_IN))

                        def refill():
                            while free_cx and nxt_i[0] < len(tds):
                                i_ = nxt_i[0]
                                nxt_i[0] += 1
                                g_ = free_cx.pop(0)
                                active.append((tile_gen(i_, tds[i_], CX[g_], g_), g_))

                        refill()
                        while active:
                            for item in list(active):
                                try:
                                    next(item[0])
                                except StopIteration:
                                    active.remove(item)
                                    free_cx.append(item[1])
                                    refill()

                    fw.barrier()
                    chk("IN%d" % l)
                    with ExitStack() as ph:
                        crs = fw.sb([R, 3, R], BF16, "crs", ph)
                        tw = fw.sb([R, 2, 64], F32, "tw", ph)
                        c64 = fw.sb([64, 2, 64], BF16, "c64", ph)
                        bft = fw.sb([128, 2, 256], F32, "bft", ph)
                        fw.dma(sp, crs[:], crs_d[:], reads=[crs_d], writes=[crs])
                        fw.dma(sp, tw[:], tw_d[:], reads=[tw_d], writes=[tw])
                        fw.dma(sp, c64[:], c64_d[:], reads=[c64_d], writes=[c64])
                        fw.dma(sp, bft[:], b_four[l], reads=[b_four], writes=[bft])
                        abg = Ring([fw.sb([R, 16, 512], BF16, f"abg{i}", ph) for i in range(2)])
                        y2g = Ring([fw.sb([R, 16, 2, 256], BF16, f"y2g{i}", ph) for i in range(2)])
                        y2t = Ring([fw.sb([64, 16, 2, 256], BF16, f"y2t{i}", ph) for i in range(2)])
                        fog = Ring([fw.sb([64, 16, 256], BF16, f"fog{i}", ph) for i in range(2)])
                        tt = Ring([fw.sb([R, 256], F32, f"tt{i}", ph) for i in range(4)])
                        psy = Ring([fw.ps([128, 512], F32, f"psy{i}", ph) for i in range(8)])
                        ABv = ABd.t[0:NLAT, :].rearrange("(a b) c -> a b c", b=64)
                        for g in range(4):
                            ab_ = abg.next()
                            fw.dma(sp, ab_[:], ABv[:, g * 16:(g + 1) * 16, :], reads=[ABd], writes=[ab_])
                            y2_ = y2g.next()
                            for pm in range(8):
                                pr = psy.next()
                                pi_ = psy.next()
                                a_ = ab_[:, 2 * pm:2 * pm + 2, 0:256]
                                b_ = ab_[:, 2 * pm:2 * pm + 2, 256:512]
                                prv = pr[0:R, :].rearrange("p (a b) -> p a b", b=256)
                                piv = pi_[0:R, :].rearrange("p (a b) -> p a b", b=256)
                                fw.op(pe, lambda e: e.matmul(prv, lhsT=crs[:, 0, :], rhs=a_, start=True, stop=False), reads=[crs, ab_], writes=[pr])
                                fw.op(pe, lambda e: e.matmul(prv, lhsT=crs[:, 1, :], rhs=b_, start=False, stop=True), reads=[crs, ab_], writes=[pr])
                                fw.op(pe, lambda e: e.matmul(piv, lhsT=crs[:, 0, :], rhs=b_, start=True, stop=False), reads=[crs, ab_], writes=[pi_])
                                fw.op(pe, lambda e: e.matmul(piv, lhsT=crs[:, 2, :], rhs=a_, start=False, stop=True), reads=[crs, ab_], writes=[pi_])
                                for j in range(2):
                                    ml = 2 * pm + j
                                    m2 = g * 16 + ml
                                    t1 = tt.next()
                                    t2 = tt.next()
                                    fw.op(dve, lambda e: e.tensor_scalar(out=t1[:], in0=pi_[0:R, j * 256:(j + 1) * 256], scalar1=tw[:, 1, m2:m2 + 1],
                                                                         scalar2=None, op0=ALU.mult), reads=[pi_, tw], writes=[t1])
                                    fw.op(dve, lambda e: e.scalar_tensor_tensor(out=y2_[:, ml, 0, :], in0=pr[0:R, j * 256:(j + 1) * 256],
                                                                                scalar=tw[:, 0, m2:m2 + 1], in1=t1[:], op0=ALU.mult, op1=ALU.add),
                                          reads=[pr, tw, t1], writes=[y2_])
                                    fw.op(dve, lambda e: e.tensor_scalar(out=t2[:], in0=pr[0:R, j * 256:(j + 1) * 256], scalar1=tw[:, 1, m2:m2 + 1],
                                                                         scalar2=None, op0=ALU.mult), reads=[pr, tw], writes=[t2])
                                    fw.op(dve, lambda e: e.scalar_tensor_tensor(out=y2_[:, ml, 1, :], in0=pi_[0:R, j * 256:(j + 1) * 256],
                                                                                scalar=tw[:, 0, m2:m2 + 1], in1=t2[:], op0=ALU.mult, op1=ALU.subtract),
                                          reads=[pi_, tw, t2], writes=[y2_])
                            fw.dma(pool, Y2d[:, g * 16:(g + 1) * 16, :, :], y2_[:], reads=[y2_], writes=[Y2d])
                        Y2v = Y2d.t.rearrange("a b c d -> b a c d")
                        FOv = FOURd.t[0:NLAT, :].rearrange("(a b) c -> a b c", b=R)
                        for g in range(R // 16):
                            yt_ = y2t.next()
                            fw.dma(sp, yt_[:], Y2v[:, g * 16:(g + 1) * 16, :, :], reads=[Y2d], writes=[yt_])
                            fo_ = fog.next()
                            for p2 in range(8):
                                pz = psy.next()
                                pzv = pz[0:64, :].rearrange("p (a b) -> p a b", b=256)
                                fw.op(pe, lambda e: e.matmul(pzv, lhsT=c64[:, 0, :], rhs=yt_[:, 2 * p2:2 * p2 + 2, 0, :], start=True, stop=False),
                                      reads=[c64, yt_], writes=[pz])
                                fw.op(pe, lambda e: e.matmul(pzv, lhsT=c64[:, 1, :], rhs=yt_[:, 2 * p2:2 * p2 + 2, 1, :], start=False, stop=True),
                                      reads=[c64, yt_], writes=[pz])
                                fw.op(dve, lambda e: e.tensor_tensor(out=fo_[:, 2 * p2:2 * p2 + 2, :], in0=pzv, in1=bft[0:64, :, :], op=ALU.add),
                                      reads=[pz, bft], writes=[fo_])
                            fw.dma(pool, FOv[:, g * 16:(g + 1) * 16, :], fo_[:], reads=[fo_], writes=[FOURd])
                        if not last:
                            c256 = fw.sb([128, 2, 2, 256], BF16, "c256", ph)
                            abc = fw.sb([128, 2, 512], BF16, "abc", ph)
                            foc = fw.sb([128, 2, 256], BF16, "foc", ph)
                            fw.dma(sp, c256[:], c256_d[:], reads=[c256_d], writes=[c256])
                            fw.dma(sp, abc[:], ABd.t[NLAT:NTOK, :].rearrange("(a p) c -> p a c", p=128), reads=[ABd], writes=[abc])
                            for n_ in range(2):
                                pz = psy.next()
                                for mc in range(2):
                                    fw.op(pe, lambda e: e.matmul(pz[:, 0:256], lhsT=c256[:, mc, 0, n_ * 128:(n_ + 1) * 128], rhs=abc[:, mc, 0:256],
                                                                 start=(mc == 0), stop=False), reads=[c256, abc], writes=[pz])
                                for mc in range(2):
                                    fw.op(pe, lambda e: e.matmul(pz[:, 0:256], lhsT=c256[:, mc, 1, n_ * 128:(n_ + 1) * 128], rhs=abc[:, mc, 256:512],
                                                                 start=False, stop=(mc == 1)), reads=[c256, abc], writes=[pz])
                                fw.op(dve, lambda e: e.tensor_tensor(out=foc[:, n_, :], in0=pz[:, 0:256], in1=bft[:, 0, :], op=ALU.add),
                                      reads=[pz, bft], writes=[foc])
                            fw.dma(pool, FOURd.t[NLAT:NTOK, :].rearrange("(a p) c -> p a c", p=128), foc[:], reads=[foc], writes=[FOURd])

                    fw.barrier()
                    chk("F%d" % l)
                    with ExitStack() as ph:
                        Va = [fw.sb([128, NT, 65], BF16, f"Va{i}", ph, multi=True) for i in range(2)]
                        NQ = NQS
                        qts = [fw.sb([96, NQ], BF16, f"qts{i}", ph) for i in range(2)]
                        ptr = Ring([fw.sb([128, 2, QB], BF16, f"pt{i}", ph) for i in range(3)])
                        rcr = Ring([fw.sb([128, 4], F32, f"rc{i}", ph) for i in range(2)])
                        att = Ring([fw.sb([128, 4, 64], BF16, f"att{i}", ph) for i in range(3)])
                        pss = Ring([fw.ps([128, 1024], F32, f"pss{i}", ph) for i in range(2)])
                        pso = Ring([fw.ps([128, 512], F32, f"pso{i}", ph) for i in range(2)])
                        pkv = Ring([fw.ps([128, 512], F32, f"pkv{i}", ph) for i in range(2)])
                        for i in range(2):
                            fw.op(pool, lambda e: e.memset(Va[i][:, :, 64:65], 1.0), writes=[Va[i]])
                        pstg = Ring([fw.sb([128, 1408], F32, f"pstg{i}", ph) for i in range(2)])
                        pcst = Ring([fw.sb([128, 1408], BF16, f"pcst{i}", ph) for i in range(2)])
                        WUv = WUPd.t.rearrange("c p k n -> p c k n")
                        for k in range(8):
                            for cb in range(4):
                                sg_ = pstg.next()
                                fw.dma(pool, sg_[:], w_up[l, k * 128:(k + 1) * 128, cb * 1408:(cb + 1) * 1408], reads=[w_up], writes=[sg_])
                                ct = pcst.next()
                                fw.op(pool, lambda e: e.tensor_copy(out=ct[:], in_=sg_[:]), reads=[sg_], writes=[ct])
                                fw.dma(pool, WUv[:, cb * 11:(cb + 1) * 11, k, :], ct[:].rearrange("p (c n) -> p c n", n=128),
                                       reads=[ct], writes=[WUPd])
                        for (src_w, dst_w, n_) in ((w_down, WDNd, NCH), (w_out, WOUTd, 8)):
                            for i in range(n_):
                                sg_ = pstg.next()
                                fw.dma(pool, sg_[:, 0:D], src_w[l, i * 128:(i + 1) * 128, :], reads=[src_w], writes=[sg_])
                                ct = pcst.next()
                                fw.op(pool, lambda e: e.tensor_copy(out=ct[:, 0:D], in_=sg_[:, 0:D]), reads=[sg_], writes=[ct])
                                fw.dma(pool, dst_w[i], ct[:, 0:D], reads=[ct], writes=[dst_w])

                        def build_kv(h):
                            for _ in build_kv_gen(h):
                                pass

                        def build_kv_gen(h):
                            b = h % 2
                            fw.dma(sp, qts[b][:], QTd[h, :, 0:NQ], reads=[QTd], writes=[qts[b]])
                            yield
                            c0 = 0
                            while c0 < NTOK:
                                w_ = min(512, NTOK - c0)
                                p_ = pkv.next()
                                fw.op(pe, lambda e: e.matmul(p_[0:64, 0:w_], lhsT=wkvb[:, h * 128:h * 128 + 64], rhs=ckvnT[:, c0:c0 + w_],
                                                             start=True, stop=True), reads=[wkvb, ckvnT], writes=[p_])
                                fw.op(dve, lambda e: e.tensor_copy(out=KT[b][0:64, c0:c0 + w_], in_=p_[0:64, 0:w_]), reads=[p_], writes=[KT[b]])
                                c0 += w_
                                yield
                            c0 = 0
                            while c0 < NT:
                                n_ = min(8, NT - c0)
                                p_ = pkv.next()
                                for c in range(n_):
                                    fw.op(pe, lambda e: e.matmul(p_[:, c * 64:(c + 1) * 64], lhsT=ckvnT[:, (c0 + c) * 128:(c0 + c + 1) * 128],
                                                                 rhs=wkvb[:, h * 128 + 64:h * 128 + 128], start=True, stop=True),
                                          reads=[wkvb, ckvnT], writes=[p_])
                                fw.op(pool if False else dve, lambda e: e.tensor_copy(out=Va[b][:, c0:c0 + n_, 0:64],
                                                                                     in_=p_[:, 0:n_ * 64].rearrange("p (a b) -> p a b", b=64)),
                                      reads=[p_], writes=[Va[b]])
                                c0 += n_
                                yield

                        steps = []
                        for h in range(12):
                            blks = [(q0, QB, list(range(NT))) for q0 in range(0, HALF, QB)]
                            blks.append((QS_EXT, 128, list(range(NT))))
                            if not last:
                                blks.append((QS_CTX, NCTX, list(range(NT_L, NT))))
                            for bi, (q0, W, kcs) in enumerate(blks):
                                nk = len(kcs)
                                for i0 in range(0, nk, 2):
                                    steps.append(dict(h=h, q0=q0, W=W, grp=kcs[i0:i0 + 2], i0=i0, nk=nk, first=(i0 == 0),
                                                      last=(i0 + 2 >= nk), newhead=(bi == 0 and i0 == 0)))
                        cur = {}

                        kvg = {}

                        def emit_S(st_):
                            h, q0, W = st_["h"], st_["q0"], st_["W"]
                            b = h % 2
                            if st_["newhead"] and h in kvg:
                                for _ in kvg.pop(h):
                                    pass
                            p_ = pss.next()
                            for j, kc in enumerate(st_["grp"]):
                                fw.op(pe, lambda e: e.matmul(p_[:, j * 512:j * 512 + W], lhsT=KT[b][0:96, kc * 128:(kc + 1) * 128],
                                                             rhs=qts[b][0:96, q0:q0 + W], start=True, stop=True),
                                      reads=[KT[b], qts[b]], writes=[p_])
                            st_["p"] = p_

                        def emit_exp(st_):
                            W = st_["W"]
                            p_ = st_["p"]
                            pt = ptr.next()
                            ng = len(st_["grp"])
                            fw.op(act, lambda e: e.activation(out=pt[:, 0:ng, 0:W], in_=p_[:, 0:ng * 512].rearrange("p (a b) -> p a b", b=512)[:, :, 0:W],
                                                              func=ACTF.Exp, scale=SCALE), reads=[p_], writes=[pt])
                            st_["pt"] = pt

                        def emit_PV(st_):
                            h, q0, W, i0, nk = st_["h"], st_["q0"], st_["W"], st_["i0"], st_["nk"]
                            b = h % 2
                            nq = W // 128
                            pt = st_["pt"]
                            if st_["newhead"] and h + 1 < 12:
                                kvg[h + 1] = build_kv_gen(h + 1)
                            if (h + 1) in kvg:
                                if next(kvg[h + 1], "done") == "done":
                                    kvg.pop(h + 1)
                            if st_["first"]:
                                pob = pso.next()
                                po = Buf(pob.t[:, 0:260].rearrange("p (a b) -> p a b", b=65), "po")
                                cur["pob"], cur["po"] = pob, po
                            pob, po = cur["pob"], cur["po"]
                            for j, kc in enumerate(st_["grp"]):
                                for qi in range(nq):
                                    fw.op(pe, lambda e: e.matmul(po[:, qi, :], lhsT=pt[:, j, qi * 128:(qi + 1) * 128], rhs=Va[b][:, kc, :],
                                                                 start=(i0 + j == 0 and qi == 0), stop=(i0 + j == nk - 1 and qi == nq - 1)),
                                          reads=[pt, Va[b]], writes=[pob])
                            if st_["last"]:
                                rc = rcr.next()
                                fw.op(dve, lambda e: e.reciprocal(out=rc[:, 0:nq], in_=po[:, 0:nq, 64]), reads=[pob], writes=[rc])
                                at = att.next()
                                for qi in range(nq):
                                    fw.op(dve, lambda e: e.tensor_scalar(out=at[:, qi, :], in0=po[:, qi, 0:64], scalar1=rc[:, qi:qi + 1], scalar2=None,
                                                                         op0=ALU.mult), reads=[pob, rc], writes=[at])
                                fw.dma(sp, ATTd.t[q0:q0 + W, h * 64:(h + 1) * 64].rearrange("(a p) d -> p a d", p=128), at[:, 0:nq, :],
                                       reads=[at], writes=[ATTd])

                        build_kv(0)
                        emit_S(steps[0])
                        for i_, st_ in enumerate(steps):
                            emit_exp(st_)
                            if i_ + 1 < len(steps):
                                emit_S(steps[i_ + 1])
                            emit_PV(st_)

                fw.barrier()
                chk("ATT%d" % l)
                with ExitStack() as ph:
                    stg = Ring([fw.sb([128, 1024], F32, f"stg{i}", ph) for i in range(2)])
                    woutb = fw.sb([128, 8, D], BF16, "woutb", ph, multi=True)
                    GT = [[fw.sb([128, D], F32, f"gt{s}{v}", ph) for v in range(3)] for s in range(2)]
                    for s in range(2):
                        for v, mv in enumerate((2, 4, 3)):
                            fw.dma(sp, GT[s][v][:], MB[s, mv], reads=[MB], writes=[GT[s][v]])
                    fw.dma(sp, woutb[:], WOUTd.t.rearrange("k p n -> p k n"), reads=[WOUTd], writes=[woutb])
                    G_O = 4
                    XHlv = XHTl.t.rearrange("k p t -> p k t")
                    XHcv = XHTc.t.rearrange("k p t -> p k t")
                    XHev = XHTe.t.rearrange("k p t -> p k t")
                    otds = [td for td in TD if td["q"] and not (last and td["kind"] == "ctx")]
                    OCX = []
                    for g in range(G_O):
                        OCX.append(dict(
                            m=fw.sb([128, D], BF16, f"mix{g}", ph), mT=fw.sb([128, 8, 128], BF16, f"mixT{g}", ph),
                            xt=fw.sb([128, D], F32, f"xo{g}", ph), tm=fw.sb([128, D], F32, f"tmp{g}", ph),
                            xn=fw.sb([128, D], F32, f"xn{g}", ph), junk=fw.sb([128, D], BF16, f"junk2{g}", ph),
                            s=fw.sb([128, 8], F32, f"stt{g}", ph), xh=fw.sb([128, D], F32, f"xh{g}", ph),
                            xb=fw.sb([128, D], BF16, f"xb{g}", ph), hT=fw.sb([128, 8, 128], BF16, f"hT{g}", ph),
                        ))
                    opb = Ring([fw.ps([128, 1024], BF16, f"psb{g}", ph) for g in range(2)])
                    opd = Ring([fw.ps([128, 1024], F32, f"psd{g}", ph) for g in range(3)])

                    def out_gen(ti, td, cx):
                        s = td["s"]
                        t = ti
                        m_, mT, xt, tm, xn, junk, s_, xh_, xb_, hT_ = (cx[k] for k in ("m", "mT", "xt", "tm", "xn", "junk", "s", "xh", "xb", "hT"))
                        for (p0, r0, n) in td["pieces"]:
                            fr = r0 if td["kind"] != "ctx" else NLAT + r0
                            fw.dma(sp, m_[p0:p0 + n, 0:256], FOURd[fr:fr + n, :], reads=[FOURd], writes=[m_])
                        qs = td["qslot"]
                        fw.dma(sp, m_[:, 256:1024], ATTd[qs:qs + 128, :], reads=[ATTd], writes=[m_])
                        load_rows(xt, td)
                        yield
                        pb = opb.next()
                        for k in range(8):
                            fw.op(pe, lambda e: e.transpose(out=pb[:, k * 128:(k + 1) * 128], in_=m_[:, k * 128:(k + 1) * 128], identity=ident[:]),
                                  reads=[m_, ident], writes=[pb])
                        fw.op(act, lambda e: e.copy(out=mT[:].rearrange("p k t -> p (k t)"), in_=pb[:, :]), reads=[pb], writes=[mT])
                        yield
                        pd = opd.next()
                        for hh in range(2):
                            for k in range(8):
                                fw.op(pe, lambda e: e.matmul(pd[:, hh * 512:(hh + 1) * 512], lhsT=mT[:, k, :], rhs=woutb[:, k, hh * 512:(hh + 1) * 512],
                                                             start=(k == 0), stop=(k == 7)), reads=[mT, woutb], writes=[pd])
                        fw.op(dve, lambda e: e.tensor_tensor(out=tm[:], in0=pd[:, :], in1=GT[s][0][:], op=ALU.mult), reads=[pd, GT[s][0]], writes=[tm])
                        yield
                        fw.op(pool, lambda e: e.tensor_tensor(out=xn[:], in0=tm[:], in1=xt[:], op=ALU.add), reads=[tm, xt], writes=[xn])
                        fw.op(dve, lambda e: e.memset(s_[:], 0.0), writes=[s_])
                        yield
                        if td["xres"] is not None:
                            xr_ = XRES[td["xres"]]
                            fw.dma(sp, xr_[:, :], xn[:], reads=[xn], writes=[xr_])
                        fw.op(act, lambda e: e.activation(out=junk[:], in_=xn[:], func=ACTF.Square, accum_out=s_[:, 0:1]), reads=[xn], writes=[junk, s_])
                        yield
                        fw.op(dve, lambda e: e.tensor_scalar(out=s_[:, 2:3], in0=s_[:, 0:1], scalar1=1.0 / D, scalar2=EPS, op0=ALU.mult, op1=ALU.add),
                              reads=[s_], writes=[s_])
                        yield
                        fw.op(act, lambda e: e.sqrt(out=s_[:, 2:3], in_=s_[:, 2:3]), reads=[s_], writes=[s_])
                        yield
                        fw.op(dve, lambda e: e.reciprocal(out=s_[:, 4:5], in_=s_[:, 2:3]), reads=[s_], writes=[s_])
                        yield
                        fw.op(dve, lambda e: e.scalar_tensor_tensor(out=xh_[:], in0=xn[:], scalar=s_[:, 4:5], in1=GT[s][1][:],
                                                                    op0=ALU.mult, op1=ALU.mult), reads=[xn, s_, GT[s][1]], writes=[xh_])
                        yield
                        fw.op(dve, lambda e: e.tensor_tensor(out=xb_[:], in0=xh_[:], in1=GT[s][2][:], op=ALU.add), reads=[xh_, GT[s][2]], writes=[xb_])
                        yield
                        pb = opb.next()
                        for k in range(8):
                            fw.op(pe, lambda e: e.transpose(out=pb[:, k * 128:(k + 1) * 128], in_=xb_[:, k * 128:(k + 1) * 128], identity=ident[:]),
                                  reads=[xb_, ident], writes=[pb])
                        fw.op(act, lambda e: e.copy(out=hT_[:].rearrange("p k t -> p (k t)"), in_=pb[:, :]), reads=[pb], writes=[hT_])
                        yield
                        if td["kind"] == "lat":
                            fw.dma(sp, XHlv[:, :, 1 + t * 128:1 + (t + 1) * 128], hT_[:], reads=[hT_], writes=[XHTl])
                        elif td["kind"] == "ext":
                            fw.dma(sp, XHev[:, :, :], hT_[:], reads=[hT_], writes=[XHTe])
                        else:
                            tc_ = td["xres"] - NT_L
                            fw.dma(sp, XHcv[:, :, 1 + tc_ * 128:1 + (tc_ + 1) * 128], hT_[:], reads=[hT_], writes=[XHTc])

                    active = []
                    nxt_i = [0]

                    free_ocx = list(range(G_O))

                    def refill_o():
                        while free_ocx and nxt_i[0] < len(otds):
                            i_ = nxt_i[0]
                            nxt_i[0] += 1
                            g_ = free_ocx.pop(0)
                            active.append((out_gen(i_, otds[i_], OCX[g_]), g_))

                    refill_o()
                    while active:
                        for item in list(active):
                            try:
                                next(item[0])
                            except StopIteration:
                                active.remove(item)
                                free_ocx.append(item[1])
                                refill_o()

                fw.barrier()
                chk("OUT%d" % l)
                with ExitStack() as ph:
                    wdnb = fw.sb([128, NCH, D], BF16, "wdnb", ph, multi=True)
                    wdwt = fw.sb([128, 2 * NCH, 3], F32, "wdwt", ph)
                    bdwt = fw.sb([128, 2 * NCH], F32, "bdwt", ph)
                    GT2 = [fw.sb([128, D], F32, f"gt2{s}", ph) for s in range(2)]
                    fw.dma(sp, wdwt[:], wdw[l], reads=[wdw], writes=[wdwt])
                    fw.dma(sp, bdwt[:], bdw[l], reads=[bdw], writes=[bdwt])
                    for s in range(2):
                        fw.dma(sp, GT2[s][:], MB[s, 5], reads=[MB], writes=[GT2[s]])
                    if last:
                        gfint = fw.sb([128, D], F32, "gfint", ph)
                        fw.dma(sp, gfint[:], gfin[:], reads=[gfin], writes=[gfint])
                    fw.dma(sp, wdnb[:], WDNd.t.rearrange("c p n -> p c n"), reads=[WDNd], writes=[wdnb])
                    fw.barrier()
                    wur = Ring([fw.sb([128, 8, 128], BF16, f"wur{i}", ph) for i in range(6)])
                    xhb = Ring([fw.sb([128, 8, 514], BF16, f"xhb{i}", ph) for i in range(2)])
                    yT = fw.sb([128, NCH, 512], BF16, "yT", ph)
                    cr_ = Ring([fw.sb([128, 512], F32, f"cv{i}", ph) for i in range(4)])
                    sgr = Ring([fw.sb([128, 512], F32, f"sg{i}", ph) for i in range(2)])
                    xr = Ring([fw.sb([128, D], F32, f"xf{i}", ph) for i in range(2)])
                    tmpr = Ring([fw.sb([128, D], F32, f"tf{i}", ph) for i in range(2)])
                    junk = fw.sb([128, D], BF16, "junk3", ph)
                    stt = Ring([fw.sb([128, 8], F32, f"stt{i}", ph) for i in range(2)])
                    psu = Ring([fw.ps([128, 512], F32, f"psu{i}", ph) for i in range(3)])
                    psh = Ring([fw.ps([128, 512], F32, f"psh{i}", ph) for i in range(2)])
                    psd = Ring([fw.ps([128, 1024], F32, f"psd{i}", ph) for i in range(1)])
                    XHlv = XHTl.t.rearrange("k p t -> p k t")
                    XHcv = XHTc.t.rearrange("k p t -> p k t")
                    blocks = [(0, t0, min(512, HALF - t0)) for t0 in range(0, HALF, 512)]
                    if not last:
                        blocks.append((1, 0, NCTX))
                    xe = fw.sb([128, 8, 128], BF16, "xe", ph)
                    fw.dma(sp, xe[:], XHTe.t.rearrange("k p t -> p k t"), reads=[XHTe], writes=[xe])

                    def load_blk(bi):
                        s, t0, W = blocks[bi]
                        xb_ = xhb.next()
                        v = XHlv if s == 0 else XHcv
                        n_ = HALF if s == 0 else NCTX
                        lo = 1 if t0 == 0 else 0
                        hi = W + 1 if t0 + W == n_ else W + 2
                        if lo:
                            if s == 0:
                                fw.op(dve, lambda e: e.tensor_scalar(out=xb_[:, :, 0:1], in0=xe[:, :, 127:128], scalar1=hm[:, 0:1], scalar2=None,
                                                                     op0=ALU.mult), reads=[xe, hm], writes=[xb_])
                            else:
                                fw.op(pool, lambda e: e.memset(xb_[:, :, 0:1], 0.0), writes=[xb_])
                        if hi == W + 1:
                            if s == 0:
                                fw.op(dve, lambda e: e.tensor_scalar(out=xb_[:, :, W + 1:W + 2], in0=xe[:, :, 0:1], scalar1=hm[:, 1:2], scalar2=None,
                                                                     op0=ALU.mult), reads=[xe, hm], writes=[xb_])
                            else:
                                fw.op(pool, lambda e: e.memset(xb_[:, :, W + 1:W + 2], 0.0), writes=[xb_])
                        fw.dma(sp, xb_[:, :, lo:hi], v[:, :, t0 + lo:t0 + hi], reads=[XHTl if s == 0 else XHTc], writes=[xb_])
                        return xb_

                    nxt = load_blk(0)
                    for bi, (s, t0, W) in enumerate(blocks):
                        xb_ = nxt
                        if bi + 1 < len(blocks):
                            nxt = load_blk(bi + 1)
                        for i in range(NCH):
                            cvs = []
                            for hv, ch in ((0, i), (1, NCH + i)):
                                pu = psu.next()
                                ph_ = psh.next()
                                wu = wur.next()
                                fw.dma(sp, wu[:], WUPd[ch], reads=[WUPd], writes=[wu])
                                for k in range(8):
                                    fw.op(pe, lambda e: e.matmul(pu[:, 0:W], lhsT=wu[:, k, :], rhs=xb_[:, k, 1:W + 1],
                                                                 start=(k == 0), stop=(k == 7)), reads=[wu, xb_], writes=[pu])
                                for k in range(8):
                                    fw.op(pe, lambda e: e.matmul(ph_[:, 0:2], lhsT=wu[:, k, :], rhs=xb_[:, k, 0:W + 2:W + 1],
                                                                 start=(k == 0), stop=(k == 7)), reads=[wu, xb_], writes=[ph_])
                                c_ = cr_.next()
                                w0 = wdwt[:, ch, 0:1]
                                w2 = wdwt[:, ch, 2:3]
                                fw.op(act, lambda e: e.activation(out=c_[:, 0:W], in_=pu[:, 0:W], func=ACTF.Identity, scale=wdwt[:, ch, 1:2],
                                                                  bias=bdwt[:, ch:ch + 1]), reads=[pu, wdwt, bdwt], writes=[c_])
                                fw.op(dve, lambda e: e.scalar_tensor_tensor(out=c_[:, 1:W], in0=pu[:, 0:W - 1], scalar=w0, in1=c_[:, 1:W],
                                                                            op0=ALU.mult, op1=ALU.add), reads=[pu, wdwt, c_], writes=[c_])
                                fw.op(dve, lambda e: e.scalar_tensor_tensor(out=c_[:, 0:W - 1], in0=pu[:, 1:W], scalar=w2, in1=c_[:, 0:W - 1],
                                                                            op0=ALU.mult, op1=ALU.add), reads=[pu, wdwt, c_], writes=[c_])
                                fw.op(dve, lambda e: e.scalar_tensor_tensor(out=c_[:, 0:1], in0=ph_[:, 0:1], scalar=w0, in1=c_[:, 0:1],
                                                                            op0=ALU.mult, op1=ALU.add), reads=[ph_, wdwt, c_], writes=[c_])
                                fw.op(dve, lambda e: e.scalar_tensor_tensor(out=c_[:, W - 1:W], in0=ph_[:, 1:2], scalar=w2, in1=c_[:, W - 1:W],
                                                                            op0=ALU.mult, op1=ALU.add), reads=[ph_, wdwt, c_], writes=[c_])
                                cvs.append(c_)
                            sg_ = sgr.next()
                            fw.op(act, lambda e: e.activation(out=sg_[:, 0:W], in_=cvs[0][:, 0:W], func=ACTF.Silu), reads=[cvs[0]], writes=[sg_])
                            fw.op(pool, lambda e: e.tensor_tensor(out=yT[:, i, 0:W], in0=sg_[:, 0:W], in1=cvs[1][:, 0:W], op=ALU.mult),
                                  reads=[sg_, cvs[1]], writes=[yT])
                        for qi in range(W // 128):
                            tg = (t0 // 128 + qi) if s == 0 else NT_L + qi
                            pd = psd.next()
                            for hh in range(2):
                                for i in range(NCH):
                                    fw.op(pe, lambda e: e.matmul(pd[:, hh * 512:(hh + 1) * 512], lhsT=yT[:, i, qi * 128:(qi + 1) * 128],
                                                                 rhs=wdnb[:, i, hh * 512:(hh + 1) * 512], start=(i == 0), stop=(i == NCH - 1)),
                                          reads=[yT, wdnb], writes=[pd])
                            xt = xr.next()
                            fw.dma(sp, xt[:], XRES[tg][:, :], reads=[XRES[tg]], writes=[xt])
                            tm = tmpr.next()
                            fw.op(dve, lambda e: e.tensor_tensor(out=tm[:], in0=pd[:, :], in1=GT2[s][:], op=ALU.mult), reads=[pd, GT2[s]], writes=[tm])
                            fw.op(pool, lambda e: e.tensor_tensor(out=tm[:], in0=tm[:], in1=xt[:], op=ALU.add), reads=[tm, xt], writes=[tm])
                            if not last:
                                fw.dma(pool, XRES[tg][:, :], tm[:], reads=[tm], writes=[XRES[tg]])
                            else:
                                s_ = stt.next()
                                fw.op(dve, lambda e: e.memset(s_[:], 0.0), writes=[s_])
                                fw.op(act, lambda e: e.activation(out=junk[:], in_=tm[:], func=ACTF.Square, accum_out=s_[:, 0:1]),
                                      reads=[tm], writes=[junk, s_])
                                rstd_from_ss(s_, 1, 1.0 / D)
                                fw.op(dve, lambda e: e.scalar_tensor_tensor(out=xt[:], in0=tm[:], scalar=s_[:, 4:5], in1=gfint[:],
                                                                            op0=ALU.mult, op1=ALU.mult), reads=[tm, s_, gfint], writes=[xt])
                                fw.dma(pool, out_d[tg * 128:(tg + 1) * 128, :], xt[:], reads=[xt], writes=[out_d])
                    if not last and not fw.stopped:
                        for j in range(NCK):
                            own_bufs = [XRES[2 * j], XRES[2 * j + 1]]
                            fw._deps(pool, own_bufs, [XAG[j]])
                            ins_ = nc.gpsimd.collective_compute("AllGather", ALU.bypass, replica_groups=[[0, 1], [2, 3], [4, 5], [6, 7]],
                                                                ins=[XOWN.t[j * 256:(j + 1) * 256, :]], outs=[XAG[j].t])
                            fw.cc_cnt += 1
                            ins_.then_inc(fw.cc_sem)
                            fw._record((fw.cc_sem, fw.cc_cnt), own_bufs, [XAG[j]])
                        for j in range(NT_O):
                            a_ = xr.next()
                            b_ = tmpr.next()
                            g_ = XAG[j // 2]
                            o_ = (j % 2) * 128
                            fw.dma(sp, a_[:], g_[o_:o_ + 128, :], reads=[g_], writes=[a_])
                            fw.dma(sp, b_[:], g_[256 + o_:256 + o_ + 128, :], reads=[g_], writes=[b_])
                            fw.op(dve, lambda e: e.tensor_scalar(out=a_[:], in0=a_[:], scalar1=hm[:, 2:3], scalar2=None, op0=ALU.mult),
                                  reads=[a_, hm], writes=[a_])
                            fw.op(dve, lambda e: e.scalar_tensor_tensor(out=b_[:], in0=b_[:], scalar=hm[:, 3:4], in1=a_[:], op0=ALU.mult, op1=ALU.add),
                                  reads=[b_, hm, a_], writes=[b_])
                            fw.dma(pool, XOTH[j * 128:(j + 1) * 128, :], b_[:], reads=[b_], writes=[XOTH])

        except _Stop:
            pass
        fw.barrier()
        fw.finish([out_d])
        stats = {e.name: (e.nins, e.nwait) for e in (pe, act, dve, pool, sp)}
    return nc, stats


def _consts(NLAT, hf):
    R = NLAT // 64
    NT_L = NLAT // 128
    HALF = NLAT // 2
    c = {}
    c["ident"] = np.eye(128, dtype=np.float32).astype(bf)
    sel = np.zeros((2, 256), np.float32)
    sel[0, :128] = 1.0
    sel[1, 128:] = 1.0
    c["sel"] = sel
    i64 = np.arange(64)
    a64 = 2 * np.pi * np.outer(i64, i64) / 64.0
    C64 = np.cos(a64) / 8.0
    S64 = np.sin(a64) / 8.0
    C4 = np.kron(np.eye(4), C64)
    S4 = np.kron(np.eye(4), S64)
    c4s4 = np.stack([C4, S4], 0).reshape(2, 2, 128, 256).transpose(2, 1, 0, 3)
    c["c4s4"] = np.ascontiguousarray(c4s4).astype(bf)
    ir = np.arange(R)
    ar = 2 * np.pi * np.outer(ir, ir) / R
    CR = np.cos(ar) / np.sqrt(R)
    SR = np.sin(ar) / np.sqrt(R)
    C64p, S64p = C64, S64
    if hf:
        sg = (-1.0) ** ir
        CR = CR * sg[None, :]
        SR = SR * sg[None, :]
        perm = (i64 + 32) % 64
        C64p, S64p = C64[:, perm], S64[:, perm]
    c["crs"] = np.ascontiguousarray(np.stack([CR, SR, -SR], 1)).astype(bf)
    at = 2 * np.pi * np.outer(ir, i64) / NLAT
    c["tw"] = np.ascontiguousarray(np.stack([np.cos(at), np.sin(at)], 1)).astype(np.float32)
    c["c64"] = np.ascontiguousarray(np.stack([C64p, S64p], 1)).astype(bf)
    i256 = np.arange(256)
    a256 = 2 * np.pi * np.outer(i256, i256) / 256.0
    C256 = np.cos(a256) / 16.0
    S256 = np.sin(a256) / 16.0
    c256 = np.stack([C256, S256], 0).reshape(2, 2, 128, 256).transpose(2, 1, 0, 3)
    c["c256"] = np.ascontiguousarray(c256).astype(bf)
    loc = np.arange(NLAT)
    ext = np.concatenate([np.arange(HALF, HALF + 64), np.arange(NLAT - 64, NLAT)])
    loc = np.concatenate([loc, ext])
    tok = (loc + hf * HALF) % NLAT
    row = (tok // 64).astype(np.float64)
    col = (tok % 64).astype(np.float64)
    inv = 10000.0 ** (-np.arange(8, dtype=np.float64) / 8)
    ang = np.concatenate([row[:, None] * inv, col[:, None] * inv], -1).astype(np.float32)
    cs = np.stack([np.cos(ang), np.sin(ang)], 1)
    cs = np.concatenate([cs, cs], -1)
    c["rope"] = np.ascontiguousarray(cs.reshape(NT_L + 1, 128, 2, 32).transpose(1, 0, 2, 3)).astype(np.float32)
    hm = np.zeros((128, 4), np.float32)
    hm[:, 0] = float(hf)
    hm[:, 1] = float(1 - hf)
    hm[:, 2] = float(hf)
    hm[:, 3] = float(1 - hf)
    c["hm"] = hm
    return c


def make_inmaps(inputs, NLAT, nb):
    f = lambda a: np.ascontiguousarray(np.asarray(a, dtype=np.float32))
    L = inputs["w_ada"].shape[0]
    HALF = NLAT // 2
    shared = dict(
        w_ada=f(inputs["w_ada"]),
        b_ada2=f(np.repeat(np.asarray(inputs["b_ada"])[:, None, :], 2, 1)),
        gmix2=f(np.repeat(np.asarray(inputs["g_mix"])[:, None, :], 2, 1)),
        gffn2=f(np.repeat(np.asarray(inputs["g_ffn"])[:, None, :], 2, 1)),
        gfin=f(np.repeat(np.asarray(inputs["g_final"])[None, :], 128, 0)),
        w_in=f(inputs["w_in"]),
        w_four=f(inputs["w_fourier"]),
        b_four=f(np.broadcast_to(np.asarray(inputs["b_fourier"])[:, None, None, :], (L, 128, 2, 256))),
        gq=f(np.asarray(inputs["g_q_a"]).reshape(L, 3, 128).transpose(0, 2, 1)),
        w_qb=f(inputs["w_q_b"]),
        gkv=f(np.asarray(inputs["g_kv_a"]).reshape(L, 128, 1)),
        w_kvb=f(inputs["w_kv_b"]),
        w_out=f(inputs["w_out"]),
        w_up=f(inputs["w_up"]),
        wdw=f(np.asarray(inputs["w_dw"]).reshape(L, 3, 2 * NCH, 128).transpose(0, 3, 2, 1)),
        bdw=f(np.asarray(inputs["b_dw"]).reshape(L, 2 * NCH, 128).transpose(0, 2, 1)),
        w_down=f(inputs["w_down"]),
    )
    cons = [_consts(NLAT, 0), _consts(NLAT, 1)]
    maps = []
    x = np.asarray(inputs["x"])
    c = np.asarray(inputs["c"], dtype=np.float32)
    ctx = np.asarray(inputs["ctx"])
    cc = np.asarray(inputs["c_ctx"], dtype=np.float32)
    for b in range(nb):
        cv = np.stack([c[b].reshape(8, 128).T, cc.reshape(8, 128).T], -1)
        for hf in range(2):
            m = dict(shared)
            m.update(cons[hf])
            m["x"] = f(np.roll(x[b], -hf * HALF, axis=0))
            m["ctx"] = f(ctx[b])
            m["cvec"] = f(cv)
            maps.append(m)
    return maps


_CACHE = {}


def kernel(**inputs):
    x = np.asarray(inputs["x"])
    B, NLAT, _ = x.shape
    HALF = NLAT // 2
    key = (NLAT,)
    if key not in _CACHE:
        _CACHE[key] = build_program(NLAT)
    nc, _ = _CACHE[key]
    maps = make_inmaps(inputs, NLAT, B)
    assert len(maps) == 8
    res = run_bass_kernel_spmd(nc, maps, core_ids=list(range(8)))
    out = np.empty((B, NLAT, D), np.float32)
    for b in range(B):
        for hf in range(2):
            out[b, hf * HALF:(hf + 1) * HALF] = np.asarray(res.results[2 * b + hf]["out"], dtype=np.float32)
    return out
```
